# Optimizing a Trainium2 kernel written in Bass

```python
import jax, jax.numpy as jnp
from jax import lax
import numpy as np

D_MODEL = 1024
BATCH = 2
SEQ = 16384
DEPTH = 2
DEC_BATCH = 32
DEC_SEQ = 32
PAST_LEN = 2048

CHUNK = 64
PLE_DIM = 256
NORM_EPS = 1e-6

A_HEADS = 8
A_HEAD_DIM = 64
A_WIDTH = A_HEADS * A_HEAD_DIM
A_PREV_CHUNKS = 8
A_WIN = A_PREV_CHUNKS * CHUNK
A_BAND = A_WIN + CHUNK
A_REL_MAX = 256
A_REL_SIZE = CHUNK + A_REL_MAX

B_HEADS = 4
B_HEAD_DIM = 128
B_WIDTH = B_HEADS * B_HEAD_DIM
ROPE_BASE = 10000.0

C_HEADS = 8
C_HEAD_DIM = 64
C_WIDTH = C_HEADS * C_HEAD_DIM
C_RANK_W = 64
C_RANK_A = 64
C_RANK_G = 128
C_SIZES = (C_WIDTH, C_WIDTH, C_WIDTH, C_RANK_W, C_RANK_A, C_RANK_G)
C_SHIFT_WIDTH = sum(C_SIZES)
C_GN_EPS = 64e-5

N_BRANCHES = 3
BRANCH_WIDTH = 512
D_FF = ((8 * D_MODEL // 3 + 255) // 256) * 256
IN_SIZES = (A_WIDTH, A_WIDTH, A_WIDTH, B_WIDTH, B_WIDTH, B_WIDTH, B_WIDTH, C_SHIFT_WIDTH, N_BRANCHES * D_MODEL)
IN_WIDTH = sum(IN_SIZES)

kernel_name = "hybrid_streaming_encoder_step"


def _split(z, sizes):
    out, off = [], 0
    for s in sizes:
        out.append(z[..., off:off + s])
        off += s
    return out


def _rms_norm(x, gain=None):
    xf = x.astype(jnp.float32)
    y = xf * lax.rsqrt(jnp.mean(xf * xf, axis=-1, keepdims=True) + NORM_EPS)
    if gain is not None:
        y = y * gain.astype(jnp.float32)
    return y.astype(x.dtype)


def _band_attend(q, k, v, q_pos, k_pos, rel_bias):
    f32 = jnp.float32
    s = jnp.einsum("bqhd,bkhd->bhqk", q.astype(f32), k.astype(f32)) * (A_HEAD_DIM ** -0.5)
    rel = q_pos[:, None] - k_pos[None, :]
    idx = jnp.clip(rel, -(CHUNK - 1), A_REL_MAX) + (CHUNK - 1)
    s = s + rel_bias.astype(f32)[:, idx][None]
    q_chunk = q_pos // CHUNK
    k_chunk = k_pos // CHUNK
    ok = (k_pos[None, :] >= 0) & (k_chunk[None, :] <= q_chunk[:, None]) & (k_chunk[None, :] >= q_chunk[:, None] - A_PREV_CHUNKS)
    s = jnp.where(ok[None, None], s, -1e30)
    p = jax.nn.softmax(s, axis=-1)
    return jnp.einsum("bhqk,bkhd->bqhd", p, v.astype(f32)).astype(q.dtype)


def _band_attention_prompt(q, k, v, rel_bias):
    B, L, H, Dh = q.shape
    pad = jnp.zeros((B, A_WIN, H, Dh), k.dtype)
    kp = jnp.concatenate([pad, k], axis=1)
    vp = jnp.concatenate([pad, v], axis=1)

    def one_chunk(c):
        start = c * CHUNK
        q_c = lax.dynamic_slice_in_dim(q, start, CHUNK, axis=1)
        k_b = lax.dynamic_slice_in_dim(kp, start, A_BAND, axis=1)
        v_b = lax.dynamic_slice_in_dim(vp, start, A_BAND, axis=1)
        q_pos = start + jnp.arange(CHUNK, dtype=jnp.int32)
        k_pos = start - A_WIN + jnp.arange(A_BAND, dtype=jnp.int32)
        return _band_attend(q_c, k_b, v_b, q_pos, k_pos, rel_bias)

    out = lax.map(one_chunk, jnp.arange(L // CHUNK, dtype=jnp.int32))
    return jnp.moveaxis(out, 0, 1).reshape(B, L, H * Dh)


def _rotary(x, pos):
    half = x.shape[-1] // 2
    inv = ROPE_BASE ** (-jnp.arange(half, dtype=jnp.float32) / half)
    ang = pos.astype(jnp.float32)[:, None] * inv[None, :]
    cos = jnp.cos(ang)[None, :, None, :]
    sin = jnp.sin(ang)[None, :, None, :]
    x1, x2 = x[..., :half], x[..., half:]
    return jnp.concatenate([x1 * cos - x2 * sin, x1 * sin + x2 * cos], axis=-1)


def _retention_log_gamma():
    return jnp.log1p(-jnp.exp2(-5.0 - jnp.arange(B_HEADS, dtype=jnp.float32)))


def _retention_chunk(S, qkv):
    q, k, v = qkv
    L = q.shape[1]
    log_g = _retention_log_gamma()
    n = jnp.arange(L)
    diff = n[:, None] - n[None, :]
    dmat = jnp.where(diff >= 0, jnp.exp(log_g[:, None, None] * jnp.maximum(diff, 0)), 0.0)
    scores = jnp.einsum("blhd,bmhd->bhlm", q, k) * dmat[None]
    o_inner = jnp.einsum("bhlm,bmhe->blhe", scores, v)
    decay_q = jnp.exp(log_g[None, :] * (n[:, None] + 1))
    o_cross = jnp.einsum("blhd,bhde->blhe", q, S) * decay_q[None, :, :, None]
    decay_k = jnp.exp(log_g[None, :] * (L - 1 - n)[:, None])
    S_new = jnp.exp(log_g * L)[None, :, None, None] * S + jnp.einsum("blhd,blhe->bhde", k * decay_k[None, :, :, None], v)
    return S_new, o_inner + o_cross


def _retention(q, k, v, S0):
    B, L, H, _ = q.shape
    c = min(CHUNK, L)
    nc = L // c

    def to_chunks(t):
        return jnp.moveaxis(t.reshape(B, nc, c, H, t.shape[-1]), 1, 0)

    S, o = lax.scan(_retention_chunk, S0, (to_chunks(q), to_chunks(k), to_chunks(v)))
    return jnp.moveaxis(o, 0, 1).reshape(B, L, H, -1), S


def _token_shift(z, prev, mu):
    z_prev = jnp.concatenate([prev, z[:, :-1]], axis=1)
    return z + (z_prev - z) * mu


def _rwkv7_scan(r, w, k, v, kk, a, S0):
    def step(S, inp):
        r_t, w_t, k_t, v_t, kk_t, a_t = inp
        sa = jnp.einsum("bhij,bhj->bhi", S, -kk_t)
        S = S * w_t[:, :, None, :] + sa[..., None] * (kk_t * a_t)[:, :, None, :] + v_t[..., None] * k_t[:, :, None, :]
        return S, jnp.einsum("bhij,bhj->bhi", S, r_t)

    xs = tuple(jnp.moveaxis(t, 1, 0) for t in (r, w, k, v, kk, a))
    S, y = lax.scan(step, S0, xs)
    return jnp.moveaxis(y, 0, 1), S


def _rwkv7(r, k, v, w_lo, a_lo, g_lo, S0, lw):
    B, L, _ = r.shape
    f32 = jnp.float32
    w_log = -jax.nn.softplus(-(lw["c_w0"].astype(f32) + jnp.tanh(w_lo) @ lw["c_w2"].astype(f32))) - 0.5
    decay = jnp.exp(-jnp.exp(w_log))
    a = jax.nn.sigmoid(lw["c_a0"].astype(f32) + a_lo @ lw["c_a2"].astype(f32))
    g = jax.nn.sigmoid(g_lo) @ lw["c_g2"].astype(f32)

    def heads(t):
        return t.reshape(B, L, C_HEADS, C_HEAD_DIM)

    kk = heads(k * lw["c_k_k"].astype(f32))
    kk = kk / jnp.maximum(jnp.sqrt(jnp.sum(kk * kk, axis=-1, keepdims=True)), 1e-12)
    k = k * (1.0 + (a - 1.0) * lw["c_k_a"].astype(f32))
    rh, kh, vh = heads(r), heads(k), heads(v)
    y, S = _rwkv7_scan(rh, heads(decay), kh, vh, kk, heads(a), S0)
    mu = jnp.mean(y, axis=-1, keepdims=True)
    var = jnp.mean(jnp.square(y - mu), axis=-1, keepdims=True)
    yn = ((y - mu) * lax.rsqrt(var + C_GN_EPS)).reshape(B, L, C_WIDTH) * lw["c_ln_w"].astype(f32) + lw["c_ln_b"].astype(f32)
    bonus = jnp.sum(rh * kh * lw["c_r_k"].astype(f32), axis=-1, keepdims=True) * vh
    return (yn + bonus.reshape(B, L, C_WIDTH)) * g, S


def _layer(x, p_l, pos0, a_ck, a_cv, ret_s0, rwkv_s0, shift_prev, lw):
    B, L, _ = x.shape
    dt = x.dtype
    f32 = jnp.float32
    pos = pos0 + jnp.arange(L, dtype=jnp.int32)
    h = _rms_norm(x, lw["norm_mix"])
    z = h @ lw["w_in"]
    aq, ak, av, bq, bk, bv, bg, cz, gl = _split(z, IN_SIZES)

    aq = _rms_norm(aq.reshape(B, L, A_HEADS, A_HEAD_DIM), lw["a_q_norm"])
    ak = _rms_norm(ak.reshape(B, L, A_HEADS, A_HEAD_DIM), lw["a_k_norm"])
    av = av.reshape(B, L, A_HEADS, A_HEAD_DIM)
    if a_ck is None:
        oa = _band_attention_prompt(aq, ak, av, lw["a_rel_bias"])
        keep = min(A_WIN, L)
        new_ak, new_av = ak[:, L - keep:], av[:, L - keep:]
    else:
        n_c = a_ck.shape[1]
        k_all = jnp.concatenate([a_ck.astype(dt), ak], axis=1)
        v_all = jnp.concatenate([a_cv.astype(dt), av], axis=1)
        k_pos = jnp.concatenate([pos0 - n_c + jnp.arange(n_c, dtype=jnp.int32), pos])
        oa = _band_attend(aq, k_all, v_all, pos, k_pos, lw["a_rel_bias"]).reshape(B, L, A_WIDTH)
        new_ak, new_av = ak, av

    bq = _rotary(bq.reshape(B, L, B_HEADS, B_HEAD_DIM).astype(f32), pos)
    bk = _rotary(bk.reshape(B, L, B_HEADS, B_HEAD_DIM).astype(f32), pos) * (B_HEAD_DIM ** -0.5)
    bv = bv.reshape(B, L, B_HEADS, B_HEAD_DIM).astype(f32)
    ob, new_ret = _retention(bq, bk, bv, ret_s0.astype(f32))
    ob = (_rms_norm(ob).reshape(B, L, B_WIDTH) * jax.nn.silu(bg.astype(f32))).astype(dt)

    cs = _token_shift(cz, shift_prev.astype(dt), lw["c_shift_mu"])
    new_shift = cz[:, -1:]
    cr, ck, cv, cw_lo, ca_lo, cg_lo = _split(cs.astype(f32), C_SIZES)
    oc, new_rwkv = _rwkv7(cr, ck, cv, cw_lo, ca_lo, cg_lo, rwkv_s0.astype(f32), lw)
    oc = oc.astype(dt)

    gl = gl.reshape(B, L, N_BRANCHES, D_MODEL)
    wb = lw["w_branch"]
    m = (jax.nn.sigmoid(gl[:, :, 0]) * (oa @ wb[0])
         + jax.nn.sigmoid(gl[:, :, 1]) * (ob @ wb[1])
         + jax.nn.sigmoid(gl[:, :, 2]) * (oc @ wb[2]))
    x = x + m @ lw["w_out"]

    hf = _rms_norm(x, lw["norm_ffn"])
    x = x + (jax.nn.silu(hf @ lw["w_ffn_gate"]) * (hf @ lw["w_ffn_up"])) @ lw["w_ffn_down"]

    e = _rms_norm(p_l.astype(dt) @ lw["w_ple_proj"], lw["ple_norm"])
    gate = jax.nn.sigmoid(_rms_norm(x) @ lw["w_ple_gate"])
    x = x + gate * e
    return x, (new_ak, new_av, new_ret, new_rwkv, new_shift)


def setup_inputs(seed: int = 0) -> dict:
    key = jax.random.key(seed)
    keys = jax.random.split(key, 40)
    cnt = [0]
    f32 = jnp.float32

    def nk():
        cnt[0] += 1
        return keys[cnt[0] - 1]

    def nrm(shape, scale):
        return scale * jax.random.normal(nk(), shape, f32)

    def unif(shape, lo, hi):
        return jax.random.uniform(nk(), shape, f32, lo, hi)

    a_cache = min(A_WIN, PAST_LEN)
    return {
        "x_prompt": nrm((BATCH, SEQ, D_MODEL), 1.0),
        "x_sample": nrm((DEC_BATCH, DEC_SEQ, D_MODEL), 1.0),
        "p_prompt": nrm((DEPTH, BATCH, SEQ, PLE_DIM), 1.0),
        "p_sample": nrm((DEPTH, DEC_BATCH, DEC_SEQ, PLE_DIM), 1.0),
        "cache_a_k": nrm((DEPTH, DEC_BATCH, a_cache, A_HEADS, A_HEAD_DIM), 1.0),
        "cache_a_v": nrm((DEPTH, DEC_BATCH, a_cache, A_HEADS, A_HEAD_DIM), 1.0),
        "state_ret": nrm((DEPTH, DEC_BATCH, B_HEADS, B_HEAD_DIM, B_HEAD_DIM), 0.5),
        "state_rwkv": nrm((DEPTH, DEC_BATCH, C_HEADS, C_HEAD_DIM, C_HEAD_DIM), 0.5),
        "state_rwkv_shift": nrm((DEPTH, DEC_BATCH, 1, C_SHIFT_WIDTH), 1.0),
        "norm_mix": 1.0 + nrm((DEPTH, D_MODEL), 0.02),
        "w_in": nrm((DEPTH, D_MODEL, IN_WIDTH), D_MODEL ** -0.5),
        "a_q_norm": 1.0 + nrm((DEPTH, A_HEAD_DIM), 0.02),
        "a_k_norm": 1.0 + nrm((DEPTH, A_HEAD_DIM), 0.02),
        "a_rel_bias": nrm((DEPTH, A_HEADS, A_REL_SIZE), 0.1),
        "c_shift_mu": unif((DEPTH, C_SHIFT_WIDTH), 0.1, 0.9),
        "c_w0": unif((DEPTH, C_WIDTH), -6.5, -1.5),
        "c_w2": nrm((DEPTH, C_RANK_W, C_WIDTH), 0.1),
        "c_a0": nrm((DEPTH, C_WIDTH), 0.1),
        "c_a2": nrm((DEPTH, C_RANK_A, C_WIDTH), 0.1),
        "c_g2": nrm((DEPTH, C_RANK_G, C_WIDTH), C_RANK_G ** -0.5),
        "c_k_k": 0.85 + nrm((DEPTH, C_WIDTH), 0.02),
        "c_k_a": 1.0 + nrm((DEPTH, C_WIDTH), 0.02),
        "c_r_k": nrm((DEPTH, C_HEADS, C_HEAD_DIM), 0.1),
        "c_ln_w": 1.0 + nrm((DEPTH, C_WIDTH), 0.02),
        "c_ln_b": nrm((DEPTH, C_WIDTH), 0.01),
        "w_branch": nrm((DEPTH, N_BRANCHES, BRANCH_WIDTH, D_MODEL), BRANCH_WIDTH ** -0.5),
        "w_out": nrm((DEPTH, D_MODEL, D_MODEL), D_MODEL ** -0.5),
        "norm_ffn": 1.0 + nrm((DEPTH, D_MODEL), 0.02),
        "w_ffn_gate": nrm((DEPTH, D_MODEL, D_FF), D_MODEL ** -0.5),
        "w_ffn_up": nrm((DEPTH, D_MODEL, D_FF), D_MODEL ** -0.5),
        "w_ffn_down": nrm((DEPTH, D_FF, D_MODEL), D_FF ** -0.5),
        "w_ple_proj": nrm((DEPTH, PLE_DIM, D_MODEL), PLE_DIM ** -0.5),
        "ple_norm": 1.0 + nrm((DEPTH, D_MODEL), 0.02),
        "w_ple_gate": nrm((DEPTH, D_MODEL, D_MODEL), D_MODEL ** -0.5),
    }


def reference(x_prompt, x_sample, p_prompt, p_sample, cache_a_k, cache_a_v, state_ret, state_rwkv, state_rwkv_shift,
              norm_mix, w_in, a_q_norm, a_k_norm, a_rel_bias, c_shift_mu, c_w0, c_w2, c_a0, c_a2, c_g2,
              c_k_k, c_k_a, c_r_k, c_ln_w, c_ln_b, w_branch, w_out, norm_ffn, w_ffn_gate, w_ffn_up, w_ffn_down,
              w_ple_proj, ple_norm, w_ple_gate):
    def layer_weights(i):
        return dict(norm_mix=norm_mix[i], w_in=w_in[i], a_q_norm=a_q_norm[i], a_k_norm=a_k_norm[i],
                    a_rel_bias=a_rel_bias[i], c_shift_mu=c_shift_mu[i], c_w0=c_w0[i], c_w2=c_w2[i],
                    c_a0=c_a0[i], c_a2=c_a2[i], c_g2=c_g2[i], c_k_k=c_k_k[i], c_k_a=c_k_a[i], c_r_k=c_r_k[i],
                    c_ln_w=c_ln_w[i], c_ln_b=c_ln_b[i], w_branch=w_branch[i], w_out=w_out[i],
                    norm_ffn=norm_ffn[i], w_ffn_gate=w_ffn_gate[i], w_ffn_up=w_ffn_up[i],
                    w_ffn_down=w_ffn_down[i], w_ple_proj=w_ple_proj[i], ple_norm=ple_norm[i],
                    w_ple_gate=w_ple_gate[i])

    bp = x_prompt.shape[0]
    ret0 = jnp.zeros((bp, B_HEADS, B_HEAD_DIM, B_HEAD_DIM), jnp.float32)
    rwkv0 = jnp.zeros((bp, C_HEADS, C_HEAD_DIM, C_HEAD_DIM), jnp.float32)
    shift0 = jnp.zeros((bp, 1, C_SHIFT_WIDTH), x_prompt.dtype)
    y_prompt = x_prompt
    st_p = []
    for i in range(DEPTH):
        y_prompt, st = _layer(y_prompt, p_prompt[i], 0, None, None, ret0, rwkv0, shift0, layer_weights(i))
        st_p.append(st)

    y_sample = x_sample
    st_s = []
    for i in range(DEPTH):
        y_sample, st = _layer(y_sample, p_sample[i], PAST_LEN, cache_a_k[i], cache_a_v[i], state_ret[i],
                              state_rwkv[i], state_rwkv_shift[i], layer_weights(i))
        st_s.append(st)

    a_k_prompt = jnp.stack([s[0] for s in st_p])
    a_v_prompt = jnp.stack([s[1] for s in st_p])
    ret_prompt = jnp.stack([s[2] for s in st_p])
    rwkv_prompt = jnp.stack([s[3] for s in st_p])
    shift_prompt = jnp.stack([s[4] for s in st_p])
    a_k_sample = jnp.stack([s[0] for s in st_s])
    a_v_sample = jnp.stack([s[1] for s in st_s])
    ret_sample = jnp.stack([s[2] for s in st_s])
    rwkv_sample = jnp.stack([s[3] for s in st_s])
    shift_sample = jnp.stack([s[4] for s in st_s])
    return (y_prompt, y_sample, a_k_prompt, a_v_prompt, ret_prompt, rwkv_prompt, shift_prompt,
            a_k_sample, a_v_sample, ret_sample, rwkv_sample, shift_sample)
```

```python
import numpy as np
from contextlib import ExitStack
import concourse.bass as bass
import concourse.mybir as mybir
from concourse.bass_utils import run_bass_kernel_spmd

F32 = mybir.dt.float32
BF16 = mybir.dt.bfloat16
ALU = mybir.AluOpType
AF = mybir.ActivationFunctionType
AX = mybir.AxisListType

D = 1024
KC = 8
INW = 8448
DFF = 2816
NFF = 22
PLE = 256
DEPTH = 2
NSMP = 4
LS = 32
TP = 512
PAST = 2048
NEG = -30000.0
CDEC = 0.6065306597126334
GN_EPS = 64e-5

_c = {}
_o = 0
for _n, _w in [("ident", 128), ("ones", 128), ("blk", 128), ("m01", 128), ("msk", 256), ("lsm", 64),
               ("cm64", 512), ("cm32", 128),
               ("gc128", 4), ("gc32", 4), ("tq128", 4), ("tk128", 4), ("tq32", 4), ("tk32", 4),
               ("eps6", 1), ("eps12", 1), ("gneps", 1), ("one", 1), ("negone", 1), ("zero", 1)]:
    _c[_n] = _o
    _o += _w
NCST = _o
_p = {}
_o = 0
for _n, _w in [("nmix", 8), ("nffn", 8), ("nple", 8), ("aqn", 1), ("akn", 1), ("mu", 14), ("w0", 4), ("a0", 4),
               ("kk", 4), ("ka", 4), ("rk", 4), ("lnw", 4), ("lnb", 4)]:
    _p[_n] = _o
    _o += _w
NPRM = _o


def _gammas():
    return 1.0 - np.exp2(-5.0 - np.arange(4, dtype=np.float64))


def host_consts(SEQ):
    g = _gammas()
    cst = np.zeros((128, NCST), np.float32)
    p = np.arange(128)
    cst[:, _c["ident"]:_c["ident"] + 128] = np.eye(128)
    cst[:, _c["ones"]:_c["ones"] + 128] = 1.0
    cst[:, _c["blk"]:_c["blk"] + 128] = (p[:, None] // 64 == p[None, :] // 64)
    cst[:, _c["m01"]:_c["m01"] + 128] = (p[None, :] >= p[:, None])
    m = np.arange(64)
    strict = (m[:, None] < m[None, :]).astype(np.float32)
    incl = (m[:, None] <= m[None, :]).astype(np.float32)
    msk = np.stack([strict, incl, strict, incl], axis=1)
    cst[:64, _c["msk"]:_c["msk"] + 256] = msk.reshape(64, 256)
    cst[:64, _c["lsm"]:_c["lsm"] + 64] = (m[None, :] < m[:, None])
    t = np.arange(512)
    cst[:, _c["cm64"]:_c["cm64"] + 512] = (t % 64 != 0).astype(np.float32)[None]
    cst[:, _c["cm32"]:_c["cm32"] + 128] = (t[:128] % 32 != 0).astype(np.float32)[None]
    cst[:, _c["gc128"]:_c["gc128"] + 4] = (g ** 128)[None]
    cst[:, _c["gc32"]:_c["gc32"] + 4] = (g ** 32)[None]
    sc = 128.0 ** -0.5
    cst[:, _c["tq128"]:_c["tq128"] + 4] = g[None, :] ** (p + 1.0)[:, None]
    cst[:, _c["tk128"]:_c["tk128"] + 4] = sc * g[None, :] ** (-(p + 1.0))[:, None]
    cst[:, _c["tq32"]:_c["tq32"] + 4] = g[None, :] ** ((p % 32) + 1.0)[:, None]
    cst[:, _c["tk32"]:_c["tk32"] + 4] = sc * g[None, :] ** (-((p % 32) + 1.0))[:, None]
    cst[:, _c["eps6"]] = 1e-6
    cst[:, _c["eps12"]] = 1e-12
    cst[:, _c["gneps"]] = GN_EPS
    cst[:, _c["one"]] = 1.0
    cst[:, _c["negone"]] = -1.0
    inv = (10000.0 ** (-np.arange(64, dtype=np.float32) / np.float32(64))).astype(np.float32)

    def rope(pos):
        ang = (pos.astype(np.float32)[:, None] * inv[None, :]).astype(np.float32)
        return np.concatenate([np.cos(ang), np.sin(ang)], axis=1).astype(np.float32)
    rope_p = rope(np.arange(SEQ))
    rope_s = np.tile(rope(PAST + np.arange(LS)), (NSMP, 1))
    return cst, rope_p, rope_s


def host_bias(rb):
    j = np.arange(128)[:, None, None]
    kt = np.arange(5)[None, :, None]
    qq = np.arange(128)[None, None, :]
    kk = 128 * kt + j
    idx = np.clip(512 + qq - kk, -63, 256) + 63
    ok = np.where(qq < 64, kk < 576, kk >= 64)
    out = rb[:, :, idx]
    out = np.where(ok[None, None], out, np.float32(NEG)).astype(np.float32)
    slot_h = [(s_ % 4) * 2 + s_ // 4 for s_ in range(8)]
    out = out[:, slot_h]
    return np.ascontiguousarray(out.transpose(0, 2, 3, 1, 4))


def host_params(inp):
    prm = np.zeros((DEPTH, 128, NPRM), np.float32)

    def fm(v, n):
        return v.reshape(n, 128).T
    for l in range(DEPTH):
        prm[l, :, _p["nmix"]:_p["nmix"] + 8] = fm(inp["norm_mix"][l], 8)
        prm[l, :, _p["nffn"]:_p["nffn"] + 8] = fm(inp["norm_ffn"][l], 8)
        prm[l, :, _p["nple"]:_p["nple"] + 8] = fm(inp["ple_norm"][l], 8)
        prm[l, :, _p["aqn"]] = np.tile(inp["a_q_norm"][l], 2)
        prm[l, :, _p["akn"]] = np.tile(inp["a_k_norm"][l], 2)
        prm[l, :, _p["mu"]:_p["mu"] + 14] = fm(inp["c_shift_mu"][l], 14)
        for nm, key in [("w0", "c_w0"), ("a0", "c_a0"), ("kk", "c_k_k"), ("ka", "c_k_a"), ("lnw", "c_ln_w"),
                        ("lnb", "c_ln_b")]:
            prm[l, :, _p[nm]:_p[nm] + 4] = fm(inp[key][l], 4)
        prm[l, :, _p["rk"]:_p["rk"] + 4] = fm(inp["c_r_k"][l].reshape(-1), 4)
    cwa2 = np.concatenate([inp["c_w2"], inp["c_a2"]], axis=1).astype(np.float32)
    return prm, np.ascontiguousarray(cwa2)


class Prog:
    ENGS = ("pe", "act", "dve", "pool", "sp")

    def __init__(self):
        self.ins = []
        self.last_w = {}
        self.readers = {}
        self.readonly = set()
        self.sub = {}
        self.excl = set()

    def keys(self, x):
        if isinstance(x, str):
            return [x]
        if isinstance(x, tuple):
            return [self.name(x[0]) + ":" + str(x[1])]
        n = self.name(x)
        if n in self.sub:
            return [n + ":" + str(i) for i in range(self.sub[n])]
        return [n]

    @staticmethod
    def name(x):
        t = getattr(x, "tensor", x)
        return t.name

    def op(self, eng, fn, reads=(), writes=(), dma=None):
        i = len(self.ins)
        deps = set()
        for r in reads:
            for k in self.keys(r):
                if k in self.readonly:
                    continue
                if k in self.last_w:
                    deps.add(self.last_w[k])
                rd = self.readers.setdefault(k, {})
                if k.split(":")[0] in self.excl:
                    for ek, r in rd.items():
                        if ek != eng:
                            deps.add(r)
                rd[("d", i) if dma is not None else eng] = i
        for w in writes:
            for k in self.keys(w):
                if k in self.last_w:
                    deps.add(self.last_w[k])
                for r in self.readers.get(k, {}).values():
                    if r != i:
                        deps.add(r)
                self.last_w[k] = i
                self.readers[k] = {}
        self.ins.append([eng, fn, dma, sorted(deps)])
        return i

    def emit(self, nc, es):
        ins = self.ins
        n = len(ins)
        need = [False] * n
        chans = {}
        for i, (eng, fn, dma, deps) in enumerate(ins):
            if dma is not None:
                need[i] = True
                chans.setdefault(dma, 0)
            for j in deps:
                ej, _, dj, _ = ins[j]
                if dj is not None or dma is not None or ej != eng or eng != "pe":
                    need[j] = True
        comp = [None] * n
        cnt = {e: 0 for e in self.ENGS}
        for i, (eng, fn, dma, deps) in enumerate(ins):
            if dma is not None:
                chans[dma] += 16
                comp[i] = ("d_" + dma, chans[dma])
            elif need[i]:
                cnt[eng] += 1
                comp[i] = ("e_" + eng, cnt[eng])
        sems = {}
        for s in ["e_" + e for e in self.ENGS] + ["d_" + c for c in chans]:
            sems[s] = es.enter_context(nc.semaphore(s))
        self.n_sems = len(sems)
        final = {("d_" + c): v for c, v in chans.items()}
        block = es.enter_context(nc.Block())
        per_eng = {e: [] for e in self.ENGS}
        for i, rec in enumerate(ins):
            per_eng[rec[0]].append(i)

        def run(eng_name, e):
            waited = {}
            for i in per_eng[eng_name]:
                _, fn, dma, deps = ins[i]
                wl = {}
                for j in deps:
                    ej, _, dj, _ = ins[j]
                    if dj is None and dma is None and ej == eng_name and eng_name == "pe":
                        continue
                    s, v = comp[j]
                    if wl.get(s, 0) < v:
                        wl[s] = v
                for s, v in wl.items():
                    if waited.get(s, 0) >= v:
                        continue
                    e.wait_ge(sems[s], v)
                    waited[s] = v
                r = fn(e)
                if comp[i] is not None:
                    r.then_inc(sems[comp[i][0]], 16 if dma is not None else 1)
            if eng_name == "sp":
                for s, v in final.items():
                    e.wait_ge(sems[s], v)

        @block.tensor
        def _(e):
            run("pe", e)

        @block.scalar
        def _(e):
            run("act", e)

        @block.vector
        def _(e):
            run("dve", e)

        @block.gpsimd
        def _(e):
            run("pool", e)

        @block.sync
        def _(e):
            run("sp", e)


def build(SEQ, dbg=False, skip=""):
    import os as _os
    assert SEQ % TP == 0
    NG = SEQ // TP
    TSM = NSMP * LS
    nc = bass.Bass("TRN2", target_bir_lowering=False)
    P = Prog()
    es = ExitStack()

    def din(name, shape):
        P.readonly.add(name)
        return nc.dram_tensor(name, list(shape), F32, kind="ExternalInput").ap()

    def dout(name, shape):
        return nc.dram_tensor(name, list(shape), F32, kind="ExternalOutput").ap()

    xp = din("xp", [SEQ, D]); pp = din("pp", [DEPTH, SEQ, PLE])
    xs = din("xs", [TSM, D]); psm = din("psm", [DEPTH, TSM, PLE])
    cak = din("cak", [DEPTH, NSMP, 512, 512]); cav = din("cav", [DEPTH, NSMP, 512, 512])
    sret = din("sret", [DEPTH, NSMP, 4, 128, 128]); srw = din("srw", [DEPTH, NSMP, 8, 64, 64])
    ssh = din("ssh", [DEPTH, NSMP, 1792])
    w_in = din("w_in", [DEPTH, D, INW]); w_br = din("w_br", [DEPTH, 3, 512, D]); w_out = din("w_out", [DEPTH, D, D])
    w_fg = din("w_fg", [DEPTH, D, DFF]); w_fu = din("w_fu", [DEPTH, D, DFF]); w_fd = din("w_fd", [DEPTH, DFF, D])
    w_pp = din("w_pp", [DEPTH, PLE, D]); w_pg = din("w_pg", [DEPTH, D, D])
    cg2 = din("cg2", [DEPTH, 128, 512]); cwa2 = din("cwa2", [DEPTH, 128, 512])
    prm_d = din("prm", [DEPTH, 128, NPRM]); cst_d = din("cst", [128, NCST])
    bias_d = din("biasT", [DEPTH, 128, 5 * 1024])
    rope_p = din("rope_p", [SEQ, 128]); rope_s = din("rope_s", [TSM, 128])

    yp = dout("yp", [SEQ, D]); ys = dout("ys", [TSM, D])
    akp = dout("akp", [DEPTH, 512, 512]); avp = dout("avp", [DEPTH, 512, 512])
    retp = dout("retp", [DEPTH, 4, 128, 128]); rwp = dout("rwp", [DEPTH, 8, 64, 64]); shp = dout("shp", [DEPTH, 1792])
    aks = dout("aks", [DEPTH, TSM, 512]); avs = dout("avs", [DEPTH, TSM, 512])
    rets = dout("rets", [DEPTH, NSMP, 4, 128, 128]); rws = dout("rws", [DEPTH, NSMP, 8, 64, 64])
    shs = dout("shs", [DEPTH, NSMP, 1792])
    wsrc = dict(w_in=w_in, w_br=w_br, w_out=w_out, w_fg=w_fg, w_fu=w_fu, w_fd=w_fd, w_pp=w_pp, w_pg=w_pg)
    wbf = {k: nc.dram_tensor(k + "_b", list(v.shape), BF16).ap() for k, v in wsrc.items()}
    x1p = nc.dram_tensor("x1p", [KC, 128, SEQ], F32).ap()
    x1s = nc.dram_tensor("x1s", [KC, 128, TSM], F32).ap()
    dbg_t = {}
    if dbg:
        for nm in ("d_oa", "d_ob", "d_oc"):
            dbg_t[nm] = dout(nm, [128, 4 * TP])
        dbg_t["d_x"] = dout("d_x", [128, KC * TP])

    with es:
        def sb(name, shape, dt=F32):
            return es.enter_context(nc.sbuf_tensor(name, list(shape), dt))

        def ps(name, shape, dt=F32, sub=None):
            if sub:
                P.sub[name] = sub
            P.excl.add(name)
            return es.enter_context(nc.psum_tensor(name, list(shape), dt))

        cst = sb("cst_sb", [128, NCST])
        prm = sb("prm_sb", [128, DEPTH * NPRM])
        identb = sb("identb", [128, 128], BF16)
        onesb = sb("onesb", [128, 128], BF16)
        blkb = sb("blkb", [128, 128], BF16)
        sqb = [sb(f"sqb{i}", [128, TP], BF16) for i in range(2)]
        aqs = sb("aqs", [128, DEPTH])
        cwa2b = sb("cwa2b", [128, 512], BF16)
        cg2b = sb("cg2b", [128, 512], BF16)
        biasT = sb("biasT_sb", [128, 5 * 1024], BF16)
        xT = sb("xT", [128, KC * TP])
        hT = sb("hT", [128, KC * TP], BF16)
        NWB = 3
        WBN = 4096
        wb = [sb(f"wb{i}", [128, WBN], BF16) for i in range(NWB)]
        Kwin = sb("Kwin", [128, 4 * 1024], BF16)
        Vwin = sb("Vwin", [128, 8 * 520], BF16)
        Sret = sb("Sret", [128, 512]); Sretb = sb("Sretb", [128, 512], BF16)
        Srw = sb("Srw", [128, 512]); Srwb = sb("Srwb", [128, 512], BF16)
        shcol = sb("shcol", [128, 14])
        oT3 = [sb(f"oT{b}", [128, 4 * TP], BF16) for b in range(3)]
        FA = sb("FA", [128, 4 * 2048]); P.sub["FA"] = 4
        HA = sb("HA", [128, 6 * 2048], BF16); P.sub["HA"] = 6
        tmp = [sb(f"tmp{i}", [128, TP + 1]) for i in range(6)]
        t1, t2, t3, t4, t5, t6 = tmp
        ropet = [sb("ropet0", [128, 128])] * 2
        scsb = sb("scsb", [128, 1024])
        xin = scsb
        ptsb = [sb(f"ptsb{i}", [128, 1024], BF16) for i in range(2)]
        oatm = sb("oatm", [128, 512], BF16)
        scs = oatm
        rden = sb("rden", [128, 8])
        lob = sb("lob", [128, TP], BF16); sgl = sb("sgl", [128, TP], BF16)
        MM = sb("MM", [64, 4 * 512], BF16)
        CB = [sb(f"CB{i}", [64, 1024], BF16) for i in range(2)]
        AT_ = [sb(f"AaT{i}", [64, 512], BF16) for i in range(2)]
        hbk = sb("hbk", [128, 2 * 256], BF16)
        BhTM = sb("BhTM", [64, 512], BF16); KhTM = sb("KhTM", [64, 512], BF16)
        Vb = sb("Vb", [64, 512], BF16); Vpad = sb("Vpad", [64, 1024], BF16)
        Tf0 = sb("Tf0", [64, 512], BF16)
        Gpad = sb("Gpad", [64, 1024], BF16); Ub = sb("Ub", [64, 512], BF16); Upad = sb("Upad", [64, 1024], BF16)
        stmp = sb("stmp", [128, 512])
        otr = stmp
        rwcp = stmp[:, 0:256]
        wc = sb("wc", [128, 4 * 8])
        rwst = scsb[0:64, 0:512]
        shin = sb("shin", [128, 14 * NSMP]); shout = sb("shout", [128, 14 * NSMP])

        PA = ps("PA", [128, 1024], sub=2); PB = ps("PB", [128, 1024], sub=2); PC = ps("PC", [128, 1024], sub=2)
        PD = ps("PD", [128, 512]); PT = ps("PT", [128, 1024], BF16)

        def C(n, w=1):
            return cst[:, _c[n]:_c[n] + w]

        def PR(l, n, i=0, w=1):
            return prm[:, l * NPRM + _p[n] + i:l * NPRM + _p[n] + i + w]

        def A(x):
            return x[0] if isinstance(x, tuple) else x

        def mm(out, lhsT, rhs, start=True, stop=True, skip=False):
            kw = dict(skip_group_check=True) if skip else {}
            P.op("pe", lambda e: e.matmul(A(out), A(lhsT), A(rhs), start=start, stop=stop, **kw), [lhsT, rhs], [out])

        def tr(out, in_, ident):
            P.op("pe", lambda e: e.transpose(A(out), A(in_), A(ident)), [in_, ident], [out])

        def act(out, in_, func, bias=None, scale=1.0, eng="act"):
            kw = {}
            rd = [in_]
            if bias is not None:
                kw["bias"] = A(bias)
                if not isinstance(bias, float):
                    rd.append(bias)
            if not isinstance(scale, float):
                rd.append(scale)
            P.op(eng, lambda e: e.activation(out=A(out), in_=A(in_), func=func, scale=A(scale), **kw), rd, [out])

        def tt(out, in0, in1, op, eng="dve"):
            P.op(eng, lambda e: e.tensor_tensor(out=A(out), in0=A(in0), in1=A(in1), op=op), [in0, in1], [out])

        def ts(out, in0, s1, op0, s2=None, op1=None, eng="dve"):
            rd = [in0] + [s for s in (s1, s2) if s is not None and not isinstance(s, float)]
            kw = {}
            if op1 is not None:
                kw["op1"] = op1
            P.op(eng, lambda e: e.tensor_scalar(out=A(out), in0=A(in0), scalar1=A(s1),
                                                scalar2=(A(s2) if s2 is not None else None), op0=op0, **kw), rd, [out])

        def stt(out, in0, s, in1, op0, op1, eng="dve"):
            rd = [in0, in1] + ([s] if not isinstance(s, float) else [])
            P.op("dve", lambda e: e.scalar_tensor_tensor(out=A(out), in0=A(in0), scalar=A(s), in1=A(in1), op0=op0, op1=op1),
                 rd, [out])

        def cp(out, in_, eng="dve"):
            if eng == "act":
                P.op("act", lambda e: e.copy(out=A(out), in_=A(in_)), [in_], [out])
            else:
                P.op(eng, lambda e: e.tensor_copy(out=A(out), in_=A(in_)), [in_], [out])

        def recip(out, in_):
            P.op("dve", lambda e: e.reciprocal(out=A(out), in_=A(in_)), [in_], [out])

        def mset(out, val, eng="pool"):
            P.op(eng, lambda e: e.memset(A(out), val), [], [out])

        def scan(out, d0, d1, eng="dve"):
            P.op(eng, lambda e: e.tensor_tensor_scan(out=A(out), data0=A(d0), data1=A(d1), initial=0.0,
                                                     op0=ALU.mult, op1=ALU.add), [d0, d1], [out])

        def dma(out, in_, eng="sp", chan=None, wkey=None, rkey=None, slow=False):
            on = Prog.name(A(out))
            wr = [wkey] if wkey is not None else ([out] if on not in P.readonly else [])
            rd = (list(rkey) if isinstance(rkey, (list, tuple)) else [rkey]) if rkey is not None else [in_]
            ch = chan or (P.keys(wr[0])[0] if wr else Prog.name(A(in_)))
            ch = ch.replace(":", "_")
            kw = dict(allow_slow_non_contiguous=True) if slow else {}
            P.op(eng, lambda e: e.dma_start(out=A(out), in_=A(in_), **kw), rd, wr, dma=ch)

        def r3(ap, a):
            return ap.rearrange("p (a b) -> p a b", a=a)

        def bc_mid(base, n_mid):
            (ps_, rows), (st, w) = base.ap
            return bass.AP(base.tensor, base.offset, [[ps_, rows], [0, n_mid], [st, w]])

        def bc_in(base, w):
            (ps_, rows), (st, n) = base.ap
            return bass.AP(base.tensor, base.offset, [[ps_, rows], [st, n], [0, w]])

        def fa(k, j=0, c0=0, c1=TP):
            return (FA[:, k * 2048 + j * TP + c0:k * 2048 + j * TP + c1], k)

        def ha(k, j=0, c0=0, c1=TP, r0=0, r1=128):
            return (HA[r0:r1, k * 2048 + j * TP + c0:k * 2048 + j * TP + c1], k)

        class WS:
            def __init__(self):
                self.seq = []
                self.issued = 0
                self.used = 0

            def plan(self, l):
                w_in, w_br, w_out, w_fg, w_fu, w_fd, w_pp, w_pg = (wbf[k] for k in
                                                                      ("w_in", "w_br", "w_out", "w_fg", "w_fu", "w_fd", "w_pp", "w_pg"))
                wi = w_in[l]

                def colblk(src, c0, ncol, tag):
                    nk = src.shape[0] // 128
                    self.seq.append((tag, src[:, c0:c0 + ncol].rearrange("(k p) c -> p k c", p=128), nk, ncol))
                for i, tg in enumerate(["aq", "ak", "av", "bq", "bk", "bv", "bg"]):
                    colblk(wi, 512 * i, 512, tg)
                colblk(wi, 5120, 256, "cl")
                colblk(wi, 4096, 512, "ck")
                colblk(wi, 3584, 512, "cr")
                colblk(wi, 4608, 512, "cv")
                for b in range(3):
                    colblk(wi, 5376 + 1024 * b, 512, f"gl{b}a")
                    colblk(wi, 5376 + 1024 * b + 512, 512, f"gl{b}b")
                    colblk(w_br[l, b], 0, 1024, f"br{b}")
                colblk(w_out[l], 0, 512, "outa")
                colblk(w_out[l], 512, 512, "outb")
                for i in range(6):
                    ncol = 512 if i < 5 else 256
                    colblk(w_fg[l], 512 * i, ncol, f"fg{i}")
                    colblk(w_fu[l], 512 * i, ncol, f"fu{i}")
                for j in range(8):
                    colblk(w_fd[l], 128 * j, 128, f"fd{j}")
                colblk(w_pp[l], 0, 1024, "pp")
                colblk(w_pg[l], 0, 512, "pga")
                colblk(w_pg[l], 512, 512, "pgb")

            def _issue(self):
                i = self.issued
                tag, src, nk, ncol = self.seq[i]
                buf = wb[i % NWB]
                assert nk * ncol <= WBN
                dma(r3(buf[:, 0:nk * ncol], nk), src, eng="sp", rkey=["wcast:%d" % i_ for i_ in range(4)])
                self.issued += 1

            def get(self, tag):
                i = self.used
                assert self.seq[i][0] == tag, (self.seq[i][0], tag)
                while self.issued < min(len(self.seq), i + NWB - 1):
                    self._issue()
                self.used += 1
                _, _, nk, ncol = self.seq[i]
                buf = wb[i % NWB]
                return (lambda kc, c0, c1: buf[:, kc * ncol + c0:kc * ncol + c1]), nk, ncol

        W = WS()
        banks = [(PA, 0), (PA, 1), (PB, 0), (PB, 1)]
        bstate = [0]

        def nbank():
            t_, h = banks[bstate[0] % 4]
            bstate[0] += 1
            return (t_[:, h * 512:(h + 1) * 512], h)

        def bk(pb, r0, r1, c0, c1):
            return (pb[0][r0:r1, c0:c1], pb[1])

        def fm_proj(pb, T, wf, nk, c0, rhs_fn):
            for kc in range(nk):
                mm(bk(pb, 0, 128, 0, T), wf(kc, c0, c0 + 128), rhs_fn(kc), start=(kc == 0), stop=(kc == nk - 1))

        identf = C("ident", 128)
        onesf = C("ones", 128)
        blkf = C("blk", 128)

        dma(cst[:], cst_d)
        dma(r3(prm[:], DEPTH), prm_d.rearrange("l p n -> p l n"))
        cp(identb[:], identf, eng="pool")
        cp(onesb[:], onesf, eng="pool")
        cp(blkb[:], blkf, eng="pool")
        for l in range(DEPTH):
            ts(aqs[:, l:l + 1], PR(l, "aqn"), 0.125, ALU.mult)
        ci_ = 0
        for k_, src_ in wsrc.items():
            s2 = src_.flatten_outer_dims() if len(src_.shape) > 2 else src_
            d2 = wbf[k_].flatten_outer_dims() if len(src_.shape) > 2 else wbf[k_]
            rows = s2.shape[0]
            step = 256 if s2.shape[1] >= 4096 else 1024
            for r0 in range(0, rows, step):
                r1_ = min(rows, r0 + step)
                dma(d2[r0:r1_, :], s2[r0:r1_, :], eng="pool", wkey="wcast:%d" % (ci_ % 4), chan="wcast%d" % (ci_ % 4))
                ci_ += 1
        mset(Vwin[:], 1.0)
        for t_ in (Vpad, Gpad, Upad):
            mset(t_[:], 0.0)

        def load_x_tm(src, T):
            for ti in range(T // 128):
                dma(xin[:], src[ti * 128:(ti + 1) * 128, :])
                for half in range(2):
                    pb = nbank()
                    for k4 in range(4):
                        kc = half * 4 + k4
                        tr(bk(pb, 0, 128, k4 * 128, (k4 + 1) * 128), xin[:, kc * 128:(kc + 1) * 128], identf)
                    dst = r3(xT[:, half * 4 * TP:(half * 4 + 4) * TP], 4)[:, :, ti * 128:(ti + 1) * 128]
                    cp(dst, (r3(pb[0], 4), pb[1]), eng="act" if half else "dve")

        def store_x_tm(dst, T):
            for ti in range(T // 128):
                for half in range(2):
                    pb = nbank()
                    for k4 in range(4):
                        kc = half * 4 + k4
                        tr(bk(pb, 0, 128, k4 * 128, (k4 + 1) * 128), xT[:, kc * TP + ti * 128:kc * TP + (ti + 1) * 128], identf)
                    cp(xin[:, half * 512:(half + 1) * 512], pb, eng="act" if half else "dve")
                dma(dst[ti * 128:(ti + 1) * 128, :], xin[:])

        def rms_stats(T, dst, srcs_fn, nk, div, epsname, lhs=None):
            lhs = lhs if lhs is not None else onesb[:]
            for kc in range(nk):
                sq = sqb[kc % 2]
                act(sq[:, 0:T], srcs_fn(kc), AF.Square)
                mm(PD[:, 0:T], lhs, sq[:, 0:T], start=(kc == 0), stop=(kc == nk - 1))
            act(dst, PD[:, 0:T], AF.Ln, bias=C(epsname), scale=1.0 / div)
            act(dst, dst, AF.Exp, scale=-0.5)

        def norm_to_hT(l, T, gain):
            rms_stats(T, t3[:, 0:T], lambda kc: xT[:, kc * TP:kc * TP + T], KC, float(D), "eps6")
            for kc in range(KC):
                eng = "dve" if kc % 2 == 0 else "pool"
                if gain is None:
                    tt(hT[:, kc * TP:kc * TP + T], xT[:, kc * TP:kc * TP + T], t3[:, 0:T], ALU.mult, eng=eng)
                else:
                    stt(hT[:, kc * TP:kc * TP + T], xT[:, kc * TP:kc * TP + T], PR(l, gain, kc), t3[:, 0:T],
                        ALU.mult, ALU.mult, eng=eng)

        def hrhs(T):
            return lambda kc: hT[:, kc * TP:kc * TP + T]

        def qk_norm(pb, T, gain_ap, dst_f32):
            cp(t4[:, 0:T], bk(pb, 0, 128, 0, T), eng="act")
            act(sqb[0][:, 0:T], bk(pb, 0, 128, 0, T), AF.Square)
            mm(PD[:, 0:T], blkb[:], sqb[0][:, 0:T])
            act(t6[:, 0:T], PD[:, 0:T], AF.Ln, bias=C("eps6"), scale=1.0 / 64)
            act(t6[:, 0:T], t6[:, 0:T], AF.Exp, scale=-0.5)
            stt(dst_f32, t4[:, 0:T], gain_ap, t6[:, 0:T], ALU.mult, ALU.mult)

        def mixer_A(l, T, G):
            kslot = G["kslot"]
            wf, nk, _ = W.get("aq")
            for j in range(4):
                pb = nbank()
                fm_proj(pb, T, wf, nk, j * 128, hrhs(T))
                qk_norm(pb, T, aqs[:, l:l + 1], ha(0, j, 0, T))
            wf, nk, _ = W.get("ak")
            for j in range(4):
                pb = nbank()
                fm_proj(pb, T, wf, nk, j * 128, hrhs(T))
                qk_norm(pb, T, PR(l, "akn"), fa(0, j, 0, T))
                cp(Kwin[:, j * 1024 + kslot * 512:j * 1024 + kslot * 512 + T], fa(0, j, 0, T), eng="pool")
            for (dst_k, dst_v, t0, L) in G["a_out"]:
                pb = nbank()
                for j in range(4):
                    tr(bk(pb, 0, L, j * 128, (j + 1) * 128), fa(0, j, t0, t0 + L), identf)
                cp(otr[0:L, :], bk(pb, 0, L, 0, 512), eng="act")
                dma(dst_k, otr[0:L, :])
            wf, nk, _ = W.get("av")
            for (t0, L, vtile) in G["vt"]:
                pb = nbank()
                for kc in range(nk):
                    mm(bk(pb, 0, L, 0, 512), hT[:, kc * TP + t0:kc * TP + t0 + L], wf(kc, 0, 512),
                       start=(kc == 0), stop=(kc == nk - 1))
                vdst = Vwin[0:L, vtile * 520:(vtile + 1) * 520].rearrange("p (h d) -> p h d", h=8)[:, :, 0:64]
                cp(vdst, (pb[0][0:L, :].rearrange("p (h d) -> p h d", h=8), pb[1]), eng="act")
                for (dst_k, dst_v, ot0, oL) in G["a_out"]:
                    if ot0 == t0 and oL == L:
                        cp(otr[0:L, :], bk(pb, 0, L, 0, 512), eng="dve")
                        dma(dst_v, otr[0:L, :])
            oaT = oT3[0]
            for sg in G["segs"]:
                if sg.get("cache") is not None:
                    b = sg["cache"]
                    dma((r3(HA[:, 2048:4096], 4), 1), cak[l, b].rearrange("(t p) c -> p t c", p=128), eng="pool")
                    for t_ in range(4):
                        dma(Vwin[:, t_ * 520:(t_ + 1) * 520].rearrange("p (h d) -> p h d", h=8)[:, :, 0:64],
                            cav[l, b, t_ * 128:(t_ + 1) * 128, :].rearrange("p (h d) -> p h d", h=8), eng="pool")
                    for t_ in range(4):
                        for j in range(4):
                            tr(PT[:, j * 128:(j + 1) * 128], (HA[:, 2048 + t_ * 512 + j * 128:2048 + t_ * 512 + (j + 1) * 128], 1), identb[:])
                        dst = r3(Kwin[:, :], 4)[:, :, t_ * 128:(t_ + 1) * 128]
                        cp(dst, r3(PT[:, 0:512], 4), eng="act" if t_ % 2 else "dve")
                for qb in sg["qblocks"]:
                    q0, nq, kts = qb["q0"], qb["nq"], qb["kts"]
                    def scores(ki):
                        kcol, nkk, vtile, bkt = kts[ki]
                        scp = PA if ki % 2 == 0 else PB
                        for h in range(8):
                            j, b0 = h // 2, (h % 2) * 64
                            sl_ = (h % 2) * 4 + h // 2
                            mm((scp[0:nkk, sl_ * 128:sl_ * 128 + nq], sl_ // 4),
                               Kwin[b0:b0 + 64, j * 1024 + kcol:j * 1024 + kcol + nkk],
                               ha(0, j, q0, q0 + nq, b0, b0 + 64))

                    def soft_pv(ki):
                        kcol, nkk, vtile, bkt = kts[ki]
                        scp = PA if ki % 2 == 0 else PB
                        tt(r3(scsb[0:nkk, :], 8)[:, :, 0:nq], r3(scp[0:nkk, :], 8)[:, :, 0:nq],
                           r3(biasT[0:nkk, bkt * 1024:(bkt + 1) * 1024], 8)[:, :, 0:nq], ALU.add)
                        pt_ = ptsb[ki % 2]
                        act(r3(pt_[0:nkk, :], 8)[:, :, 0:nq], r3(scsb[0:nkk, :], 8)[:, :, 0:nq], AF.Exp)
                        for h in range(8):
                            half = h // 4
                            oc0 = half * 512 + (h % 4) * 65
                            sl_ = (h % 2) * 4 + h // 2
                            mm((PC[0:nq, oc0:oc0 + 65], half), pt_[0:nkk, sl_ * 128:sl_ * 128 + nq],
                               Vwin[0:nkk, vtile * 520 + h * 65:vtile * 520 + (h + 1) * 65],
                               start=(ki == 0 and h % 4 == 0), stop=(ki == len(kts) - 1), skip=True)
                    scores(0)
                    for ki in range(len(kts)):
                        if ki + 1 < len(kts):
                            scores(ki + 1)
                        soft_pv(ki)
                    for half in range(2):
                        ov = PC[0:nq, half * 512:half * 512 + 260].rearrange("p (h d) -> p h d", h=4)
                        recip(rden[0:nq, half * 4:half * 4 + 4], (ov[:, :, 64], half))
                        tt(oatm[0:nq, half * 256:(half + 1) * 256].rearrange("p (h d) -> p h d", h=4), (ov[:, :, 0:64], half),
                           bc_in(rden[0:nq, half * 4:half * 4 + 4], 64), ALU.mult)
                    for j in range(4):
                        tr(PT[:, j * 128:j * 128 + nq], oatm[0:nq, j * 128:(j + 1) * 128], identb[0:nq, 0:nq])
                    cp(r3(oaT[:, :], 4)[:, :, q0:q0 + nq], r3(PT[:, 0:512], 4)[:, :, 0:nq], eng="act")

        def mixer_B(l, T, G):
            obT = oT3[1]
            for bi, blk in enumerate(("bq", "bk", "bv")):
                wf, nk, _ = W.get(blk)
                for (t0, L, idx, Cc, rp_src) in G["tm"]:
                    pb = nbank()
                    for kc in range(nk):
                        mm(bk(pb, 0, L, 0, 512), hT[:, kc * TP + t0:kc * TP + t0 + L], wf(kc, 0, 512),
                           start=(kc == 0), stop=(kc == nk - 1))
                    src = bk(pb, 0, L, 0, 512)
                    if blk == "bv":
                        if _os.environ.get("KV", "ab").find("a") >= 0:
                            cp(ha(4, idx, 0, 512, 0, L), src, eng="act")
                        if _os.environ.get("KV", "ab").find("b") >= 0:
                            tt((r3(ha(5, idx, 0, 512, 0, L)[0], 4), 5), (r3(src[0], 4), src[1]),
                               bc_in(C("gc%d" % Cc, 4)[0:L, :], 128), ALU.mult)
                        continue
                    rp = ropet[bi % 2]
                    dma(rp[0:L, :], rp_src)
                    x4 = pb[0][0:L, :].rearrange("p (h d) -> p h d", h=4)
                    x1 = (x4[:, :, 0:64], pb[1]); x2 = (x4[:, :, 64:128], pb[1])
                    cosb = bc_mid(rp[0:L, 0:64], 4)
                    sinb = bc_mid(rp[0:L, 64:128], 4)
                    a1 = r3(t4[0:L, 0:256], 4); a2 = r3(t5[0:L, 0:256], 4)
                    o4 = r3(t6[0:L, 0:512], 4)
                    tt(a1, x1, cosb, ALU.mult)
                    tt(a2, x2, sinb, ALU.mult)
                    tt(o4[:, :, 0:64], a1, a2, ALU.subtract, eng="pool")
                    tt(a1, x1, sinb, ALU.mult)
                    tt(a2, x2, cosb, ALU.mult)
                    tt(o4[:, :, 64:128], a1, a2, ALU.add, eng="pool")
                    slot = 2 if blk == "bq" else 3
                    tname = ("tq%d" if blk == "bq" else "tk%d") % Cc
                    dst = ha(slot, idx, 0, 512, 0, L)
                    tt((r3(dst[0], 4), slot), o4, bc_in(C(tname, 4)[0:L, :], 128), ALU.mult)
                    for h in range(4):
                        tr(PT[:, h * 128:h * 128 + L], ha(slot, idx, h * 128, (h + 1) * 128, 0, L), identb[0:L, 0:L])
                    fslot = 0 if blk == "bq" else 1
                    cp((r3(HA[:, fslot * 2048:(fslot + 1) * 2048], 4)[:, :, t0:t0 + L], fslot), r3(PT[:, 0:512], 4)[:, :, 0:L],
                       eng="act")
                chk("B_" + blk)
            wf, nk, _ = W.get("bg")
            for h in range(4):
                pb = nbank()
                fm_proj(pb, T, wf, nk, h * 128, hrhs(T))
                act(fa(0, h, 0, T), bk(pb, 0, 128, 0, T), AF.Silu)
            chk("B_bg")
            for sg in G["segs"]:
                if sg["state_in"] == "zero":
                    mset(Sret[:], 0.0)
                    mset(Sretb[:], 0.0)
                elif sg["state_in"] is not None:
                    dma(r3(Sret[:], 4), sret[l, sg["state_in"]].rearrange("h d e -> d h e"))
                    cp(Sretb[:], Sret[:], eng="act")
                for (t0, L, idx, Cc, _rs) in sg["tm"]:
                    for h in range(4):
                        mm((PC[0:L, h * 128:h * 128 + L], 0), ha(1, h, t0, t0 + L), ha(0, h, t0, t0 + L))
                    tt(r3(scs[0:L, :], 4)[:, :, 0:L], (r3(PC[0:L, 0:512], 4)[:, :, 0:L], 0),
                       bc_mid(C("m01", 128)[0:L, 0:L], 4), ALU.mult)
                    for h in range(4):
                        mm((PC[:, 512 + h * 128:512 + h * 128 + L], 1), ha(4, idx, h * 128, (h + 1) * 128, 0, L),
                           scs[0:L, h * 128:h * 128 + L], start=True, stop=False)
                        mm((PC[:, 512 + h * 128:512 + h * 128 + L], 1), Sretb[:, h * 128:(h + 1) * 128],
                           ha(0, h, t0, t0 + L), start=False, stop=True)
                    cp((r3(FA[:, 2048:4096], 4)[:, :, t0:t0 + L], 1), (r3(PC[:, 512:1024], 4)[:, :, 0:L], 1), eng="act")
                    for h in range(4):
                        mm(PD[:, h * 128:(h + 1) * 128], ha(3, idx, h * 128, (h + 1) * 128, 0, L),
                           ha(5, idx, h * 128, (h + 1) * 128, 0, L))
                    for h in range(4):
                        stt(Sret[:, h * 128:(h + 1) * 128], Sret[:, h * 128:(h + 1) * 128],
                            C("gc%d" % Cc, 4)[:, h:h + 1], PD[:, h * 128:(h + 1) * 128], ALU.mult, ALU.add)
                    cp(Sretb[:], Sret[:], eng="act")
                if sg["ret_out"] is not None:
                    dma(sg["ret_out"].rearrange("h d e -> d h e"), r3(Sret[:], 4))
            chk("B_chunks")
            for h in range(4):
                rms_stats(T, t3[:, 0:T], lambda kc, h=h: fa(1, h, 0, T), 1, 128.0, "eps6")
                tt(t4[:, 0:T], fa(1, h, 0, T), t3[:, 0:T], ALU.mult)
                tt(obT[:, h * TP:h * TP + T], t4[:, 0:T], fa(0, h, 0, T), ALU.mult, eng="pool")

        def mixer_C(l, T, G):
            ocT = oT3[2]
            nseg, Ls = G["nseg"], G["Ls"]
            Cc = G["Cc"]
            cmn = "cm%d" % Cc
            nch_seg = Ls // Cc
            nch = T // Cc

            def shifted(pb, c, dst):
                r_seg = t1[:, 0:nseg * (Ls + 1)].rearrange("p (s u) -> p s u", s=nseg)
                cp(r_seg[:, :, 1:Ls + 1], (pb[0][:, 0:T].rearrange("p (s u) -> p s u", s=nseg), pb[1]), eng="act")
                cp(r_seg[:, :, 0:1], G["prevcol"](c), eng="pool")
                for fn in G["savecol"]:
                    fn(c, r_seg)
                d_seg = t2[:, 0:T].rearrange("p (s u) -> p s u", s=nseg)
                tt(d_seg, r_seg[:, :, 0:Ls], r_seg[:, :, 1:Ls + 1], ALU.subtract)
                dst_seg = dst.rearrange("p (s u) -> p s u", s=nseg)
                stt(dst_seg, d_seg, PR(l, "mu", c), r_seg[:, :, 1:Ls + 1], ALU.mult, ALU.add)

            wf, nk, _ = W.get("cl")
            pb = nbank()
            fm_proj(pb, T, wf, nk, 0, hrhs(T))
            shifted(pb, 12, t3[:, 0:T])
            act(lob[0:64, 0:T], t3[0:64, 0:T], AF.Tanh)
            cp(lob[64:128, 0:T], t3[64:128, 0:T], eng="act")
            pb = nbank()
            fm_proj(pb, T, wf, nk, 128, hrhs(T))
            shifted(pb, 13, t3[:, 0:T])
            act(sgl[:, 0:T], t3[:, 0:T], AF.Sigmoid)
            for j in range(4):
                pb = nbank()
                mm(bk(pb, 0, 128, 0, T), cwa2b[0:64, j * 128:(j + 1) * 128], lob[0:64, 0:T])
                act(fa(0, j, 0, T), bk(pb, 0, 128, 0, T), AF.Sigmoid, bias=PR(l, "w0", j))
                scan(fa(1, j, 0, T), C(cmn, T), fa(0, j, 0, T))
                pb = nbank()
                mm(bk(pb, 0, 128, 0, T), cwa2b[64:128, j * 128:(j + 1) * 128], lob[64:128, 0:T])
                act(fa(2, j, 0, T), bk(pb, 0, 128, 0, T), AF.Sigmoid, bias=PR(l, "a0", j))
                pb = nbank()
                mm(bk(pb, 0, 128, 0, T), cg2b[:, j * 128:(j + 1) * 128], sgl[:, 0:T])
                cp(ha(5, j, 0, T), bk(pb, 0, 128, 0, T), eng="act")
            for j in range(4):
                csg_j = FA[:, 2048 + j * TP:2048 + j * TP + T]
                act(wc[:, j * 8:j * 8 + nch], (csg_j[:, Cc - 1:T:Cc], 1), AF.Exp, scale=-CDEC)
            wf, nk, _ = W.get("ck")
            for j in range(4):
                pb = nbank()
                fm_proj(pb, T, wf, nk, j * 128, hrhs(T))
                shifted(pb, 4 + j, t3[:, 0:T])
                ts(t4[:, 0:T], t3[:, 0:T], PR(l, "kk", j), ALU.mult)
                act(sqb[0][:, 0:T], t4[:, 0:T], AF.Square)
                mm(PD[:, 0:T], blkb[:], sqb[0][:, 0:T])
                ts(t5[:, 0:T], PD[:, 0:T], 1e-24, ALU.max)
                act(t5[:, 0:T], t5[:, 0:T], AF.Ln)
                act(t5[:, 0:T], t5[:, 0:T], AF.Exp, scale=-0.5)
                tt(t4[:, 0:T], t4[:, 0:T], t5[:, 0:T], ALU.mult)
                tt(t5[:, 0:T], fa(1, j, 0, T), fa(0, j, 0, T), ALU.subtract)
                act(t5[:, 0:T], t5[:, 0:T], AF.Exp, scale=-CDEC)
                stt(ha(0, j, 0, T), t4[:, 0:T], C("negone"), t5[:, 0:T], ALU.mult, ALU.mult)
                act(t5[:, 0:T], fa(1, j, 0, T), AF.Exp, scale=CDEC)
                tt(t4[:, 0:T], t4[:, 0:T], fa(2, j, 0, T), ALU.mult)
                tt(ha(2, j, 0, T), t4[:, 0:T], t5[:, 0:T], ALU.mult)
                ts(t4[:, 0:T], fa(2, j, 0, T), C("negone"), ALU.add, PR(l, "ka", j), ALU.mult)
                stt(fa(3, j, 0, T), t4[:, 0:T], C("one"), t3[:, 0:T], ALU.add, ALU.mult)
                tt(ha(3, j, 0, T), fa(3, j, 0, T), t5[:, 0:T], ALU.mult)
            wf, nk, _ = W.get("cr")
            for j in range(4):
                pb = nbank()
                fm_proj(pb, T, wf, nk, j * 128, hrhs(T))
                shifted(pb, j, t3[:, 0:T])
                act(t5[:, 0:T], fa(1, j, 0, T), AF.Exp, scale=-CDEC)
                tt(ha(1, j, 0, T), t3[:, 0:T], t5[:, 0:T], ALU.mult)
                stt(sqb[1][:, 0:T], t3[:, 0:T], PR(l, "rk", j), fa(3, j, 0, T), ALU.mult, ALU.mult)
                mm(PD[:, 0:T], blkb[:], sqb[1][:, 0:T])
                cp(fa(3, j, 0, T), PD[:, 0:T], eng="act")
            wf, nk, _ = W.get("cv")
            for j in range(4):
                pb = nbank()
                fm_proj(pb, T, wf, nk, j * 128, hrhs(T))
                shifted(pb, 8 + j, t3[:, 0:T])
                cp(ha(4, j, 0, T), t3[:, 0:T], eng="act")
                tt(fa(3, j, 0, T), fa(3, j, 0, T), t3[:, 0:T], ALU.mult)
            chk("C_prep")
            nst = {64: 6, 32: 5}[Cc]
            mskv = C("msk", 256)[0:Cc, :]
            X1 = FA[:, 2048:4096].bitcast(BF16)
            X2 = FA[:, 4096:6144].bitcast(BF16)
            sets = [dict(MM=(MM[:, :], None), Bh=(BhTM[:, :], None), Kh=(KhTM[:, :], None), Vb=(Vb[:, :], None),
                         Vp=(Vpad[:, :], None), Tf=(Tf0[:, :], None)),
                    dict(MM=(X1[0:64, 0:2048], 1), Bh=(X1[0:64, 2048:2560], 1), Kh=(X1[0:64, 2560:3072], 1),
                         Vb=(X1[0:64, 3072:3584], 1), Vp=(X2[0:64, 0:1024], 2), Tf=(X2[0:64, 1024:1536], 2))]

            def V(buf, r0, r1, c0, c1):
                ap, key = buf
                v = ap[r0:r1, c0:c1]
                return v if key is None else (v, key)

            def K_(buf, ap):
                return ap if buf[1] is None else (ap, buf[1])
            mset(V(sets[1]["Vp"], 0, 64, 0, 1024), 0.0)

            def front(t0, ci, S_):
                MMb, Bhb, Khb, Vbb, Vpb, Tfb = S_["MM"], S_["Bh"], S_["Kh"], S_["Vb"], S_["Vp"], S_["Tf"]
                for (src_slot, dstb) in ((2, Bhb), (3, Khb)):
                    for q in range(4):
                        ts(hbk[:, (q % 2) * 256:(q % 2) * 256 + Cc], ha(src_slot, q, t0, t0 + Cc),
                           wc[:, q * 8 + ci:q * 8 + ci + 1], ALU.mult, eng="pool")
                        tr(PT[0:Cc, q * 128:(q + 1) * 128], hbk[:, (q % 2) * 256:(q % 2) * 256 + Cc], identb[:])
                    cp(V(dstb, 0, Cc, 0, 512), PT[0:Cc, 0:512], eng="act")
                    yield
                for q in range(4):
                    tr(PT[0:Cc, q * 128:(q + 1) * 128], ha(4, q, t0, t0 + Cc), identb[:])
                cp(V(Vbb, 0, Cc, 0, 512), PT[0:Cc, 0:512], eng="act")
                for hh in range(2):
                    cp(K_(Vpb, Vpb[0][0:Cc, :].rearrange("p (q h c) -> p q h c", q=4, h=2)[:, :, hh, hh * 64:(hh + 1) * 64]),
                       r3(PT[0:Cc, 0:512], 4)[:, :, hh * 64:(hh + 1) * 64], eng="dve")
                yield
                for q in range(4):
                    for hh in range(2):
                        mt = PA if hh == 0 else PC
                        mh = q // 2
                        b0 = hh * 64
                        a_ap = HA[b0:b0 + 64, 0 * 2048 + q * TP + t0:0 * 2048 + q * TP + t0 + Cc]
                        (pst, _), (st, _) = a_ap.ap
                        rhs2 = bass.AP(a_ap.tensor, a_ap.offset, [[pst, 64], [2048, 2], [st, Cc]])
                        base = mh * 512 + (q % 2) * 256
                        o13 = mt[0:Cc, base:base + 128].rearrange("p (a b) -> p a b", a=2)[:, :, 0:Cc]
                        o24 = mt[0:Cc, base + 128:base + 256].rearrange("p (a b) -> p a b", a=2)[:, :, 0:Cc]
                        P.op("pe", lambda e, o=o13, lh=ha(2, q, t0, t0 + Cc, b0, b0 + 64)[0], rh=rhs2:
                             e.matmul(o, lh, rh, start=True, stop=True), [(HA, 2), (HA, 0), (HA, 1)], [(mt, mh)])
                        P.op("pe", lambda e, o=o24, lh=ha(3, q, t0, t0 + Cc, b0, b0 + 64)[0], rh=rhs2:
                             e.matmul(o, lh, rh, start=True, stop=True), [(HA, 3), (HA, 0), (HA, 1)], [(mt, mh)])
                mm_base = MMb[0][0:Cc, 0:64]
                mm_ps = mm_base.ap[0][0]
                for hh in range(2):
                    mt = PA if hh == 0 else PC
                    for mh in range(2):
                        mview = mt[0:Cc, mh * 512:(mh + 1) * 512].rearrange("p (q m t) -> p q m t", q=2, m=4)[:, :, :, 0:Cc]
                        mskb = bass.AP(mskv.tensor, mskv.offset, [[mskv.ap[0][0], Cc], [0, 2], [64, 4], [1, Cc]])
                        dstv = bass.AP(mm_base.tensor, mm_base.offset + (2 * mh) * 512 + hh * 256,
                                       [[mm_ps, Cc], [512, 2], [64, 4], [1, Cc]])
                        tt(K_(MMb, dstv), (mview, mh), mskb, ALU.mult)
                yield
                for h in range(8):
                    q, b0 = h // 2, (h % 2) * 64
                    bnk = PA if h % 2 == 0 else PC
                    mm((bnk[0:Cc, q * 64:q * 64 + Cc], 0),
                       ha(0, q, t0, t0 + Cc, b0, b0 + 64), ha(2, q, t0, t0 + Cc, b0, b0 + 64))
                at0 = AT_[0][0:Cc, 0:64]
                for hh in range(2):
                    bnk = PA if hh == 0 else PC
                    dstv = bass.AP(at0.tensor, at0.offset + hh * 64, [[at0.ap[0][0], Cc], [128, 4], [1, Cc]])
                    tt(dstv, (r3(bnk[0:Cc, 0:256], 4)[:, :, 0:Cc], 0),
                       bc_mid(C("lsm", 64)[0:Cc, 0:Cc], 4), ALU.mult)
                n1v = K_(MMb, MMb[0][0:Cc, :].rearrange("p (h m t) -> p h m t", h=8, m=4)[:, :, 0, 0:Cc])

                def cbv(p_, part, h0=0, h1=8):
                    return CB[p_][0:Cc, h0 * 128:h1 * 128].rearrange("p (h c) -> p h c", h=h1 - h0)[:, :, part * 64:part * 64 + Cc]

                def pcv(part):
                    return PC[0:Cc, :].rearrange("p (h c) -> p h c", h=8)[:, :, part * 64:part * 64 + Cc]
                cp(cbv(0, 0), n1v, eng="pool")
                tt(cbv(1, 1), n1v, bc_mid(identf[0:Cc, 0:Cc], 8), ALU.add)
                yield
                cur = 0
                for r in range(1, nst + 1):
                    nxt = 1 - cur
                    p_, n_ = (r - 1) % 2, r % 2
                    for h in range(8):
                        lh = AT_[cur][0:Cc, h * 64:h * 64 + Cc]
                        if r == 1:
                            mm((PC[0:Cc, h * 128:h * 128 + Cc], h // 4), lh, CB[p_][0:Cc, h * 128:h * 128 + Cc])
                        elif r <= nst - 1:
                            mm((PC[0:Cc, h * 128:(h + 1) * 128].rearrange("p (a c) -> p a c", a=2)[:, :, 0:Cc], h // 4), lh,
                               CB[p_][0:Cc, h * 128:(h + 1) * 128].rearrange("p (a c) -> p a c", a=2)[:, :, 0:Cc])
                        else:
                            mm((PC[0:Cc, h * 128 + 64:h * 128 + 64 + Cc], h // 4), lh, CB[p_][0:Cc, h * 128 + 64:h * 128 + 64 + Cc])
                    if r <= nst - 1:
                        for h in range(8):
                            mm((PA[0:Cc, 512 + h * 64:512 + h * 64 + Cc], 1), CB[p_][0:Cc, h * 128:h * 128 + Cc],
                               AT_[cur][0:Cc, h * 64:h * 64 + Cc])
                    if r >= 2:
                        dst_t = cbv(n_, 1) if r < nst else K_(Tfb, r3(Tfb[0][0:Cc, :], 8)[:, :, 0:Cc])
                        tt(dst_t, cbv(p_, 1), pcv(1), ALU.add)
                    if r <= nst - 1:
                        cp(cbv(n_, 0), pcv(0), eng="act")
                        cp(r3(AT_[nxt][0:Cc, :], 8)[:, :, 0:Cc], (r3(PA[0:Cc, 512:1024], 8)[:, :, 0:Cc], 1), eng="act")
                        cur = nxt
                    yield

            def chain(t0, ci, S_):
                MMb, Bhb, Khb, Vbb, Vpb, Tfb = S_["MM"], S_["Bh"], S_["Kh"], S_["Vb"], S_["Vp"], S_["Tf"]
                for q in range(4):
                    mm((PB[0:Cc, q * 128:(q + 1) * 128], 0), ha(0, q, t0, t0 + Cc), Srwb[:, q * 128:(q + 1) * 128],
                       start=True, stop=False)
                    for hh in range(2):
                        mm((PB[0:Cc, q * 128:(q + 1) * 128], 0),
                           V(MMb, 0, Cc, q * 512 + hh * 256 + 128, q * 512 + hh * 256 + 128 + Cc),
                           V(Vpb, 0, Cc, (q * 2 + hh) * 128, (q * 2 + hh + 1) * 128), start=False, stop=(hh == 1))
                for hh in range(2):
                    cp(Gpad[0:Cc, :].rearrange("p (q h c) -> p q h c", q=4, h=2)[:, :, hh, hh * 64:(hh + 1) * 64],
                       (r3(PB[0:Cc, 0:512], 4)[:, :, hh * 64:(hh + 1) * 64], 0), eng="act" if hh else "dve")
                yield
                for q in range(4):
                    for hh in range(2):
                        h = q * 2 + hh
                        mm((PB[0:Cc, 512 + q * 128:512 + (q + 1) * 128], 1), V(Tfb, 0, Cc, h * 64, h * 64 + Cc),
                           Gpad[0:Cc, h * 128:(h + 1) * 128], start=(hh == 0), stop=(hh == 1))
                cp(Ub[0:Cc, :], (PB[0:Cc, 512:1024], 1), eng="act")
                for hh in range(2):
                    cp(Upad[0:Cc, :].rearrange("p (q h c) -> p q h c", q=4, h=2)[:, :, hh, hh * 64:(hh + 1) * 64],
                       (r3(PB[0:Cc, 512:1024], 4)[:, :, hh * 64:(hh + 1) * 64], 1), eng="dve")
                yield
                for q in range(4):
                    oy = PD[:, q * 64:q * 64 + Cc]
                    mm(oy, Srwb[:, q * 128:(q + 1) * 128], ha(1, q, t0, t0 + Cc), start=True, stop=False)
                    for hh in range(2):
                        h = q * 2 + hh
                        mm(oy, Upad[0:Cc, h * 128:(h + 1) * 128],
                           V(MMb, 0, Cc, q * 512 + hh * 256 + 64, q * 512 + hh * 256 + 64 + Cc), start=False, stop=False)
                        mm(oy, V(Vpb, 0, Cc, h * 128, (h + 1) * 128),
                           V(MMb, 0, Cc, q * 512 + hh * 256 + 192, q * 512 + hh * 256 + 192 + Cc), start=False, stop=(hh == 1))
                cp((r3(FA[:, 0:2048], 4)[:, :, t0:t0 + Cc], 0), r3(PD[:, 0:256], 4)[:, :, 0:Cc], eng="act")
                yield
                for q in range(4):
                    mm(PD[:, q * 128:(q + 1) * 128], V(Bhb, 0, Cc, q * 128, (q + 1) * 128), Ub[0:Cc, q * 128:(q + 1) * 128],
                       start=True, stop=False)
                    mm(PD[:, q * 128:(q + 1) * 128], V(Khb, 0, Cc, q * 128, (q + 1) * 128), V(Vbb, 0, Cc, q * 128, (q + 1) * 128),
                       start=False, stop=True)
                tt(r3(stmp[:], 4), r3(PD[:], 4), bc_mid(blkf, 4), ALU.mult)
                for q in range(4):
                    stt(Srw[:, q * 128:(q + 1) * 128], Srw[:, q * 128:(q + 1) * 128], wc[:, q * 8 + ci:q * 8 + ci + 1],
                        stmp[:, q * 128:(q + 1) * 128], ALU.mult, ALU.add)
                cp(Srwb[:], Srw[:], eng="act")
                yield

            def state_in(sg):
                if sg["rw_in"] == "zero":
                    mset(Srw[:], 0.0)
                    mset(Srwb[:], 0.0)
                elif sg["rw_in"] is not None:
                    mset(Srw[:], 0.0)
                    dma(r3(rwst, 8), srw[l, sg["rw_in"]].rearrange("h i j -> i h j"))
                    for q in range(4):
                        tr(PD[:, q * 64:(q + 1) * 64], rwst[:, q * 128:(q + 1) * 128], identf[0:64, 0:64])
                    for hh in range(2):
                        cp(r3(Srw[hh * 64:(hh + 1) * 64, :], 4)[:, :, hh * 64:(hh + 1) * 64],
                           r3(PD[hh * 64:(hh + 1) * 64, 0:256], 4), eng="act")
                    cp(Srwb[:], Srw[:], eng="act")

            def state_out(sg):
                if sg["rw_out"] is not None:
                    for hh in range(2):
                        cp(r3(rwcp[hh * 64:(hh + 1) * 64, :], 4), r3(Srw[hh * 64:(hh + 1) * 64, :], 4)[:, :, hh * 64:(hh + 1) * 64],
                           eng="pool")
                    for q in range(4):
                        tr(PD[0:64, q * 128:(q + 1) * 128], rwcp[:, q * 64:(q + 1) * 64], identf)
                    cp(rwst, PD[0:64, :], eng="act")
                    dma(sg["rw_out"].rearrange("h i j -> i h j"), r3(rwst, 8))

            items = [(sg, t0, ci, k == 0, k == len(sg["chunks"]) - 1) for sg in G["segs"]
                     for k, (t0, ci) in enumerate(sg["chunks"])]
            for _ in front(items[0][1], items[0][2], sets[0]):
                pass
            for n_, (sg, t0, ci, first, last_) in enumerate(items):
                gf = front(items[n_ + 1][1], items[n_ + 1][2], sets[(n_ + 1) % 2]) if n_ + 1 < len(items) else iter(())
                if first:
                    state_in(sg)
                gc = chain(t0, ci, sets[n_ % 2])
                alive_f = alive_c = True
                step = 0
                while alive_f or alive_c:
                    if alive_f:
                        alive_f = next(gf, "END") != "END"
                    if alive_c and (step % 2 == 1 or not alive_f):
                        alive_c = next(gc, "END") != "END"
                    step += 1
                if last_:
                    state_out(sg)
            chk("C_chunks")
            for q in range(4):
                cp(sqb[0][:, 0:T], fa(0, q, 0, T), eng="pool")
                mm(PD[:, 0:T], blkb[:], sqb[0][:, 0:T])
                stt(t3[:, 0:T], PD[:, 0:T], -1.0 / 64, fa(0, q, 0, T), ALU.mult, ALU.add)
                act(sqb[1][:, 0:T], t3[:, 0:T], AF.Square)
                mm(PD[:, 0:T], blkb[:], sqb[1][:, 0:T])
                act(t4[:, 0:T], PD[:, 0:T], AF.Ln, bias=C("gneps"), scale=1.0 / 64)
                act(t4[:, 0:T], t4[:, 0:T], AF.Exp, scale=-0.5)
                tt(t3[:, 0:T], t3[:, 0:T], t4[:, 0:T], ALU.mult)
                ts(t3[:, 0:T], t3[:, 0:T], PR(l, "lnw", q), ALU.mult, PR(l, "lnb", q), ALU.add)
                tt(t3[:, 0:T], t3[:, 0:T], fa(3, q, 0, T), ALU.add)
                tt(ocT[:, q * TP:q * TP + T], t3[:, 0:T], ha(5, q, 0, T), ALU.mult)

        def phase_G(l, T):
            for b in range(3):
                for half, tg in enumerate((f"gl{b}a", f"gl{b}b")):
                    wf, nk, _ = W.get(tg)
                    for j in range(4):
                        c = half * 4 + j
                        pb = nbank()
                        fm_proj(pb, T, wf, nk, j * 128, hrhs(T))
                        act(ha(c // 4, c % 4, 0, T), bk(pb, 0, 128, 0, T), AF.Sigmoid)
                wf, nk, _ = W.get(f"br{b}")
                for c in range(8):
                    pb = nbank()
                    fm_proj(pb, T, wf, nk, c * 128, lambda kc: oT3[b][:, kc * TP:kc * TP + T])
                    if b == 0:
                        tt(fa(c // 4, c % 4, 0, T), bk(pb, 0, 128, 0, T), ha(c // 4, c % 4, 0, T), ALU.mult)
                    else:
                        tt(t4[:, 0:T], bk(pb, 0, 128, 0, T), ha(c // 4, c % 4, 0, T), ALU.mult)
                        tt(fa(c // 4, c % 4, 0, T), fa(c // 4, c % 4, 0, T), t4[:, 0:T], ALU.add, eng="pool")
            for c in range(8):
                cp(ha(2 + c // 4, c % 4, 0, T), fa(c // 4, c % 4, 0, T), eng="act" if c % 2 else "pool")
            for half, tg in enumerate(("outa", "outb")):
                wf, nk, _ = W.get(tg)
                for j in range(4):
                    c = half * 4 + j
                    pb = nbank()
                    fm_proj(pb, T, wf, nk, j * 128, lambda kc: ha(2 + kc // 4, kc % 4, 0, T))
                    tt(xT[:, c * TP:c * TP + T], xT[:, c * TP:c * TP + T], bk(pb, 0, 128, 0, T), ALU.add)

        def phase_F(l, T):
            norm_to_hT(l, T, "nffn")
            for i in range(6):
                wg, nk, ncol = W.get(f"fg{i}")
                wu, _, _ = W.get(f"fu{i}")
                for j in range(ncol // 128):
                    c = i * 4 + j
                    pg = nbank()
                    fm_proj(pg, T, wg, nk, j * 128, hrhs(T))
                    pu = nbank()
                    fm_proj(pu, T, wu, nk, j * 128, hrhs(T))
                    act(t4[:, 0:T], bk(pg, 0, 128, 0, T), AF.Silu)
                    tt(ha(c // 4, c % 4, 0, T), t4[:, 0:T], bk(pu, 0, 128, 0, T), ALU.mult)
            for j in range(8):
                wf, nk, _ = W.get(f"fd{j}")
                pb = nbank()
                fm_proj(pb, T, wf, nk, 0, lambda kc: ha(kc // 4, kc % 4, 0, T))
                tt(xT[:, j * TP:j * TP + T], xT[:, j * TP:j * TP + T], bk(pb, 0, 128, 0, T), ALU.add)

        def phase_P(l, T, psrc):
            for ti in range((T + 127) // 128):
                dma(xin[:, 0:PLE], psrc[ti * 128:(ti + 1) * 128, :])
                pb = nbank()
                for k in range(2):
                    tr(bk(pb, 0, 128, k * 128, (k + 1) * 128), xin[:, k * 128:(k + 1) * 128], identf)
                cp((r3(HA[:, 4 * 2048:4 * 2048 + 2 * TP], 2)[:, :, ti * 128:(ti + 1) * 128], 4), (r3(pb[0][:, 0:256], 2), pb[1]),
                   eng="act")
            wf, nk, _ = W.get("pp")
            for c in range(8):
                pb = nbank()
                fm_proj(pb, T, wf, nk, c * 128, lambda kc: ha(4, kc, 0, T))
                cp(fa(c // 4, c % 4, 0, T), bk(pb, 0, 128, 0, T), eng="act")
            rms_stats(T, t3[:, 0:T], lambda kc: fa(kc // 4, kc % 4, 0, T), 8, float(D), "eps6")
            for c in range(8):
                stt(fa(c // 4, c % 4, 0, T), fa(c // 4, c % 4, 0, T), PR(l, "nple", c), t3[:, 0:T], ALU.mult, ALU.mult,
                    eng="dve" if c % 2 else "pool")
            norm_to_hT(l, T, None)
            for half, tg in enumerate(("pga", "pgb")):
                wf, nk, _ = W.get(tg)
                for j in range(4):
                    c = half * 4 + j
                    pb = nbank()
                    fm_proj(pb, T, wf, nk, j * 128, hrhs(T))
                    act(t4[:, 0:T], bk(pb, 0, 128, 0, T), AF.Sigmoid)
                    tt(t4[:, 0:T], t4[:, 0:T], fa(c // 4, c % 4, 0, T), ALU.mult)
                    tt(xT[:, c * TP:c * TP + T], xT[:, c * TP:c * TP + T], t4[:, 0:T], ALU.add, eng="pool")

        KSTOP = _os.environ.get("KSTOP", "")

        class _Stop(Exception):
            pass

        def chk(tag):
            if KSTOP == tag:
                raise _Stop()

        def layer(l, T, G):
            chk("load")
            norm_to_hT(l, T, "nmix")
            chk("norm")
            mixer_A(l, T, G)
            chk("A")
            mixer_B(l, T, G)
            chk("B")
            mixer_C(l, T, G)
            chk("C")
            if dbg and G.get("dbg"):
                for b, nm in enumerate(("d_oa", "d_ob", "d_oc")):
                    dma(dbg_t[nm], oT3[b][:], eng="pool")
            phase_G(l, T)
            chk("G")
            phase_F(l, T)
            chk("F")
            phase_P(l, T, G["psrc"])
            chk("P")

        def x_in(l, src_tm, src_fm, T, key):
            if l == 0:
                load_x_tm(src_tm, T)
            else:
                dma(r3(xT[:], KC)[:, :, 0:T], src_fm.rearrange("k p t -> p k t"), rkey=key)

        def x_out(l, dst_tm, dst_fm, T, key):
            if l == DEPTH - 1:
                store_x_tm(dst_tm, T)
            else:
                dma(dst_fm.rearrange("k p t -> p k t"), r3(xT[:], KC)[:, :, 0:T], wkey=key)

        for l in range(DEPTH):
            for g in range(NG + 1):
                W.plan(l)

        def _main():
          for l in range(DEPTH):
              dma(cwa2b[:], cwa2[l], eng="pool")
              dma(cg2b[:], cg2[l], eng="pool")
              dma(biasT[:], bias_d[l], eng="pool")
              mset(shcol[:], 0.0)
              for g in range(NG):
                  kslot = g % 2
                  last = (g == NG - 1)
                  T = TP
                  x_in(l, xp[g * TP:(g + 1) * TP, :], x1p[:, :, g * TP:(g + 1) * TP], T, "x1p:%d" % (g % 4))
                  qbs = []
                  for m_ in range(4):
                      kts = []
                      for kt in range(5):
                          wcd = 128 * (m_ + kt)
                          if wcd < 512:
                              if g == 0:
                                  continue
                              slot, off = 1 - kslot, wcd
                          else:
                              slot, off = kslot, wcd - 512
                          kts.append((slot * 512 + off, 128, slot * 4 + off // 128, kt))
                      qbs.append(dict(q0=128 * m_, nq=128, kts=kts))
                  tmt = [(ti * 128, 128, ti, 128, rope_p[g * TP + ti * 128:g * TP + (ti + 1) * 128, :]) for ti in range(4)]

                  def prevcol(c):
                      return shcol[:, c:c + 1].rearrange("p (s u) -> p s u", s=1)

                  def savecol(c, r_seg):
                      cp(shcol[:, c:c + 1], r_seg[:, 0, TP:TP + 1], eng="pool")
                  G = dict(kslot=kslot, vt=[(ti * 128, 128, kslot * 4 + ti) for ti in range(4)],
                           a_out=([(akp[l, ti * 128:(ti + 1) * 128, :], avp[l, ti * 128:(ti + 1) * 128, :], ti * 128, 128)
                                   for ti in range(4)] if last else []),
                           tm=tmt,
                           segs=[dict(qblocks=qbs, tm=tmt, state_in=("zero" if g == 0 else None),
                                      ret_out=(retp[l] if last else None),
                                      rw_in=("zero" if g == 0 else None), rw_out=(rwp[l] if last else None),
                                      chunks=[(64 * ci, ci) for ci in range(8)])],
                           nseg=1, Ls=TP, Cc=64, prevcol=prevcol, savecol=[savecol],
                           psrc=pp[l, g * TP:(g + 1) * TP, :], dbg=(l == 0 and g == 0))
                  layer(l, T, G)
                  if last:
                      dma(shp[l].rearrange("(c p) -> p c", p=128), shcol[:], slow=True)
                  if dbg and l == 0 and g == 0:
                      dma(dbg_t["d_x"], xT[:])
                  x_out(l, yp[g * TP:(g + 1) * TP, :], x1p[:, :, g * TP:(g + 1) * TP], T, "x1p:%d" % (g % 4))
              T = TSM
              x_in(l, xs, x1s, T, "x1s")
              segs = []
              tmt = []
              dma(shin[:, 0:14 * NSMP].rearrange("p (s c) -> p s c", s=NSMP), ssh[l].rearrange("s (c p) -> p s c", p=128), slow=True)
              for s in range(NSMP):
                  tms = [(LS * s, LS, s, 32, rope_s[LS * s:LS * (s + 1), :])]
                  tmt += tms
                  kts = [(kt * 128, 128, kt, kt) for kt in range(4)] + [(512 + LS * s, LS, 4 + s, 4)]
                  segs.append(dict(cache=s, qblocks=[dict(q0=LS * s, nq=LS, kts=kts)], tm=tms, state_in=s, ret_out=rets[l, s],
                                   rw_in=s, rw_out=rws[l, s], chunks=[(LS * s, s)]))

              def prevcol_s(c):
                  return shin[:, 0:14 * NSMP].rearrange("p (s c) -> p s c", s=NSMP)[:, :, c:c + 1]

              def savecol_s(c, r_seg):
                  cp(shout[:, 0:14 * NSMP].rearrange("p (s c) -> p s c", s=NSMP)[:, :, c:c + 1], r_seg[:, :, LS:LS + 1], eng="pool")
              G = dict(kslot=1, vt=[(LS * s, LS, 4 + s) for s in range(NSMP)],
                       a_out=[(aks[l, LS * s:LS * (s + 1), :], avs[l, LS * s:LS * (s + 1), :], LS * s, LS) for s in range(NSMP)],
                       tm=tmt, segs=segs, nseg=NSMP, Ls=LS, Cc=32, prevcol=prevcol_s, savecol=[savecol_s],
                       psrc=psm[l], dbg=False)
              layer(l, T, G)
              dma(shs[l].rearrange("s (c p) -> p s c", p=128), shout[:, 0:14 * NSMP].rearrange("p (s c) -> p s c", s=NSMP), slow=True)
              x_out(l, ys, x1s, T, "x1s")

        try:
            _main()
        except _Stop:
            pass
        P.emit(nc, es)
    return nc, P


_NC_CACHE = {}


def _run(inp, ncores, dbg=False):
    xpr = np.asarray(inp["x_prompt"], np.float32)
    B, SEQ, _ = xpr.shape
    xsm = np.asarray(inp["x_sample"], np.float32)
    assert xsm.shape[0] == ncores * NSMP and xsm.shape[1] == LS and B <= ncores
    key = (SEQ, dbg)
    if key not in _NC_CACHE:
        _NC_CACHE[key] = build(SEQ, dbg=dbg)[0]
    nc = _NC_CACHE[key]
    cst, rope_p, rope_s = host_consts(SEQ)
    f = lambda k: np.ascontiguousarray(np.asarray(inp[k], np.float32))
    prm, cwa2 = host_params({k: f(k) for k in ("norm_mix", "norm_ffn", "ple_norm", "a_q_norm", "a_k_norm", "c_shift_mu",
                                                "c_w0", "c_a0", "c_k_k", "c_k_a", "c_ln_w", "c_ln_b", "c_r_k", "c_w2", "c_a2")})
    biasT = host_bias(f("a_rel_bias")).reshape(DEPTH, 128, 5 * 1024)
    shared = dict(w_in=f("w_in"), w_br=f("w_branch"), w_out=f("w_out"), w_fg=f("w_ffn_gate"), w_fu=f("w_ffn_up"),
                  w_fd=f("w_ffn_down"), w_pp=f("w_ple_proj"), w_pg=f("w_ple_gate"), cg2=f("c_g2"), cwa2=cwa2, prm=prm,
                  cst=cst, biasT=biasT, rope_p=rope_p, rope_s=rope_s)
    pp_ = f("p_prompt"); ps_ = f("p_sample"); cak = f("cache_a_k"); cav = f("cache_a_v")
    sr = f("state_ret"); sw = f("state_rwkv"); sh = f("state_rwkv_shift")
    zx = np.zeros((SEQ, D), np.float32); zp = np.zeros((DEPTH, SEQ, PLE), np.float32)
    in_maps = []
    for c in range(ncores):
        m = dict(shared)
        if c < B:
            m["xp"] = np.ascontiguousarray(xpr[c]); m["pp"] = np.ascontiguousarray(pp_[:, c])
        else:
            m["xp"] = zx; m["pp"] = zp
        sl = slice(c * NSMP, (c + 1) * NSMP)
        m["xs"] = np.ascontiguousarray(xsm[sl].reshape(NSMP * LS, D))
        m["psm"] = np.ascontiguousarray(ps_[:, sl].reshape(DEPTH, NSMP * LS, PLE))
        m["cak"] = np.ascontiguousarray(cak[:, sl].reshape(DEPTH, NSMP, 512, 512))
        m["cav"] = np.ascontiguousarray(cav[:, sl].reshape(DEPTH, NSMP, 512, 512))
        m["sret"] = np.ascontiguousarray(sr[:, sl]); m["srw"] = np.ascontiguousarray(sw[:, sl])
        m["ssh"] = np.ascontiguousarray(sh[:, sl].reshape(DEPTH, NSMP, 1792))
        in_maps.append(m)
    res = run_bass_kernel_spmd(nc, in_maps, core_ids=list(range(ncores))).results
    R = lambda k, cs: [np.asarray(res[c][k], np.float32) for c in cs]
    pc = list(range(B)); ac = list(range(ncores))
    NB = ncores * NSMP
    out = (
        np.stack(R("yp", pc)),
        np.concatenate(R("ys", ac)).reshape(NB, LS, D),
        np.stack(R("akp", pc), axis=1).reshape(DEPTH, B, 512, 8, 64),
        np.stack(R("avp", pc), axis=1).reshape(DEPTH, B, 512, 8, 64),
        np.stack(R("retp", pc), axis=1),
        np.stack(R("rwp", pc), axis=1),
        np.stack(R("shp", pc), axis=1).reshape(DEPTH, B, 1, 1792),
        np.concatenate([r.reshape(DEPTH, NSMP, LS, 8, 64) for r in R("aks", ac)], axis=1),
        np.concatenate([r.reshape(DEPTH, NSMP, LS, 8, 64) for r in R("avs", ac)], axis=1),
        np.concatenate(R("rets", ac), axis=1),
        np.concatenate(R("rws", ac), axis=1),
        np.concatenate(R("shs", ac), axis=1).reshape(DEPTH, NB, 1, 1792),
    )
    if dbg:
        return out, res
    return out


def kernel(**inputs):
    return _run(inputs, 8)
```

```python
import numpy as np
from contextlib import ExitStack
import concourse.bass as bass
import concourse.mybir as mybir
from concourse.bass_utils import run_bass_kernel_spmd

F32 = mybir.dt.float32
BF16 = mybir.dt.bfloat16
ALU = mybir.AluOpType
AF = mybir.ActivationFunctionType
AX = mybir.AxisListType

D = 1024
KC = 8
INW = 8448
DFF = 2816
NFF = 22
PLE = 256
DEPTH = 2
NSMP = 4
LS = 32
TP = 512
PAST = 2048
NEG = -30000.0
CDEC = 0.6065306597126334
GN_EPS = 64e-5

_c = {}
_o = 0
for _n, _w in [("ident", 128), ("ones", 128), ("blk", 128), ("m01", 128), ("msk", 256), ("lsm", 64),
               ("cm64", 512), ("cm32", 128),
               ("gc128", 4), ("gc32", 4), ("tq128", 4), ("tk128", 4), ("tq32", 4), ("tk32", 4),
               ("eps6", 1), ("eps12", 1), ("gneps", 1), ("one", 1), ("negone", 1), ("zero", 1)]:
    _c[_n] = _o
    _o += _w
NCST = _o
_p = {}
_o = 0
for _n, _w in [("nmix", 8), ("nffn", 8), ("nple", 8), ("aqn", 1), ("akn", 1), ("mu", 14), ("w0", 4), ("a0", 4),
               ("kk", 4), ("ka", 4), ("rk", 4), ("lnw", 4), ("lnb", 4)]:
    _p[_n] = _o
    _o += _w
NPRM = _o


def _gammas():
    return 1.0 - np.exp2(-5.0 - np.arange(4, dtype=np.float64))


def host_consts(SEQ):
    g = _gammas()
    cst = np.zeros((128, NCST), np.float32)
    p = np.arange(128)
    cst[:, _c["ident"]:_c["ident"] + 128] = np.eye(128)
    cst[:, _c["ones"]:_c["ones"] + 128] = 1.0
    cst[:, _c["blk"]:_c["blk"] + 128] = (p[:, None] // 64 == p[None, :] // 64)
    cst[:, _c["m01"]:_c["m01"] + 128] = (p[None, :] >= p[:, None])
    m = np.arange(64)
    strict = (m[:, None] < m[None, :]).astype(np.float32)
    incl = (m[:, None] <= m[None, :]).astype(np.float32)
    msk = np.stack([strict, incl, strict, incl], axis=1)
    cst[:64, _c["msk"]:_c["msk"] + 256] = msk.reshape(64, 256)
    cst[:64, _c["lsm"]:_c["lsm"] + 64] = (m[None, :] < m[:, None])
    t = np.arange(512)
    cst[:, _c["cm64"]:_c["cm64"] + 512] = (t % 64 != 0).astype(np.float32)[None]
    cst[:, _c["cm32"]:_c["cm32"] + 128] = (t[:128] % 32 != 0).astype(np.float32)[None]
    cst[:, _c["gc128"]:_c["gc128"] + 4] = (g ** 128)[None]
    cst[:, _c["gc32"]:_c["gc32"] + 4] = (g ** 32)[None]
    sc = 128.0 ** -0.5
    cst[:, _c["tq128"]:_c["tq128"] + 4] = g[None, :] ** (p + 1.0)[:, None]
    cst[:, _c["tk128"]:_c["tk128"] + 4] = sc * g[None, :] ** (-(p + 1.0))[:, None]
    cst[:, _c["tq32"]:_c["tq32"] + 4] = g[None, :] ** ((p % 32) + 1.0)[:, None]
    cst[:, _c["tk32"]:_c["tk32"] + 4] = sc * g[None, :] ** (-((p % 32) + 1.0))[:, None]
    cst[:, _c["eps6"]] = 1e-6
    cst[:, _c["eps12"]] = 1e-12
    cst[:, _c["gneps"]] = GN_EPS
    cst[:, _c["one"]] = 1.0
    cst[:, _c["negone"]] = -1.0
    inv = (10000.0 ** (-np.arange(64, dtype=np.float32) / np.float32(64))).astype(np.float32)

    def rope(pos):
        ang = (pos.astype(np.float32)[:, None] * inv[None, :]).astype(np.float32)
        return np.concatenate([np.cos(ang), np.sin(ang)], axis=1).astype(np.float32)
    rope_p = rope(np.arange(SEQ))
    rope_s = np.tile(rope(PAST + np.arange(LS)), (NSMP, 1))
    return cst, rope_p, rope_s


def host_bias(rb):
    j = np.arange(128)[:, None, None]
    kt = np.arange(5)[None, :, None]
    qq = np.arange(128)[None, None, :]
    kk = 128 * kt + j
    idx = np.clip(512 + qq - kk, -63, 256) + 63
    ok = np.where(qq < 64, kk < 576, kk >= 64)
    out = rb[:, :, idx]
    out = np.where(ok[None, None], out, np.float32(NEG)).astype(np.float32)
    slot_h = [(s_ % 4) * 2 + s_ // 4 for s_ in range(8)]
    out = out[:, slot_h]
    return np.ascontiguousarray(out.transpose(0, 2, 3, 1, 4))


def host_params(inp):
    prm = np.zeros((DEPTH, 128, NPRM), np.float32)

    def fm(v, n):
        return v.reshape(n, 128).T
    for l in range(DEPTH):
        prm[l, :, _p["nmix"]:_p["nmix"] + 8] = fm(inp["norm_mix"][l], 8)
        prm[l, :, _p["nffn"]:_p["nffn"] + 8] = fm(inp["norm_ffn"][l], 8)
        prm[l, :, _p["nple"]:_p["nple"] + 8] = fm(inp["ple_norm"][l], 8)
        prm[l, :, _p["aqn"]] = np.tile(inp["a_q_norm"][l], 2)
        prm[l, :, _p["akn"]] = np.tile(inp["a_k_norm"][l], 2)
        prm[l, :, _p["mu"]:_p["mu"] + 14] = fm(inp["c_shift_mu"][l], 14)
        for nm, key in [("w0", "c_w0"), ("a0", "c_a0"), ("kk", "c_k_k"), ("ka", "c_k_a"), ("lnw", "c_ln_w"),
                        ("lnb", "c_ln_b")]:
            prm[l, :, _p[nm]:_p[nm] + 4] = fm(inp[key][l], 4)
        prm[l, :, _p["rk"]:_p["rk"] + 4] = fm(inp["c_r_k"][l].reshape(-1), 4)
    cwa2 = np.concatenate([inp["c_w2"], inp["c_a2"]], axis=1).astype(np.float32)
    return prm, np.ascontiguousarray(cwa2)


class Prog:
    ENGS = ("pe", "act", "dve", "pool", "sp")

    def __init__(self):
        self.ins = []
        self.last_w = {}
        self.readers = {}
        self.readonly = set()
        self.sub = {}
        self.excl = set()

    def keys(self, x):
        if isinstance(x, str):
            return [x]
        if isinstance(x, tuple):
            return [self.name(x[0]) + ":" + str(x[1])]
        n = self.name(x)
        if n in self.sub:
            return [n + ":" + str(i) for i in range(self.sub[n])]
        return [n]

    @staticmethod
    def name(x):
        t = getattr(x, "tensor", x)
        return t.name

    def op(self, eng, fn, reads=(), writes=(), dma=None):
        i = len(self.ins)
        deps = set()
        for r in reads:
            for k in self.keys(r):
                if k in self.readonly:
                    continue
                if k in self.last_w:
                    deps.add(self.last_w[k])
                rd = self.readers.setdefault(k, {})
                if k.split(":")[0] in self.excl:
                    for ek, r in rd.items():
                        if ek != eng:
                            deps.add(r)
                rd[("d", i) if dma is not None else eng] = i
        for w in writes:
            for k in self.keys(w):
                if k in self.last_w:
                    deps.add(self.last_w[k])
                for r in self.readers.get(k, {}).values():
                    if r != i:
                        deps.add(r)
                self.last_w[k] = i
                self.readers[k] = {}
        self.ins.append([eng, fn, dma, sorted(deps)])
        return i

    def emit(self, nc, es):
        ins = self.ins
        n = len(ins)
        need = [False] * n
        chans = {}
        for i, (eng, fn, dma, deps) in enumerate(ins):
            if dma is not None:
                need[i] = True
                chans.setdefault(dma, 0)
            for j in deps:
                ej, _, dj, _ = ins[j]
                if dj is not None or dma is not None or ej != eng or eng != "pe":
                    need[j] = True
        comp = [None] * n
        cnt = {e: 0 for e in self.ENGS}
        for i, (eng, fn, dma, deps) in enumerate(ins):
            if dma is not None:
                chans[dma] += 16
                comp[i] = ("d_" + dma, chans[dma])
            elif need[i]:
                cnt[eng] += 1
                comp[i] = ("e_" + eng, cnt[eng])
        sems = {}
        for s in ["e_" + e for e in self.ENGS] + ["d_" + c for c in chans]:
            sems[s] = es.enter_context(nc.semaphore(s))
        self.n_sems = len(sems)
        final = {("d_" + c): v for c, v in chans.items()}
        block = es.enter_context(nc.Block())
        per_eng = {e: [] for e in self.ENGS}
        for i, rec in enumerate(ins):
            per_eng[rec[0]].append(i)

        def run(eng_name, e):
            waited = {}
            for i in per_eng[eng_name]:
                _, fn, dma, deps = ins[i]
                wl = {}
                for j in deps:
                    ej, _, dj, _ = ins[j]
                    if dj is None and dma is None and ej == eng_name and eng_name == "pe":
                        continue
                    s, v = comp[j]
                    if wl.get(s, 0) < v:
                        wl[s] = v
                for s, v in wl.items():
                    if waited.get(s, 0) >= v:
                        continue
                    e.wait_ge(sems[s], v)
                    waited[s] = v
                r = fn(e)
                if comp[i] is not None:
                    r.then_inc(sems[comp[i][0]], 16 if dma is not None else 1)
            if eng_name == "sp":
                for s, v in final.items():
                    e.wait_ge(sems[s], v)

        @block.tensor
        def _(e):
            run("pe", e)

        @block.scalar
        def _(e):
            run("act", e)

        @block.vector
        def _(e):
            run("dve", e)

        @block.gpsimd
        def _(e):
            run("pool", e)

        @block.sync
        def _(e):
            run("sp", e)


def build(SEQ, dbg=False, skip=""):
    import os as _os
    assert SEQ % TP == 0
    NG = SEQ // TP
    TSM = NSMP * LS
    nc = bass.Bass("TRN2", target_bir_lowering=False)
    P = Prog()
    es = ExitStack()

    def din(name, shape):
        P.readonly.add(name)
        return nc.dram_tensor(name, list(shape), F32, kind="ExternalInput").ap()

    def dout(name, shape):
        return nc.dram_tensor(name, list(shape), F32, kind="ExternalOutput").ap()

    xp = din("xp", [SEQ, D]); pp = din("pp", [DEPTH, SEQ, PLE])
    xs = din("xs", [TSM, D]); psm = din("psm", [DEPTH, TSM, PLE])
    cak = din("cak", [DEPTH, NSMP, 512, 512]); cav = din("cav", [DEPTH, NSMP, 512, 512])
    sret = din("sret", [DEPTH, NSMP, 4, 128, 128]); srw = din("srw", [DEPTH, NSMP, 8, 64, 64])
    ssh = din("ssh", [DEPTH, NSMP, 1792])
    w_in = din("w_in", [DEPTH, D, INW]); w_br = din("w_br", [DEPTH, 3, 512, D]); w_out = din("w_out", [DEPTH, D, D])
    w_fg = din("w_fg", [DEPTH, D, DFF]); w_fu = din("w_fu", [DEPTH, D, DFF]); w_fd = din("w_fd", [DEPTH, DFF, D])
    w_pp = din("w_pp", [DEPTH, PLE, D]); w_pg = din("w_pg", [DEPTH, D, D])
    cg2 = din("cg2", [DEPTH, 128, 512]); cwa2 = din("cwa2", [DEPTH, 128, 512])
    prm_d = din("prm", [DEPTH, 128, NPRM]); cst_d = din("cst", [128, NCST])
    bias_d = din("biasT", [DEPTH, 128, 5 * 1024])
    rope_p = din("rope_p", [SEQ, 128]); rope_s = din("rope_s", [TSM, 128])

    yp = dout("yp", [SEQ, D]); ys = dout("ys", [TSM, D])
    akp = dout("akp", [DEPTH, 512, 512]); avp = dout("avp", [DEPTH, 512, 512])
    retp = dout("retp", [DEPTH, 4, 128, 128]); rwp = dout("rwp", [DEPTH, 8, 64, 64]); shp = dout("shp", [DEPTH, 1792])
    aks = dout("aks", [DEPTH, TSM, 512]); avs = dout("avs", [DEPTH, TSM, 512])
    rets = dout("rets", [DEPTH, NSMP, 4, 128, 128]); rws = dout("rws", [DEPTH, NSMP, 8, 64, 64])
    shs = dout("shs", [DEPTH, NSMP, 1792])
    wsrc = dict(w_in=w_in, w_br=w_br, w_out=w_out, w_fg=w_fg, w_fu=w_fu, w_fd=w_fd, w_pp=w_pp, w_pg=w_pg)
    wbf = {k: nc.dram_tensor(k + "_b", list(v.shape), BF16).ap() for k, v in wsrc.items()}
    x1p = nc.dram_tensor("x1p", [KC, 128, SEQ], F32).ap()
    x1s = nc.dram_tensor("x1s", [KC, 128, TSM], F32).ap()
    dbg_t = {}
    if dbg:
        for nm in ("d_oa", "d_ob", "d_oc"):
            dbg_t[nm] = dout(nm, [128, 4 * TP])
        dbg_t["d_x"] = dout("d_x", [128, KC * TP])

    with es:
        def sb(name, shape, dt=F32):
            return es.enter_context(nc.sbuf_tensor(name, list(shape), dt))

        def ps(name, shape, dt=F32, sub=None):
            if sub:
                P.sub[name] = sub
            P.excl.add(name)
            return es.enter_context(nc.psum_tensor(name, list(shape), dt))

        cst = sb("cst_sb", [128, NCST])
        prm = sb("prm_sb", [128, DEPTH * NPRM])
        identb = sb("identb", [128, 128], BF16)
        onesb = sb("onesb", [128, 128], BF16)
        blkb = sb("blkb", [128, 128], BF16)
        sqb = [sb(f"sqb{i}", [128, TP], BF16) for i in range(2)]
        aqs = sb("aqs", [128, DEPTH])
        cwa2b = sb("cwa2b", [128, 512], BF16)
        cg2b = sb("cg2b", [128, 512], BF16)
        biasT = sb("biasT_sb", [128, 5 * 1024], BF16)
        xT = sb("xT", [128, KC * TP])
        hT = sb("hT", [128, KC * TP], BF16)
        NWB = 3
        WBN = 4096
        wb = [sb(f"wb{i}", [128, WBN], BF16) for i in range(NWB)]
        Kwin = sb("Kwin", [128, 4 * 1024], BF16)
        Vwin = sb("Vwin", [128, 8 * 520], BF16)
        Sret = sb("Sret", [128, 512]); Sretb = sb("Sretb", [128, 512], BF16)
        Srw = sb("Srw", [128, 512]); Srwb = sb("Srwb", [128, 512], BF16)
        shcol = sb("shcol", [128, 14])
        oT3 = [sb(f"oT{b}", [128, 4 * TP], BF16) for b in range(3)]
        FA = sb("FA", [128, 4 * 2048]); P.sub["FA"] = 4
        HA = sb("HA", [128, 6 * 2048], BF16); P.sub["HA"] = 6
        tmp = [sb(f"tmp{i}", [128, TP + 1]) for i in range(6)]
        t1, t2, t3, t4, t5, t6 = tmp
        ropet = [sb("ropet0", [128, 128])] * 2
        scsb = sb("scsb", [128, 1024])
        xin = scsb
        ptsb = [sb(f"ptsb{i}", [128, 1024], BF16) for i in range(2)]
        oatm = sb("oatm", [128, 512], BF16)
        scs = oatm
        rden = sb("rden", [128, 8])
        lob = sb("lob", [128, TP], BF16); sgl = sb("sgl", [128, TP], BF16)
        MM = sb("MM", [64, 4 * 512], BF16)
        CB = [sb(f"CB{i}", [64, 1024], BF16) for i in range(2)]
        AT_ = [sb(f"AaT{i}", [64, 512], BF16) for i in range(2)]
        hbk = sb("hbk", [128, 2 * 256], BF16)
        BhTM = sb("BhTM", [64, 512], BF16); KhTM = sb("KhTM", [64, 512], BF16)
        Vb = sb("Vb", [64, 512], BF16); Vpad = sb("Vpad", [64, 1024], BF16)
        Tf0 = sb("Tf0", [64, 512], BF16)
        Gpad = sb("Gpad", [64, 1024], BF16); Ub = sb("Ub", [64, 512], BF16); Upad = sb("Upad", [64, 1024], BF16)
        stmp = sb("stmp", [128, 512])
        otr = stmp
        rwcp = stmp[:, 0:256]
        wc = sb("wc", [128, 4 * 8])
        rwst = scsb[0:64, 0:512]
        shin = sb("shin", [128, 14 * NSMP]); shout = sb("shout", [128, 14 * NSMP])

        PA = ps("PA", [128, 1024], sub=2); PB = ps("PB", [128, 1024], sub=2); PC = ps("PC", [128, 1024], sub=2)
        PD = ps("PD", [128, 512]); PT = ps("PT", [128, 1024], BF16)

        def C(n, w=1):
            return cst[:, _c[n]:_c[n] + w]

        def PR(l, n, i=0, w=1):
            return prm[:, l * NPRM + _p[n] + i:l * NPRM + _p[n] + i + w]

        def A(x):
            return x[0] if isinstance(x, tuple) else x

        def mm(out, lhsT, rhs, start=True, stop=True, skip=False):
            kw = dict(skip_group_check=True) if skip else {}
            P.op("pe", lambda e: e.matmul(A(out), A(lhsT), A(rhs), start=start, stop=stop, **kw), [lhsT, rhs], [out])

        def tr(out, in_, ident):
            P.op("pe", lambda e: e.transpose(A(out), A(in_), A(ident)), [in_, ident], [out])

        def act(out, in_, func, bias=None, scale=1.0, eng="act"):
            kw = {}
            rd = [in_]
            if bias is not None:
                kw["bias"] = A(bias)
                if not isinstance(bias, float):
                    rd.append(bias)
            if not isinstance(scale, float):
                rd.append(scale)
            P.op(eng, lambda e: e.activation(out=A(out), in_=A(in_), func=func, scale=A(scale), **kw), rd, [out])

        def tt(out, in0, in1, op, eng="dve"):
            P.op(eng, lambda e: e.tensor_tensor(out=A(out), in0=A(in0), in1=A(in1), op=op), [in0, in1], [out])

        def ts(out, in0, s1, op0, s2=None, op1=None, eng="dve"):
            rd = [in0] + [s for s in (s1, s2) if s is not None and not isinstance(s, float)]
            kw = {}
            if op1 is not None:
                kw["op1"] = op1
            P.op(eng, lambda e: e.tensor_scalar(out=A(out), in0=A(in0), scalar1=A(s1),
                                                scalar2=(A(s2) if s2 is not None else None), op0=op0, **kw), rd, [out])

        def stt(out, in0, s, in1, op0, op1, eng="dve"):
            rd = [in0, in1] + ([s] if not isinstance(s, float) else [])
            P.op("dve", lambda e: e.scalar_tensor_tensor(out=A(out), in0=A(in0), scalar=A(s), in1=A(in1), op0=op0, op1=op1),
                 rd, [out])

        def cp(out, in_, eng="dve"):
            if eng == "act":
                P.op("act", lambda e: e.copy(out=A(out), in_=A(in_)), [in_], [out])
            else:
                P.op(eng, lambda e: e.tensor_copy(out=A(out), in_=A(in_)), [in_], [out])

        def recip(out, in_):
            P.op("dve", lambda e: e.reciprocal(out=A(out), in_=A(in_)), [in_], [out])

        def mset(out, val, eng="pool"):
            P.op(eng, lambda e: e.memset(A(out), val), [], [out])

        def scan(out, d0, d1, eng="dve"):
            P.op(eng, lambda e: e.tensor_tensor_scan(out=A(out), data0=A(d0), data1=A(d1), initial=0.0,
                                                     op0=ALU.mult, op1=ALU.add), [d0, d1], [out])

        def dma(out, in_, eng="sp", chan=None, wkey=None, rkey=None, slow=False):
            on = Prog.name(A(out))
            wr = [wkey] if wkey is not None else ([out] if on not in P.readonly else [])
            rd = (list(rkey) if isinstance(rkey, (list, tuple)) else [rkey]) if rkey is not None else [in_]
            ch = chan or (P.keys(wr[0])[0] if wr else Prog.name(A(in_)))
            ch = ch.replace(":", "_")
            kw = dict(allow_slow_non_contiguous=True) if slow else {}
            P.op(eng, lambda e: e.dma_start(out=A(out), in_=A(in_), **kw), rd, wr, dma=ch)

        def r3(ap, a):
            return ap.rearrange("p (a b) -> p a b", a=a)

        def bc_mid(base, n_mid):
            (ps_, rows), (st, w) = base.ap
            return bass.AP(base.tensor, base.offset, [[ps_, rows], [0, n_mid], [st, w]])

        def bc_in(base, w):
            (ps_, rows), (st, n) = base.ap
            return bass.AP(base.tensor, base.offset, [[ps_, rows], [st, n], [0, w]])

        def fa(k, j=0, c0=0, c1=TP):
            return (FA[:, k * 2048 + j * TP + c0:k * 2048 + j * TP + c1], k)

        def ha(k, j=0, c0=0, c1=TP, r0=0, r1=128):
            return (HA[r0:r1, k * 2048 + j * TP + c0:k * 2048 + j * TP + c1], k)

        class WS:
            def __init__(self):
                self.seq = []
                self.issued = 0
                self.used = 0

            def plan(self, l):
                w_in, w_br, w_out, w_fg, w_fu, w_fd, w_pp, w_pg = (wbf[k] for k in
                                                                      ("w_in", "w_br", "w_out", "w_fg", "w_fu", "w_fd", "w_pp", "w_pg"))
                wi = w_in[l]

                def colblk(src, c0, ncol, tag):
                    nk = src.shape[0] // 128
                    self.seq.append((tag, src[:, c0:c0 + ncol].rearrange("(k p) c -> p k c", p=128), nk, ncol))
                for i, tg in enumerate(["aq", "ak", "av", "bq", "bk", "bv", "bg"]):
                    colblk(wi, 512 * i, 512, tg)
                colblk(wi, 5120, 256, "cl")
                colblk(wi, 4096, 512, "ck")
                colblk(wi, 3584, 512, "cr")
                colblk(wi, 4608, 512, "cv")
                for b in range(3):
                    colblk(wi, 5376 + 1024 * b, 512, f"gl{b}a")
                    colblk(wi, 5376 + 1024 * b + 512, 512, f"gl{b}b")
                    colblk(w_br[l, b], 0, 1024, f"br{b}")
                colblk(w_out[l], 0, 512, "outa")
                colblk(w_out[l], 512, 512, "outb")
                for i in range(6):
                    ncol = 512 if i < 5 else 256
                    colblk(w_fg[l], 512 * i, ncol, f"fg{i}")
                    colblk(w_fu[l], 512 * i, ncol, f"fu{i}")
                for j in range(8):
                    colblk(w_fd[l], 128 * j, 128, f"fd{j}")
                colblk(w_pp[l], 0, 1024, "pp")
                colblk(w_pg[l], 0, 512, "pga")
                colblk(w_pg[l], 512, 512, "pgb")

            def _issue(self):
                i = self.issued
                tag, src, nk, ncol = self.seq[i]
                buf = wb[i % NWB]
                assert nk * ncol <= WBN
                dma(r3(buf[:, 0:nk * ncol], nk), src, eng="sp", rkey=["wcast:%d" % i_ for i_ in range(4)])
                self.issued += 1

            def get(self, tag):
                i = self.used
                assert self.seq[i][0] == tag, (self.seq[i][0], tag)
                while self.issued < min(len(self.seq), i + NWB - 1):
                    self._issue()
                self.used += 1
                _, _, nk, ncol = self.seq[i]
                buf = wb[i % NWB]
                return (lambda kc, c0, c1: buf[:, kc * ncol + c0:kc * ncol + c1]), nk, ncol

        W = WS()
        banks = [(PA, 0), (PA, 1), (PB, 0), (PB, 1)]
        bstate = [0]

        def nbank():
            t_, h = banks[bstate[0] % 4]
            bstate[0] += 1
            return (t_[:, h * 512:(h + 1) * 512], h)

        def bk(pb, r0, r1, c0, c1):
            return (pb[0][r0:r1, c0:c1], pb[1])

        def fm_proj(pb, T, wf, nk, c0, rhs_fn):
            for kc in range(nk):
                mm(bk(pb, 0, 128, 0, T), wf(kc, c0, c0 + 128), rhs_fn(kc), start=(kc == 0), stop=(kc == nk - 1))

        identf = C("ident", 128)
        onesf = C("ones", 128)
        blkf = C("blk", 128)

        dma(cst[:], cst_d)
        dma(r3(prm[:], DEPTH), prm_d.rearrange("l p n -> p l n"))
        cp(identb[:], identf, eng="pool")
        cp(onesb[:], onesf, eng="pool")
        cp(blkb[:], blkf, eng="pool")
        for l in range(DEPTH):
            ts(aqs[:, l:l + 1], PR(l, "aqn"), 0.125, ALU.mult)
        ci_ = 0
        for k_, src_ in wsrc.items():
            s2 = src_.flatten_outer_dims() if len(src_.shape) > 2 else src_
            d2 = wbf[k_].flatten_outer_dims() if len(src_.shape) > 2 else wbf[k_]
            rows = s2.shape[0]
            step = 256 if s2.shape[1] >= 4096 else 1024
            for r0 in range(0, rows, step):
                r1_ = min(rows, r0 + step)
                dma(d2[r0:r1_, :], s2[r0:r1_, :], eng="pool", wkey="wcast:%d" % (ci_ % 4), chan="wcast%d" % (ci_ % 4))
                ci_ += 1
        mset(Vwin[:], 1.0)
        for t_ in (Vpad, Gpad, Upad):
            mset(t_[:], 0.0)

        def load_x_tm(src, T):
            for ti in range(T // 128):
                dma(xin[:], src[ti * 128:(ti + 1) * 128, :])
                for half in range(2):
                    pb = nbank()
                    for k4 in range(4):
                        kc = half * 4 + k4
                        tr(bk(pb, 0, 128, k4 * 128, (k4 + 1) * 128), xin[:, kc * 128:(kc + 1) * 128], identf)
                    dst = r3(xT[:, half * 4 * TP:(half * 4 + 4) * TP], 4)[:, :, ti * 128:(ti + 1) * 128]
                    cp(dst, (r3(pb[0], 4), pb[1]), eng="act" if half else "dve")

        def store_x_tm(dst, T):
            for ti in range(T // 128):
                for half in range(2):
                    pb = nbank()
                    for k4 in range(4):
                        kc = half * 4 + k4
                        tr(bk(pb, 0, 128, k4 * 128, (k4 + 1) * 128), xT[:, kc * TP + ti * 128:kc * TP + (ti + 1) * 128], identf)
                    cp(xin[:, half * 512:(half + 1) * 512], pb, eng="act" if half else "dve")
                dma(dst[ti * 128:(ti + 1) * 128, :], xin[:])

        def rms_stats(T, dst, srcs_fn, nk, div, epsname, lhs=None):
            lhs = lhs if lhs is not None else onesb[:]
            for kc in range(nk):
                sq = sqb[kc % 2]
                if kc % 2 == 0:
                    act(sq[:, 0:T], srcs_fn(kc), AF.Square)
                else:
                    tt(sq[:, 0:T], srcs_fn(kc), srcs_fn(kc), ALU.mult)
                mm(PD[:, 0:T], lhs, sq[:, 0:T], start=(kc == 0), stop=(kc == nk - 1))
            act(dst, PD[:, 0:T], AF.Ln, bias=C(epsname), scale=1.0 / div)
            act(dst, dst, AF.Exp, scale=-0.5)

        def norm_to_hT(l, T, gain):
            rms_stats(T, t3[:, 0:T], lambda kc: xT[:, kc * TP:kc * TP + T], KC, float(D), "eps6")
            for kc in range(KC):
                eng = "dve" if kc % 2 == 0 else "pool"
                if gain is None:
                    tt(hT[:, kc * TP:kc * TP + T], xT[:, kc * TP:kc * TP + T], t3[:, 0:T], ALU.mult, eng=eng)
                else:
                    stt(hT[:, kc * TP:kc * TP + T], xT[:, kc * TP:kc * TP + T], PR(l, gain, kc), t3[:, 0:T],
                        ALU.mult, ALU.mult, eng=eng)

        def hrhs(T):
            return lambda kc: hT[:, kc * TP:kc * TP + T]

        def qk_norm(pb, T, gain_ap, dst_f32):
            cp(t4[:, 0:T], bk(pb, 0, 128, 0, T), eng="act")
            act(sqb[0][:, 0:T], bk(pb, 0, 128, 0, T), AF.Square)
            mm(PD[:, 0:T], blkb[:], sqb[0][:, 0:T])
            act(t6[:, 0:T], PD[:, 0:T], AF.Ln, bias=C("eps6"), scale=1.0 / 64)
            act(t6[:, 0:T], t6[:, 0:T], AF.Exp, scale=-0.5)
            stt(dst_f32, t4[:, 0:T], gain_ap, t6[:, 0:T], ALU.mult, ALU.mult)

        def mixer_A(l, T, G):
            kslot = G["kslot"]
            wf, nk, _ = W.get("aq")
            for j in range(4):
                pb = nbank()
                fm_proj(pb, T, wf, nk, j * 128, hrhs(T))
                qk_norm(pb, T, aqs[:, l:l + 1], ha(0, j, 0, T))
            wf, nk, _ = W.get("ak")
            for j in range(4):
                pb = nbank()
                fm_proj(pb, T, wf, nk, j * 128, hrhs(T))
                qk_norm(pb, T, PR(l, "akn"), fa(0, j, 0, T))
                cp(Kwin[:, j * 1024 + kslot * 512:j * 1024 + kslot * 512 + T], fa(0, j, 0, T), eng="pool")
            for (dst_k, dst_v, t0, L) in G["a_out"]:
                pb = nbank()
                for j in range(4):
                    tr(bk(pb, 0, L, j * 128, (j + 1) * 128), fa(0, j, t0, t0 + L), identf)
                cp(otr[0:L, :], bk(pb, 0, L, 0, 512), eng="act")
                dma(dst_k, otr[0:L, :])
            wf, nk, _ = W.get("av")
            for (t0, L, vtile) in G["vt"]:
                pb = nbank()
                for kc in range(nk):
                    mm(bk(pb, 0, L, 0, 512), hT[:, kc * TP + t0:kc * TP + t0 + L], wf(kc, 0, 512),
                       start=(kc == 0), stop=(kc == nk - 1))
                vdst = Vwin[0:L, vtile * 520:(vtile + 1) * 520].rearrange("p (h d) -> p h d", h=8)[:, :, 0:64]
                cp(vdst, (pb[0][0:L, :].rearrange("p (h d) -> p h d", h=8), pb[1]), eng="act")
                for (dst_k, dst_v, ot0, oL) in G["a_out"]:
                    if ot0 == t0 and oL == L:
                        cp(otr[0:L, :], bk(pb, 0, L, 0, 512), eng="dve")
                        dma(dst_v, otr[0:L, :])
            oaT = oT3[0]
            for sg in G["segs"]:
                if sg.get("cache") is not None:
                    b = sg["cache"]
                    dma((r3(HA[:, 2048:4096], 4), 1), cak[l, b].rearrange("(t p) c -> p t c", p=128), eng="pool")
                    for t_ in range(4):
                        dma(Vwin[:, t_ * 520:(t_ + 1) * 520].rearrange("p (h d) -> p h d", h=8)[:, :, 0:64],
                            cav[l, b, t_ * 128:(t_ + 1) * 128, :].rearrange("p (h d) -> p h d", h=8), eng="pool")
                    for t_ in range(4):
                        for j in range(4):
                            tr(PT[:, j * 128:(j + 1) * 128], (HA[:, 2048 + t_ * 512 + j * 128:2048 + t_ * 512 + (j + 1) * 128], 1), identb[:])
                        dst = r3(Kwin[:, :], 4)[:, :, t_ * 128:(t_ + 1) * 128]
                        cp(dst, r3(PT[:, 0:512], 4), eng="act" if t_ % 2 else "dve")
                for qb in sg["qblocks"]:
                    q0, nq, kts = qb["q0"], qb["nq"], qb["kts"]
                    def scores(ki):
                        kcol, nkk, vtile, bkt = kts[ki]
                        scp = PA if ki % 2 == 0 else PB
                        for bnk in range(2):
                            mm((scp[0:nkk, bnk * 512:(bnk + 1) * 512], bnk), identb[0:nkk, 0:nkk],
                               biasT[0:nkk, bkt * 1024 + bnk * 512:bkt * 1024 + (bnk + 1) * 512], start=True, stop=False, skip=True)
                        for h in range(8):
                            j, b0 = h // 2, (h % 2) * 64
                            sl_ = (h % 2) * 4 + h // 2
                            mm((scp[0:nkk, sl_ * 128:sl_ * 128 + nq], sl_ // 4),
                               Kwin[b0:b0 + 64, j * 1024 + kcol:j * 1024 + kcol + nkk],
                               ha(0, j, q0, q0 + nq, b0, b0 + 64), start=False, stop=True, skip=True)

                    def soft_pv(ki):
                        kcol, nkk, vtile, bkt = kts[ki]
                        scp = PA if ki % 2 == 0 else PB
                        pt_ = ptsb[ki % 2]
                        act(r3(pt_[0:nkk, :], 8)[:, :, 0:nq], r3(scp[0:nkk, :], 8)[:, :, 0:nq], AF.Exp)
                        for h in range(8):
                            half = h // 4
                            oc0 = half * 512 + (h % 4) * 65
                            sl_ = (h % 2) * 4 + h // 2
                            mm((PC[0:nq, oc0:oc0 + 65], half), pt_[0:nkk, sl_ * 128:sl_ * 128 + nq],
                               Vwin[0:nkk, vtile * 520 + h * 65:vtile * 520 + (h + 1) * 65],
                               start=(ki == 0 and h % 4 == 0), stop=(ki == len(kts) - 1), skip=True)
                    scores(0)
                    for ki in range(len(kts)):
                        if ki + 1 < len(kts):
                            scores(ki + 1)
                        soft_pv(ki)
                    for half in range(2):
                        ov = PC[0:nq, half * 512:half * 512 + 260].rearrange("p (h d) -> p h d", h=4)
                        recip(rden[0:nq, half * 4:half * 4 + 4], (ov[:, :, 64], half))
                        tt(oatm[0:nq, half * 256:(half + 1) * 256].rearrange("p (h d) -> p h d", h=4), (ov[:, :, 0:64], half),
                           bc_in(rden[0:nq, half * 4:half * 4 + 4], 64), ALU.mult)
                    for j in range(4):
                        tr(PT[:, j * 128:j * 128 + nq], oatm[0:nq, j * 128:(j + 1) * 128], identb[0:nq, 0:nq])
                    cp(r3(oaT[:, :], 4)[:, :, q0:q0 + nq], r3(PT[:, 0:512], 4)[:, :, 0:nq], eng="act")

        def mixer_B(l, T, G):
            obT = oT3[1]
            for bi, blk in enumerate(("bq", "bk", "bv")):
                wf, nk, _ = W.get(blk)
                for (t0, L, idx, Cc, rp_src) in G["tm"]:
                    pb = nbank()
                    for kc in range(nk):
                        mm(bk(pb, 0, L, 0, 512), hT[:, kc * TP + t0:kc * TP + t0 + L], wf(kc, 0, 512),
                           start=(kc == 0), stop=(kc == nk - 1))
                    src = bk(pb, 0, L, 0, 512)
                    if blk == "bv":
                        if _os.environ.get("KV", "ab").find("a") >= 0:
                            cp(ha(4, idx, 0, 512, 0, L), src, eng="act")
                        if _os.environ.get("KV", "ab").find("b") >= 0:
                            tt((r3(ha(5, idx, 0, 512, 0, L)[0], 4), 5), (r3(src[0], 4), src[1]),
                               bc_in(C("gc%d" % Cc, 4)[0:L, :], 128), ALU.mult)
                        continue
                    rp = ropet[bi % 2]
                    dma(rp[0:L, :], rp_src)
                    x4 = pb[0][0:L, :].rearrange("p (h d) -> p h d", h=4)
                    x1 = (x4[:, :, 0:64], pb[1]); x2 = (x4[:, :, 64:128], pb[1])
                    cosb = bc_mid(rp[0:L, 0:64], 4)
                    sinb = bc_mid(rp[0:L, 64:128], 4)
                    a1 = r3(t4[0:L, 0:256], 4); a2 = r3(t5[0:L, 0:256], 4)
                    o4 = r3(t6[0:L, 0:512], 4)
                    tt(a1, x1, cosb, ALU.mult)
                    tt(a2, x2, sinb, ALU.mult)
                    tt(o4[:, :, 0:64], a1, a2, ALU.subtract, eng="pool")
                    tt(a1, x1, sinb, ALU.mult)
                    tt(a2, x2, cosb, ALU.mult)
                    tt(o4[:, :, 64:128], a1, a2, ALU.add, eng="pool")
                    slot = 2 if blk == "bq" else 3
                    tname = ("tq%d" if blk == "bq" else "tk%d") % Cc
                    dst = ha(slot, idx, 0, 512, 0, L)
                    tt((r3(dst[0], 4), slot), o4, bc_in(C(tname, 4)[0:L, :], 128), ALU.mult)
                    for h in range(4):
                        tr(PT[:, h * 128:h * 128 + L], ha(slot, idx, h * 128, (h + 1) * 128, 0, L), identb[0:L, 0:L])
                    fslot = 0 if blk == "bq" else 1
                    cp((r3(HA[:, fslot * 2048:(fslot + 1) * 2048], 4)[:, :, t0:t0 + L], fslot), r3(PT[:, 0:512], 4)[:, :, 0:L],
                       eng="act")
                chk("B_" + blk)
            wf, nk, _ = W.get("bg")
            for h in range(4):
                pb = nbank()
                fm_proj(pb, T, wf, nk, h * 128, hrhs(T))
                act(fa(0, h, 0, T), bk(pb, 0, 128, 0, T), AF.Silu)
            chk("B_bg")
            for sg in G["segs"]:
                if sg["state_in"] == "zero":
                    mset(Sret[:], 0.0)
                    mset(Sretb[:], 0.0)
                elif sg["state_in"] is not None:
                    dma(r3(Sret[:], 4), sret[l, sg["state_in"]].rearrange("h d e -> d h e"))
                    cp(Sretb[:], Sret[:], eng="act")
                for (t0, L, idx, Cc, _rs) in sg["tm"]:
                    for h in range(4):
                        mm((PC[0:L, h * 128:h * 128 + L], 0), ha(1, h, t0, t0 + L), ha(0, h, t0, t0 + L))
                    tt(r3(scs[0:L, :], 4)[:, :, 0:L], (r3(PC[0:L, 0:512], 4)[:, :, 0:L], 0),
                       bc_mid(C("m01", 128)[0:L, 0:L], 4), ALU.mult)
                    for h in range(4):
                        mm((PC[:, 512 + h * 128:512 + h * 128 + L], 1), ha(4, idx, h * 128, (h + 1) * 128, 0, L),
                           scs[0:L, h * 128:h * 128 + L], start=True, stop=False)
                        mm((PC[:, 512 + h * 128:512 + h * 128 + L], 1), Sretb[:, h * 128:(h + 1) * 128],
                           ha(0, h, t0, t0 + L), start=False, stop=True)
                    cp((r3(FA[:, 2048:4096], 4)[:, :, t0:t0 + L], 1), (r3(PC[:, 512:1024], 4)[:, :, 0:L], 1), eng="act")
                    for h in range(4):
                        mm(PD[:, h * 128:(h + 1) * 128], ha(3, idx, h * 128, (h + 1) * 128, 0, L),
                           ha(5, idx, h * 128, (h + 1) * 128, 0, L))
                    for h in range(4):
                        stt(Sret[:, h * 128:(h + 1) * 128], Sret[:, h * 128:(h + 1) * 128],
                            C("gc%d" % Cc, 4)[:, h:h + 1], PD[:, h * 128:(h + 1) * 128], ALU.mult, ALU.add)
                    cp(Sretb[:], Sret[:], eng="act")
                if sg["ret_out"] is not None:
                    dma(sg["ret_out"].rearrange("h d e -> d h e"), r3(Sret[:], 4))
            chk("B_chunks")
            for h in range(4):
                rms_stats(T, t3[:, 0:T], lambda kc, h=h: fa(1, h, 0, T), 1, 128.0, "eps6")
                tt(t4[:, 0:T], fa(1, h, 0, T), t3[:, 0:T], ALU.mult)
                tt(obT[:, h * TP:h * TP + T], t4[:, 0:T], fa(0, h, 0, T), ALU.mult, eng="pool")

        def mixer_C(l, T, G):
            ocT = oT3[2]
            nseg, Ls = G["nseg"], G["Ls"]
            Cc = G["Cc"]
            cmn = "cm%d" % Cc
            nch_seg = Ls // Cc
            nch = T // Cc

            def shifted(pb, c, dst):
                r_seg = t1[:, 0:nseg * (Ls + 1)].rearrange("p (s u) -> p s u", s=nseg)
                cp(r_seg[:, :, 1:Ls + 1], (pb[0][:, 0:T].rearrange("p (s u) -> p s u", s=nseg), pb[1]), eng="act")
                cp(r_seg[:, :, 0:1], G["prevcol"](c), eng="pool")
                for fn in G["savecol"]:
                    fn(c, r_seg)
                d_seg = t2[:, 0:T].rearrange("p (s u) -> p s u", s=nseg)
                tt(d_seg, r_seg[:, :, 0:Ls], r_seg[:, :, 1:Ls + 1], ALU.subtract)
                dst_seg = dst.rearrange("p (s u) -> p s u", s=nseg)
                stt(dst_seg, d_seg, PR(l, "mu", c), r_seg[:, :, 1:Ls + 1], ALU.mult, ALU.add)

            wf, nk, _ = W.get("cl")
            pb = nbank()
            fm_proj(pb, T, wf, nk, 0, hrhs(T))
            shifted(pb, 12, t3[:, 0:T])
            act(lob[0:64, 0:T], t3[0:64, 0:T], AF.Tanh)
            cp(lob[64:128, 0:T], t3[64:128, 0:T], eng="act")
            pb = nbank()
            fm_proj(pb, T, wf, nk, 128, hrhs(T))
            shifted(pb, 13, t3[:, 0:T])
            act(sgl[:, 0:T], t3[:, 0:T], AF.Sigmoid)
            for j in range(4):
                pb = nbank()
                mm(bk(pb, 0, 128, 0, T), cwa2b[0:64, j * 128:(j + 1) * 128], lob[0:64, 0:T])
                act(fa(0, j, 0, T), bk(pb, 0, 128, 0, T), AF.Sigmoid, bias=PR(l, "w0", j))
                scan(fa(1, j, 0, T), C(cmn, T), fa(0, j, 0, T))
                pb = nbank()
                mm(bk(pb, 0, 128, 0, T), cwa2b[64:128, j * 128:(j + 1) * 128], lob[64:128, 0:T])
                act(fa(2, j, 0, T), bk(pb, 0, 128, 0, T), AF.Sigmoid, bias=PR(l, "a0", j))
                pb = nbank()
                mm(bk(pb, 0, 128, 0, T), cg2b[:, j * 128:(j + 1) * 128], sgl[:, 0:T])
                cp(ha(5, j, 0, T), bk(pb, 0, 128, 0, T), eng="act")
            for j in range(4):
                csg_j = FA[:, 2048 + j * TP:2048 + j * TP + T]
                act(wc[:, j * 8:j * 8 + nch], (csg_j[:, Cc - 1:T:Cc], 1), AF.Exp, scale=-CDEC)
            wf, nk, _ = W.get("ck")
            for j in range(4):
                pb = nbank()
                fm_proj(pb, T, wf, nk, j * 128, hrhs(T))
                shifted(pb, 4 + j, t3[:, 0:T])
                ts(t4[:, 0:T], t3[:, 0:T], PR(l, "kk", j), ALU.mult)
                act(sqb[0][:, 0:T], t4[:, 0:T], AF.Square)
                mm(PD[:, 0:T], blkb[:], sqb[0][:, 0:T])
                ts(t5[:, 0:T], PD[:, 0:T], 1e-24, ALU.max)
                act(t5[:, 0:T], t5[:, 0:T], AF.Ln)
                act(t5[:, 0:T], t5[:, 0:T], AF.Exp, scale=-0.5)
                tt(t4[:, 0:T], t4[:, 0:T], t5[:, 0:T], ALU.mult)
                tt(t5[:, 0:T], fa(1, j, 0, T), fa(0, j, 0, T), ALU.subtract)
                act(t5[:, 0:T], t5[:, 0:T], AF.Exp, scale=-CDEC)
                stt(ha(0, j, 0, T), t4[:, 0:T], C("negone"), t5[:, 0:T], ALU.mult, ALU.mult)
                act(t5[:, 0:T], fa(1, j, 0, T), AF.Exp, scale=CDEC)
                tt(t4[:, 0:T], t4[:, 0:T], fa(2, j, 0, T), ALU.mult)
                tt(ha(2, j, 0, T), t4[:, 0:T], t5[:, 0:T], ALU.mult)
                ts(t4[:, 0:T], fa(2, j, 0, T), C("negone"), ALU.add, PR(l, "ka", j), ALU.mult)
                stt(fa(3, j, 0, T), t4[:, 0:T], C("one"), t3[:, 0:T], ALU.add, ALU.mult)
                tt(ha(3, j, 0, T), fa(3, j, 0, T), t5[:, 0:T], ALU.mult)
            wf, nk, _ = W.get("cr")
            for j in range(4):
                pb = nbank()
                fm_proj(pb, T, wf, nk, j * 128, hrhs(T))
                shifted(pb, j, t3[:, 0:T])
                act(t5[:, 0:T], fa(1, j, 0, T), AF.Exp, scale=-CDEC)
                tt(ha(1, j, 0, T), t3[:, 0:T], t5[:, 0:T], ALU.mult)
                stt(sqb[1][:, 0:T], t3[:, 0:T], PR(l, "rk", j), fa(3, j, 0, T), ALU.mult, ALU.mult)
                mm(PD[:, 0:T], blkb[:], sqb[1][:, 0:T])
                cp(fa(3, j, 0, T), PD[:, 0:T], eng="act")
            wf, nk, _ = W.get("cv")
            for j in range(4):
                pb = nbank()
                fm_proj(pb, T, wf, nk, j * 128, hrhs(T))
                shifted(pb, 8 + j, t3[:, 0:T])
                cp(ha(4, j, 0, T), t3[:, 0:T], eng="act")
                tt(fa(3, j, 0, T), fa(3, j, 0, T), t3[:, 0:T], ALU.mult)
            chk("C_prep")
            nst = {64: 6, 32: 5}[Cc]
            mskv = C("msk", 256)[0:Cc, :]
            X1 = FA[:, 2048:4096].bitcast(BF16)
            X2 = FA[:, 4096:6144].bitcast(BF16)
            sets = [dict(MM=(MM[:, :], None), Bh=(BhTM[:, :], None), Kh=(KhTM[:, :], None), Vb=(Vb[:, :], None),
                         Vp=(Vpad[:, :], None), Tf=(Tf0[:, :], None)),
                    dict(MM=(X1[0:64, 0:2048], 1), Bh=(X1[0:64, 2048:2560], 1), Kh=(X1[0:64, 2560:3072], 1),
                         Vb=(X1[0:64, 3072:3584], 1), Vp=(X2[0:64, 0:1024], 2), Tf=(X2[0:64, 1024:1536], 2))]

            def V(buf, r0, r1, c0, c1):
                ap, key = buf
                v = ap[r0:r1, c0:c1]
                return v if key is None else (v, key)

            def K_(buf, ap):
                return ap if buf[1] is None else (ap, buf[1])
            mset(V(sets[1]["Vp"], 0, 64, 0, 1024), 0.0)

            def front(t0, ci, S_):
                MMb, Bhb, Khb, Vbb, Vpb, Tfb = S_["MM"], S_["Bh"], S_["Kh"], S_["Vb"], S_["Vp"], S_["Tf"]
                wc0 = wc[:, ci:ci + 1]
                wcb = bass.AP(wc0.tensor, wc0.offset, [[wc0.ap[0][0], 128], [8, 4], [0, Cc]])
                for hi_, (src_slot, dstb) in enumerate(((2, Bhb), (3, Khb))):
                    hv = hbk[:, hi_ * 256:(hi_ + 1) * 256].rearrange("p (q c) -> p q c", q=4)[:, :, 0:Cc]
                    tt(hv, (r3(HA[:, src_slot * 2048:(src_slot + 1) * 2048], 4)[:, :, t0:t0 + Cc], src_slot), wcb, ALU.mult)
                    for q in range(4):
                        tr(PT[0:Cc, q * 128:(q + 1) * 128], hbk[:, hi_ * 256 + q * 64:hi_ * 256 + q * 64 + Cc], identb[:])
                    cp(V(dstb, 0, Cc, 0, 512), PT[0:Cc, 0:512], eng="act")
                    yield
                for q in range(4):
                    tr(PT[0:Cc, q * 128:(q + 1) * 128], ha(4, q, t0, t0 + Cc), identb[:])
                cp(V(Vbb, 0, Cc, 0, 512), PT[0:Cc, 0:512], eng="act")
                for hh in range(2):
                    cp(K_(Vpb, Vpb[0][0:Cc, :].rearrange("p (q h c) -> p q h c", q=4, h=2)[:, :, hh, hh * 64:(hh + 1) * 64]),
                       r3(PT[0:Cc, 0:512], 4)[:, :, hh * 64:(hh + 1) * 64], eng="dve")
                yield
                for q in range(4):
                    for hh in range(2):
                        mt = PA if hh == 0 else PC
                        mh = q // 2
                        b0 = hh * 64
                        a_ap = HA[b0:b0 + 64, 0 * 2048 + q * TP + t0:0 * 2048 + q * TP + t0 + Cc]
                        (pst, _), (st, _) = a_ap.ap
                        rhs2 = bass.AP(a_ap.tensor, a_ap.offset, [[pst, 64], [2048, 2], [st, Cc]])
                        base = mh * 512 + (q % 2) * 256
                        o13 = mt[0:Cc, base:base + 128].rearrange("p (a b) -> p a b", a=2)[:, :, 0:Cc]
                        o24 = mt[0:Cc, base + 128:base + 256].rearrange("p (a b) -> p a b", a=2)[:, :, 0:Cc]
                        P.op("pe", lambda e, o=o13, lh=ha(2, q, t0, t0 + Cc, b0, b0 + 64)[0], rh=rhs2:
                             e.matmul(o, lh, rh, start=True, stop=True), [(HA, 2), (HA, 0), (HA, 1)], [(mt, mh)])
                        P.op("pe", lambda e, o=o24, lh=ha(3, q, t0, t0 + Cc, b0, b0 + 64)[0], rh=rhs2:
                             e.matmul(o, lh, rh, start=True, stop=True), [(HA, 3), (HA, 0), (HA, 1)], [(mt, mh)])
                mm_base = MMb[0][0:Cc, 0:64]
                mm_ps = mm_base.ap[0][0]
                for hh in range(2):
                    mt = PA if hh == 0 else PC
                    for mh in range(2):
                        mview = mt[0:Cc, mh * 512:(mh + 1) * 512].rearrange("p (q m t) -> p q m t", q=2, m=4)[:, :, :, 0:Cc]
                        mskb = bass.AP(mskv.tensor, mskv.offset, [[mskv.ap[0][0], Cc], [0, 2], [64, 4], [1, Cc]])
                        dstv = bass.AP(mm_base.tensor, mm_base.offset + (2 * mh) * 512 + hh * 256,
                                       [[mm_ps, Cc], [512, 2], [64, 4], [1, Cc]])
                        tt(K_(MMb, dstv), (mview, mh), mskb, ALU.mult)
                yield
                for h in range(8):
                    q, b0 = h // 2, (h % 2) * 64
                    bnk = PA if h % 2 == 0 else PC
                    mm((bnk[0:Cc, q * 64:q * 64 + Cc], 0),
                       ha(0, q, t0, t0 + Cc, b0, b0 + 64), ha(2, q, t0, t0 + Cc, b0, b0 + 64))
                at0 = AT_[0][0:Cc, 0:64]
                for hh in range(2):
                    bnk = PA if hh == 0 else PC
                    dstv = bass.AP(at0.tensor, at0.offset + hh * 64, [[at0.ap[0][0], Cc], [128, 4], [1, Cc]])
                    tt(dstv, (r3(bnk[0:Cc, 0:256], 4)[:, :, 0:Cc], 0),
                       bc_mid(C("lsm", 64)[0:Cc, 0:Cc], 4), ALU.mult)
                n1v = K_(MMb, MMb[0][0:Cc, :].rearrange("p (h m t) -> p h m t", h=8, m=4)[:, :, 0, 0:Cc])

                def cbv(p_, part, h0=0, h1=8):
                    return CB[p_][0:Cc, h0 * 128:h1 * 128].rearrange("p (h c) -> p h c", h=h1 - h0)[:, :, part * 64:part * 64 + Cc]

                def pcv(part):
                    return PC[0:Cc, :].rearrange("p (h c) -> p h c", h=8)[:, :, part * 64:part * 64 + Cc]
                cp(cbv(0, 0), n1v, eng="act")
                tt(cbv(1, 1), n1v, bc_mid(identf[0:Cc, 0:Cc], 8), ALU.add)
                yield
                cur = 0
                for r in range(1, nst + 1):
                    nxt = 1 - cur
                    p_, n_ = (r - 1) % 2, r % 2
                    for h in range(8):
                        lh = AT_[cur][0:Cc, h * 64:h * 64 + Cc]
                        if r == 1:
                            mm((PC[0:Cc, h * 128:h * 128 + Cc], h // 4), lh, CB[p_][0:Cc, h * 128:h * 128 + Cc])
                        elif r <= nst - 1:
                            mm((PC[0:Cc, h * 128:(h + 1) * 128].rearrange("p (a c) -> p a c", a=2)[:, :, 0:Cc], h // 4), lh,
                               CB[p_][0:Cc, h * 128:(h + 1) * 128].rearrange("p (a c) -> p a c", a=2)[:, :, 0:Cc])
                        else:
                            mm((PC[0:Cc, h * 128 + 64:h * 128 + 64 + Cc], h // 4), lh, CB[p_][0:Cc, h * 128 + 64:h * 128 + 64 + Cc])
                    if r <= nst - 1:
                        for h in range(8):
                            mm((PA[0:Cc, 512 + h * 64:512 + h * 64 + Cc], 1), CB[p_][0:Cc, h * 128:h * 128 + Cc],
                               AT_[cur][0:Cc, h * 64:h * 64 + Cc])
                    if r >= 2:
                        dst_t = cbv(n_, 1) if r < nst else K_(Tfb, r3(Tfb[0][0:Cc, :], 8)[:, :, 0:Cc])
                        tt(dst_t, cbv(p_, 1), pcv(1), ALU.add)
                    if r <= nst - 1:
                        cp(cbv(n_, 0), pcv(0), eng="act")
                        cp(r3(AT_[nxt][0:Cc, :], 8)[:, :, 0:Cc], (r3(PA[0:Cc, 512:1024], 8)[:, :, 0:Cc], 1), eng="act")
                        cur = nxt
                    yield

            def chain(t0, ci, S_):
                MMb, Bhb, Khb, Vbb, Vpb, Tfb = S_["MM"], S_["Bh"], S_["Kh"], S_["Vb"], S_["Vp"], S_["Tf"]
                for q in range(4):
                    mm((PB[0:Cc, q * 128:(q + 1) * 128], 0), ha(0, q, t0, t0 + Cc), Srwb[:, q * 128:(q + 1) * 128],
                       start=True, stop=False)
                    for hh in range(2):
                        mm((PB[0:Cc, q * 128:(q + 1) * 128], 0),
                           V(MMb, 0, Cc, q * 512 + hh * 256 + 128, q * 512 + hh * 256 + 128 + Cc),
                           V(Vpb, 0, Cc, (q * 2 + hh) * 128, (q * 2 + hh + 1) * 128), start=False, stop=(hh == 1))
                for hh in range(2):
                    cp(Gpad[0:Cc, :].rearrange("p (q h c) -> p q h c", q=4, h=2)[:, :, hh, hh * 64:(hh + 1) * 64],
                       (r3(PB[0:Cc, 0:512], 4)[:, :, hh * 64:(hh + 1) * 64], 0), eng="act" if hh else "dve")
                yield
                for q in range(4):
                    for hh in range(2):
                        h = q * 2 + hh
                        mm((PB[0:Cc, 512 + q * 128:512 + (q + 1) * 128], 1), V(Tfb, 0, Cc, h * 64, h * 64 + Cc),
                           Gpad[0:Cc, h * 128:(h + 1) * 128], start=(hh == 0), stop=(hh == 1))
                cp(Ub[0:Cc, :], (PB[0:Cc, 512:1024], 1), eng="act")
                for hh in range(2):
                    cp(Upad[0:Cc, :].rearrange("p (q h c) -> p q h c", q=4, h=2)[:, :, hh, hh * 64:(hh + 1) * 64],
                       (r3(PB[0:Cc, 512:1024], 4)[:, :, hh * 64:(hh + 1) * 64], 1), eng="dve")
                yield
                for q in range(4):
                    oy = PD[:, q * 64:q * 64 + Cc]
                    mm(oy, Srwb[:, q * 128:(q + 1) * 128], ha(1, q, t0, t0 + Cc), start=True, stop=False)
                    for hh in range(2):
                        h = q * 2 + hh
                        mm(oy, Upad[0:Cc, h * 128:(h + 1) * 128],
                           V(MMb, 0, Cc, q * 512 + hh * 256 + 64, q * 512 + hh * 256 + 64 + Cc), start=False, stop=False)
                        mm(oy, V(Vpb, 0, Cc, h * 128, (h + 1) * 128),
                           V(MMb, 0, Cc, q * 512 + hh * 256 + 192, q * 512 + hh * 256 + 192 + Cc), start=False, stop=(hh == 1))
                cp((r3(FA[:, 0:2048], 4)[:, :, t0:t0 + Cc], 0), r3(PD[:, 0:256], 4)[:, :, 0:Cc], eng="act")
                yield
                for q in range(4):
                    mm(PD[:, q * 128:(q + 1) * 128], V(Bhb, 0, Cc, q * 128, (q + 1) * 128), Ub[0:Cc, q * 128:(q + 1) * 128],
                       start=True, stop=False)
                    mm(PD[:, q * 128:(q + 1) * 128], V(Khb, 0, Cc, q * 128, (q + 1) * 128), V(Vbb, 0, Cc, q * 128, (q + 1) * 128),
                       start=False, stop=True)
                tt(r3(stmp[:], 4), r3(PD[:], 4), bc_mid(blkf, 4), ALU.mult)
                for q in range(4):
                    stt(Srw[:, q * 128:(q + 1) * 128], Srw[:, q * 128:(q + 1) * 128], wc[:, q * 8 + ci:q * 8 + ci + 1],
                        stmp[:, q * 128:(q + 1) * 128], ALU.mult, ALU.add)
                cp(Srwb[:], Srw[:], eng="act")
                yield

            def state_in(sg):
                if sg["rw_in"] == "zero":
                    mset(Srw[:], 0.0)
                    mset(Srwb[:], 0.0)
                elif sg["rw_in"] is not None:
                    mset(Srw[:], 0.0)
                    dma(r3(rwst, 8), srw[l, sg["rw_in"]].rearrange("h i j -> i h j"))
                    for q in range(4):
                        tr(PD[:, q * 64:(q + 1) * 64], rwst[:, q * 128:(q + 1) * 128], identf[0:64, 0:64])
                    for hh in range(2):
                        cp(r3(Srw[hh * 64:(hh + 1) * 64, :], 4)[:, :, hh * 64:(hh + 1) * 64],
                           r3(PD[hh * 64:(hh + 1) * 64, 0:256], 4), eng="act")
                    cp(Srwb[:], Srw[:], eng="act")

            def state_out(sg):
                if sg["rw_out"] is not None:
                    for hh in range(2):
                        cp(r3(rwcp[hh * 64:(hh + 1) * 64, :], 4), r3(Srw[hh * 64:(hh + 1) * 64, :], 4)[:, :, hh * 64:(hh + 1) * 64],
                           eng="pool")
                    for q in range(4):
                        tr(PD[0:64, q * 128:(q + 1) * 128], rwcp[:, q * 64:(q + 1) * 64], identf)
                    cp(rwst, PD[0:64, :], eng="act")
                    dma(sg["rw_out"].rearrange("h i j -> i h j"), r3(rwst, 8))

            items = [(sg, t0, ci, k == 0, k == len(sg["chunks"]) - 1) for sg in G["segs"]
                     for k, (t0, ci) in enumerate(sg["chunks"])]
            for _ in front(items[0][1], items[0][2], sets[0]):
                pass
            for n_, (sg, t0, ci, first, last_) in enumerate(items):
                gf = front(items[n_ + 1][1], items[n_ + 1][2], sets[(n_ + 1) % 2]) if n_ + 1 < len(items) else iter(())
                if first:
                    state_in(sg)
                gc = chain(t0, ci, sets[n_ % 2])
                alive_f = alive_c = True
                step = 0
                while alive_f or alive_c:
                    if alive_f:
                        alive_f = next(gf, "END") != "END"
                    if alive_c and (step % 2 == 1 or not alive_f):
                        alive_c = next(gc, "END") != "END"
                    step += 1
                if last_:
                    state_out(sg)
            chk("C_chunks")
            for q in range(4):
                cp(sqb[0][:, 0:T], fa(0, q, 0, T), eng="pool")
                mm(PD[:, 0:T], blkb[:], sqb[0][:, 0:T])
                stt(t3[:, 0:T], PD[:, 0:T], -1.0 / 64, fa(0, q, 0, T), ALU.mult, ALU.add)
                act(sqb[1][:, 0:T], t3[:, 0:T], AF.Square)
                mm(PD[:, 0:T], blkb[:], sqb[1][:, 0:T])
                act(t4[:, 0:T], PD[:, 0:T], AF.Ln, bias=C("gneps"), scale=1.0 / 64)
                act(t4[:, 0:T], t4[:, 0:T], AF.Exp, scale=-0.5)
                tt(t3[:, 0:T], t3[:, 0:T], t4[:, 0:T], ALU.mult)
                ts(t3[:, 0:T], t3[:, 0:T], PR(l, "lnw", q), ALU.mult, PR(l, "lnb", q), ALU.add)
                tt(t3[:, 0:T], t3[:, 0:T], fa(3, q, 0, T), ALU.add)
                tt(ocT[:, q * TP:q * TP + T], t3[:, 0:T], ha(5, q, 0, T), ALU.mult)

        def phase_G(l, T):
            for b in range(3):
                for half, tg in enumerate((f"gl{b}a", f"gl{b}b")):
                    wf, nk, _ = W.get(tg)
                    for j in range(4):
                        c = half * 4 + j
                        pb = nbank()
                        fm_proj(pb, T, wf, nk, j * 128, hrhs(T))
                        act(ha(c // 4, c % 4, 0, T), bk(pb, 0, 128, 0, T), AF.Sigmoid)
                wf, nk, _ = W.get(f"br{b}")
                for c in range(8):
                    pb = nbank()
                    fm_proj(pb, T, wf, nk, c * 128, lambda kc: oT3[b][:, kc * TP:kc * TP + T])
                    if b == 0:
                        tt(fa(c // 4, c % 4, 0, T), bk(pb, 0, 128, 0, T), ha(c // 4, c % 4, 0, T), ALU.mult)
                    else:
                        tt(t4[:, 0:T], bk(pb, 0, 128, 0, T), ha(c // 4, c % 4, 0, T), ALU.mult)
                        tt(fa(c // 4, c % 4, 0, T), fa(c // 4, c % 4, 0, T), t4[:, 0:T], ALU.add, eng="pool")
            for c in range(8):
                cp(ha(2 + c // 4, c % 4, 0, T), fa(c // 4, c % 4, 0, T), eng="act" if c % 2 else "pool")
            for half, tg in enumerate(("outa", "outb")):
                wf, nk, _ = W.get(tg)
                for j in range(4):
                    c = half * 4 + j
                    pb = nbank()
                    fm_proj(pb, T, wf, nk, j * 128, lambda kc: ha(2 + kc // 4, kc % 4, 0, T))
                    tt(xT[:, c * TP:c * TP + T], xT[:, c * TP:c * TP + T], bk(pb, 0, 128, 0, T), ALU.add)

        def phase_F(l, T):
            norm_to_hT(l, T, "nffn")
            for i in range(6):
                wg, nk, ncol = W.get(f"fg{i}")
                wu, _, _ = W.get(f"fu{i}")
                for j in range(ncol // 128):
                    c = i * 4 + j
                    pg = nbank()
                    fm_proj(pg, T, wg, nk, j * 128, hrhs(T))
                    pu = nbank()
                    fm_proj(pu, T, wu, nk, j * 128, hrhs(T))
                    act(t4[:, 0:T], bk(pg, 0, 128, 0, T), AF.Silu)
                    tt(ha(c // 4, c % 4, 0, T), t4[:, 0:T], bk(pu, 0, 128, 0, T), ALU.mult)
            for j in range(8):
                wf, nk, _ = W.get(f"fd{j}")
                pb = nbank()
                fm_proj(pb, T, wf, nk, 0, lambda kc: ha(kc // 4, kc % 4, 0, T))
                tt(xT[:, j * TP:j * TP + T], xT[:, j * TP:j * TP + T], bk(pb, 0, 128, 0, T), ALU.add)

        def phase_P(l, T, psrc):
            for ti in range((T + 127) // 128):
                dma(xin[:, 0:PLE], psrc[ti * 128:(ti + 1) * 128, :])
                pb = nbank()
                for k in range(2):
                    tr(bk(pb, 0, 128, k * 128, (k + 1) * 128), xin[:, k * 128:(k + 1) * 128], identf)
                cp((r3(HA[:, 4 * 2048:4 * 2048 + 2 * TP], 2)[:, :, ti * 128:(ti + 1) * 128], 4), (r3(pb[0][:, 0:256], 2), pb[1]),
                   eng="act")
            wf, nk, _ = W.get("pp")
            for c in range(8):
                pb = nbank()
                fm_proj(pb, T, wf, nk, c * 128, lambda kc: ha(4, kc, 0, T))
                cp(fa(c // 4, c % 4, 0, T), bk(pb, 0, 128, 0, T), eng="act")
            rms_stats(T, t3[:, 0:T], lambda kc: fa(kc // 4, kc % 4, 0, T), 8, float(D), "eps6")
            for c in range(8):
                stt(fa(c // 4, c % 4, 0, T), fa(c // 4, c % 4, 0, T), PR(l, "nple", c), t3[:, 0:T], ALU.mult, ALU.mult,
                    eng="dve" if c % 2 else "pool")
            norm_to_hT(l, T, None)
            for half, tg in enumerate(("pga", "pgb")):
                wf, nk, _ = W.get(tg)
                for j in range(4):
                    c = half * 4 + j
                    pb = nbank()
                    fm_proj(pb, T, wf, nk, j * 128, hrhs(T))
                    act(t4[:, 0:T], bk(pb, 0, 128, 0, T), AF.Sigmoid)
                    tt(t4[:, 0:T], t4[:, 0:T], fa(c // 4, c % 4, 0, T), ALU.mult)
                    tt(xT[:, c * TP:c * TP + T], xT[:, c * TP:c * TP + T], t4[:, 0:T], ALU.add, eng="pool")

        KSTOP = _os.environ.get("KSTOP", "")

        class _Stop(Exception):
            pass

        def chk(tag):
            if KSTOP == tag:
                raise _Stop()

        def layer(l, T, G):
            chk("load")
            norm_to_hT(l, T, "nmix")
            chk("norm")
            mixer_A(l, T, G)
            chk("A")
            mixer_B(l, T, G)
            chk("B")
            mixer_C(l, T, G)
            chk("C")
            if dbg and G.get("dbg"):
                for b, nm in enumerate(("d_oa", "d_ob", "d_oc")):
                    dma(dbg_t[nm], oT3[b][:], eng="pool")
            phase_G(l, T)
            chk("G")
            phase_F(l, T)
            chk("F")
            phase_P(l, T, G["psrc"])
            chk("P")

        def x_in(l, src_tm, src_fm, T, key):
            if l == 0:
                load_x_tm(src_tm, T)
            else:
                dma(r3(xT[:], KC)[:, :, 0:T], src_fm.rearrange("k p t -> p k t"), rkey=key)

        def x_out(l, dst_tm, dst_fm, T, key):
            if l == DEPTH - 1:
                store_x_tm(dst_tm, T)
            else:
                dma(dst_fm.rearrange("k p t -> p k t"), r3(xT[:], KC)[:, :, 0:T], wkey=key)

        for l in range(DEPTH):
            for g in range(NG + 1):
                W.plan(l)

        def _main():
          for l in range(DEPTH):
              dma(cwa2b[:], cwa2[l], eng="pool")
              dma(cg2b[:], cg2[l], eng="pool")
              dma(biasT[:], bias_d[l], eng="pool")
              mset(shcol[:], 0.0)
              for g in range(NG):
                  kslot = g % 2
                  last = (g == NG - 1)
                  T = TP
                  x_in(l, xp[g * TP:(g + 1) * TP, :], x1p[:, :, g * TP:(g + 1) * TP], T, "x1p:%d" % (g % 4))
                  qbs = []
                  for m_ in range(4):
                      kts = []
                      for kt in range(5):
                          wcd = 128 * (m_ + kt)
                          if wcd < 512:
                              if g == 0:
                                  continue
                              slot, off = 1 - kslot, wcd
                          else:
                              slot, off = kslot, wcd - 512
                          kts.append((slot * 512 + off, 128, slot * 4 + off // 128, kt))
                      qbs.append(dict(q0=128 * m_, nq=128, kts=kts))
                  tmt = [(ti * 128, 128, ti, 128, rope_p[g * TP + ti * 128:g * TP + (ti + 1) * 128, :]) for ti in range(4)]

                  def prevcol(c):
                      return shcol[:, c:c + 1].rearrange("p (s u) -> p s u", s=1)

                  def savecol(c, r_seg):
                      cp(shcol[:, c:c + 1], r_seg[:, 0, TP:TP + 1], eng="pool")
                  G = dict(kslot=kslot, vt=[(ti * 128, 128, kslot * 4 + ti) for ti in range(4)],
                           a_out=([(akp[l, ti * 128:(ti + 1) * 128, :], avp[l, ti * 128:(ti + 1) * 128, :], ti * 128, 128)
                                   for ti in range(4)] if last else []),
                           tm=tmt,
                           segs=[dict(qblocks=qbs, tm=tmt, state_in=("zero" if g == 0 else None),
                                      ret_out=(retp[l] if last else None),
                                      rw_in=("zero" if g == 0 else None), rw_out=(rwp[l] if last else None),
                                      chunks=[(64 * ci, ci) for ci in range(8)])],
                           nseg=1, Ls=TP, Cc=64, prevcol=prevcol, savecol=[savecol],
                           psrc=pp[l, g * TP:(g + 1) * TP, :], dbg=(l == 0 and g == 0))
                  layer(l, T, G)
                  if last:
                      dma(shp[l].rearrange("(c p) -> p c", p=128), shcol[:], slow=True)
                  if dbg and l == 0 and g == 0:
                      dma(dbg_t["d_x"], xT[:])
                  x_out(l, yp[g * TP:(g + 1) * TP, :], x1p[:, :, g * TP:(g + 1) * TP], T, "x1p:%d" % (g % 4))
              T = TSM
              x_in(l, xs, x1s, T, "x1s")
              segs = []
              tmt = []
              dma(shin[:, 0:14 * NSMP].rearrange("p (s c) -> p s c", s=NSMP), ssh[l].rearrange("s (c p) -> p s c", p=128), slow=True)
              for s in range(NSMP):
                  tms = [(LS * s, LS, s, 32, rope_s[LS * s:LS * (s + 1), :])]
                  tmt += tms
                  kts = [(kt * 128, 128, kt, kt) for kt in range(4)] + [(512 + LS * s, LS, 4 + s, 4)]
                  segs.append(dict(cache=s, qblocks=[dict(q0=LS * s, nq=LS, kts=kts)], tm=tms, state_in=s, ret_out=rets[l, s],
                                   rw_in=s, rw_out=rws[l, s], chunks=[(LS * s, s)]))

              def prevcol_s(c):
                  return shin[:, 0:14 * NSMP].rearrange("p (s c) -> p s c", s=NSMP)[:, :, c:c + 1]

              def savecol_s(c, r_seg):
                  cp(shout[:, 0:14 * NSMP].rearrange("p (s c) -> p s c", s=NSMP)[:, :, c:c + 1], r_seg[:, :, LS:LS + 1], eng="pool")
              G = dict(kslot=1, vt=[(LS * s, LS, 4 + s) for s in range(NSMP)],
                       a_out=[(aks[l, LS * s:LS * (s + 1), :], avs[l, LS * s:LS * (s + 1), :], LS * s, LS) for s in range(NSMP)],
                       tm=tmt, segs=segs, nseg=NSMP, Ls=LS, Cc=32, prevcol=prevcol_s, savecol=[savecol_s],
                       psrc=psm[l], dbg=False)
              layer(l, T, G)
              dma(shs[l].rearrange("s (c p) -> p s c", p=128), shout[:, 0:14 * NSMP].rearrange("p (s c) -> p s c", s=NSMP), slow=True)
              x_out(l, ys, x1s, T, "x1s")

        try:
            _main()
        except _Stop:
            pass
        P.emit(nc, es)
    return nc, P


_NC_CACHE = {}


def _run(inp, ncores, dbg=False):
    xpr = np.asarray(inp["x_prompt"], np.float32)
    B, SEQ, _ = xpr.shape
    xsm = np.asarray(inp["x_sample"], np.float32)
    assert xsm.shape[0] == ncores * NSMP and xsm.shape[1] == LS and B <= ncores
    key = (SEQ, dbg)
    if key not in _NC_CACHE:
        _NC_CACHE[key] = build(SEQ, dbg=dbg)[0]
    nc = _NC_CACHE[key]
    cst, rope_p, rope_s = host_consts(SEQ)
    f = lambda k: np.ascontiguousarray(np.asarray(inp[k], np.float32))
    prm, cwa2 = host_params({k: f(k) for k in ("norm_mix", "norm_ffn", "ple_norm", "a_q_norm", "a_k_norm", "c_shift_mu",
                                                "c_w0", "c_a0", "c_k_k", "c_k_a", "c_ln_w", "c_ln_b", "c_r_k", "c_w2", "c_a2")})
    biasT = host_bias(f("a_rel_bias")).reshape(DEPTH, 128, 5 * 1024)
    shared = dict(w_in=f("w_in"), w_br=f("w_branch"), w_out=f("w_out"), w_fg=f("w_ffn_gate"), w_fu=f("w_ffn_up"),
                  w_fd=f("w_ffn_down"), w_pp=f("w_ple_proj"), w_pg=f("w_ple_gate"), cg2=f("c_g2"), cwa2=cwa2, prm=prm,
                  cst=cst, biasT=biasT, rope_p=rope_p, rope_s=rope_s)
    pp_ = f("p_prompt"); ps_ = f("p_sample"); cak = f("cache_a_k"); cav = f("cache_a_v")
    sr = f("state_ret"); sw = f("state_rwkv"); sh = f("state_rwkv_shift")
    zx = np.zeros((SEQ, D), np.float32); zp = np.zeros((DEPTH, SEQ, PLE), np.float32)
    in_maps = []
    for c in range(ncores):
        m = dict(shared)
        if c < B:
            m["xp"] = np.ascontiguousarray(xpr[c]); m["pp"] = np.ascontiguousarray(pp_[:, c])
        else:
            m["xp"] = zx; m["pp"] = zp
        sl = slice(c * NSMP, (c + 1) * NSMP)
        m["xs"] = np.ascontiguousarray(xsm[sl].reshape(NSMP * LS, D))
        m["psm"] = np.ascontiguousarray(ps_[:, sl].reshape(DEPTH, NSMP * LS, PLE))
        m["cak"] = np.ascontiguousarray(cak[:, sl].reshape(DEPTH, NSMP, 512, 512))
        m["cav"] = np.ascontiguousarray(cav[:, sl].reshape(DEPTH, NSMP, 512, 512))
        m["sret"] = np.ascontiguousarray(sr[:, sl]); m["srw"] = np.ascontiguousarray(sw[:, sl])
        m["ssh"] = np.ascontiguousarray(sh[:, sl].reshape(DEPTH, NSMP, 1792))
        in_maps.append(m)
    res = run_bass_kernel_spmd(nc, in_maps, core_ids=list(range(ncores))).results
    R = lambda k, cs: [np.asarray(res[c][k], np.float32) for c in cs]
    pc = list(range(B)); ac = list(range(ncores))
    NB = ncores * NSMP
    out = (
        np.stack(R("yp", pc)),
        np.concatenate(R("ys", ac)).reshape(NB, LS, D),
        np.stack(R("akp", pc), axis=1).reshape(DEPTH, B, 512, 8, 64),
        np.stack(R("avp", pc), axis=1).reshape(DEPTH, B, 512, 8, 64),
        np.stack(R("retp", pc), axis=1),
        np.stack(R("rwp", pc), axis=1),
        np.stack(R("shp", pc), axis=1).reshape(DEPTH, B, 1, 1792),
        np.concatenate([r.reshape(DEPTH, NSMP, LS, 8, 64) for r in R("aks", ac)], axis=1),
        np.concatenate([r.reshape(DEPTH, NSMP, LS, 8, 64) for r in R("avs", ac)], axis=1),
        np.concatenate(R("rets", ac), axis=1),
        np.concatenate(R("rws", ac), axis=1),
        np.concatenate(R("shs", ac), axis=1).reshape(DEPTH, NB, 1, 1792),
    )
    if dbg:
        return out, res
    return out


def kernel(**inputs):
    return _run(inputs, 8)
```

```python
import numpy as np
from contextlib import ExitStack
import concourse.bass as bass
import concourse.mybir as mybir
from concourse.bass_utils import run_bass_kernel_spmd

F32 = mybir.dt.float32
BF16 = mybir.dt.bfloat16
ALU = mybir.AluOpType
AF = mybir.ActivationFunctionType
AX = mybir.AxisListType

D = 1024
KC = 8
INW = 8448
DFF = 2816
NFF = 22
PLE = 256
DEPTH = 2
NSMP = 4
LS = 32
TP = 512
PAST = 2048
NEG = -30000.0
CDEC = 0.6065306597126334
GN_EPS = 64e-5

_c = {}
_o = 0
for _n, _w in [("ident", 128), ("ones", 128), ("blk", 128), ("m01", 128), ("msk", 256), ("lsm", 64),
               ("cm64", 512), ("cm32", 128),
               ("gc128", 4), ("gc32", 4), ("tq128", 4), ("tk128", 4), ("tq32", 4), ("tk32", 4),
               ("eps6", 1), ("eps12", 1), ("gneps", 1), ("one", 1), ("negone", 1), ("zero", 1)]:
    _c[_n] = _o
    _o += _w
NCST = _o
_p = {}
_o = 0
for _n, _w in [("nmix", 8), ("nffn", 8), ("nple", 8), ("aqn", 1), ("akn", 1), ("mu", 14), ("w0", 4), ("a0", 4),
               ("kk", 4), ("ka", 4), ("rk", 4), ("lnw", 4), ("lnb", 4)]:
    _p[_n] = _o
    _o += _w
NPRM = _o


def _gammas():
    return 1.0 - np.exp2(-5.0 - np.arange(4, dtype=np.float64))


def host_consts(SEQ):
    g = _gammas()
    cst = np.zeros((128, NCST), np.float32)
    p = np.arange(128)
    cst[:, _c["ident"]:_c["ident"] + 128] = np.eye(128)
    cst[:, _c["ones"]:_c["ones"] + 128] = 1.0
    cst[:, _c["blk"]:_c["blk"] + 128] = (p[:, None] // 64 == p[None, :] // 64)
    cst[:, _c["m01"]:_c["m01"] + 128] = (p[None, :] >= p[:, None])
    m = np.arange(64)
    strict = (m[:, None] < m[None, :]).astype(np.float32)
    incl = (m[:, None] <= m[None, :]).astype(np.float32)
    msk = np.stack([strict, incl, strict, incl], axis=1)
    cst[:64, _c["msk"]:_c["msk"] + 256] = msk.reshape(64, 256)
    cst[:64, _c["lsm"]:_c["lsm"] + 64] = (m[None, :] < m[:, None])
    t = np.arange(512)
    cst[:, _c["cm64"]:_c["cm64"] + 512] = (t % 64 != 0).astype(np.float32)[None]
    cst[:, _c["cm32"]:_c["cm32"] + 128] = (t[:128] % 32 != 0).astype(np.float32)[None]
    cst[:, _c["gc128"]:_c["gc128"] + 4] = (g ** 128)[None]
    cst[:, _c["gc32"]:_c["gc32"] + 4] = (g ** 32)[None]
    sc = 128.0 ** -0.5
    cst[:, _c["tq128"]:_c["tq128"] + 4] = g[None, :] ** (p + 1.0)[:, None]
    cst[:, _c["tk128"]:_c["tk128"] + 4] = sc * g[None, :] ** (-(p + 1.0))[:, None]
    cst[:, _c["tq32"]:_c["tq32"] + 4] = g[None, :] ** ((p % 32) + 1.0)[:, None]
    cst[:, _c["tk32"]:_c["tk32"] + 4] = sc * g[None, :] ** (-((p % 32) + 1.0))[:, None]
    cst[:, _c["eps6"]] = 1e-6
    cst[:, _c["eps12"]] = 1e-12
    cst[:, _c["gneps"]] = GN_EPS
    cst[:, _c["one"]] = 1.0
    cst[:, _c["negone"]] = -1.0
    inv = (10000.0 ** (-np.arange(64, dtype=np.float32) / np.float32(64))).astype(np.float32)

    def rope(pos):
        ang = (pos.astype(np.float32)[:, None] * inv[None, :]).astype(np.float32)
        return np.concatenate([np.cos(ang), np.sin(ang)], axis=1).astype(np.float32)
    rope_p = rope(np.arange(SEQ))
    rope_s = np.tile(rope(PAST + np.arange(LS)), (NSMP, 1))
    return cst, rope_p, rope_s


def host_bias(rb):
    j = np.arange(128)[:, None, None]
    kt = np.arange(5)[None, :, None]
    qq = np.arange(128)[None, None, :]
    kk = 128 * kt + j
    idx = np.clip(512 + qq - kk, -63, 256) + 63
    ok = np.where(qq < 64, kk < 576, kk >= 64)
    out = rb[:, :, idx]
    out = np.where(ok[None, None], out, np.float32(NEG)).astype(np.float32)
    slot_h = [(s_ % 4) * 2 + s_ // 4 for s_ in range(8)]
    out = out[:, slot_h]
    return np.ascontiguousarray(out.transpose(0, 2, 3, 1, 4))


def host_params(inp):
    prm = np.zeros((DEPTH, 128, NPRM), np.float32)

    def fm(v, n):
        return v.reshape(n, 128).T
    for l in range(DEPTH):
        prm[l, :, _p["nmix"]:_p["nmix"] + 8] = fm(inp["norm_mix"][l], 8)
        prm[l, :, _p["nffn"]:_p["nffn"] + 8] = fm(inp["norm_ffn"][l], 8)
        prm[l, :, _p["nple"]:_p["nple"] + 8] = fm(inp["ple_norm"][l], 8)
        prm[l, :, _p["aqn"]] = np.tile(inp["a_q_norm"][l], 2)
        prm[l, :, _p["akn"]] = np.tile(inp["a_k_norm"][l], 2)
        prm[l, :, _p["mu"]:_p["mu"] + 14] = fm(inp["c_shift_mu"][l], 14)
        for nm, key in [("w0", "c_w0"), ("a0", "c_a0"), ("kk", "c_k_k"), ("ka", "c_k_a"), ("lnw", "c_ln_w"),
                        ("lnb", "c_ln_b")]:
            prm[l, :, _p[nm]:_p[nm] + 4] = fm(inp[key][l], 4)
        prm[l, :, _p["rk"]:_p["rk"] + 4] = fm(inp["c_r_k"][l].reshape(-1), 4)
    cwa2 = np.concatenate([inp["c_w2"], inp["c_a2"]], axis=1).astype(np.float32)
    return prm, np.ascontiguousarray(cwa2)


class Prog:
    ENGS = ("pe", "act", "dve", "pool", "sp")

    def __init__(self):
        self.ins = []
        self.last_w = {}
        self.readers = {}
        self.readonly = set()
        self.sub = {}
        self.excl = set()

    def keys(self, x):
        if isinstance(x, str):
            return [x]
        if isinstance(x, tuple):
            return [self.name(x[0]) + ":" + str(x[1])]
        n = self.name(x)
        if n in self.sub:
            return [n + ":" + str(i) for i in range(self.sub[n])]
        return [n]

    @staticmethod
    def name(x):
        t = getattr(x, "tensor", x)
        return t.name

    def op(self, eng, fn, reads=(), writes=(), dma=None):
        i = len(self.ins)
        deps = set()
        for r in reads:
            for k in self.keys(r):
                if k in self.readonly:
                    continue
                if k in self.last_w:
                    deps.add(self.last_w[k])
                rd = self.readers.setdefault(k, {})
                if k.split(":")[0] in self.excl:
                    for ek, r in rd.items():
                        if ek != eng:
                            deps.add(r)
                rd[("d", i) if dma is not None else eng] = i
        for w in writes:
            for k in self.keys(w):
                if k in self.last_w:
                    deps.add(self.last_w[k])
                for r in self.readers.get(k, {}).values():
                    if r != i:
                        deps.add(r)
                self.last_w[k] = i
                self.readers[k] = {}
        self.ins.append([eng, fn, dma, sorted(deps)])
        return i

    def emit(self, nc, es):
        ins = self.ins
        n = len(ins)
        need = [False] * n
        chans = {}
        for i, (eng, fn, dma, deps) in enumerate(ins):
            if dma is not None:
                need[i] = True
                chans.setdefault(dma, 0)
            for j in deps:
                ej, _, dj, _ = ins[j]
                if dj is not None or dma is not None or ej != eng or eng != "pe":
                    need[j] = True
        comp = [None] * n
        cnt = {e: 0 for e in self.ENGS}
        for i, (eng, fn, dma, deps) in enumerate(ins):
            if dma is not None:
                chans[dma] += 16
                comp[i] = ("d_" + dma, chans[dma])
            elif need[i]:
                cnt[eng] += 1
                comp[i] = ("e_" + eng, cnt[eng])
        sems = {}
        for s in ["e_" + e for e in self.ENGS] + ["d_" + c for c in chans]:
            sems[s] = es.enter_context(nc.semaphore(s))
        self.n_sems = len(sems)
        final = {("d_" + c): v for c, v in chans.items()}
        block = es.enter_context(nc.Block())
        per_eng = {e: [] for e in self.ENGS}
        for i, rec in enumerate(ins):
            per_eng[rec[0]].append(i)

        def run(eng_name, e):
            waited = {}
            for i in per_eng[eng_name]:
                _, fn, dma, deps = ins[i]
                wl = {}
                for j in deps:
                    ej, _, dj, _ = ins[j]
                    if dj is None and dma is None and ej == eng_name and eng_name == "pe":
                        continue
                    s, v = comp[j]
                    if wl.get(s, 0) < v:
                        wl[s] = v
                for s, v in wl.items():
                    if waited.get(s, 0) >= v:
                        continue
                    e.wait_ge(sems[s], v)
                    waited[s] = v
                r = fn(e)
                if comp[i] is not None:
                    r.then_inc(sems[comp[i][0]], 16 if dma is not None else 1)
            if eng_name == "sp":
                for s, v in final.items():
                    e.wait_ge(sems[s], v)

        @block.tensor
        def _(e):
            run("pe", e)

        @block.scalar
        def _(e):
            run("act", e)

        @block.vector
        def _(e):
            run("dve", e)

        @block.gpsimd
        def _(e):
            run("pool", e)

        @block.sync
        def _(e):
            run("sp", e)


def build(SEQ, dbg=False, skip=""):
    import os as _os
    assert SEQ % TP == 0
    NG = SEQ // TP
    TSM = NSMP * LS
    nc = bass.Bass("TRN2", target_bir_lowering=False)
    P = Prog()
    es = ExitStack()

    def din(name, shape):
        P.readonly.add(name)
        return nc.dram_tensor(name, list(shape), F32, kind="ExternalInput").ap()

    def dout(name, shape):
        return nc.dram_tensor(name, list(shape), F32, kind="ExternalOutput").ap()

    xp = din("xp", [SEQ, D]); pp = din("pp", [DEPTH, SEQ, PLE])
    xs = din("xs", [TSM, D]); psm = din("psm", [DEPTH, TSM, PLE])
    cak = din("cak", [DEPTH, NSMP, 512, 512]); cav = din("cav", [DEPTH, NSMP, 512, 512])
    sret = din("sret", [DEPTH, NSMP, 4, 128, 128]); srw = din("srw", [DEPTH, NSMP, 8, 64, 64])
    ssh = din("ssh", [DEPTH, NSMP, 1792])
    w_in = din("w_in", [DEPTH, D, INW]); w_br = din("w_br", [DEPTH, 3, 512, D]); w_out = din("w_out", [DEPTH, D, D])
    w_fg = din("w_fg", [DEPTH, D, DFF]); w_fu = din("w_fu", [DEPTH, D, DFF]); w_fd = din("w_fd", [DEPTH, DFF, D])
    w_pp = din("w_pp", [DEPTH, PLE, D]); w_pg = din("w_pg", [DEPTH, D, D])
    cg2 = din("cg2", [DEPTH, 128, 512]); cwa2 = din("cwa2", [DEPTH, 128, 512])
    prm_d = din("prm", [DEPTH, 128, NPRM]); cst_d = din("cst", [128, NCST])
    bias_d = din("biasT", [DEPTH, 128, 5 * 1024])
    rope_p = din("rope_p", [SEQ, 128]); rope_s = din("rope_s", [TSM, 128])

    yp = dout("yp", [SEQ, D]); ys = dout("ys", [TSM, D])
    akp = dout("akp", [DEPTH, 512, 512]); avp = dout("avp", [DEPTH, 512, 512])
    retp = dout("retp", [DEPTH, 4, 128, 128]); rwp = dout("rwp", [DEPTH, 8, 64, 64]); shp = dout("shp", [DEPTH, 1792])
    aks = dout("aks", [DEPTH, TSM, 512]); avs = dout("avs", [DEPTH, TSM, 512])
    rets = dout("rets", [DEPTH, NSMP, 4, 128, 128]); rws = dout("rws", [DEPTH, NSMP, 8, 64, 64])
    shs = dout("shs", [DEPTH, NSMP, 1792])
    wsrc = dict(w_in=w_in, w_br=w_br, w_out=w_out, w_fg=w_fg, w_fu=w_fu, w_fd=w_fd, w_pp=w_pp, w_pg=w_pg)
    wbf = {k: nc.dram_tensor(k + "_b", list(v.shape), BF16).ap() for k, v in wsrc.items()}
    x1p = nc.dram_tensor("x1p", [KC, 128, SEQ], F32).ap()
    x1s = nc.dram_tensor("x1s", [KC, 128, TSM], F32).ap()
    dbg_t = {}
    if dbg:
        for nm in ("d_oa", "d_ob", "d_oc"):
            dbg_t[nm] = dout(nm, [128, 4 * TP])
        dbg_t["d_x"] = dout("d_x", [128, KC * TP])

    with es:
        def sb(name, shape, dt=F32):
            return es.enter_context(nc.sbuf_tensor(name, list(shape), dt))

        def ps(name, shape, dt=F32, sub=None):
            if sub:
                P.sub[name] = sub
            P.excl.add(name)
            return es.enter_context(nc.psum_tensor(name, list(shape), dt))

        cst = sb("cst_sb", [128, NCST])
        prm = sb("prm_sb", [128, DEPTH * NPRM])
        identb = sb("identb", [128, 128], BF16)
        onesb = sb("onesb", [128, 128], BF16)
        blkb = sb("blkb", [128, 128], BF16)
        sqb = [sb(f"sqb{i}", [128, TP], BF16) for i in range(2)]
        aqs = sb("aqs", [128, DEPTH])
        cwa2b = sb("cwa2b", [128, 512], BF16)
        cg2b = sb("cg2b", [128, 512], BF16)
        biasT = sb("biasT_sb", [128, 5 * 1024], BF16)
        xT = sb("xT", [128, KC * TP]); P.sub["xT"] = KC
        hT = sb("hT", [128, KC * TP], BF16); P.sub["hT"] = KC
        NWB = 3
        WBN = 4096
        wb = [sb(f"wb{i}", [128, WBN], BF16) for i in range(NWB)]
        Kwin = sb("Kwin", [128, 4 * 1024], BF16)
        Vwin = sb("Vwin", [128, 8 * 520], BF16)
        Sret = sb("Sret", [128, 512]); Sretb = sb("Sretb", [128, 512], BF16)
        Srw = sb("Srw", [128, 512]); Srwb = sb("Srwb", [128, 512], BF16)
        shcol = sb("shcol", [128, 14])
        oT3 = [sb(f"oT{b}", [128, 4 * TP], BF16) for b in range(3)]
        FA = sb("FA", [128, 4 * 2048]); P.sub["FA"] = 4
        HA = sb("HA", [128, 6 * 2048], BF16); P.sub["HA"] = 6
        tmp = [sb(f"tmp{i}", [128, TP + 1]) for i in range(6)]
        t1, t2, t3, t4, t5, t6 = tmp
        ropet = [sb("ropet0", [128, 128])] * 2
        scsb = sb("scsb", [128, 1024])
        xin = scsb
        ptsb = [sb(f"ptsb{i}", [128, 1024], BF16) for i in range(2)]
        oatm = sb("oatm", [128, 512], BF16)
        scs = oatm
        rden = sb("rden", [128, 8])
        lob = sb("lob", [128, TP], BF16); sgl = sb("sgl", [128, TP], BF16)
        MM = sb("MM", [64, 4 * 512], BF16)
        CB = [sb(f"CB{i}", [64, 1024], BF16) for i in range(2)]
        AT_ = [sb(f"AaT{i}", [64, 512], BF16) for i in range(2)]
        hbk = sb("hbk", [128, 2 * 256], BF16)
        BhTM = sb("BhTM", [64, 512], BF16); KhTM = sb("KhTM", [64, 512], BF16)
        Vb = sb("Vb", [64, 512], BF16); Vpad = sb("Vpad", [64, 1024], BF16)
        Tf0 = sb("Tf0", [64, 512], BF16)
        Gpad = sb("Gpad", [64, 1024], BF16); Ub = sb("Ub", [64, 512], BF16); Upad = sb("Upad", [64, 1024], BF16)
        stmp = sb("stmp", [128, 512])
        otr = stmp
        rwcp = stmp[:, 0:256]
        wc = sb("wc", [128, 4 * 8])
        rwst = scsb[0:64, 0:512]
        shin = sb("shin", [128, 14 * NSMP]); shout = sb("shout", [128, 14 * NSMP])

        PA = ps("PA", [128, 1024], sub=2); PB = ps("PB", [128, 1024], sub=2); PC = ps("PC", [128, 1024], sub=2)
        PD = ps("PD", [128, 512]); PT = ps("PT", [128, 1024], BF16)

        def C(n, w=1):
            return cst[:, _c[n]:_c[n] + w]

        def PR(l, n, i=0, w=1):
            return prm[:, l * NPRM + _p[n] + i:l * NPRM + _p[n] + i + w]

        def A(x):
            return x[0] if isinstance(x, tuple) else x

        def mm(out, lhsT, rhs, start=True, stop=True, skip=False):
            kw = dict(skip_group_check=True) if skip else {}
            P.op("pe", lambda e: e.matmul(A(out), A(lhsT), A(rhs), start=start, stop=stop, **kw), [lhsT, rhs], [out])

        def tr(out, in_, ident):
            P.op("pe", lambda e: e.transpose(A(out), A(in_), A(ident)), [in_, ident], [out])

        def act(out, in_, func, bias=None, scale=1.0, eng="act"):
            kw = {}
            rd = [in_]
            if bias is not None:
                kw["bias"] = A(bias)
                if not isinstance(bias, float):
                    rd.append(bias)
            if not isinstance(scale, float):
                rd.append(scale)
            P.op(eng, lambda e: e.activation(out=A(out), in_=A(in_), func=func, scale=A(scale), **kw), rd, [out])

        def tt(out, in0, in1, op, eng="dve"):
            P.op(eng, lambda e: e.tensor_tensor(out=A(out), in0=A(in0), in1=A(in1), op=op), [in0, in1], [out])

        def ts(out, in0, s1, op0, s2=None, op1=None, eng="dve"):
            rd = [in0] + [s for s in (s1, s2) if s is not None and not isinstance(s, float)]
            kw = {}
            if op1 is not None:
                kw["op1"] = op1
            P.op(eng, lambda e: e.tensor_scalar(out=A(out), in0=A(in0), scalar1=A(s1),
                                                scalar2=(A(s2) if s2 is not None else None), op0=op0, **kw), rd, [out])

        def stt(out, in0, s, in1, op0, op1, eng="dve"):
            rd = [in0, in1] + ([s] if not isinstance(s, float) else [])
            P.op("dve", lambda e: e.scalar_tensor_tensor(out=A(out), in0=A(in0), scalar=A(s), in1=A(in1), op0=op0, op1=op1),
                 rd, [out])

        def cp(out, in_, eng="dve"):
            if eng == "act":
                P.op("act", lambda e: e.copy(out=A(out), in_=A(in_)), [in_], [out])
            else:
                P.op(eng, lambda e: e.tensor_copy(out=A(out), in_=A(in_)), [in_], [out])

        def recip(out, in_):
            P.op("dve", lambda e: e.reciprocal(out=A(out), in_=A(in_)), [in_], [out])

        def mset(out, val, eng="pool"):
            P.op(eng, lambda e: e.memset(A(out), val), [], [out])

        def scan(out, d0, d1, eng="dve"):
            P.op(eng, lambda e: e.tensor_tensor_scan(out=A(out), data0=A(d0), data1=A(d1), initial=0.0,
                                                     op0=ALU.mult, op1=ALU.add), [d0, d1], [out])

        def dma(out, in_, eng="sp", chan=None, wkey=None, rkey=None, slow=False):
            on = Prog.name(A(out))
            wr = [wkey] if wkey is not None else ([out] if on not in P.readonly else [])
            rd = (list(rkey) if isinstance(rkey, (list, tuple)) else [rkey]) if rkey is not None else [in_]
            ch = chan or (P.keys(wr[0])[0] if wr else Prog.name(A(in_)))
            ch = ch.replace(":", "_")
            kw = dict(allow_slow_non_contiguous=True) if slow else {}
            P.op(eng, lambda e: e.dma_start(out=A(out), in_=A(in_), **kw), rd, wr, dma=ch)

        def r3(ap, a):
            return ap.rearrange("p (a b) -> p a b", a=a)

        def bc_mid(base, n_mid):
            (ps_, rows), (st, w) = base.ap
            return bass.AP(base.tensor, base.offset, [[ps_, rows], [0, n_mid], [st, w]])

        def bc_in(base, w):
            (ps_, rows), (st, n) = base.ap
            return bass.AP(base.tensor, base.offset, [[ps_, rows], [st, n], [0, w]])

        def fa(k, j=0, c0=0, c1=TP):
            return (FA[:, k * 2048 + j * TP + c0:k * 2048 + j * TP + c1], k)

        def ha(k, j=0, c0=0, c1=TP, r0=0, r1=128):
            return (HA[r0:r1, k * 2048 + j * TP + c0:k * 2048 + j * TP + c1], k)

        class WS:
            def __init__(self):
                self.seq = []
                self.issued = 0
                self.used = 0

            def plan(self, l):
                w_in, w_br, w_out, w_fg, w_fu, w_fd, w_pp, w_pg = (wbf[k] for k in
                                                                      ("w_in", "w_br", "w_out", "w_fg", "w_fu", "w_fd", "w_pp", "w_pg"))
                wi = w_in[l]

                def colblk(src, c0, ncol, tag):
                    nk = src.shape[0] // 128
                    self.seq.append((tag, src[:, c0:c0 + ncol].rearrange("(k p) c -> p k c", p=128), nk, ncol))
                for i, tg in enumerate(["aq", "ak", "av", "bq", "bk", "bv", "bg"]):
                    colblk(wi, 512 * i, 512, tg)
                colblk(wi, 5120, 256, "cl")
                colblk(wi, 4096, 512, "ck")
                colblk(wi, 3584, 512, "cr")
                colblk(wi, 4608, 512, "cv")
                for b in range(3):
                    colblk(wi, 5376 + 1024 * b, 512, f"gl{b}a")
                    colblk(wi, 5376 + 1024 * b + 512, 512, f"gl{b}b")
                    colblk(w_br[l, b], 0, 1024, f"br{b}")
                colblk(w_out[l], 0, 512, "outa")
                colblk(w_out[l], 512, 512, "outb")
                for i in range(6):
                    ncol = 512 if i < 5 else 256
                    colblk(w_fg[l], 512 * i, ncol, f"fg{i}")
                    colblk(w_fu[l], 512 * i, ncol, f"fu{i}")
                for j in range(8):
                    colblk(w_fd[l], 128 * j, 128, f"fd{j}")
                colblk(w_pp[l], 0, 1024, "pp")
                colblk(w_pg[l], 0, 512, "pga")
                colblk(w_pg[l], 512, 512, "pgb")

            def _issue(self):
                i = self.issued
                tag, src, nk, ncol = self.seq[i]
                buf = wb[i % NWB]
                assert nk * ncol <= WBN
                dma(r3(buf[:, 0:nk * ncol], nk), src, eng="sp", rkey=["wcast:%d" % i_ for i_ in range(4)])
                self.issued += 1

            def get(self, tag):
                i = self.used
                assert self.seq[i][0] == tag, (self.seq[i][0], tag)
                while self.issued < min(len(self.seq), i + NWB - 1):
                    self._issue()
                self.used += 1
                _, _, nk, ncol = self.seq[i]
                buf = wb[i % NWB]
                return (lambda kc, c0, c1: buf[:, kc * ncol + c0:kc * ncol + c1]), nk, ncol

        W = WS()
        banks = [(PA, 0), (PA, 1), (PB, 0), (PB, 1)]
        bstate = [0]

        def nbank():
            t_, h = banks[bstate[0] % 4]
            bstate[0] += 1
            return (t_[:, h * 512:(h + 1) * 512], h)

        def bk(pb, r0, r1, c0, c1):
            return (pb[0][r0:r1, c0:c1], pb[1])

        def fm_proj(pb, T, wf, nk, c0, rhs_fn):
            for kc in range(nk):
                mm(bk(pb, 0, 128, 0, T), wf(kc, c0, c0 + 128), rhs_fn(kc), start=(kc == 0), stop=(kc == nk - 1))

        identf = C("ident", 128)
        onesf = C("ones", 128)
        blkf = C("blk", 128)

        dma(cst[:], cst_d)
        dma(r3(prm[:], DEPTH), prm_d.rearrange("l p n -> p l n"))
        cp(identb[:], identf, eng="pool")
        cp(onesb[:], onesf, eng="pool")
        cp(blkb[:], blkf, eng="pool")
        for l in range(DEPTH):
            ts(aqs[:, l:l + 1], PR(l, "aqn"), 0.125, ALU.mult)
        ci_ = 0
        for k_, src_ in wsrc.items():
            s2 = src_.flatten_outer_dims() if len(src_.shape) > 2 else src_
            d2 = wbf[k_].flatten_outer_dims() if len(src_.shape) > 2 else wbf[k_]
            rows = s2.shape[0]
            step = 256 if s2.shape[1] >= 4096 else 1024
            for r0 in range(0, rows, step):
                r1_ = min(rows, r0 + step)
                dma(d2[r0:r1_, :], s2[r0:r1_, :], eng="pool", wkey="wcast:%d" % (ci_ % 4), chan="wcast%d" % (ci_ % 4))
                ci_ += 1
        mset(Vwin[:], 1.0)
        for t_ in (Vpad, Gpad, Upad):
            mset(t_[:], 0.0)

        xin2 = [scsb[:, :], (FA[:, 0:1024], 0)]

        def load_x_tm(src, T):
            for ti in range(T // 128):
                xin = xin2[ti % 2]
                dma(xin, src[ti * 128:(ti + 1) * 128, :])
                xk = (lambda a_: (a_, 0)) if ti % 2 else (lambda a_: a_)
                xa = A(xin)
                for half in range(2):
                    pb = nbank()
                    for k4 in range(4):
                        kc = half * 4 + k4
                        tr(bk(pb, 0, 128, k4 * 128, (k4 + 1) * 128), xk(xa[:, kc * 128:(kc + 1) * 128]), identf)
                    dst = r3(xT[:, half * 4 * TP:(half * 4 + 4) * TP], 4)[:, :, ti * 128:(ti + 1) * 128]
                    cp(dst, (r3(pb[0], 4), pb[1]), eng="act" if half else "dve")

        def store_x_tm(dst, T):
            for ti in range(T // 128):
                xin = xin2[ti % 2]
                xk = (lambda a_: (a_, 0)) if ti % 2 else (lambda a_: a_)
                xa = A(xin)
                for half in range(2):
                    pb = nbank()
                    for k4 in range(4):
                        kc = half * 4 + k4
                        tr(bk(pb, 0, 128, k4 * 128, (k4 + 1) * 128), xT[:, kc * TP + ti * 128:kc * TP + (ti + 1) * 128], identf)
                    cp(xk(xa[:, half * 512:(half + 1) * 512]), pb, eng="act" if half else "dve")
                dma(dst[ti * 128:(ti + 1) * 128, :], xin)

        def rms_stats(T, dst, srcs_fn, nk, div, epsname, lhs=None):
            lhs = lhs if lhs is not None else onesb[:]
            for kc in range(nk):
                sq = sqb[kc % 2]
                if kc % 2 == 0:
                    act(sq[:, 0:T], srcs_fn(kc), AF.Square)
                else:
                    tt(sq[:, 0:T], srcs_fn(kc), srcs_fn(kc), ALU.mult)
                mm(PD[:, 0:T], lhs, sq[:, 0:T], start=(kc == 0), stop=(kc == nk - 1))
            act(dst, PD[:, 0:T], AF.Ln, bias=C(epsname), scale=1.0 / div)
            act(dst, dst, AF.Exp, scale=-0.5)

        def norm_to_hT(l, T, gain):
            rms_stats(T, t3[:, 0:T], lambda kc: (xT[:, kc * TP:kc * TP + T], kc), KC, float(D), "eps6")
            for kc in range(KC):
                eng = "dve" if kc % 2 == 0 else "pool"
                if gain is None:
                    tt((hT[:, kc * TP:kc * TP + T], kc), (xT[:, kc * TP:kc * TP + T], kc), t3[:, 0:T], ALU.mult, eng=eng)
                else:
                    stt((hT[:, kc * TP:kc * TP + T], kc), (xT[:, kc * TP:kc * TP + T], kc), PR(l, gain, kc), t3[:, 0:T],
                        ALU.mult, ALU.mult, eng=eng)

        def hrhs(T):
            return lambda kc: (hT[:, kc * TP:kc * TP + T], kc)

        def qk_norm(pb, T, gain_ap, dst_f32):
            cp(t4[:, 0:T], bk(pb, 0, 128, 0, T), eng="act")
            act(sqb[0][:, 0:T], bk(pb, 0, 128, 0, T), AF.Square)
            mm(PD[:, 0:T], blkb[:], sqb[0][:, 0:T])
            act(t6[:, 0:T], PD[:, 0:T], AF.Ln, bias=C("eps6"), scale=1.0 / 64)
            act(t6[:, 0:T], t6[:, 0:T], AF.Exp, scale=-0.5)
            stt(dst_f32, t4[:, 0:T], gain_ap, t6[:, 0:T], ALU.mult, ALU.mult)

        def mixer_A(l, T, G):
            kslot = G["kslot"]
            wf, nk, _ = W.get("aq")
            for j in range(4):
                pb = nbank()
                fm_proj(pb, T, wf, nk, j * 128, hrhs(T))
                qk_norm(pb, T, aqs[:, l:l + 1], ha(0, j, 0, T))
            wf, nk, _ = W.get("ak")
            for j in range(4):
                pb = nbank()
                fm_proj(pb, T, wf, nk, j * 128, hrhs(T))
                qk_norm(pb, T, PR(l, "akn"), fa(0, j, 0, T))
                cp(Kwin[:, j * 1024 + kslot * 512:j * 1024 + kslot * 512 + T], fa(0, j, 0, T), eng="pool")
            for (dst_k, dst_v, t0, L) in G["a_out"]:
                pb = nbank()
                for j in range(4):
                    tr(bk(pb, 0, L, j * 128, (j + 1) * 128), fa(0, j, t0, t0 + L), identf)
                cp(otr[0:L, :], bk(pb, 0, L, 0, 512), eng="act")
                dma(dst_k, otr[0:L, :])
            wf, nk, _ = W.get("av")
            for (t0, L, vtile) in G["vt"]:
                pb = nbank()
                for kc in range(nk):
                    mm(bk(pb, 0, L, 0, 512), hT[:, kc * TP + t0:kc * TP + t0 + L], wf(kc, 0, 512),
                       start=(kc == 0), stop=(kc == nk - 1))
                vdst = Vwin[0:L, vtile * 520:(vtile + 1) * 520].rearrange("p (h d) -> p h d", h=8)[:, :, 0:64]
                cp(vdst, (pb[0][0:L, :].rearrange("p (h d) -> p h d", h=8), pb[1]), eng="act")
                for (dst_k, dst_v, ot0, oL) in G["a_out"]:
                    if ot0 == t0 and oL == L:
                        cp(otr[0:L, :], bk(pb, 0, L, 0, 512), eng="dve")
                        dma(dst_v, otr[0:L, :])
            oaT = oT3[0]
            for sg in G["segs"]:
                if sg.get("cache") is not None:
                    b = sg["cache"]
                    dma((r3(HA[:, 2048:4096], 4), 1), cak[l, b].rearrange("(t p) c -> p t c", p=128), eng="pool")
                    for t_ in range(4):
                        dma(Vwin[:, t_ * 520:(t_ + 1) * 520].rearrange("p (h d) -> p h d", h=8)[:, :, 0:64],
                            cav[l, b, t_ * 128:(t_ + 1) * 128, :].rearrange("p (h d) -> p h d", h=8), eng="pool")
                    for t_ in range(4):
                        for j in range(4):
                            tr(PT[:, j * 128:(j + 1) * 128], (HA[:, 2048 + t_ * 512 + j * 128:2048 + t_ * 512 + (j + 1) * 128], 1), identb[:])
                        dst = r3(Kwin[:, :], 4)[:, :, t_ * 128:(t_ + 1) * 128]
                        cp(dst, r3(PT[:, 0:512], 4), eng="act" if t_ % 2 else "dve")
                for qb in sg["qblocks"]:
                    q0, nq, kts = qb["q0"], qb["nq"], qb["kts"]
                    def scores(ki):
                        kcol, nkk, vtile, bkt = kts[ki]
                        scp = PA if ki % 2 == 0 else PB
                        for bnk in range(2):
                            mm((scp[0:nkk, bnk * 512:(bnk + 1) * 512], bnk), identb[0:nkk, 0:nkk],
                               biasT[0:nkk, bkt * 1024 + bnk * 512:bkt * 1024 + (bnk + 1) * 512], start=True, stop=False, skip=True)
                        for h in range(8):
                            j, b0 = h // 2, (h % 2) * 64
                            sl_ = (h % 2) * 4 + h // 2
                            mm((scp[0:nkk, sl_ * 128:sl_ * 128 + nq], sl_ // 4),
                               Kwin[b0:b0 + 64, j * 1024 + kcol:j * 1024 + kcol + nkk],
                               ha(0, j, q0, q0 + nq, b0, b0 + 64), start=False, stop=True, skip=True)

                    def soft_pv(ki):
                        kcol, nkk, vtile, bkt = kts[ki]
                        scp = PA if ki % 2 == 0 else PB
                        pt_ = ptsb[ki % 2]
                        act(r3(pt_[0:nkk, :], 8)[:, :, 0:nq], r3(scp[0:nkk, :], 8)[:, :, 0:nq], AF.Exp)
                        for h in range(8):
                            half = h // 4
                            oc0 = half * 512 + (h % 4) * 65
                            sl_ = (h % 2) * 4 + h // 2
                            mm((PC[0:nq, oc0:oc0 + 65], half), pt_[0:nkk, sl_ * 128:sl_ * 128 + nq],
                               Vwin[0:nkk, vtile * 520 + h * 65:vtile * 520 + (h + 1) * 65],
                               start=(ki == 0 and h % 4 == 0), stop=(ki == len(kts) - 1), skip=True)
                    scores(0)
                    for ki in range(len(kts)):
                        if ki + 1 < len(kts):
                            scores(ki + 1)
                        soft_pv(ki)
                    for half in range(2):
                        ov = PC[0:nq, half * 512:half * 512 + 260].rearrange("p (h d) -> p h d", h=4)
                        recip(rden[0:nq, half * 4:half * 4 + 4], (ov[:, :, 64], half))
                        tt(oatm[0:nq, half * 256:(half + 1) * 256].rearrange("p (h d) -> p h d", h=4), (ov[:, :, 0:64], half),
                           bc_in(rden[0:nq, half * 4:half * 4 + 4], 64), ALU.mult)
                    for j in range(4):
                        tr(PT[:, j * 128:j * 128 + nq], oatm[0:nq, j * 128:(j + 1) * 128], identb[0:nq, 0:nq])
                    cp(r3(oaT[:, :], 4)[:, :, q0:q0 + nq], r3(PT[:, 0:512], 4)[:, :, 0:nq], eng="act")

        def mixer_B(l, T, G):
            obT = oT3[1]
            for bi, blk in enumerate(("bq", "bk", "bv")):
                wf, nk, _ = W.get(blk)
                tiles = G["tm"]

                def proj(ti_):
                    t0, L, idx, Cc, rp_src = tiles[ti_]
                    pb = nbank()
                    for kc in range(nk):
                        mm(bk(pb, 0, L, 0, 512), hT[:, kc * TP + t0:kc * TP + t0 + L], wf(kc, 0, 512),
                           start=(kc == 0), stop=(kc == nk - 1))
                    return pb

                def rot(ti_, pb):
                    t0, L, idx, Cc, rp_src = tiles[ti_]
                    src = bk(pb, 0, L, 0, 512)
                    if blk == "bv":
                        cp(ha(4, idx, 0, 512, 0, L), src, eng="act")
                        tt((r3(ha(5, idx, 0, 512, 0, L)[0], 4), 5), (r3(src[0], 4), src[1]),
                           bc_in(C("gc%d" % Cc, 4)[0:L, :], 128), ALU.mult)
                        return
                    rp = ropet[0]
                    dma(rp[0:L, :], rp_src)
                    x4 = pb[0][0:L, :].rearrange("p (h d) -> p h d", h=4)
                    x1 = (x4[:, :, 0:64], pb[1]); x2 = (x4[:, :, 64:128], pb[1])
                    cosb = bc_mid(rp[0:L, 0:64], 4)
                    sinb = bc_mid(rp[0:L, 64:128], 4)
                    ta, tb_ = (t4, t5) if ti_ % 2 == 0 else (t1, t2)
                    a1 = r3(ta[0:L, 0:256], 4); a2 = r3(ta[0:L, 256:512], 4)
                    a3 = r3(tb_[0:L, 0:256], 4); a4 = r3(tb_[0:L, 256:512], 4)
                    o4 = r3(t6[0:L, 0:512], 4)
                    tt(a1, x1, cosb, ALU.mult)
                    tt(a2, x2, sinb, ALU.mult)
                    tt(a3, x1, sinb, ALU.mult)
                    tt(a4, x2, cosb, ALU.mult)
                    tt(o4[:, :, 0:64], a1, a2, ALU.subtract, eng="pool")
                    tt(o4[:, :, 64:128], a3, a4, ALU.add, eng="pool")
                    slot = 2 if blk == "bq" else 3
                    tname = ("tq%d" if blk == "bq" else "tk%d") % Cc
                    dst = ha(slot, idx, 0, 512, 0, L)
                    tt((r3(dst[0], 4), slot), o4, bc_in(C(tname, 4)[0:L, :], 128), ALU.mult)

                def trans(ti_):
                    t0, L, idx, Cc, rp_src = tiles[ti_]
                    if blk == "bv":
                        return
                    slot = 2 if blk == "bq" else 3
                    for h in range(4):
                        tr(PT[:, h * 128:h * 128 + L], ha(slot, idx, h * 128, (h + 1) * 128, 0, L), identb[0:L, 0:L])
                    fslot = 0 if blk == "bq" else 1
                    cp((r3(HA[:, fslot * 2048:(fslot + 1) * 2048], 4)[:, :, t0:t0 + L], fslot), r3(PT[:, 0:512], 4)[:, :, 0:L],
                       eng="act")
                pbs = {0: proj(0)}
                for ti_ in range(len(tiles)):
                    if ti_ + 1 < len(tiles):
                        pbs[ti_ + 1] = proj(ti_ + 1)
                    rot(ti_, pbs.pop(ti_))
                    trans(ti_)
                chk("B_" + blk)
            wf, nk, _ = W.get("bg")
            for h in range(4):
                pb = nbank()
                fm_proj(pb, T, wf, nk, h * 128, hrhs(T))
                act(fa(0, h, 0, T), bk(pb, 0, 128, 0, T), AF.Silu)
            chk("B_bg")
            for sg in G["segs"]:
                if sg["state_in"] == "zero":
                    mset(Sret[:], 0.0)
                    mset(Sretb[:], 0.0)
                elif sg["state_in"] is not None:
                    dma(r3(Sret[:], 4), sret[l, sg["state_in"]].rearrange("h d e -> d h e"))
                    cp(Sretb[:], Sret[:], eng="act")
                for (t0, L, idx, Cc, _rs) in sg["tm"]:
                    for h in range(4):
                        mm((PC[0:L, h * 128:h * 128 + L], 0), ha(1, h, t0, t0 + L), ha(0, h, t0, t0 + L))
                    tt(r3(scs[0:L, :], 4)[:, :, 0:L], (r3(PC[0:L, 0:512], 4)[:, :, 0:L], 0),
                       bc_mid(C("m01", 128)[0:L, 0:L], 4), ALU.mult)
                    for h in range(4):
                        mm((PC[:, 512 + h * 128:512 + h * 128 + L], 1), ha(4, idx, h * 128, (h + 1) * 128, 0, L),
                           scs[0:L, h * 128:h * 128 + L], start=True, stop=False)
                        mm((PC[:, 512 + h * 128:512 + h * 128 + L], 1), Sretb[:, h * 128:(h + 1) * 128],
                           ha(0, h, t0, t0 + L), start=False, stop=True)
                    cp((r3(FA[:, 2048:4096], 4)[:, :, t0:t0 + L], 1), (r3(PC[:, 512:1024], 4)[:, :, 0:L], 1), eng="act")
                    for h in range(4):
                        mm(PD[:, h * 128:(h + 1) * 128], ha(3, idx, h * 128, (h + 1) * 128, 0, L),
                           ha(5, idx, h * 128, (h + 1) * 128, 0, L))
                    for h in range(4):
                        stt(Sret[:, h * 128:(h + 1) * 128], Sret[:, h * 128:(h + 1) * 128],
                            C("gc%d" % Cc, 4)[:, h:h + 1], PD[:, h * 128:(h + 1) * 128], ALU.mult, ALU.add)
                    cp(Sretb[:], Sret[:], eng="act")
                if sg["ret_out"] is not None:
                    dma(sg["ret_out"].rearrange("h d e -> d h e"), r3(Sret[:], 4))
            chk("B_chunks")
            for h in range(4):
                rms_stats(T, t3[:, 0:T], lambda kc, h=h: fa(1, h, 0, T), 1, 128.0, "eps6")
                tt(t4[:, 0:T], fa(1, h, 0, T), t3[:, 0:T], ALU.mult)
                tt(obT[:, h * TP:h * TP + T], t4[:, 0:T], fa(0, h, 0, T), ALU.mult, eng="pool")

        def mixer_C(l, T, G):
            ocT = oT3[2]
            nseg, Ls = G["nseg"], G["Ls"]
            Cc = G["Cc"]
            cmn = "cm%d" % Cc
            nch_seg = Ls // Cc
            nch = T // Cc

            def shifted(pb, c, dst):
                r_seg = t1[:, 0:nseg * (Ls + 1)].rearrange("p (s u) -> p s u", s=nseg)
                cp(r_seg[:, :, 1:Ls + 1], (pb[0][:, 0:T].rearrange("p (s u) -> p s u", s=nseg), pb[1]), eng="act")
                cp(r_seg[:, :, 0:1], G["prevcol"](c), eng="pool")
                for fn in G["savecol"]:
                    fn(c, r_seg)
                d_seg = t2[:, 0:T].rearrange("p (s u) -> p s u", s=nseg)
                tt(d_seg, r_seg[:, :, 0:Ls], r_seg[:, :, 1:Ls + 1], ALU.subtract)
                dst_seg = dst.rearrange("p (s u) -> p s u", s=nseg)
                stt(dst_seg, d_seg, PR(l, "mu", c), r_seg[:, :, 1:Ls + 1], ALU.mult, ALU.add)

            wf, nk, _ = W.get("cl")
            pb = nbank()
            fm_proj(pb, T, wf, nk, 0, hrhs(T))
            shifted(pb, 12, t3[:, 0:T])
            act(lob[0:64, 0:T], t3[0:64, 0:T], AF.Tanh)
            cp(lob[64:128, 0:T], t3[64:128, 0:T], eng="act")
            pb = nbank()
            fm_proj(pb, T, wf, nk, 128, hrhs(T))
            shifted(pb, 13, t3[:, 0:T])
            act(sgl[:, 0:T], t3[:, 0:T], AF.Sigmoid)
            for j in range(4):
                pb = nbank()
                mm(bk(pb, 0, 128, 0, T), cwa2b[0:64, j * 128:(j + 1) * 128], lob[0:64, 0:T])
                act(fa(0, j, 0, T), bk(pb, 0, 128, 0, T), AF.Sigmoid, bias=PR(l, "w0", j))
                scan(fa(1, j, 0, T), C(cmn, T), fa(0, j, 0, T))
                pb = nbank()
                mm(bk(pb, 0, 128, 0, T), cwa2b[64:128, j * 128:(j + 1) * 128], lob[64:128, 0:T])
                act(fa(2, j, 0, T), bk(pb, 0, 128, 0, T), AF.Sigmoid, bias=PR(l, "a0", j))
                pb = nbank()
                mm(bk(pb, 0, 128, 0, T), cg2b[:, j * 128:(j + 1) * 128], sgl[:, 0:T])
                cp(ha(5, j, 0, T), bk(pb, 0, 128, 0, T), eng="act")
            for j in range(4):
                csg_j = FA[:, 2048 + j * TP:2048 + j * TP + T]
                act(wc[:, j * 8:j * 8 + nch], (csg_j[:, Cc - 1:T:Cc], 1), AF.Exp, scale=-CDEC)
            wf, nk, _ = W.get("ck")
            for j in range(4):
                pb = nbank()
                fm_proj(pb, T, wf, nk, j * 128, hrhs(T))
                shifted(pb, 4 + j, t3[:, 0:T])
                ts(t4[:, 0:T], t3[:, 0:T], PR(l, "kk", j), ALU.mult)
                act(sqb[0][:, 0:T], t4[:, 0:T], AF.Square)
                mm(PD[:, 0:T], blkb[:], sqb[0][:, 0:T])
                ts(t5[:, 0:T], PD[:, 0:T], 1e-24, ALU.max)
                act(t5[:, 0:T], t5[:, 0:T], AF.Ln)
                act(t5[:, 0:T], t5[:, 0:T], AF.Exp, scale=-0.5)
                tt(t4[:, 0:T], t4[:, 0:T], t5[:, 0:T], ALU.mult)
                tt(t5[:, 0:T], fa(1, j, 0, T), fa(0, j, 0, T), ALU.subtract)
                act(t5[:, 0:T], t5[:, 0:T], AF.Exp, scale=-CDEC)
                stt(ha(0, j, 0, T), t4[:, 0:T], C("negone"), t5[:, 0:T], ALU.mult, ALU.mult)
                act(t5[:, 0:T], fa(1, j, 0, T), AF.Exp, scale=CDEC)
                tt(t4[:, 0:T], t4[:, 0:T], fa(2, j, 0, T), ALU.mult)
                tt(ha(2, j, 0, T), t4[:, 0:T], t5[:, 0:T], ALU.mult)
                ts(t4[:, 0:T], fa(2, j, 0, T), C("negone"), ALU.add, PR(l, "ka", j), ALU.mult)
                stt(fa(3, j, 0, T), t4[:, 0:T], C("one"), t3[:, 0:T], ALU.add, ALU.mult)
                tt(ha(3, j, 0, T), fa(3, j, 0, T), t5[:, 0:T], ALU.mult)
            wf, nk, _ = W.get("cr")
            for j in range(4):
                pb = nbank()
                fm_proj(pb, T, wf, nk, j * 128, hrhs(T))
                shifted(pb, j, t3[:, 0:T])
                act(t5[:, 0:T], fa(1, j, 0, T), AF.Exp, scale=-CDEC)
                tt(ha(1, j, 0, T), t3[:, 0:T], t5[:, 0:T], ALU.mult)
                stt(sqb[1][:, 0:T], t3[:, 0:T], PR(l, "rk", j), fa(3, j, 0, T), ALU.mult, ALU.mult)
                mm(PD[:, 0:T], blkb[:], sqb[1][:, 0:T])
                cp(fa(3, j, 0, T), PD[:, 0:T], eng="act")
            wf, nk, _ = W.get("cv")
            for j in range(4):
                pb = nbank()
                fm_proj(pb, T, wf, nk, j * 128, hrhs(T))
                shifted(pb, 8 + j, t3[:, 0:T])
                cp(ha(4, j, 0, T), t3[:, 0:T], eng="act")
                tt(fa(3, j, 0, T), fa(3, j, 0, T), t3[:, 0:T], ALU.mult)
            chk("C_prep")
            nst = {64: 6, 32: 5}[Cc]
            mskv = C("msk", 256)[0:Cc, :]
            X1 = FA[:, 2048:4096].bitcast(BF16)
            X2 = FA[:, 4096:6144].bitcast(BF16)
            sets = [dict(MM=(MM[:, :], None), Bh=(BhTM[:, :], None), Kh=(KhTM[:, :], None), Vb=(Vb[:, :], None),
                         Vp=(Vpad[:, :], None), Tf=(Tf0[:, :], None)),
                    dict(MM=(X1[0:64, 0:2048], 1), Bh=(X1[0:64, 2048:2560], 1), Kh=(X1[0:64, 2560:3072], 1),
                         Vb=(X1[0:64, 3072:3584], 1), Vp=(X2[0:64, 0:1024], 2), Tf=(X2[0:64, 1024:1536], 2))]

            def V(buf, r0, r1, c0, c1):
                ap, key = buf
                v = ap[r0:r1, c0:c1]
                return v if key is None else (v, key)

            def K_(buf, ap):
                return ap if buf[1] is None else (ap, buf[1])
            mset(V(sets[1]["Vp"], 0, 64, 0, 1024), 0.0)

            def front(t0, ci, S_):
                MMb, Bhb, Khb, Vbb, Vpb, Tfb = S_["MM"], S_["Bh"], S_["Kh"], S_["Vb"], S_["Vp"], S_["Tf"]
                wc0 = wc[:, ci:ci + 1]
                wcb = bass.AP(wc0.tensor, wc0.offset, [[wc0.ap[0][0], 128], [8, 4], [0, Cc]])
                for hi_, (src_slot, dstb) in enumerate(((2, Bhb), (3, Khb))):
                    hv = hbk[:, hi_ * 256:(hi_ + 1) * 256].rearrange("p (q c) -> p q c", q=4)[:, :, 0:Cc]
                    tt(hv, (r3(HA[:, src_slot * 2048:(src_slot + 1) * 2048], 4)[:, :, t0:t0 + Cc], src_slot), wcb, ALU.mult)
                    for q in range(4):
                        tr(PT[0:Cc, q * 128:(q + 1) * 128], hbk[:, hi_ * 256 + q * 64:hi_ * 256 + q * 64 + Cc], identb[:])
                    cp(V(dstb, 0, Cc, 0, 512), PT[0:Cc, 0:512], eng="act")
                    yield
                for q in range(4):
                    tr(PT[0:Cc, q * 128:(q + 1) * 128], ha(4, q, t0, t0 + Cc), identb[:])
                cp(V(Vbb, 0, Cc, 0, 512), PT[0:Cc, 0:512], eng="act")
                for hh in range(2):
                    cp(K_(Vpb, Vpb[0][0:Cc, :].rearrange("p (q h c) -> p q h c", q=4, h=2)[:, :, hh, hh * 64:(hh + 1) * 64]),
                       r3(PT[0:Cc, 0:512], 4)[:, :, hh * 64:(hh + 1) * 64], eng="dve")
                yield
                for q in range(4):
                    for hh in range(2):
                        mt = PA if hh == 0 else PC
                        mh = q // 2
                        b0 = hh * 64
                        a_ap = HA[b0:b0 + 64, 0 * 2048 + q * TP + t0:0 * 2048 + q * TP + t0 + Cc]
                        (pst, _), (st, _) = a_ap.ap
                        rhs2 = bass.AP(a_ap.tensor, a_ap.offset, [[pst, 64], [2048, 2], [st, Cc]])
                        base = mh * 512 + (q % 2) * 256
                        o13 = mt[0:Cc, base:base + 128].rearrange("p (a b) -> p a b", a=2)[:, :, 0:Cc]
                        o24 = mt[0:Cc, base + 128:base + 256].rearrange("p (a b) -> p a b", a=2)[:, :, 0:Cc]
                        P.op("pe", lambda e, o=o13, lh=ha(2, q, t0, t0 + Cc, b0, b0 + 64)[0], rh=rhs2:
                             e.matmul(o, lh, rh, start=True, stop=True), [(HA, 2), (HA, 0), (HA, 1)], [(mt, mh)])
                        P.op("pe", lambda e, o=o24, lh=ha(3, q, t0, t0 + Cc, b0, b0 + 64)[0], rh=rhs2:
                             e.matmul(o, lh, rh, start=True, stop=True), [(HA, 3), (HA, 0), (HA, 1)], [(mt, mh)])
                mm_base = MMb[0][0:Cc, 0:64]
                mm_ps = mm_base.ap[0][0]
                for hh in range(2):
                    mt = PA if hh == 0 else PC
                    for mh in range(2):
                        mview = mt[0:Cc, mh * 512:(mh + 1) * 512].rearrange("p (q m t) -> p q m t", q=2, m=4)[:, :, :, 0:Cc]
                        mskb = bass.AP(mskv.tensor, mskv.offset, [[mskv.ap[0][0], Cc], [0, 2], [64, 4], [1, Cc]])
                        dstv = bass.AP(mm_base.tensor, mm_base.offset + (2 * mh) * 512 + hh * 256,
                                       [[mm_ps, Cc], [512, 2], [64, 4], [1, Cc]])
                        tt(K_(MMb, dstv), (mview, mh), mskb, ALU.mult)
                yield
                for h in range(8):
                    q, b0 = h // 2, (h % 2) * 64
                    bnk = PA if h % 2 == 0 else PC
                    mm((bnk[0:Cc, q * 64:q * 64 + Cc], 0),
                       ha(0, q, t0, t0 + Cc, b0, b0 + 64), ha(2, q, t0, t0 + Cc, b0, b0 + 64))
                at0 = AT_[0][0:Cc, 0:64]
                for hh in range(2):
                    bnk = PA if hh == 0 else PC
                    dstv = bass.AP(at0.tensor, at0.offset + hh * 64, [[at0.ap[0][0], Cc], [128, 4], [1, Cc]])
                    tt(dstv, (r3(bnk[0:Cc, 0:256], 4)[:, :, 0:Cc], 0),
                       bc_mid(C("lsm", 64)[0:Cc, 0:Cc], 4), ALU.mult)
                n1v = K_(MMb, MMb[0][0:Cc, :].rearrange("p (h m t) -> p h m t", h=8, m=4)[:, :, 0, 0:Cc])

                def cbv(p_, part, h0=0, h1=8):
                    return CB[p_][0:Cc, h0 * 128:h1 * 128].rearrange("p (h c) -> p h c", h=h1 - h0)[:, :, part * 64:part * 64 + Cc]

                def pcv(part):
                    return PC[0:Cc, :].rearrange("p (h c) -> p h c", h=8)[:, :, part * 64:part * 64 + Cc]
                cp(cbv(0, 0), n1v, eng="act")
                tt(cbv(1, 1), n1v, bc_mid(identf[0:Cc, 0:Cc], 8), ALU.add)
                yield
                cur = 0
                for r in range(1, nst + 1):
                    nxt = 1 - cur
                    p_, n_ = (r - 1) % 2, r % 2
                    for h in range(8):
                        lh = AT_[cur][0:Cc, h * 64:h * 64 + Cc]
                        if r == 1:
                            mm((PC[0:Cc, h * 128:h * 128 + Cc], h // 4), lh, CB[p_][0:Cc, h * 128:h * 128 + Cc])
                        elif r <= nst - 1:
                            mm((PC[0:Cc, h * 128:(h + 1) * 128].rearrange("p (a c) -> p a c", a=2)[:, :, 0:Cc], h // 4), lh,
                               CB[p_][0:Cc, h * 128:(h + 1) * 128].rearrange("p (a c) -> p a c", a=2)[:, :, 0:Cc])
                        else:
                            mm((PC[0:Cc, h * 128 + 64:h * 128 + 64 + Cc], h // 4), lh, CB[p_][0:Cc, h * 128 + 64:h * 128 + 64 + Cc])
                    if r <= nst - 1:
                        for h in range(8):
                            mm((PA[0:Cc, 512 + h * 64:512 + h * 64 + Cc], 1), CB[p_][0:Cc, h * 128:h * 128 + Cc],
                               AT_[cur][0:Cc, h * 64:h * 64 + Cc])
                    if r >= 2:
                        dst_t = cbv(n_, 1) if r < nst else K_(Tfb, r3(Tfb[0][0:Cc, :], 8)[:, :, 0:Cc])
                        tt(dst_t, cbv(p_, 1), pcv(1), ALU.add)
                    if r <= nst - 1:
                        cp(cbv(n_, 0), pcv(0), eng="act")
                        cp(r3(AT_[nxt][0:Cc, :], 8)[:, :, 0:Cc], (r3(PA[0:Cc, 512:1024], 8)[:, :, 0:Cc], 1), eng="act")
                        cur = nxt
                    yield

            def chain(t0, ci, S_):
                MMb, Bhb, Khb, Vbb, Vpb, Tfb = S_["MM"], S_["Bh"], S_["Kh"], S_["Vb"], S_["Vp"], S_["Tf"]
                for q in range(4):
                    mm((PB[0:Cc, q * 128:(q + 1) * 128], 0), ha(0, q, t0, t0 + Cc), Srwb[:, q * 128:(q + 1) * 128],
                       start=True, stop=False)
                    for hh in range(2):
                        mm((PB[0:Cc, q * 128:(q + 1) * 128], 0),
                           V(MMb, 0, Cc, q * 512 + hh * 256 + 128, q * 512 + hh * 256 + 128 + Cc),
                           V(Vpb, 0, Cc, (q * 2 + hh) * 128, (q * 2 + hh + 1) * 128), start=False, stop=(hh == 1))
                for hh in range(2):
                    cp(Gpad[0:Cc, :].rearrange("p (q h c) -> p q h c", q=4, h=2)[:, :, hh, hh * 64:(hh + 1) * 64],
                       (r3(PB[0:Cc, 0:512], 4)[:, :, hh * 64:(hh + 1) * 64], 0), eng="act" if hh else "dve")
                yield
                for q in range(4):
                    for hh in range(2):
                        h = q * 2 + hh
                        mm((PB[0:Cc, 512 + q * 128:512 + (q + 1) * 128], 1), V(Tfb, 0, Cc, h * 64, h * 64 + Cc),
                           Gpad[0:Cc, h * 128:(h + 1) * 128], start=(hh == 0), stop=(hh == 1))
                cp(Ub[0:Cc, :], (PB[0:Cc, 512:1024], 1), eng="act")
                for hh in range(2):
                    cp(Upad[0:Cc, :].rearrange("p (q h c) -> p q h c", q=4, h=2)[:, :, hh, hh * 64:(hh + 1) * 64],
                       (r3(PB[0:Cc, 512:1024], 4)[:, :, hh * 64:(hh + 1) * 64], 1), eng="dve")
                yield
                for q in range(4):
                    oy = PD[:, q * 64:q * 64 + Cc]
                    mm(oy, Srwb[:, q * 128:(q + 1) * 128], ha(1, q, t0, t0 + Cc), start=True, stop=False)
                    for hh in range(2):
                        h = q * 2 + hh
                        mm(oy, Upad[0:Cc, h * 128:(h + 1) * 128],
                           V(MMb, 0, Cc, q * 512 + hh * 256 + 64, q * 512 + hh * 256 + 64 + Cc), start=False, stop=False)
                        mm(oy, V(Vpb, 0, Cc, h * 128, (h + 1) * 128),
                           V(MMb, 0, Cc, q * 512 + hh * 256 + 192, q * 512 + hh * 256 + 192 + Cc), start=False, stop=(hh == 1))
                cp((r3(FA[:, 0:2048], 4)[:, :, t0:t0 + Cc], 0), r3(PD[:, 0:256], 4)[:, :, 0:Cc], eng="act")
                yield
                for q in range(4):
                    mm(PD[:, q * 128:(q + 1) * 128], V(Bhb, 0, Cc, q * 128, (q + 1) * 128), Ub[0:Cc, q * 128:(q + 1) * 128],
                       start=True, stop=False)
                    mm(PD[:, q * 128:(q + 1) * 128], V(Khb, 0, Cc, q * 128, (q + 1) * 128), V(Vbb, 0, Cc, q * 128, (q + 1) * 128),
                       start=False, stop=True)
                tt(r3(stmp[:], 4), r3(PD[:], 4), bc_mid(blkf, 4), ALU.mult)
                for q in range(4):
                    stt(Srw[:, q * 128:(q + 1) * 128], Srw[:, q * 128:(q + 1) * 128], wc[:, q * 8 + ci:q * 8 + ci + 1],
                        stmp[:, q * 128:(q + 1) * 128], ALU.mult, ALU.add)
                cp(Srwb[:], Srw[:], eng="act")
                yield

            def state_in(sg):
                if sg["rw_in"] == "zero":
                    mset(Srw[:], 0.0)
                    mset(Srwb[:], 0.0)
                elif sg["rw_in"] is not None:
                    mset(Srw[:], 0.0)
                    dma(r3(rwst, 8), srw[l, sg["rw_in"]].rearrange("h i j -> i h j"))
                    for q in range(4):
                        tr(PD[:, q * 64:(q + 1) * 64], rwst[:, q * 128:(q + 1) * 128], identf[0:64, 0:64])
                    for hh in range(2):
                        cp(r3(Srw[hh * 64:(hh + 1) * 64, :], 4)[:, :, hh * 64:(hh + 1) * 64],
                           r3(PD[hh * 64:(hh + 1) * 64, 0:256], 4), eng="act")
                    cp(Srwb[:], Srw[:], eng="act")

            def state_out(sg):
                if sg["rw_out"] is not None:
                    for hh in range(2):
                        cp(r3(rwcp[hh * 64:(hh + 1) * 64, :], 4), r3(Srw[hh * 64:(hh + 1) * 64, :], 4)[:, :, hh * 64:(hh + 1) * 64],
                           eng="pool")
                    for q in range(4):
                        tr(PD[0:64, q * 128:(q + 1) * 128], rwcp[:, q * 64:(q + 1) * 64], identf)
                    cp(rwst, PD[0:64, :], eng="act")
                    dma(sg["rw_out"].rearrange("h i j -> i h j"), r3(rwst, 8))

            items = [(sg, t0, ci, k == 0, k == len(sg["chunks"]) - 1) for sg in G["segs"]
                     for k, (t0, ci) in enumerate(sg["chunks"])]
            for _ in front(items[0][1], items[0][2], sets[0]):
                pass
            for n_, (sg, t0, ci, first, last_) in enumerate(items):
                gf = front(items[n_ + 1][1], items[n_ + 1][2], sets[(n_ + 1) % 2]) if n_ + 1 < len(items) else iter(())
                if first:
                    state_in(sg)
                gc = chain(t0, ci, sets[n_ % 2])
                alive_f = alive_c = True
                step = 0
                while alive_f or alive_c:
                    if alive_f:
                        alive_f = next(gf, "END") != "END"
                    if alive_c and (step % 2 == 1 or not alive_f):
                        alive_c = next(gc, "END") != "END"
                    step += 1
                if last_:
                    state_out(sg)
            chk("C_chunks")
            for q in range(4):
                cp(sqb[0][:, 0:T], fa(0, q, 0, T), eng="pool")
                mm(PD[:, 0:T], blkb[:], sqb[0][:, 0:T])
                stt(t3[:, 0:T], PD[:, 0:T], -1.0 / 64, fa(0, q, 0, T), ALU.mult, ALU.add)
                act(sqb[1][:, 0:T], t3[:, 0:T], AF.Square)
                mm(PD[:, 0:T], blkb[:], sqb[1][:, 0:T])
                act(t4[:, 0:T], PD[:, 0:T], AF.Ln, bias=C("gneps"), scale=1.0 / 64)
                act(t4[:, 0:T], t4[:, 0:T], AF.Exp, scale=-0.5)
                tt(t3[:, 0:T], t3[:, 0:T], t4[:, 0:T], ALU.mult)
                ts(t3[:, 0:T], t3[:, 0:T], PR(l, "lnw", q), ALU.mult, PR(l, "lnb", q), ALU.add)
                tt(t3[:, 0:T], t3[:, 0:T], fa(3, q, 0, T), ALU.add)
                tt(ocT[:, q * TP:q * TP + T], t3[:, 0:T], ha(5, q, 0, T), ALU.mult)

        def phase_G(l, T):
            for b in range(3):
                for half, tg in enumerate((f"gl{b}a", f"gl{b}b")):
                    wf, nk, _ = W.get(tg)
                    for j in range(4):
                        c = half * 4 + j
                        pb = nbank()
                        fm_proj(pb, T, wf, nk, j * 128, hrhs(T))
                        act(ha(c // 4, c % 4, 0, T), bk(pb, 0, 128, 0, T), AF.Sigmoid)
                wf, nk, _ = W.get(f"br{b}")
                for c in range(8):
                    pb = nbank()
                    fm_proj(pb, T, wf, nk, c * 128, lambda kc: oT3[b][:, kc * TP:kc * TP + T])
                    if b == 0:
                        tt(fa(c // 4, c % 4, 0, T), bk(pb, 0, 128, 0, T), ha(c // 4, c % 4, 0, T), ALU.mult)
                    else:
                        tt(t4[:, 0:T], bk(pb, 0, 128, 0, T), ha(c // 4, c % 4, 0, T), ALU.mult)
                        tt(fa(c // 4, c % 4, 0, T), fa(c // 4, c % 4, 0, T), t4[:, 0:T], ALU.add, eng="pool")
            for c in range(8):
                cp(ha(2 + c // 4, c % 4, 0, T), fa(c // 4, c % 4, 0, T), eng="act" if c % 2 else "pool")
            for half, tg in enumerate(("outa", "outb")):
                wf, nk, _ = W.get(tg)
                for j in range(4):
                    c = half * 4 + j
                    pb = nbank()
                    fm_proj(pb, T, wf, nk, j * 128, lambda kc: ha(2 + kc // 4, kc % 4, 0, T))
                    tt((xT[:, c * TP:c * TP + T], c), (xT[:, c * TP:c * TP + T], c), bk(pb, 0, 128, 0, T), ALU.add)

        def phase_F(l, T):
            norm_to_hT(l, T, "nffn")
            for i in range(6):
                wg, nk, ncol = W.get(f"fg{i}")
                wu, _, _ = W.get(f"fu{i}")
                for j in range(ncol // 128):
                    c = i * 4 + j
                    pg = nbank()
                    fm_proj(pg, T, wg, nk, j * 128, hrhs(T))
                    pu = nbank()
                    fm_proj(pu, T, wu, nk, j * 128, hrhs(T))
                    act(t4[:, 0:T], bk(pg, 0, 128, 0, T), AF.Silu)
                    tt(ha(c // 4, c % 4, 0, T), t4[:, 0:T], bk(pu, 0, 128, 0, T), ALU.mult)
            for j in range(8):
                wf, nk, _ = W.get(f"fd{j}")
                pb = nbank()
                fm_proj(pb, T, wf, nk, 0, lambda kc: ha(kc // 4, kc % 4, 0, T))
                tt((xT[:, j * TP:j * TP + T], j), (xT[:, j * TP:j * TP + T], j), bk(pb, 0, 128, 0, T), ALU.add)

        def phase_P(l, T, psrc):
            for ti in range((T + 127) // 128):
                dma(xin[:, 0:PLE], psrc[ti * 128:(ti + 1) * 128, :])
                pb = nbank()
                for k in range(2):
                    tr(bk(pb, 0, 128, k * 128, (k + 1) * 128), xin[:, k * 128:(k + 1) * 128], identf)
                cp((r3(HA[:, 4 * 2048:4 * 2048 + 2 * TP], 2)[:, :, ti * 128:(ti + 1) * 128], 4), (r3(pb[0][:, 0:256], 2), pb[1]),
                   eng="act")
            wf, nk, _ = W.get("pp")
            for c in range(8):
                pb = nbank()
                fm_proj(pb, T, wf, nk, c * 128, lambda kc: ha(4, kc, 0, T))
                cp(fa(c // 4, c % 4, 0, T), bk(pb, 0, 128, 0, T), eng="act")
            rms_stats(T, t3[:, 0:T], lambda kc: fa(kc // 4, kc % 4, 0, T), 8, float(D), "eps6")
            for c in range(8):
                stt(fa(c // 4, c % 4, 0, T), fa(c // 4, c % 4, 0, T), PR(l, "nple", c), t3[:, 0:T], ALU.mult, ALU.mult,
                    eng="dve" if c % 2 else "pool")
            norm_to_hT(l, T, None)
            for half, tg in enumerate(("pga", "pgb")):
                wf, nk, _ = W.get(tg)
                for j in range(4):
                    c = half * 4 + j
                    pb = nbank()
                    fm_proj(pb, T, wf, nk, j * 128, hrhs(T))
                    tg = t4 if c % 2 == 0 else t5
                    act(tg[:, 0:T], bk(pb, 0, 128, 0, T), AF.Sigmoid)
                    tt(tg[:, 0:T], tg[:, 0:T], fa(c // 4, c % 4, 0, T), ALU.mult)
                    tt((xT[:, c * TP:c * TP + T], c), (xT[:, c * TP:c * TP + T], c), tg[:, 0:T], ALU.add,
                       eng="dve" if c % 2 == 0 else "pool")

        KSTOP = _os.environ.get("KSTOP", "")

        class _Stop(Exception):
            pass

        def chk(tag):
            if KSTOP == tag:
                raise _Stop()

        def layer(l, T, G):
            chk("load")
            norm_to_hT(l, T, "nmix")
            chk("norm")
            mixer_A(l, T, G)
            chk("A")
            mixer_B(l, T, G)
            chk("B")
            mixer_C(l, T, G)
            chk("C")
            if dbg and G.get("dbg"):
                for b, nm in enumerate(("d_oa", "d_ob", "d_oc")):
                    dma(dbg_t[nm], oT3[b][:], eng="pool")
            phase_G(l, T)
            chk("G")
            phase_F(l, T)
            chk("F")
            phase_P(l, T, G["psrc"])
            chk("P")

        def x_in(l, src_tm, src_fm, T, key):
            if l == 0:
                load_x_tm(src_tm, T)
            else:
                dma(r3(xT[:], KC)[:, :, 0:T], src_fm.rearrange("k p t -> p k t"), rkey=key)

        def x_out(l, dst_tm, dst_fm, T, key):
            if l == DEPTH - 1:
                store_x_tm(dst_tm, T)
            else:
                dma(dst_fm.rearrange("k p t -> p k t"), r3(xT[:], KC)[:, :, 0:T], wkey=key)

        for l in range(DEPTH):
            for g in range(NG + 1):
                W.plan(l)

        def _main():
          for l in range(DEPTH):
              dma(cwa2b[:], cwa2[l], eng="pool")
              dma(cg2b[:], cg2[l], eng="pool")
              dma(biasT[:], bias_d[l], eng="pool")
              mset(shcol[:], 0.0)
              for g in range(NG):
                  kslot = g % 2
                  last = (g == NG - 1)
                  T = TP
                  x_in(l, xp[g * TP:(g + 1) * TP, :], x1p[:, :, g * TP:(g + 1) * TP], T, "x1p:%d" % (g % 4))
                  qbs = []
                  for m_ in range(4):
                      kts = []
                      for kt in range(5):
                          wcd = 128 * (m_ + kt)
                          if wcd < 512:
                              if g == 0:
                                  continue
                              slot, off = 1 - kslot, wcd
                          else:
                              slot, off = kslot, wcd - 512
                          kts.append((slot * 512 + off, 128, slot * 4 + off // 128, kt))
                      qbs.append(dict(q0=128 * m_, nq=128, kts=kts))
                  tmt = [(ti * 128, 128, ti, 128, rope_p[g * TP + ti * 128:g * TP + (ti + 1) * 128, :]) for ti in range(4)]

                  def prevcol(c):
                      return shcol[:, c:c + 1].rearrange("p (s u) -> p s u", s=1)

                  def savecol(c, r_seg):
                      cp(shcol[:, c:c + 1], r_seg[:, 0, TP:TP + 1], eng="pool")
                  G = dict(kslot=kslot, vt=[(ti * 128, 128, kslot * 4 + ti) for ti in range(4)],
                           a_out=([(akp[l, ti * 128:(ti + 1) * 128, :], avp[l, ti * 128:(ti + 1) * 128, :], ti * 128, 128)
                                   for ti in range(4)] if last else []),
                           tm=tmt,
                           segs=[dict(qblocks=qbs, tm=tmt, state_in=("zero" if g == 0 else None),
                                      ret_out=(retp[l] if last else None),
                                      rw_in=("zero" if g == 0 else None), rw_out=(rwp[l] if last else None),
                                      chunks=[(64 * ci, ci) for ci in range(8)])],
                           nseg=1, Ls=TP, Cc=64, prevcol=prevcol, savecol=[savecol],
                           psrc=pp[l, g * TP:(g + 1) * TP, :], dbg=(l == 0 and g == 0))
                  layer(l, T, G)
                  if last:
                      dma(shp[l].rearrange("(c p) -> p c", p=128), shcol[:], slow=True)
                  if dbg and l == 0 and g == 0:
                      dma(dbg_t["d_x"], xT[:])
                  x_out(l, yp[g * TP:(g + 1) * TP, :], x1p[:, :, g * TP:(g + 1) * TP], T, "x1p:%d" % (g % 4))
              T = TSM
              x_in(l, xs, x1s, T, "x1s")
              segs = []
              tmt = []
              dma(shin[:, 0:14 * NSMP].rearrange("p (s c) -> p s c", s=NSMP), ssh[l].rearrange("s (c p) -> p s c", p=128), slow=True)
              for s in range(NSMP):
                  tms = [(LS * s, LS, s, 32, rope_s[LS * s:LS * (s + 1), :])]
                  tmt += tms
                  kts = [(kt * 128, 128, kt, kt) for kt in range(4)] + [(512 + LS * s, LS, 4 + s, 4)]
                  segs.append(dict(cache=s, qblocks=[dict(q0=LS * s, nq=LS, kts=kts)], tm=tms, state_in=s, ret_out=rets[l, s],
                                   rw_in=s, rw_out=rws[l, s], chunks=[(LS * s, s)]))

              def prevcol_s(c):
                  return shin[:, 0:14 * NSMP].rearrange("p (s c) -> p s c", s=NSMP)[:, :, c:c + 1]

              def savecol_s(c, r_seg):
                  cp(shout[:, 0:14 * NSMP].rearrange("p (s c) -> p s c", s=NSMP)[:, :, c:c + 1], r_seg[:, :, LS:LS + 1], eng="pool")
              G = dict(kslot=1, vt=[(LS * s, LS, 4 + s) for s in range(NSMP)],
                       a_out=[(aks[l, LS * s:LS * (s + 1), :], avs[l, LS * s:LS * (s + 1), :], LS * s, LS) for s in range(NSMP)],
                       tm=tmt, segs=segs, nseg=NSMP, Ls=LS, Cc=32, prevcol=prevcol_s, savecol=[savecol_s],
                       psrc=psm[l], dbg=False)
              layer(l, T, G)
              dma(shs[l].rearrange("s (c p) -> p s c", p=128), shout[:, 0:14 * NSMP].rearrange("p (s c) -> p s c", s=NSMP), slow=True)
              x_out(l, ys, x1s, T, "x1s")

        try:
            _main()
        except _Stop:
            pass
        P.emit(nc, es)
    return nc, P


_NC_CACHE = {}


def _run(inp, ncores, dbg=False):
    xpr = np.asarray(inp["x_prompt"], np.float32)
    B, SEQ, _ = xpr.shape
    xsm = np.asarray(inp["x_sample"], np.float32)
    assert xsm.shape[0] == ncores * NSMP and xsm.shape[1] == LS and B <= ncores
    key = (SEQ, dbg)
    if key not in _NC_CACHE:
        _NC_CACHE[key] = build(SEQ, dbg=dbg)[0]
    nc = _NC_CACHE[key]
    cst, rope_p, rope_s = host_consts(SEQ)
    f = lambda k: np.ascontiguousarray(np.asarray(inp[k], np.float32))
    prm, cwa2 = host_params({k: f(k) for k in ("norm_mix", "norm_ffn", "ple_norm", "a_q_norm", "a_k_norm", "c_shift_mu",
                                                "c_w0", "c_a0", "c_k_k", "c_k_a", "c_ln_w", "c_ln_b", "c_r_k", "c_w2", "c_a2")})
    biasT = host_bias(f("a_rel_bias")).reshape(DEPTH, 128, 5 * 1024)
    shared = dict(w_in=f("w_in"), w_br=f("w_branch"), w_out=f("w_out"), w_fg=f("w_ffn_gate"), w_fu=f("w_ffn_up"),
                  w_fd=f("w_ffn_down"), w_pp=f("w_ple_proj"), w_pg=f("w_ple_gate"), cg2=f("c_g2"), cwa2=cwa2, prm=prm,
                  cst=cst, biasT=biasT, rope_p=rope_p, rope_s=rope_s)
    pp_ = f("p_prompt"); ps_ = f("p_sample"); cak = f("cache_a_k"); cav = f("cache_a_v")
    sr = f("state_ret"); sw = f("state_rwkv"); sh = f("state_rwkv_shift")
    zx = np.zeros((SEQ, D), np.float32); zp = np.zeros((DEPTH, SEQ, PLE), np.float32)
    in_maps = []
    for c in range(ncores):
        m = dict(shared)
        if c < B:
            m["xp"] = np.ascontiguousarray(xpr[c]); m["pp"] = np.ascontiguousarray(pp_[:, c])
        else:
            m["xp"] = zx; m["pp"] = zp
        sl = slice(c * NSMP, (c + 1) * NSMP)
        m["xs"] = np.ascontiguousarray(xsm[sl].reshape(NSMP * LS, D))
        m["psm"] = np.ascontiguousarray(ps_[:, sl].reshape(DEPTH, NSMP * LS, PLE))
        m["cak"] = np.ascontiguousarray(cak[:, sl].reshape(DEPTH, NSMP, 512, 512))
        m["cav"] = np.ascontiguousarray(cav[:, sl].reshape(DEPTH, NSMP, 512, 512))
        m["sret"] = np.ascontiguousarray(sr[:, sl]); m["srw"] = np.ascontiguousarray(sw[:, sl])
        m["ssh"] = np.ascontiguousarray(sh[:, sl].reshape(DEPTH, NSMP, 1792))
        in_maps.append(m)
    res = run_bass_kernel_spmd(nc, in_maps, core_ids=list(range(ncores))).results
    R = lambda k, cs: [np.asarray(res[c][k], np.float32) for c in cs]
    pc = list(range(B)); ac = list(range(ncores))
    NB = ncores * NSMP
    out = (
        np.stack(R("yp", pc)),
        np.concatenate(R("ys", ac)).reshape(NB, LS, D),
        np.stack(R("akp", pc), axis=1).reshape(DEPTH, B, 512, 8, 64),
        np.stack(R("avp", pc), axis=1).reshape(DEPTH, B, 512, 8, 64),
        np.stack(R("retp", pc), axis=1),
        np.stack(R("rwp", pc), axis=1),
        np.stack(R("shp", pc), axis=1).reshape(DEPTH, B, 1, 1792),
        np.concatenate([r.reshape(DEPTH, NSMP, LS, 8, 64) for r in R("aks", ac)], axis=1),
        np.concatenate([r.reshape(DEPTH, NSMP, LS, 8, 64) for r in R("avs", ac)], axis=1),
        np.concatenate(R("rets", ac), axis=1),
        np.concatenate(R("rws", ac), axis=1),
        np.concatenate(R("shs", ac), axis=1).reshape(DEPTH, NB, 1, 1792),
    )
    if dbg:
        return out, res
    return out


def kernel(**inputs):
    return _run(inputs, 8)
```

```python
import numpy as np
from contextlib import ExitStack
import concourse.bass as bass
import concourse.mybir as mybir
from concourse.bass_utils import run_bass_kernel_spmd

F32 = mybir.dt.float32
BF16 = mybir.dt.bfloat16
ALU = mybir.AluOpType
AF = mybir.ActivationFunctionType
AX = mybir.AxisListType

D = 1024
KC = 8
INW = 8448
DFF = 2816
NFF = 22
PLE = 256
DEPTH = 2
NSMP = 4
LS = 32
TP = 512
PAST = 2048
NEG = -30000.0
CDEC = 0.6065306597126334
GN_EPS = 64e-5

_c = {}
_o = 0
for _n, _w in [("ident", 128), ("ones", 128), ("blk", 128), ("m01", 128), ("msk", 256), ("lsm", 64),
               ("cm64", 512), ("cm32", 128),
               ("gc128", 4), ("gc32", 4), ("tq128", 4), ("tk128", 4), ("tq32", 4), ("tk32", 4),
               ("eps6", 1), ("eps12", 1), ("gneps", 1), ("one", 1), ("negone", 1), ("zero", 1)]:
    _c[_n] = _o
    _o += _w
NCST = _o
_p = {}
_o = 0
for _n, _w in [("nmix", 8), ("nffn", 8), ("nple", 8), ("aqn", 1), ("akn", 1), ("mu", 14), ("w0", 4), ("a0", 4),
               ("kk", 4), ("ka", 4), ("rk", 4), ("lnw", 4), ("lnb", 4)]:
    _p[_n] = _o
    _o += _w
NPRM = _o


def _gammas():
    return 1.0 - np.exp2(-5.0 - np.arange(4, dtype=np.float64))


def host_consts(SEQ):
    g = _gammas()
    cst = np.zeros((128, NCST), np.float32)
    p = np.arange(128)
    cst[:, _c["ident"]:_c["ident"] + 128] = np.eye(128)
    cst[:, _c["ones"]:_c["ones"] + 128] = 1.0
    cst[:, _c["blk"]:_c["blk"] + 128] = (p[:, None] // 64 == p[None, :] // 64)
    cst[:, _c["m01"]:_c["m01"] + 128] = (p[None, :] >= p[:, None])
    m = np.arange(64)
    strict = (m[:, None] < m[None, :]).astype(np.float32)
    incl = (m[:, None] <= m[None, :]).astype(np.float32)
    msk = np.stack([strict, incl, strict, incl], axis=1)
    cst[:64, _c["msk"]:_c["msk"] + 256] = msk.reshape(64, 256)
    cst[:64, _c["lsm"]:_c["lsm"] + 64] = (m[None, :] < m[:, None])
    t = np.arange(512)
    cst[:, _c["cm64"]:_c["cm64"] + 512] = (t % 64 != 0).astype(np.float32)[None]
    cst[:, _c["cm32"]:_c["cm32"] + 128] = (t[:128] % 32 != 0).astype(np.float32)[None]
    cst[:, _c["gc128"]:_c["gc128"] + 4] = (g ** 128)[None]
    cst[:, _c["gc32"]:_c["gc32"] + 4] = (g ** 32)[None]
    sc = 128.0 ** -0.5
    cst[:, _c["tq128"]:_c["tq128"] + 4] = g[None, :] ** (p + 1.0)[:, None]
    cst[:, _c["tk128"]:_c["tk128"] + 4] = sc * g[None, :] ** (-(p + 1.0))[:, None]
    cst[:, _c["tq32"]:_c["tq32"] + 4] = g[None, :] ** ((p % 32) + 1.0)[:, None]
    cst[:, _c["tk32"]:_c["tk32"] + 4] = sc * g[None, :] ** (-((p % 32) + 1.0))[:, None]
    cst[:, _c["eps6"]] = 1e-6
    cst[:, _c["eps12"]] = 1e-12
    cst[:, _c["gneps"]] = GN_EPS
    cst[:, _c["one"]] = 1.0
    cst[:, _c["negone"]] = -1.0
    inv = (10000.0 ** (-np.arange(64, dtype=np.float32) / np.float32(64))).astype(np.float32)

    def rope(pos):
        ang = (pos.astype(np.float32)[:, None] * inv[None, :]).astype(np.float32)
        return np.concatenate([np.cos(ang), np.sin(ang)], axis=1).astype(np.float32)
    rope_p = rope(np.arange(SEQ))
    rope_s = np.tile(rope(PAST + np.arange(LS)), (NSMP, 1))
    return cst, rope_p, rope_s


def host_bias(rb):
    j = np.arange(128)[:, None, None]
    kt = np.arange(5)[None, :, None]
    qq = np.arange(128)[None, None, :]
    kk = 128 * kt + j
    idx = np.clip(512 + qq - kk, -63, 256) + 63
    ok = np.where(qq < 64, kk < 576, kk >= 64)
    out = rb[:, :, idx]
    out = np.where(ok[None, None], out, np.float32(NEG)).astype(np.float32)
    slot_h = [(s_ % 4) * 2 + s_ // 4 for s_ in range(8)]
    out = out[:, slot_h]
    return np.ascontiguousarray(out.transpose(0, 2, 3, 1, 4))


def host_params(inp):
    prm = np.zeros((DEPTH, 128, NPRM), np.float32)

    def fm(v, n):
        return v.reshape(n, 128).T
    for l in range(DEPTH):
        prm[l, :, _p["nmix"]:_p["nmix"] + 8] = fm(inp["norm_mix"][l], 8)
        prm[l, :, _p["nffn"]:_p["nffn"] + 8] = fm(inp["norm_ffn"][l], 8)
        prm[l, :, _p["nple"]:_p["nple"] + 8] = fm(inp["ple_norm"][l], 8)
        prm[l, :, _p["aqn"]] = np.tile(inp["a_q_norm"][l], 2)
        prm[l, :, _p["akn"]] = np.tile(inp["a_k_norm"][l], 2)
        prm[l, :, _p["mu"]:_p["mu"] + 14] = fm(inp["c_shift_mu"][l], 14)
        for nm, key in [("w0", "c_w0"), ("a0", "c_a0"), ("kk", "c_k_k"), ("ka", "c_k_a"), ("lnw", "c_ln_w"),
                        ("lnb", "c_ln_b")]:
            prm[l, :, _p[nm]:_p[nm] + 4] = fm(inp[key][l], 4)
        prm[l, :, _p["rk"]:_p["rk"] + 4] = fm(inp["c_r_k"][l].reshape(-1), 4)
    cwa2 = np.concatenate([inp["c_w2"], inp["c_a2"]], axis=1).astype(np.float32)
    return prm, np.ascontiguousarray(cwa2)


class Prog:
    ENGS = ("pe", "act", "dve", "pool", "sp")

    def __init__(self):
        self.ins = []
        self.last_w = {}
        self.readers = {}
        self.readonly = set()
        self.sub = {}
        self.excl = set()

    def keys(self, x):
        if isinstance(x, str):
            return [x]
        if isinstance(x, tuple):
            return [self.name(x[0]) + ":" + str(x[1])]
        n = self.name(x)
        if n in self.sub:
            return [n + ":" + str(i) for i in range(self.sub[n])]
        return [n]

    @staticmethod
    def name(x):
        t = getattr(x, "tensor", x)
        return t.name

    def op(self, eng, fn, reads=(), writes=(), dma=None):
        i = len(self.ins)
        deps = set()
        for r in reads:
            for k in self.keys(r):
                if k in self.readonly:
                    continue
                if k in self.last_w:
                    deps.add(self.last_w[k])
                rd = self.readers.setdefault(k, {})
                if k.split(":")[0] in self.excl:
                    for ek, r in rd.items():
                        if ek != eng:
                            deps.add(r)
                rd[("d", i) if dma is not None else eng] = i
        for w in writes:
            for k in self.keys(w):
                if k in self.last_w:
                    deps.add(self.last_w[k])
                for r in self.readers.get(k, {}).values():
                    if r != i:
                        deps.add(r)
                self.last_w[k] = i
                self.readers[k] = {}
        self.ins.append([eng, fn, dma, sorted(deps)])
        return i

    def emit(self, nc, es):
        ins = self.ins
        n = len(ins)
        need = [False] * n
        chans = {}
        for i, (eng, fn, dma, deps) in enumerate(ins):
            if dma is not None:
                need[i] = True
                chans.setdefault(dma, 0)
            for j in deps:
                ej, _, dj, _ = ins[j]
                if dj is not None or dma is not None or ej != eng or eng != "pe":
                    need[j] = True
        comp = [None] * n
        cnt = {e: 0 for e in self.ENGS}
        for i, (eng, fn, dma, deps) in enumerate(ins):
            if dma is not None:
                chans[dma] += 16
                comp[i] = ("d_" + dma, chans[dma])
            elif need[i]:
                cnt[eng] += 1
                comp[i] = ("e_" + eng, cnt[eng])
        sems = {}
        for s in ["e_" + e for e in self.ENGS] + ["d_" + c for c in chans]:
            sems[s] = es.enter_context(nc.semaphore(s))
        self.n_sems = len(sems)
        final = {("d_" + c): v for c, v in chans.items()}
        block = es.enter_context(nc.Block())
        per_eng = {e: [] for e in self.ENGS}
        for i, rec in enumerate(ins):
            per_eng[rec[0]].append(i)

        def run(eng_name, e):
            waited = {}
            for i in per_eng[eng_name]:
                _, fn, dma, deps = ins[i]
                wl = {}
                for j in deps:
                    ej, _, dj, _ = ins[j]
                    if dj is None and dma is None and ej == eng_name and eng_name == "pe":
                        continue
                    s, v = comp[j]
                    if wl.get(s, 0) < v:
                        wl[s] = v
                for s, v in wl.items():
                    if waited.get(s, 0) >= v:
                        continue
                    e.wait_ge(sems[s], v)
                    waited[s] = v
                r = fn(e)
                if comp[i] is not None:
                    r.then_inc(sems[comp[i][0]], 16 if dma is not None else 1)
            if eng_name == "sp":
                for s, v in final.items():
                    e.wait_ge(sems[s], v)

        @block.tensor
        def _(e):
            run("pe", e)

        @block.scalar
        def _(e):
            run("act", e)

        @block.vector
        def _(e):
            run("dve", e)

        @block.gpsimd
        def _(e):
            run("pool", e)

        @block.sync
        def _(e):
            run("sp", e)


def build(SEQ, dbg=False, skip=""):
    import os as _os
    assert SEQ % TP == 0
    NG = SEQ // TP
    TSM = NSMP * LS
    nc = bass.Bass("TRN2", target_bir_lowering=False)
    P = Prog()
    es = ExitStack()

    def din(name, shape):
        P.readonly.add(name)
        return nc.dram_tensor(name, list(shape), F32, kind="ExternalInput").ap()

    def dout(name, shape):
        return nc.dram_tensor(name, list(shape), F32, kind="ExternalOutput").ap()

    xp = din("xp", [SEQ, D]); pp = din("pp", [DEPTH, SEQ, PLE])
    xs = din("xs", [TSM, D]); psm = din("psm", [DEPTH, TSM, PLE])
    cak = din("cak", [DEPTH, NSMP, 512, 512]); cav = din("cav", [DEPTH, NSMP, 512, 512])
    sret = din("sret", [DEPTH, NSMP, 4, 128, 128]); srw = din("srw", [DEPTH, NSMP, 8, 64, 64])
    ssh = din("ssh", [DEPTH, NSMP, 1792])
    w_in = din("w_in", [DEPTH, D, INW]); w_br = din("w_br", [DEPTH, 3, 512, D]); w_out = din("w_out", [DEPTH, D, D])
    w_fg = din("w_fg", [DEPTH, D, DFF]); w_fu = din("w_fu", [DEPTH, D, DFF]); w_fd = din("w_fd", [DEPTH, DFF, D])
    w_pp = din("w_pp", [DEPTH, PLE, D]); w_pg = din("w_pg", [DEPTH, D, D])
    cg2 = din("cg2", [DEPTH, 128, 512]); cwa2 = din("cwa2", [DEPTH, 128, 512])
    prm_d = din("prm", [DEPTH, 128, NPRM]); cst_d = din("cst", [128, NCST])
    bias_d = din("biasT", [DEPTH, 128, 5 * 1024])
    rope_p = din("rope_p", [SEQ, 128]); rope_s = din("rope_s", [TSM, 128])

    yp = dout("yp", [SEQ, D]); ys = dout("ys", [TSM, D])
    akp = dout("akp", [DEPTH, 512, 512]); avp = dout("avp", [DEPTH, 512, 512])
    retp = dout("retp", [DEPTH, 4, 128, 128]); rwp = dout("rwp", [DEPTH, 8, 64, 64]); shp = dout("shp", [DEPTH, 1792])
    aks = dout("aks", [DEPTH, TSM, 512]); avs = dout("avs", [DEPTH, TSM, 512])
    rets = dout("rets", [DEPTH, NSMP, 4, 128, 128]); rws = dout("rws", [DEPTH, NSMP, 8, 64, 64])
    shs = dout("shs", [DEPTH, NSMP, 1792])
    wsrc = dict(w_in=w_in, w_br=w_br, w_out=w_out, w_fg=w_fg, w_fu=w_fu, w_fd=w_fd, w_pp=w_pp, w_pg=w_pg)
    wbf = {k: nc.dram_tensor(k + "_b", list(v.shape), BF16).ap() for k, v in wsrc.items()}
    x1p = nc.dram_tensor("x1p", [KC, 128, SEQ], F32).ap()
    x1s = nc.dram_tensor("x1s", [KC, 128, TSM], F32).ap()
    dbg_t = {}
    if dbg:
        for nm in ("d_oa", "d_ob", "d_oc"):
            dbg_t[nm] = dout(nm, [128, 4 * TP])
        dbg_t["d_x"] = dout("d_x", [128, KC * TP])

    with es:
        def sb(name, shape, dt=F32):
            return es.enter_context(nc.sbuf_tensor(name, list(shape), dt))

        def ps(name, shape, dt=F32, sub=None):
            if sub:
                P.sub[name] = sub
            P.excl.add(name)
            return es.enter_context(nc.psum_tensor(name, list(shape), dt))

        cst = sb("cst_sb", [128, NCST])
        prm = sb("prm_sb", [128, DEPTH * NPRM])
        identb = sb("identb", [128, 128], BF16)
        onesb = sb("onesb", [128, 128], BF16)
        blkb = sb("blkb", [128, 128], BF16)
        sqb = [sb(f"sqb{i}", [128, TP], BF16) for i in range(2)]
        aqs = sb("aqs", [128, DEPTH])
        cwa2b = sb("cwa2b", [128, 512], BF16)
        cg2b = sb("cg2b", [128, 512], BF16)
        biasT = sb("biasT_sb", [128, 5 * 1024], BF16)
        xT = sb("xT", [128, KC * TP]); P.sub["xT"] = KC
        hT = sb("hT", [128, KC * TP], BF16); P.sub["hT"] = KC
        NWB = 3
        WBN = 4096
        wb = [sb(f"wb{i}", [128, WBN], BF16) for i in range(NWB)]
        Kwin = sb("Kwin", [128, 4 * 1024], BF16)
        Vwin = sb("Vwin", [128, 8 * 520], BF16)
        Sret = sb("Sret", [128, 512]); Sretb = sb("Sretb", [128, 512], BF16)
        Srw = sb("Srw", [128, 512]); Srwb = sb("Srwb", [128, 512], BF16)
        shcol = sb("shcol", [128, 14])
        oT3 = [sb(f"oT{b}", [128, 4 * TP], BF16) for b in range(3)]
        FA = sb("FA", [128, 4 * 2048]); P.sub["FA"] = 4
        HA = sb("HA", [128, 6 * 2048], BF16); P.sub["HA"] = 6
        tmp = [sb(f"tmp{i}", [128, TP + 1]) for i in range(6)]
        t1, t2, t3, t4, t5, t6 = tmp
        ropet = [sb("ropet0", [128, 128])] * 2
        scsb = sb("scsb", [128, 1024])
        xin = scsb
        ptsb = [sb(f"ptsb{i}", [128, 1024], BF16) for i in range(2)]
        oatm = sb("oatm", [128, 512], BF16)
        scs = oatm
        rden = sb("rden", [128, 8])
        lob = sb("lob", [128, TP], BF16); sgl = sb("sgl", [128, TP], BF16)
        MM = sb("MM", [64, 4 * 512], BF16)
        CB = [sb(f"CB{i}", [64, 1024], BF16) for i in range(2)]
        AT_ = [sb(f"AaT{i}", [64, 512], BF16) for i in range(2)]
        hbk = sb("hbk", [128, 2 * 256], BF16)
        BhTM = sb("BhTM", [64, 512], BF16); KhTM = sb("KhTM", [64, 512], BF16)
        Vb = sb("Vb", [64, 512], BF16); Vpad = sb("Vpad", [64, 1024], BF16)
        Tf0 = sb("Tf0", [64, 512], BF16)
        Gpad = sb("Gpad", [64, 1024], BF16); Ub = sb("Ub", [64, 512], BF16); Upad = sb("Upad", [64, 1024], BF16)
        stmp = sb("stmp", [128, 512])
        otr = stmp
        rwcp = stmp[:, 0:256]
        wc = sb("wc", [128, 4 * 8])
        rwst = scsb[0:64, 0:512]
        shin = sb("shin", [128, 14 * NSMP]); shout = sb("shout", [128, 14 * NSMP])

        PA = ps("PA", [128, 1024], sub=2); PB = ps("PB", [128, 1024], sub=2); PC = ps("PC", [128, 1024], sub=2)
        PD = ps("PD", [128, 512]); PT = ps("PT", [128, 1024], BF16)

        def C(n, w=1):
            return cst[:, _c[n]:_c[n] + w]

        def PR(l, n, i=0, w=1):
            return prm[:, l * NPRM + _p[n] + i:l * NPRM + _p[n] + i + w]

        def A(x):
            return x[0] if isinstance(x, tuple) else x

        def mm(out, lhsT, rhs, start=True, stop=True, skip=False):
            kw = dict(skip_group_check=True) if skip else {}
            P.op("pe", lambda e: e.matmul(A(out), A(lhsT), A(rhs), start=start, stop=stop, **kw), [lhsT, rhs], [out])

        def tr(out, in_, ident):
            P.op("pe", lambda e: e.transpose(A(out), A(in_), A(ident)), [in_, ident], [out])

        def act(out, in_, func, bias=None, scale=1.0, eng="act"):
            kw = {}
            rd = [in_]
            if bias is not None:
                kw["bias"] = A(bias)
                if not isinstance(bias, float):
                    rd.append(bias)
            if not isinstance(scale, float):
                rd.append(scale)
            P.op(eng, lambda e: e.activation(out=A(out), in_=A(in_), func=func, scale=A(scale), **kw), rd, [out])

        def tt(out, in0, in1, op, eng="dve"):
            P.op(eng, lambda e: e.tensor_tensor(out=A(out), in0=A(in0), in1=A(in1), op=op), [in0, in1], [out])

        def ts(out, in0, s1, op0, s2=None, op1=None, eng="dve"):
            rd = [in0] + [s for s in (s1, s2) if s is not None and not isinstance(s, float)]
            kw = {}
            if op1 is not None:
                kw["op1"] = op1
            P.op(eng, lambda e: e.tensor_scalar(out=A(out), in0=A(in0), scalar1=A(s1),
                                                scalar2=(A(s2) if s2 is not None else None), op0=op0, **kw), rd, [out])

        def stt(out, in0, s, in1, op0, op1, eng="dve"):
            rd = [in0, in1] + ([s] if not isinstance(s, float) else [])
            P.op("dve", lambda e: e.scalar_tensor_tensor(out=A(out), in0=A(in0), scalar=A(s), in1=A(in1), op0=op0, op1=op1),
                 rd, [out])

        def cp(out, in_, eng="dve"):
            if eng == "act":
                P.op("act", lambda e: e.copy(out=A(out), in_=A(in_)), [in_], [out])
            else:
                P.op(eng, lambda e: e.tensor_copy(out=A(out), in_=A(in_)), [in_], [out])

        def recip(out, in_):
            P.op("dve", lambda e: e.reciprocal(out=A(out), in_=A(in_)), [in_], [out])

        def mset(out, val, eng="pool"):
            P.op(eng, lambda e: e.memset(A(out), val), [], [out])

        def scan(out, d0, d1, eng="dve"):
            P.op(eng, lambda e: e.tensor_tensor_scan(out=A(out), data0=A(d0), data1=A(d1), initial=0.0,
                                                     op0=ALU.mult, op1=ALU.add), [d0, d1], [out])

        def dma(out, in_, eng="sp", chan=None, wkey=None, rkey=None, slow=False):
            on = Prog.name(A(out))
            wr = [wkey] if wkey is not None else ([out] if on not in P.readonly else [])
            rd = (list(rkey) if isinstance(rkey, (list, tuple)) else [rkey]) if rkey is not None else [in_]
            ch = chan or (P.keys(wr[0])[0] if wr else Prog.name(A(in_)))
            ch = ch.replace(":", "_")
            kw = dict(allow_slow_non_contiguous=True) if slow else {}
            P.op(eng, lambda e: e.dma_start(out=A(out), in_=A(in_), **kw), rd, wr, dma=ch)

        def r3(ap, a):
            return ap.rearrange("p (a b) -> p a b", a=a)

        def bc_mid(base, n_mid):
            (ps_, rows), (st, w) = base.ap
            return bass.AP(base.tensor, base.offset, [[ps_, rows], [0, n_mid], [st, w]])

        def bc_in(base, w):
            (ps_, rows), (st, n) = base.ap
            return bass.AP(base.tensor, base.offset, [[ps_, rows], [st, n], [0, w]])

        def fa(k, j=0, c0=0, c1=TP):
            return (FA[:, k * 2048 + j * TP + c0:k * 2048 + j * TP + c1], k)

        def ha(k, j=0, c0=0, c1=TP, r0=0, r1=128):
            return (HA[r0:r1, k * 2048 + j * TP + c0:k * 2048 + j * TP + c1], k)

        class WS:
            def __init__(self):
                self.seq = []
                self.issued = 0
                self.used = 0

            def plan(self, l):
                w_in, w_br, w_out, w_fg, w_fu, w_fd, w_pp, w_pg = (wbf[k] for k in
                                                                      ("w_in", "w_br", "w_out", "w_fg", "w_fu", "w_fd", "w_pp", "w_pg"))
                wi = w_in[l]

                def colblk(src, c0, ncol, tag):
                    nk = src.shape[0] // 128
                    self.seq.append((tag, src[:, c0:c0 + ncol].rearrange("(k p) c -> p k c", p=128), nk, ncol))
                for i, tg in enumerate(["aq", "ak", "av", "bq", "bk", "bv", "bg"]):
                    colblk(wi, 512 * i, 512, tg)
                colblk(wi, 5120, 256, "cl")
                colblk(wi, 4096, 512, "ck")
                colblk(wi, 3584, 512, "cr")
                colblk(wi, 4608, 512, "cv")
                for b in range(3):
                    colblk(wi, 5376 + 1024 * b, 512, f"gl{b}a")
                    colblk(wi, 5376 + 1024 * b + 512, 512, f"gl{b}b")
                    colblk(w_br[l, b], 0, 1024, f"br{b}")
                colblk(w_out[l], 0, 512, "outa")
                colblk(w_out[l], 512, 512, "outb")
                for i in range(6):
                    ncol = 512 if i < 5 else 256
                    colblk(w_fg[l], 512 * i, ncol, f"fg{i}")
                    colblk(w_fu[l], 512 * i, ncol, f"fu{i}")
                for j in range(8):
                    colblk(w_fd[l], 128 * j, 128, f"fd{j}")
                colblk(w_pp[l], 0, 1024, "pp")
                colblk(w_pg[l], 0, 512, "pga")
                colblk(w_pg[l], 512, 512, "pgb")

            def _issue(self):
                i = self.issued
                tag, src, nk, ncol = self.seq[i]
                buf = wb[i % NWB]
                assert nk * ncol <= WBN
                dma(r3(buf[:, 0:nk * ncol], nk), src, eng="sp", rkey=["wcast:%d" % i_ for i_ in range(4)])
                self.issued += 1

            def get(self, tag):
                i = self.used
                assert self.seq[i][0] == tag, (self.seq[i][0], tag)
                while self.issued < min(len(self.seq), i + NWB - 1):
                    self._issue()
                self.used += 1
                _, _, nk, ncol = self.seq[i]
                buf = wb[i % NWB]
                return (lambda kc, c0, c1: buf[:, kc * ncol + c0:kc * ncol + c1]), nk, ncol

        W = WS()
        banks = [(PA, 0), (PA, 1), (PB, 0), (PB, 1)]
        bstate = [0]

        def nbank():
            t_, h = banks[bstate[0] % 4]
            bstate[0] += 1
            return (t_[:, h * 512:(h + 1) * 512], h)

        def bk(pb, r0, r1, c0, c1):
            return (pb[0][r0:r1, c0:c1], pb[1])

        def fm_proj(pb, T, wf, nk, c0, rhs_fn):
            for kc in range(nk):
                mm(bk(pb, 0, 128, 0, T), wf(kc, c0, c0 + 128), rhs_fn(kc), start=(kc == 0), stop=(kc == nk - 1))

        identf = C("ident", 128)
        onesf = C("ones", 128)
        blkf = C("blk", 128)

        dma(cst[:], cst_d)
        dma(r3(prm[:], DEPTH), prm_d.rearrange("l p n -> p l n"))
        cp(identb[:], identf, eng="pool")
        cp(onesb[:], onesf, eng="pool")
        cp(blkb[:], blkf, eng="pool")
        for l in range(DEPTH):
            ts(aqs[:, l:l + 1], PR(l, "aqn"), 0.125, ALU.mult)
        ci_ = 0
        for k_, src_ in wsrc.items():
            s2 = src_.flatten_outer_dims() if len(src_.shape) > 2 else src_
            d2 = wbf[k_].flatten_outer_dims() if len(src_.shape) > 2 else wbf[k_]
            rows = s2.shape[0]
            step = 256 if s2.shape[1] >= 4096 else 1024
            for r0 in range(0, rows, step):
                r1_ = min(rows, r0 + step)
                dma(d2[r0:r1_, :], s2[r0:r1_, :], eng="pool", wkey="wcast:%d" % (ci_ % 4), chan="wcast%d" % (ci_ % 4))
                ci_ += 1
        mset(Vwin[:], 1.0)
        for t_ in (Vpad, Gpad, Upad):
            mset(t_[:], 0.0)

        xin2 = [scsb[:, :], (FA[:, 0:1024], 0)]

        def load_x_tm(src, T):
            for ti in range(T // 128):
                xin = xin2[ti % 2]
                dma(xin, src[ti * 128:(ti + 1) * 128, :])
                xk = (lambda a_: (a_, 0)) if ti % 2 else (lambda a_: a_)
                xa = A(xin)
                for half in range(2):
                    pb = nbank()
                    for k4 in range(4):
                        kc = half * 4 + k4
                        tr(bk(pb, 0, 128, k4 * 128, (k4 + 1) * 128), xk(xa[:, kc * 128:(kc + 1) * 128]), identf)
                    dst = r3(xT[:, half * 4 * TP:(half * 4 + 4) * TP], 4)[:, :, ti * 128:(ti + 1) * 128]
                    cp(dst, (r3(pb[0], 4), pb[1]), eng="act" if half else "dve")

        def store_x_tm(dst, T):
            for ti in range(T // 128):
                xin = xin2[ti % 2]
                xk = (lambda a_: (a_, 0)) if ti % 2 else (lambda a_: a_)
                xa = A(xin)
                for half in range(2):
                    pb = nbank()
                    for k4 in range(4):
                        kc = half * 4 + k4
                        tr(bk(pb, 0, 128, k4 * 128, (k4 + 1) * 128), xT[:, kc * TP + ti * 128:kc * TP + (ti + 1) * 128], identf)
                    cp(xk(xa[:, half * 512:(half + 1) * 512]), pb, eng="act" if half else "dve")
                dma(dst[ti * 128:(ti + 1) * 128, :], xin)

        def rms_stats(T, dst, srcs_fn, nk, div, epsname, lhs=None):
            lhs = lhs if lhs is not None else onesb[:]
            for kc in range(nk):
                sq = sqb[kc % 2]
                if kc % 2 == 0:
                    act(sq[:, 0:T], srcs_fn(kc), AF.Square)
                else:
                    tt(sq[:, 0:T], srcs_fn(kc), srcs_fn(kc), ALU.mult)
                mm(PD[:, 0:T], lhs, sq[:, 0:T], start=(kc == 0), stop=(kc == nk - 1))
            act(dst, PD[:, 0:T], AF.Ln, bias=C(epsname), scale=1.0 / div)
            act(dst, dst, AF.Exp, scale=-0.5)

        def norm_to_hT(l, T, gain):
            rms_stats(T, t3[:, 0:T], lambda kc: (xT[:, kc * TP:kc * TP + T], kc), KC, float(D), "eps6")
            for kc in range(KC):
                eng = "dve" if kc % 2 == 0 else "pool"
                if gain is None:
                    tt((hT[:, kc * TP:kc * TP + T], kc), (xT[:, kc * TP:kc * TP + T], kc), t3[:, 0:T], ALU.mult, eng=eng)
                else:
                    stt((hT[:, kc * TP:kc * TP + T], kc), (xT[:, kc * TP:kc * TP + T], kc), PR(l, gain, kc), t3[:, 0:T],
                        ALU.mult, ALU.mult, eng=eng)

        def hrhs(T):
            return lambda kc: (hT[:, kc * TP:kc * TP + T], kc)

        def qk_norm(pb, T, gain_ap, dst_f32):
            cp(t4[:, 0:T], bk(pb, 0, 128, 0, T), eng="act")
            act(sqb[0][:, 0:T], bk(pb, 0, 128, 0, T), AF.Square)
            mm(PD[:, 0:T], blkb[:], sqb[0][:, 0:T])
            act(t6[:, 0:T], PD[:, 0:T], AF.Ln, bias=C("eps6"), scale=1.0 / 64)
            act(t6[:, 0:T], t6[:, 0:T], AF.Exp, scale=-0.5)
            stt(dst_f32, t4[:, 0:T], gain_ap, t6[:, 0:T], ALU.mult, ALU.mult)

        def mixer_A(l, T, G):
            kslot = G["kslot"]
            wf, nk, _ = W.get("aq")
            for j in range(4):
                pb = nbank()
                fm_proj(pb, T, wf, nk, j * 128, hrhs(T))
                qk_norm(pb, T, aqs[:, l:l + 1], ha(0, j, 0, T))
            wf, nk, _ = W.get("ak")
            for j in range(4):
                pb = nbank()
                fm_proj(pb, T, wf, nk, j * 128, hrhs(T))
                qk_norm(pb, T, PR(l, "akn"), fa(0, j, 0, T))
                cp(Kwin[:, j * 1024 + kslot * 512:j * 1024 + kslot * 512 + T], fa(0, j, 0, T), eng="pool")
            for (dst_k, dst_v, t0, L) in G["a_out"]:
                pb = nbank()
                for j in range(4):
                    tr(bk(pb, 0, L, j * 128, (j + 1) * 128), fa(0, j, t0, t0 + L), identf)
                cp(otr[0:L, :], bk(pb, 0, L, 0, 512), eng="act")
                dma(dst_k, otr[0:L, :])
            wf, nk, _ = W.get("av")
            for (t0, L, vtile) in G["vt"]:
                pb = nbank()
                for kc in range(nk):
                    mm(bk(pb, 0, L, 0, 512), hT[:, kc * TP + t0:kc * TP + t0 + L], wf(kc, 0, 512),
                       start=(kc == 0), stop=(kc == nk - 1))
                vdst = Vwin[0:L, vtile * 520:(vtile + 1) * 520].rearrange("p (h d) -> p h d", h=8)[:, :, 0:64]
                cp(vdst, (pb[0][0:L, :].rearrange("p (h d) -> p h d", h=8), pb[1]), eng="act")
                for (dst_k, dst_v, ot0, oL) in G["a_out"]:
                    if ot0 == t0 and oL == L:
                        cp(otr[0:L, :], bk(pb, 0, L, 0, 512), eng="dve")
                        dma(dst_v, otr[0:L, :])
            oaT = oT3[0]
            for sg in G["segs"]:
                if sg.get("cache") is not None:
                    b = sg["cache"]
                    dma((r3(HA[:, 2048:4096], 4), 1), cak[l, b].rearrange("(t p) c -> p t c", p=128), eng="pool")
                    for t_ in range(4):
                        dma(Vwin[:, t_ * 520:(t_ + 1) * 520].rearrange("p (h d) -> p h d", h=8)[:, :, 0:64],
                            cav[l, b, t_ * 128:(t_ + 1) * 128, :].rearrange("p (h d) -> p h d", h=8), eng="pool")
                    for t_ in range(4):
                        for j in range(4):
                            tr(PT[:, j * 128:(j + 1) * 128], (HA[:, 2048 + t_ * 512 + j * 128:2048 + t_ * 512 + (j + 1) * 128], 1), identb[:])
                        dst = r3(Kwin[:, :], 4)[:, :, t_ * 128:(t_ + 1) * 128]
                        cp(dst, r3(PT[:, 0:512], 4), eng="act" if t_ % 2 else "dve")
                for qb in sg["qblocks"]:
                    q0, nq, kts = qb["q0"], qb["nq"], qb["kts"]
                    def scores(ki):
                        kcol, nkk, vtile, bkt = kts[ki]
                        scp = PA if ki % 2 == 0 else PB
                        for bnk in range(2):
                            mm((scp[0:nkk, bnk * 512:(bnk + 1) * 512], bnk), identb[0:nkk, 0:nkk],
                               biasT[0:nkk, bkt * 1024 + bnk * 512:bkt * 1024 + (bnk + 1) * 512], start=True, stop=False, skip=True)
                        for h in range(8):
                            j, b0 = h // 2, (h % 2) * 64
                            sl_ = (h % 2) * 4 + h // 2
                            mm((scp[0:nkk, sl_ * 128:sl_ * 128 + nq], sl_ // 4),
                               Kwin[b0:b0 + 64, j * 1024 + kcol:j * 1024 + kcol + nkk],
                               ha(0, j, q0, q0 + nq, b0, b0 + 64), start=False, stop=True, skip=True)

                    def soft_pv(ki):
                        kcol, nkk, vtile, bkt = kts[ki]
                        scp = PA if ki % 2 == 0 else PB
                        pt_ = ptsb[ki % 2]
                        act(r3(pt_[0:nkk, :], 8)[:, :, 0:nq], r3(scp[0:nkk, :], 8)[:, :, 0:nq], AF.Exp)
                        for h in range(8):
                            half = h // 4
                            oc0 = half * 512 + (h % 4) * 65
                            sl_ = (h % 2) * 4 + h // 2
                            mm((PC[0:nq, oc0:oc0 + 65], half), pt_[0:nkk, sl_ * 128:sl_ * 128 + nq],
                               Vwin[0:nkk, vtile * 520 + h * 65:vtile * 520 + (h + 1) * 65],
                               start=(ki == 0 and h % 4 == 0), stop=(ki == len(kts) - 1), skip=True)
                    scores(0)
                    for ki in range(len(kts)):
                        if ki + 1 < len(kts):
                            scores(ki + 1)
                        soft_pv(ki)
                    for half in range(2):
                        ov = PC[0:nq, half * 512:half * 512 + 260].rearrange("p (h d) -> p h d", h=4)
                        recip(rden[0:nq, half * 4:half * 4 + 4], (ov[:, :, 64], half))
                        tt(oatm[0:nq, half * 256:(half + 1) * 256].rearrange("p (h d) -> p h d", h=4), (ov[:, :, 0:64], half),
                           bc_in(rden[0:nq, half * 4:half * 4 + 4], 64), ALU.mult)
                    for j in range(4):
                        tr(PT[:, j * 128:j * 128 + nq], oatm[0:nq, j * 128:(j + 1) * 128], identb[0:nq, 0:nq])
                    cp(r3(oaT[:, :], 4)[:, :, q0:q0 + nq], r3(PT[:, 0:512], 4)[:, :, 0:nq], eng="act")

        def mixer_B(l, T, G):
            obT = oT3[1]
            for bi, blk in enumerate(("bq", "bk", "bv")):
                wf, nk, _ = W.get(blk)
                tiles = G["tm"]

                def proj(ti_):
                    t0, L, idx, Cc, rp_src = tiles[ti_]
                    pb = nbank()
                    for kc in range(nk):
                        mm(bk(pb, 0, L, 0, 512), hT[:, kc * TP + t0:kc * TP + t0 + L], wf(kc, 0, 512),
                           start=(kc == 0), stop=(kc == nk - 1))
                    return pb

                def rot(ti_, pb):
                    t0, L, idx, Cc, rp_src = tiles[ti_]
                    src = bk(pb, 0, L, 0, 512)
                    if blk == "bv":
                        cp(ha(4, idx, 0, 512, 0, L), src, eng="act")
                        tt((r3(ha(5, idx, 0, 512, 0, L)[0], 4), 5), (r3(src[0], 4), src[1]),
                           bc_in(C("gc%d" % Cc, 4)[0:L, :], 128), ALU.mult)
                        return
                    rp = ropet[0]
                    dma(rp[0:L, :], rp_src)
                    x4 = pb[0][0:L, :].rearrange("p (h d) -> p h d", h=4)
                    x1 = (x4[:, :, 0:64], pb[1]); x2 = (x4[:, :, 64:128], pb[1])
                    cosb = bc_mid(rp[0:L, 0:64], 4)
                    sinb = bc_mid(rp[0:L, 64:128], 4)
                    ta, tb_ = (t4, t5) if ti_ % 2 == 0 else (t1, t2)
                    a1 = r3(ta[0:L, 0:256], 4); a2 = r3(ta[0:L, 256:512], 4)
                    a3 = r3(tb_[0:L, 0:256], 4); a4 = r3(tb_[0:L, 256:512], 4)
                    o4 = r3(t6[0:L, 0:512], 4)
                    tt(a1, x1, cosb, ALU.mult)
                    tt(a2, x2, sinb, ALU.mult)
                    tt(a3, x1, sinb, ALU.mult)
                    tt(a4, x2, cosb, ALU.mult)
                    tt(o4[:, :, 0:64], a1, a2, ALU.subtract, eng="pool")
                    tt(o4[:, :, 64:128], a3, a4, ALU.add, eng="pool")
                    slot = 2 if blk == "bq" else 3
                    tname = ("tq%d" if blk == "bq" else "tk%d") % Cc
                    dst = ha(slot, idx, 0, 512, 0, L)
                    tt((r3(dst[0], 4), slot), o4, bc_in(C(tname, 4)[0:L, :], 128), ALU.mult)

                def trans(ti_):
                    t0, L, idx, Cc, rp_src = tiles[ti_]
                    if blk == "bv":
                        return
                    slot = 2 if blk == "bq" else 3
                    for h in range(4):
                        tr(PT[:, h * 128:h * 128 + L], ha(slot, idx, h * 128, (h + 1) * 128, 0, L), identb[0:L, 0:L])
                    fslot = 0 if blk == "bq" else 1
                    cp((r3(HA[:, fslot * 2048:(fslot + 1) * 2048], 4)[:, :, t0:t0 + L], fslot), r3(PT[:, 0:512], 4)[:, :, 0:L],
                       eng="act")
                pbs = {0: proj(0)}
                for ti_ in range(len(tiles)):
                    if ti_ + 1 < len(tiles):
                        pbs[ti_ + 1] = proj(ti_ + 1)
                    rot(ti_, pbs.pop(ti_))
                    trans(ti_)
                chk("B_" + blk)
            wf, nk, _ = W.get("bg")
            for h in range(4):
                pb = nbank()
                fm_proj(pb, T, wf, nk, h * 128, hrhs(T))
                act(fa(0, h, 0, T), bk(pb, 0, 128, 0, T), AF.Silu)
            chk("B_bg")
            for sg in G["segs"]:
                if sg["state_in"] == "zero":
                    mset(Sret[:], 0.0)
                    mset(Sretb[:], 0.0)
                elif sg["state_in"] is not None:
                    dma(r3(Sret[:], 4), sret[l, sg["state_in"]].rearrange("h d e -> d h e"))
                    cp(Sretb[:], Sret[:], eng="act")
                for (t0, L, idx, Cc, _rs) in sg["tm"]:
                    for h in range(4):
                        mm((PC[0:L, h * 128:h * 128 + L], 0), ha(1, h, t0, t0 + L), ha(0, h, t0, t0 + L))
                    tt(r3(scs[0:L, :], 4)[:, :, 0:L], (r3(PC[0:L, 0:512], 4)[:, :, 0:L], 0),
                       bc_mid(C("m01", 128)[0:L, 0:L], 4), ALU.mult)
                    for h in range(4):
                        mm((PC[:, 512 + h * 128:512 + h * 128 + L], 1), ha(4, idx, h * 128, (h + 1) * 128, 0, L),
                           scs[0:L, h * 128:h * 128 + L], start=True, stop=False)
                        mm((PC[:, 512 + h * 128:512 + h * 128 + L], 1), Sretb[:, h * 128:(h + 1) * 128],
                           ha(0, h, t0, t0 + L), start=False, stop=True)
                    cp((r3(FA[:, 2048:4096], 4)[:, :, t0:t0 + L], 1), (r3(PC[:, 512:1024], 4)[:, :, 0:L], 1), eng="act")
                    for h in range(4):
                        mm(PD[:, h * 128:(h + 1) * 128], ha(3, idx, h * 128, (h + 1) * 128, 0, L),
                           ha(5, idx, h * 128, (h + 1) * 128, 0, L))
                    for h in range(4):
                        stt(Sret[:, h * 128:(h + 1) * 128], Sret[:, h * 128:(h + 1) * 128],
                            C("gc%d" % Cc, 4)[:, h:h + 1], PD[:, h * 128:(h + 1) * 128], ALU.mult, ALU.add)
                    cp(Sretb[:], Sret[:], eng="act")
                if sg["ret_out"] is not None:
                    dma(sg["ret_out"].rearrange("h d e -> d h e"), r3(Sret[:], 4))
            chk("B_chunks")
            for h in range(4):
                rms_stats(T, t3[:, 0:T], lambda kc, h=h: fa(1, h, 0, T), 1, 128.0, "eps6")
                tt(t4[:, 0:T], fa(1, h, 0, T), t3[:, 0:T], ALU.mult)
                tt(obT[:, h * TP:h * TP + T], t4[:, 0:T], fa(0, h, 0, T), ALU.mult, eng="pool")

        def mixer_C(l, T, G):
            ocT = oT3[2]
            nseg, Ls = G["nseg"], G["Ls"]
            Cc = G["Cc"]
            cmn = "cm%d" % Cc
            nch_seg = Ls // Cc
            nch = T // Cc

            def shifted(pb, c, dst):
                r_seg = t1[:, 0:nseg * (Ls + 1)].rearrange("p (s u) -> p s u", s=nseg)
                cp(r_seg[:, :, 1:Ls + 1], (pb[0][:, 0:T].rearrange("p (s u) -> p s u", s=nseg), pb[1]), eng="act")
                cp(r_seg[:, :, 0:1], G["prevcol"](c), eng="pool")
                for fn in G["savecol"]:
                    fn(c, r_seg)
                d_seg = t2[:, 0:T].rearrange("p (s u) -> p s u", s=nseg)
                tt(d_seg, r_seg[:, :, 0:Ls], r_seg[:, :, 1:Ls + 1], ALU.subtract)
                dst_seg = dst.rearrange("p (s u) -> p s u", s=nseg)
                stt(dst_seg, d_seg, PR(l, "mu", c), r_seg[:, :, 1:Ls + 1], ALU.mult, ALU.add)

            wf, nk, _ = W.get("cl")
            pb = nbank()
            fm_proj(pb, T, wf, nk, 0, hrhs(T))
            shifted(pb, 12, t3[:, 0:T])
            act(lob[0:64, 0:T], t3[0:64, 0:T], AF.Tanh)
            cp(lob[64:128, 0:T], t3[64:128, 0:T], eng="act")
            pb = nbank()
            fm_proj(pb, T, wf, nk, 128, hrhs(T))
            shifted(pb, 13, t3[:, 0:T])
            act(sgl[:, 0:T], t3[:, 0:T], AF.Sigmoid)
            for j in range(4):
                pb = nbank()
                mm(bk(pb, 0, 128, 0, T), cwa2b[0:64, j * 128:(j + 1) * 128], lob[0:64, 0:T])
                act(fa(0, j, 0, T), bk(pb, 0, 128, 0, T), AF.Sigmoid, bias=PR(l, "w0", j))
                scan(fa(1, j, 0, T), C(cmn, T), fa(0, j, 0, T))
                pb = nbank()
                mm(bk(pb, 0, 128, 0, T), cwa2b[64:128, j * 128:(j + 1) * 128], lob[64:128, 0:T])
                act(fa(2, j, 0, T), bk(pb, 0, 128, 0, T), AF.Sigmoid, bias=PR(l, "a0", j))
                pb = nbank()
                mm(bk(pb, 0, 128, 0, T), cg2b[:, j * 128:(j + 1) * 128], sgl[:, 0:T])
                cp(ha(5, j, 0, T), bk(pb, 0, 128, 0, T), eng="act")
            for j in range(4):
                csg_j = FA[:, 2048 + j * TP:2048 + j * TP + T]
                act(wc[:, j * 8:j * 8 + nch], (csg_j[:, Cc - 1:T:Cc], 1), AF.Exp, scale=-CDEC)
            wf, nk, _ = W.get("ck")

            def _proj(j_, wf=wf, nk=nk):
                pb_ = nbank()
                fm_proj(pb_, T, wf, nk, j_ * 128, hrhs(T))
                return pb_
            _pbs = {0: _proj(0)}
            for j in range(4):
                if j + 1 < 4:
                    _pbs[j + 1] = _proj(j + 1)
                pb = _pbs.pop(j)
                shifted(pb, 4 + j, t3[:, 0:T])
                ts(t4[:, 0:T], t3[:, 0:T], PR(l, "kk", j), ALU.mult)
                act(sqb[0][:, 0:T], t4[:, 0:T], AF.Square)
                mm(PD[:, 0:T], blkb[:], sqb[0][:, 0:T])
                ts(t5[:, 0:T], PD[:, 0:T], 1e-24, ALU.max)
                act(t5[:, 0:T], t5[:, 0:T], AF.Ln)
                act(t5[:, 0:T], t5[:, 0:T], AF.Exp, scale=-0.5)
                tt(t4[:, 0:T], t4[:, 0:T], t5[:, 0:T], ALU.mult)
                tt(t5[:, 0:T], fa(1, j, 0, T), fa(0, j, 0, T), ALU.subtract)
                act(t5[:, 0:T], t5[:, 0:T], AF.Exp, scale=-CDEC)
                stt(ha(0, j, 0, T), t4[:, 0:T], C("negone"), t5[:, 0:T], ALU.mult, ALU.mult)
                act(t5[:, 0:T], fa(1, j, 0, T), AF.Exp, scale=CDEC)
                tt(t4[:, 0:T], t4[:, 0:T], fa(2, j, 0, T), ALU.mult)
                tt(ha(2, j, 0, T), t4[:, 0:T], t5[:, 0:T], ALU.mult)
                ts(t4[:, 0:T], fa(2, j, 0, T), C("negone"), ALU.add, PR(l, "ka", j), ALU.mult)
                stt(fa(3, j, 0, T), t4[:, 0:T], C("one"), t3[:, 0:T], ALU.add, ALU.mult)
                tt(ha(3, j, 0, T), fa(3, j, 0, T), t5[:, 0:T], ALU.mult)
            wf, nk, _ = W.get("cr")

            def _proj(j_, wf=wf, nk=nk):
                pb_ = nbank()
                fm_proj(pb_, T, wf, nk, j_ * 128, hrhs(T))
                return pb_
            _pbs = {0: _proj(0)}
            for j in range(4):
                if j + 1 < 4:
                    _pbs[j + 1] = _proj(j + 1)
                pb = _pbs.pop(j)
                shifted(pb, j, t3[:, 0:T])
                act(t5[:, 0:T], fa(1, j, 0, T), AF.Exp, scale=-CDEC)
                tt(ha(1, j, 0, T), t3[:, 0:T], t5[:, 0:T], ALU.mult)
                stt(sqb[1][:, 0:T], t3[:, 0:T], PR(l, "rk", j), fa(3, j, 0, T), ALU.mult, ALU.mult)
                mm(PD[:, 0:T], blkb[:], sqb[1][:, 0:T])
                cp(fa(3, j, 0, T), PD[:, 0:T], eng="act")
            wf, nk, _ = W.get("cv")

            def _proj(j_, wf=wf, nk=nk):
                pb_ = nbank()
                fm_proj(pb_, T, wf, nk, j_ * 128, hrhs(T))
                return pb_
            _pbs = {0: _proj(0)}
            for j in range(4):
                if j + 1 < 4:
                    _pbs[j + 1] = _proj(j + 1)
                pb = _pbs.pop(j)
                shifted(pb, 8 + j, t3[:, 0:T])
                cp(ha(4, j, 0, T), t3[:, 0:T], eng="act")
                tt(fa(3, j, 0, T), fa(3, j, 0, T), t3[:, 0:T], ALU.mult)
            chk("C_prep")
            nst = {64: 6, 32: 5}[Cc]
            mskv = C("msk", 256)[0:Cc, :]
            X1 = FA[:, 2048:4096].bitcast(BF16)
            X2 = FA[:, 4096:6144].bitcast(BF16)
            sets = [dict(MM=(MM[:, :], None), Bh=(BhTM[:, :], None), Kh=(KhTM[:, :], None), Vb=(Vb[:, :], None),
                         Vp=(Vpad[:, :], None), Tf=(Tf0[:, :], None)),
                    dict(MM=(X1[0:64, 0:2048], 1), Bh=(X1[0:64, 2048:2560], 1), Kh=(X1[0:64, 2560:3072], 1),
                         Vb=(X1[0:64, 3072:3584], 1), Vp=(X2[0:64, 0:1024], 2), Tf=(X2[0:64, 1024:1536], 2))]

            def V(buf, r0, r1, c0, c1):
                ap, key = buf
                v = ap[r0:r1, c0:c1]
                return v if key is None else (v, key)

            def K_(buf, ap):
                return ap if buf[1] is None else (ap, buf[1])
            mset(V(sets[1]["Vp"], 0, 64, 0, 1024), 0.0)

            def front(t0, ci, S_):
                MMb, Bhb, Khb, Vbb, Vpb, Tfb = S_["MM"], S_["Bh"], S_["Kh"], S_["Vb"], S_["Vp"], S_["Tf"]
                wc0 = wc[:, ci:ci + 1]
                wcb = bass.AP(wc0.tensor, wc0.offset, [[wc0.ap[0][0], 128], [8, 4], [0, Cc]])
                for hi_, (src_slot, dstb) in enumerate(((2, Bhb), (3, Khb))):
                    hv = hbk[:, hi_ * 256:(hi_ + 1) * 256].rearrange("p (q c) -> p q c", q=4)[:, :, 0:Cc]
                    tt(hv, (r3(HA[:, src_slot * 2048:(src_slot + 1) * 2048], 4)[:, :, t0:t0 + Cc], src_slot), wcb, ALU.mult)
                    for q in range(4):
                        tr(PT[0:Cc, q * 128:(q + 1) * 128], hbk[:, hi_ * 256 + q * 64:hi_ * 256 + q * 64 + Cc], identb[:])
                    cp(V(dstb, 0, Cc, 0, 512), PT[0:Cc, 0:512], eng="act")
                    yield
                for q in range(4):
                    tr(PT[0:Cc, q * 128:(q + 1) * 128], ha(4, q, t0, t0 + Cc), identb[:])
                cp(V(Vbb, 0, Cc, 0, 512), PT[0:Cc, 0:512], eng="act")
                for hh in range(2):
                    cp(K_(Vpb, Vpb[0][0:Cc, :].rearrange("p (q h c) -> p q h c", q=4, h=2)[:, :, hh, hh * 64:(hh + 1) * 64]),
                       r3(PT[0:Cc, 0:512], 4)[:, :, hh * 64:(hh + 1) * 64], eng="dve")
                yield
                for q in range(4):
                    for hh in range(2):
                        mt = PA if hh == 0 else PC
                        mh = q // 2
                        b0 = hh * 64
                        a_ap = HA[b0:b0 + 64, 0 * 2048 + q * TP + t0:0 * 2048 + q * TP + t0 + Cc]
                        (pst, _), (st, _) = a_ap.ap
                        rhs2 = bass.AP(a_ap.tensor, a_ap.offset, [[pst, 64], [2048, 2], [st, Cc]])
                        base = mh * 512 + (q % 2) * 256
                        o13 = mt[0:Cc, base:base + 128].rearrange("p (a b) -> p a b", a=2)[:, :, 0:Cc]
                        o24 = mt[0:Cc, base + 128:base + 256].rearrange("p (a b) -> p a b", a=2)[:, :, 0:Cc]
                        P.op("pe", lambda e, o=o13, lh=ha(2, q, t0, t0 + Cc, b0, b0 + 64)[0], rh=rhs2:
                             e.matmul(o, lh, rh, start=True, stop=True), [(HA, 2), (HA, 0), (HA, 1)], [(mt, mh)])
                        P.op("pe", lambda e, o=o24, lh=ha(3, q, t0, t0 + Cc, b0, b0 + 64)[0], rh=rhs2:
                             e.matmul(o, lh, rh, start=True, stop=True), [(HA, 3), (HA, 0), (HA, 1)], [(mt, mh)])
                mm_base = MMb[0][0:Cc, 0:64]
                mm_ps = mm_base.ap[0][0]
                for hh in range(2):
                    mt = PA if hh == 0 else PC
                    for mh in range(2):
                        mview = mt[0:Cc, mh * 512:(mh + 1) * 512].rearrange("p (q m t) -> p q m t", q=2, m=4)[:, :, :, 0:Cc]
                        mskb = bass.AP(mskv.tensor, mskv.offset, [[mskv.ap[0][0], Cc], [0, 2], [64, 4], [1, Cc]])
                        dstv = bass.AP(mm_base.tensor, mm_base.offset + (2 * mh) * 512 + hh * 256,
                                       [[mm_ps, Cc], [512, 2], [64, 4], [1, Cc]])
                        tt(K_(MMb, dstv), (mview, mh), mskb, ALU.mult)
                yield
                for h in range(8):
                    q, b0 = h // 2, (h % 2) * 64
                    bnk = PA if h % 2 == 0 else PC
                    mm((bnk[0:Cc, q * 64:q * 64 + Cc], 0),
                       ha(0, q, t0, t0 + Cc, b0, b0 + 64), ha(2, q, t0, t0 + Cc, b0, b0 + 64))
                at0 = AT_[0][0:Cc, 0:64]
                for hh in range(2):
                    bnk = PA if hh == 0 else PC
                    dstv = bass.AP(at0.tensor, at0.offset + hh * 64, [[at0.ap[0][0], Cc], [128, 4], [1, Cc]])
                    tt(dstv, (r3(bnk[0:Cc, 0:256], 4)[:, :, 0:Cc], 0),
                       bc_mid(C("lsm", 64)[0:Cc, 0:Cc], 4), ALU.mult)
                n1v = K_(MMb, MMb[0][0:Cc, :].rearrange("p (h m t) -> p h m t", h=8, m=4)[:, :, 0, 0:Cc])

                def cbv(p_, part, h0=0, h1=8):
                    return CB[p_][0:Cc, h0 * 128:h1 * 128].rearrange("p (h c) -> p h c", h=h1 - h0)[:, :, part * 64:part * 64 + Cc]

                def pcv(part):
                    return PC[0:Cc, :].rearrange("p (h c) -> p h c", h=8)[:, :, part * 64:part * 64 + Cc]
                cp(cbv(0, 0), n1v, eng="act")
                tt(cbv(1, 1), n1v, bc_mid(identf[0:Cc, 0:Cc], 8), ALU.add)
                yield
                cur = 0
                for r in range(1, nst + 1):
                    nxt = 1 - cur
                    p_, n_ = (r - 1) % 2, r % 2
                    for h in range(8):
                        lh = AT_[cur][0:Cc, h * 64:h * 64 + Cc]
                        if r == 1:
                            mm((PC[0:Cc, h * 128:h * 128 + Cc], h // 4), lh, CB[p_][0:Cc, h * 128:h * 128 + Cc])
                        elif r <= nst - 1:
                            mm((PC[0:Cc, h * 128:(h + 1) * 128].rearrange("p (a c) -> p a c", a=2)[:, :, 0:Cc], h // 4), lh,
                               CB[p_][0:Cc, h * 128:(h + 1) * 128].rearrange("p (a c) -> p a c", a=2)[:, :, 0:Cc])
                        else:
                            mm((PC[0:Cc, h * 128 + 64:h * 128 + 64 + Cc], h // 4), lh, CB[p_][0:Cc, h * 128 + 64:h * 128 + 64 + Cc])
                    if r <= nst - 1:
                        for h in range(8):
                            mm((PA[0:Cc, 512 + h * 64:512 + h * 64 + Cc], 1), CB[p_][0:Cc, h * 128:h * 128 + Cc],
                               AT_[cur][0:Cc, h * 64:h * 64 + Cc])
                    if r >= 2:
                        dst_t = cbv(n_, 1) if r < nst else K_(Tfb, r3(Tfb[0][0:Cc, :], 8)[:, :, 0:Cc])
                        tt(dst_t, cbv(p_, 1), pcv(1), ALU.add)
                    if r <= nst - 1:
                        cp(cbv(n_, 0), pcv(0), eng="act")
                        cp(r3(AT_[nxt][0:Cc, :], 8)[:, :, 0:Cc], (r3(PA[0:Cc, 512:1024], 8)[:, :, 0:Cc], 1), eng="act")
                        cur = nxt
                    yield

            def chain(t0, ci, S_):
                MMb, Bhb, Khb, Vbb, Vpb, Tfb = S_["MM"], S_["Bh"], S_["Kh"], S_["Vb"], S_["Vp"], S_["Tf"]
                for q in range(4):
                    mm((PB[0:Cc, q * 128:(q + 1) * 128], 0), ha(0, q, t0, t0 + Cc), Srwb[:, q * 128:(q + 1) * 128],
                       start=True, stop=False)
                    for hh in range(2):
                        mm((PB[0:Cc, q * 128:(q + 1) * 128], 0),
                           V(MMb, 0, Cc, q * 512 + hh * 256 + 128, q * 512 + hh * 256 + 128 + Cc),
                           V(Vpb, 0, Cc, (q * 2 + hh) * 128, (q * 2 + hh + 1) * 128), start=False, stop=(hh == 1))
                for hh in range(2):
                    cp(Gpad[0:Cc, :].rearrange("p (q h c) -> p q h c", q=4, h=2)[:, :, hh, hh * 64:(hh + 1) * 64],
                       (r3(PB[0:Cc, 0:512], 4)[:, :, hh * 64:(hh + 1) * 64], 0), eng="act" if hh else "dve")
                yield
                for q in range(4):
                    for hh in range(2):
                        h = q * 2 + hh
                        mm((PB[0:Cc, 512 + q * 128:512 + (q + 1) * 128], 1), V(Tfb, 0, Cc, h * 64, h * 64 + Cc),
                           Gpad[0:Cc, h * 128:(h + 1) * 128], start=(hh == 0), stop=(hh == 1))
                cp(Ub[0:Cc, :], (PB[0:Cc, 512:1024], 1), eng="act")
                for hh in range(2):
                    cp(Upad[0:Cc, :].rearrange("p (q h c) -> p q h c", q=4, h=2)[:, :, hh, hh * 64:(hh + 1) * 64],
                       (r3(PB[0:Cc, 512:1024], 4)[:, :, hh * 64:(hh + 1) * 64], 1), eng="dve")
                yield
                for q in range(4):
                    oy = PD[:, q * 64:q * 64 + Cc]
                    mm(oy, Srwb[:, q * 128:(q + 1) * 128], ha(1, q, t0, t0 + Cc), start=True, stop=False)
                    for hh in range(2):
                        h = q * 2 + hh
                        mm(oy, Upad[0:Cc, h * 128:(h + 1) * 128],
                           V(MMb, 0, Cc, q * 512 + hh * 256 + 64, q * 512 + hh * 256 + 64 + Cc), start=False, stop=False)
                        mm(oy, V(Vpb, 0, Cc, h * 128, (h + 1) * 128),
                           V(MMb, 0, Cc, q * 512 + hh * 256 + 192, q * 512 + hh * 256 + 192 + Cc), start=False, stop=(hh == 1))
                cp((r3(FA[:, 0:2048], 4)[:, :, t0:t0 + Cc], 0), r3(PD[:, 0:256], 4)[:, :, 0:Cc], eng="act")
                yield
                for q in range(4):
                    mm(PD[:, q * 128:(q + 1) * 128], V(Bhb, 0, Cc, q * 128, (q + 1) * 128), Ub[0:Cc, q * 128:(q + 1) * 128],
                       start=True, stop=False)
                    mm(PD[:, q * 128:(q + 1) * 128], V(Khb, 0, Cc, q * 128, (q + 1) * 128), V(Vbb, 0, Cc, q * 128, (q + 1) * 128),
                       start=False, stop=True)
                tt(r3(stmp[:], 4), r3(PD[:], 4), bc_mid(blkf, 4), ALU.mult)
                for q in range(4):
                    stt(Srw[:, q * 128:(q + 1) * 128], Srw[:, q * 128:(q + 1) * 128], wc[:, q * 8 + ci:q * 8 + ci + 1],
                        stmp[:, q * 128:(q + 1) * 128], ALU.mult, ALU.add)
                cp(Srwb[:], Srw[:], eng="act")
                yield

            def state_in(sg):
                if sg["rw_in"] == "zero":
                    mset(Srw[:], 0.0)
                    mset(Srwb[:], 0.0)
                elif sg["rw_in"] is not None:
                    mset(Srw[:], 0.0)
                    dma(r3(rwst, 8), srw[l, sg["rw_in"]].rearrange("h i j -> i h j"))
                    for q in range(4):
                        tr(PD[:, q * 64:(q + 1) * 64], rwst[:, q * 128:(q + 1) * 128], identf[0:64, 0:64])
                    for hh in range(2):
                        cp(r3(Srw[hh * 64:(hh + 1) * 64, :], 4)[:, :, hh * 64:(hh + 1) * 64],
                           r3(PD[hh * 64:(hh + 1) * 64, 0:256], 4), eng="act")
                    cp(Srwb[:], Srw[:], eng="act")

            def state_out(sg):
                if sg["rw_out"] is not None:
                    for hh in range(2):
                        cp(r3(rwcp[hh * 64:(hh + 1) * 64, :], 4), r3(Srw[hh * 64:(hh + 1) * 64, :], 4)[:, :, hh * 64:(hh + 1) * 64],
                           eng="pool")
                    for q in range(4):
                        tr(PD[0:64, q * 128:(q + 1) * 128], rwcp[:, q * 64:(q + 1) * 64], identf)
                    cp(rwst, PD[0:64, :], eng="act")
                    dma(sg["rw_out"].rearrange("h i j -> i h j"), r3(rwst, 8))

            items = [(sg, t0, ci, k == 0, k == len(sg["chunks"]) - 1) for sg in G["segs"]
                     for k, (t0, ci) in enumerate(sg["chunks"])]
            for _ in front(items[0][1], items[0][2], sets[0]):
                pass
            for n_, (sg, t0, ci, first, last_) in enumerate(items):
                gf = front(items[n_ + 1][1], items[n_ + 1][2], sets[(n_ + 1) % 2]) if n_ + 1 < len(items) else iter(())
                if first:
                    state_in(sg)
                gc = chain(t0, ci, sets[n_ % 2])
                alive_f = alive_c = True
                step = 0
                while alive_f or alive_c:
                    if alive_f:
                        alive_f = next(gf, "END") != "END"
                    if alive_c and (step % 2 == 1 or not alive_f):
                        alive_c = next(gc, "END") != "END"
                    step += 1
                if last_:
                    state_out(sg)
            chk("C_chunks")
            for q in range(4):
                cp(sqb[0][:, 0:T], fa(0, q, 0, T), eng="pool")
                mm(PD[:, 0:T], blkb[:], sqb[0][:, 0:T])
                stt(t3[:, 0:T], PD[:, 0:T], -1.0 / 64, fa(0, q, 0, T), ALU.mult, ALU.add)
                act(sqb[1][:, 0:T], t3[:, 0:T], AF.Square)
                mm(PD[:, 0:T], blkb[:], sqb[1][:, 0:T])
                act(t4[:, 0:T], PD[:, 0:T], AF.Ln, bias=C("gneps"), scale=1.0 / 64)
                act(t4[:, 0:T], t4[:, 0:T], AF.Exp, scale=-0.5)
                tt(t3[:, 0:T], t3[:, 0:T], t4[:, 0:T], ALU.mult)
                ts(t3[:, 0:T], t3[:, 0:T], PR(l, "lnw", q), ALU.mult, PR(l, "lnb", q), ALU.add)
                tt(t3[:, 0:T], t3[:, 0:T], fa(3, q, 0, T), ALU.add)
                tt(ocT[:, q * TP:q * TP + T], t3[:, 0:T], ha(5, q, 0, T), ALU.mult)

        def phase_G(l, T):
            for b in range(3):
                for half, tg in enumerate((f"gl{b}a", f"gl{b}b")):
                    wf, nk, _ = W.get(tg)
                    for j in range(4):
                        c = half * 4 + j
                        pb = nbank()
                        fm_proj(pb, T, wf, nk, j * 128, hrhs(T))
                        act(ha(c // 4, c % 4, 0, T), bk(pb, 0, 128, 0, T), AF.Sigmoid)
                wf, nk, _ = W.get(f"br{b}")
                for c in range(8):
                    pb = nbank()
                    fm_proj(pb, T, wf, nk, c * 128, lambda kc: oT3[b][:, kc * TP:kc * TP + T])
                    if b == 0:
                        tt(fa(c // 4, c % 4, 0, T), bk(pb, 0, 128, 0, T), ha(c // 4, c % 4, 0, T), ALU.mult)
                    else:
                        tt(t4[:, 0:T], bk(pb, 0, 128, 0, T), ha(c // 4, c % 4, 0, T), ALU.mult)
                        tt(fa(c // 4, c % 4, 0, T), fa(c // 4, c % 4, 0, T), t4[:, 0:T], ALU.add, eng="pool")
            for c in range(8):
                cp(ha(2 + c // 4, c % 4, 0, T), fa(c // 4, c % 4, 0, T), eng="act" if c % 2 else "pool")
            for half, tg in enumerate(("outa", "outb")):
                wf, nk, _ = W.get(tg)
                for j in range(4):
                    c = half * 4 + j
                    pb = nbank()
                    fm_proj(pb, T, wf, nk, j * 128, lambda kc: ha(2 + kc // 4, kc % 4, 0, T))
                    tt((xT[:, c * TP:c * TP + T], c), (xT[:, c * TP:c * TP + T], c), bk(pb, 0, 128, 0, T), ALU.add)

        def phase_F(l, T):
            norm_to_hT(l, T, "nffn")
            for i in range(6):
                wg, nk, ncol = W.get(f"fg{i}")
                wu, _, _ = W.get(f"fu{i}")
                for j in range(ncol // 128):
                    c = i * 4 + j
                    pg = nbank()
                    fm_proj(pg, T, wg, nk, j * 128, hrhs(T))
                    pu = nbank()
                    fm_proj(pu, T, wu, nk, j * 128, hrhs(T))
                    act(t4[:, 0:T], bk(pg, 0, 128, 0, T), AF.Silu)
                    tt(ha(c // 4, c % 4, 0, T), t4[:, 0:T], bk(pu, 0, 128, 0, T), ALU.mult)
            for j in range(8):
                wf, nk, _ = W.get(f"fd{j}")
                pb = nbank()
                fm_proj(pb, T, wf, nk, 0, lambda kc: ha(kc // 4, kc % 4, 0, T))
                tt((xT[:, j * TP:j * TP + T], j), (xT[:, j * TP:j * TP + T], j), bk(pb, 0, 128, 0, T), ALU.add)

        def phase_P(l, T, psrc):
            for ti in range((T + 127) // 128):
                dma(xin[:, 0:PLE], psrc[ti * 128:(ti + 1) * 128, :])
                pb = nbank()
                for k in range(2):
                    tr(bk(pb, 0, 128, k * 128, (k + 1) * 128), xin[:, k * 128:(k + 1) * 128], identf)
                cp((r3(HA[:, 4 * 2048:4 * 2048 + 2 * TP], 2)[:, :, ti * 128:(ti + 1) * 128], 4), (r3(pb[0][:, 0:256], 2), pb[1]),
                   eng="act")
            wf, nk, _ = W.get("pp")
            for c in range(8):
                pb = nbank()
                fm_proj(pb, T, wf, nk, c * 128, lambda kc: ha(4, kc, 0, T))
                cp(fa(c // 4, c % 4, 0, T), bk(pb, 0, 128, 0, T), eng="act")
            rms_stats(T, t3[:, 0:T], lambda kc: fa(kc // 4, kc % 4, 0, T), 8, float(D), "eps6")
            for c in range(8):
                stt(fa(c // 4, c % 4, 0, T), fa(c // 4, c % 4, 0, T), PR(l, "nple", c), t3[:, 0:T], ALU.mult, ALU.mult,
                    eng="dve" if c % 2 else "pool")
            norm_to_hT(l, T, None)
            for half, tg in enumerate(("pga", "pgb")):
                wf, nk, _ = W.get(tg)
                for j in range(4):
                    c = half * 4 + j
                    pb = nbank()
                    fm_proj(pb, T, wf, nk, j * 128, hrhs(T))
                    tg = t4 if c % 2 == 0 else t5
                    act(tg[:, 0:T], bk(pb, 0, 128, 0, T), AF.Sigmoid)
                    tt(tg[:, 0:T], tg[:, 0:T], fa(c // 4, c % 4, 0, T), ALU.mult)
                    tt((xT[:, c * TP:c * TP + T], c), (xT[:, c * TP:c * TP + T], c), tg[:, 0:T], ALU.add,
                       eng="dve" if c % 2 == 0 else "pool")

        KSTOP = _os.environ.get("KSTOP", "")

        class _Stop(Exception):
            pass

        def chk(tag):
            if KSTOP == tag:
                raise _Stop()

        def layer(l, T, G):
            chk("load")
            norm_to_hT(l, T, "nmix")
            chk("norm")
            mixer_A(l, T, G)
            chk("A")
            mixer_B(l, T, G)
            chk("B")
            mixer_C(l, T, G)
            chk("C")
            if dbg and G.get("dbg"):
                for b, nm in enumerate(("d_oa", "d_ob", "d_oc")):
                    dma(dbg_t[nm], oT3[b][:], eng="pool")
            phase_G(l, T)
            chk("G")
            phase_F(l, T)
            chk("F")
            phase_P(l, T, G["psrc"])
            chk("P")

        def x_in(l, src_tm, src_fm, T, key):
            if l == 0:
                load_x_tm(src_tm, T)
            else:
                dma(r3(xT[:], KC)[:, :, 0:T], src_fm.rearrange("k p t -> p k t"), rkey=key)

        def x_out(l, dst_tm, dst_fm, T, key):
            if l == DEPTH - 1:
                store_x_tm(dst_tm, T)
            else:
                dma(dst_fm.rearrange("k p t -> p k t"), r3(xT[:], KC)[:, :, 0:T], wkey=key)

        for l in range(DEPTH):
            for g in range(NG + 1):
                W.plan(l)

        def _main():
          for l in range(DEPTH):
              dma(cwa2b[:], cwa2[l], eng="pool")
              dma(cg2b[:], cg2[l], eng="pool")
              dma(biasT[:], bias_d[l], eng="pool")
              mset(shcol[:], 0.0)
              for g in range(NG):
                  kslot = g % 2
                  last = (g == NG - 1)
                  T = TP
                  x_in(l, xp[g * TP:(g + 1) * TP, :], x1p[:, :, g * TP:(g + 1) * TP], T, "x1p:%d" % (g % 4))
                  qbs = []
                  for m_ in range(4):
                      kts = []
                      for kt in range(5):
                          wcd = 128 * (m_ + kt)
                          if wcd < 512:
                              if g == 0:
                                  continue
                              slot, off = 1 - kslot, wcd
                          else:
                              slot, off = kslot, wcd - 512
                          kts.append((slot * 512 + off, 128, slot * 4 + off // 128, kt))
                      qbs.append(dict(q0=128 * m_, nq=128, kts=kts))
                  tmt = [(ti * 128, 128, ti, 128, rope_p[g * TP + ti * 128:g * TP + (ti + 1) * 128, :]) for ti in range(4)]

                  def prevcol(c):
                      return shcol[:, c:c + 1].rearrange("p (s u) -> p s u", s=1)

                  def savecol(c, r_seg):
                      cp(shcol[:, c:c + 1], r_seg[:, 0, TP:TP + 1], eng="pool")
                  G = dict(kslot=kslot, vt=[(ti * 128, 128, kslot * 4 + ti) for ti in range(4)],
                           a_out=([(akp[l, ti * 128:(ti + 1) * 128, :], avp[l, ti * 128:(ti + 1) * 128, :], ti * 128, 128)
                                   for ti in range(4)] if last else []),
                           tm=tmt,
                           segs=[dict(qblocks=qbs, tm=tmt, state_in=("zero" if g == 0 else None),
                                      ret_out=(retp[l] if last else None),
                                      rw_in=("zero" if g == 0 else None), rw_out=(rwp[l] if last else None),
                                      chunks=[(64 * ci, ci) for ci in range(8)])],
                           nseg=1, Ls=TP, Cc=64, prevcol=prevcol, savecol=[savecol],
                           psrc=pp[l, g * TP:(g + 1) * TP, :], dbg=(l == 0 and g == 0))
                  layer(l, T, G)
                  if last:
                      dma(shp[l].rearrange("(c p) -> p c", p=128), shcol[:], slow=True)
                  if dbg and l == 0 and g == 0:
                      dma(dbg_t["d_x"], xT[:])
                  x_out(l, yp[g * TP:(g + 1) * TP, :], x1p[:, :, g * TP:(g + 1) * TP], T, "x1p:%d" % (g % 4))
              T = TSM
              x_in(l, xs, x1s, T, "x1s")
              segs = []
              tmt = []
              dma(shin[:, 0:14 * NSMP].rearrange("p (s c) -> p s c", s=NSMP), ssh[l].rearrange("s (c p) -> p s c", p=128), slow=True)
              for s in range(NSMP):
                  tms = [(LS * s, LS, s, 32, rope_s[LS * s:LS * (s + 1), :])]
                  tmt += tms
                  kts = [(kt * 128, 128, kt, kt) for kt in range(4)] + [(512 + LS * s, LS, 4 + s, 4)]
                  segs.append(dict(cache=s, qblocks=[dict(q0=LS * s, nq=LS, kts=kts)], tm=tms, state_in=s, ret_out=rets[l, s],
                                   rw_in=s, rw_out=rws[l, s], chunks=[(LS * s, s)]))

              def prevcol_s(c):
                  return shin[:, 0:14 * NSMP].rearrange("p (s c) -> p s c", s=NSMP)[:, :, c:c + 1]

              def savecol_s(c, r_seg):
                  cp(shout[:, 0:14 * NSMP].rearrange("p (s c) -> p s c", s=NSMP)[:, :, c:c + 1], r_seg[:, :, LS:LS + 1], eng="pool")
              G = dict(kslot=1, vt=[(LS * s, LS, 4 + s) for s in range(NSMP)],
                       a_out=[(aks[l, LS * s:LS * (s + 1), :], avs[l, LS * s:LS * (s + 1), :], LS * s, LS) for s in range(NSMP)],
                       tm=tmt, segs=segs, nseg=NSMP, Ls=LS, Cc=32, prevcol=prevcol_s, savecol=[savecol_s],
                       psrc=psm[l], dbg=False)
              layer(l, T, G)
              dma(shs[l].rearrange("s (c p) -> p s c", p=128), shout[:, 0:14 * NSMP].rearrange("p (s c) -> p s c", s=NSMP), slow=True)
              x_out(l, ys, x1s, T, "x1s")

        try:
            _main()
        except _Stop:
            pass
        P.emit(nc, es)
    return nc, P


_NC_CACHE = {}


def _run(inp, ncores, dbg=False):
    xpr = np.asarray(inp["x_prompt"], np.float32)
    B, SEQ, _ = xpr.shape
    xsm = np.asarray(inp["x_sample"], np.float32)
    assert xsm.shape[0] == ncores * NSMP and xsm.shape[1] == LS and B <= ncores
    key = (SEQ, dbg)
    if key not in _NC_CACHE:
        _NC_CACHE[key] = build(SEQ, dbg=dbg)[0]
    nc = _NC_CACHE[key]
    cst, rope_p, rope_s = host_consts(SEQ)
    f = lambda k: np.ascontiguousarray(np.asarray(inp[k], np.float32))
    prm, cwa2 = host_params({k: f(k) for k in ("norm_mix", "norm_ffn", "ple_norm", "a_q_norm", "a_k_norm", "c_shift_mu",
                                                "c_w0", "c_a0", "c_k_k", "c_k_a", "c_ln_w", "c_ln_b", "c_r_k", "c_w2", "c_a2")})
    biasT = host_bias(f("a_rel_bias")).reshape(DEPTH, 128, 5 * 1024)
    shared = dict(w_in=f("w_in"), w_br=f("w_branch"), w_out=f("w_out"), w_fg=f("w_ffn_gate"), w_fu=f("w_ffn_up"),
                  w_fd=f("w_ffn_down"), w_pp=f("w_ple_proj"), w_pg=f("w_ple_gate"), cg2=f("c_g2"), cwa2=cwa2, prm=prm,
                  cst=cst, biasT=biasT, rope_p=rope_p, rope_s=rope_s)
    pp_ = f("p_prompt"); ps_ = f("p_sample"); cak = f("cache_a_k"); cav = f("cache_a_v")
    sr = f("state_ret"); sw = f("state_rwkv"); sh = f("state_rwkv_shift")
    zx = np.zeros((SEQ, D), np.float32); zp = np.zeros((DEPTH, SEQ, PLE), np.float32)
    in_maps = []
    for c in range(ncores):
        m = dict(shared)
        if c < B:
            m["xp"] = np.ascontiguousarray(xpr[c]); m["pp"] = np.ascontiguousarray(pp_[:, c])
        else:
            m["xp"] = zx; m["pp"] = zp
        sl = slice(c * NSMP, (c + 1) * NSMP)
        m["xs"] = np.ascontiguousarray(xsm[sl].reshape(NSMP * LS, D))
        m["psm"] = np.ascontiguousarray(ps_[:, sl].reshape(DEPTH, NSMP * LS, PLE))
        m["cak"] = np.ascontiguousarray(cak[:, sl].reshape(DEPTH, NSMP, 512, 512))
        m["cav"] = np.ascontiguousarray(cav[:, sl].reshape(DEPTH, NSMP, 512, 512))
        m["sret"] = np.ascontiguousarray(sr[:, sl]); m["srw"] = np.ascontiguousarray(sw[:, sl])
        m["ssh"] = np.ascontiguousarray(sh[:, sl].reshape(DEPTH, NSMP, 1792))
        in_maps.append(m)
    res = run_bass_kernel_spmd(nc, in_maps, core_ids=list(range(ncores))).results
    R = lambda k, cs: [np.asarray(res[c][k], np.float32) for c in cs]
    pc = list(range(B)); ac = list(range(ncores))
    NB = ncores * NSMP
    out = (
        np.stack(R("yp", pc)),
        np.concatenate(R("ys", ac)).reshape(NB, LS, D),
        np.stack(R("akp", pc), axis=1).reshape(DEPTH, B, 512, 8, 64),
        np.stack(R("avp", pc), axis=1).reshape(DEPTH, B, 512, 8, 64),
        np.stack(R("retp", pc), axis=1),
        np.stack(R("rwp", pc), axis=1),
        np.stack(R("shp", pc), axis=1).reshape(DEPTH, B, 1, 1792),
        np.concatenate([r.reshape(DEPTH, NSMP, LS, 8, 64) for r in R("aks", ac)], axis=1),
        np.concatenate([r.reshape(DEPTH, NSMP, LS, 8, 64) for r in R("avs", ac)], axis=1),
        np.concatenate(R("rets", ac), axis=1),
        np.concatenate(R("rws", ac), axis=1),
        np.concatenate(R("shs", ac), axis=1).reshape(DEPTH, NB, 1, 1792),
    )
    if dbg:
        return out, res
    return out


def kernel(**inputs):
    return _run(inputs, 8)
```

```python
import numpy as np
from contextlib import ExitStack
import concourse.bass as bass
import concourse.mybir as mybir
from concourse.bass_utils import run_bass_kernel_spmd

F32 = mybir.dt.float32
BF16 = mybir.dt.bfloat16
ALU = mybir.AluOpType
AF = mybir.ActivationFunctionType
AX = mybir.AxisListType

D = 1024
KC = 8
INW = 8448
DFF = 2816
NFF = 22
PLE = 256
DEPTH = 2
NSMP = 4
LS = 32
TP = 512
PAST = 2048
NEG = -30000.0
CDEC = 0.6065306597126334
GN_EPS = 64e-5

_c = {}
_o = 0
for _n, _w in [("ident", 128), ("ones", 128), ("blk", 128), ("m01", 128), ("msk", 256), ("lsm", 64),
               ("cm64", 512), ("cm32", 128),
               ("gc128", 4), ("gc32", 4), ("tq128", 4), ("tk128", 4), ("tq32", 4), ("tk32", 4),
               ("eps6", 1), ("eps12", 1), ("gneps", 1), ("one", 1), ("negone", 1), ("zero", 1)]:
    _c[_n] = _o
    _o += _w
NCST = _o
_p = {}
_o = 0
for _n, _w in [("nmix", 8), ("nffn", 8), ("nple", 8), ("aqn", 1), ("akn", 1), ("mu", 14), ("w0", 4), ("a0", 4),
               ("kk", 4), ("ka", 4), ("rk", 4), ("lnw", 4), ("lnb", 4)]:
    _p[_n] = _o
    _o += _w
NPRM = _o


def _gammas():
    return 1.0 - np.exp2(-5.0 - np.arange(4, dtype=np.float64))


def host_consts(SEQ):
    g = _gammas()
    cst = np.zeros((128, NCST), np.float32)
    p = np.arange(128)
    cst[:, _c["ident"]:_c["ident"] + 128] = np.eye(128)
    cst[:, _c["ones"]:_c["ones"] + 128] = 1.0
    cst[:, _c["blk"]:_c["blk"] + 128] = (p[:, None] // 64 == p[None, :] // 64)
    cst[:, _c["m01"]:_c["m01"] + 128] = (p[None, :] >= p[:, None])
    m = np.arange(64)
    strict = (m[:, None] < m[None, :]).astype(np.float32)
    incl = (m[:, None] <= m[None, :]).astype(np.float32)
    msk = np.stack([strict, incl, strict, incl], axis=1)
    cst[:64, _c["msk"]:_c["msk"] + 256] = msk.reshape(64, 256)
    cst[:64, _c["lsm"]:_c["lsm"] + 64] = (m[None, :] < m[:, None])
    t = np.arange(512)
    cst[:, _c["cm64"]:_c["cm64"] + 512] = (t % 64 != 0).astype(np.float32)[None]
    cst[:, _c["cm32"]:_c["cm32"] + 128] = (t[:128] % 32 != 0).astype(np.float32)[None]
    cst[:, _c["gc128"]:_c["gc128"] + 4] = (g ** 128)[None]
    cst[:, _c["gc32"]:_c["gc32"] + 4] = (g ** 32)[None]
    sc = 128.0 ** -0.5
    cst[:, _c["tq128"]:_c["tq128"] + 4] = g[None, :] ** (p + 1.0)[:, None]
    cst[:, _c["tk128"]:_c["tk128"] + 4] = sc * g[None, :] ** (-(p + 1.0))[:, None]
    cst[:, _c["tq32"]:_c["tq32"] + 4] = g[None, :] ** ((p % 32) + 1.0)[:, None]
    cst[:, _c["tk32"]:_c["tk32"] + 4] = sc * g[None, :] ** (-((p % 32) + 1.0))[:, None]
    cst[:, _c["eps6"]] = 1e-6
    cst[:, _c["eps12"]] = 1e-12
    cst[:, _c["gneps"]] = GN_EPS
    cst[:, _c["one"]] = 1.0
    cst[:, _c["negone"]] = -1.0
    inv = (10000.0 ** (-np.arange(64, dtype=np.float32) / np.float32(64))).astype(np.float32)

    def rope(pos):
        ang = (pos.astype(np.float32)[:, None] * inv[None, :]).astype(np.float32)
        return np.concatenate([np.cos(ang), np.sin(ang)], axis=1).astype(np.float32)
    rope_p = rope(np.arange(SEQ))
    rope_s = np.tile(rope(PAST + np.arange(LS)), (NSMP, 1))
    return cst, rope_p, rope_s


def host_bias(rb):
    j = np.arange(128)[:, None, None]
    kt = np.arange(5)[None, :, None]
    qq = np.arange(128)[None, None, :]
    kk = 128 * kt + j
    idx = np.clip(512 + qq - kk, -63, 256) + 63
    ok = np.where(qq < 64, kk < 576, kk >= 64)
    out = rb[:, :, idx]
    out = np.where(ok[None, None], out, np.float32(NEG)).astype(np.float32)
    slot_h = [(s_ % 4) * 2 + s_ // 4 for s_ in range(8)]
    out = out[:, slot_h]
    return np.ascontiguousarray(out.transpose(0, 2, 3, 1, 4))


def host_params(inp):
    prm = np.zeros((DEPTH, 128, NPRM), np.float32)

    def fm(v, n):
        return v.reshape(n, 128).T
    for l in range(DEPTH):
        prm[l, :, _p["nmix"]:_p["nmix"] + 8] = fm(inp["norm_mix"][l], 8)
        prm[l, :, _p["nffn"]:_p["nffn"] + 8] = fm(inp["norm_ffn"][l], 8)
        prm[l, :, _p["nple"]:_p["nple"] + 8] = fm(inp["ple_norm"][l], 8)
        prm[l, :, _p["aqn"]] = np.tile(inp["a_q_norm"][l], 2)
        prm[l, :, _p["akn"]] = np.tile(inp["a_k_norm"][l], 2)
        prm[l, :, _p["mu"]:_p["mu"] + 14] = fm(inp["c_shift_mu"][l], 14)
        for nm, key in [("w0", "c_w0"), ("a0", "c_a0"), ("kk", "c_k_k"), ("ka", "c_k_a"), ("lnw", "c_ln_w"),
                        ("lnb", "c_ln_b")]:
            prm[l, :, _p[nm]:_p[nm] + 4] = fm(inp[key][l], 4)
        prm[l, :, _p["rk"]:_p["rk"] + 4] = fm(inp["c_r_k"][l].reshape(-1), 4)
    cwa2 = np.concatenate([inp["c_w2"], inp["c_a2"]], axis=1).astype(np.float32)
    return prm, np.ascontiguousarray(cwa2)


class Prog:
    ENGS = ("pe", "act", "dve", "pool", "sp")

    def __init__(self):
        self.ins = []
        self.last_w = {}
        self.readers = {}
        self.readonly = set()
        self.sub = {}
        self.excl = set()

    def keys(self, x):
        if isinstance(x, str):
            return [x]
        if isinstance(x, tuple):
            return [self.name(x[0]) + ":" + str(x[1])]
        n = self.name(x)
        if n in self.sub:
            return [n + ":" + str(i) for i in range(self.sub[n])]
        return [n]

    @staticmethod
    def name(x):
        t = getattr(x, "tensor", x)
        return t.name

    def op(self, eng, fn, reads=(), writes=(), dma=None):
        i = len(self.ins)
        deps = set()
        for r in reads:
            for k in self.keys(r):
                if k in self.readonly:
                    continue
                if k in self.last_w:
                    deps.add(self.last_w[k])
                rd = self.readers.setdefault(k, {})
                if k.split(":")[0] in self.excl:
                    for ek, r in rd.items():
                        if ek != eng:
                            deps.add(r)
                rd[("d", i) if dma is not None else eng] = i
        for w in writes:
            for k in self.keys(w):
                if k in self.last_w:
                    deps.add(self.last_w[k])
                for r in self.readers.get(k, {}).values():
                    if r != i:
                        deps.add(r)
                self.last_w[k] = i
                self.readers[k] = {}
        self.ins.append([eng, fn, dma, sorted(deps)])
        return i

    def emit(self, nc, es):
        ins = self.ins
        n = len(ins)
        need = [False] * n
        chans = {}
        for i, (eng, fn, dma, deps) in enumerate(ins):
            if dma is not None:
                need[i] = True
                chans.setdefault(dma, 0)
            for j in deps:
                ej, _, dj, _ = ins[j]
                if dj is not None or dma is not None or ej != eng or eng != "pe":
                    need[j] = True
        comp = [None] * n
        cnt = {e: 0 for e in self.ENGS}
        for i, (eng, fn, dma, deps) in enumerate(ins):
            if dma is not None:
                chans[dma] += 16
                comp[i] = ("d_" + dma, chans[dma])
            elif need[i]:
                cnt[eng] += 1
                comp[i] = ("e_" + eng, cnt[eng])
        sems = {}
        for s in ["e_" + e for e in self.ENGS] + ["d_" + c for c in chans]:
            sems[s] = es.enter_context(nc.semaphore(s))
        self.n_sems = len(sems)
        final = {("d_" + c): v for c, v in chans.items()}
        block = es.enter_context(nc.Block())
        per_eng = {e: [] for e in self.ENGS}
        for i, rec in enumerate(ins):
            per_eng[rec[0]].append(i)

        def run(eng_name, e):
            waited = {}
            for i in per_eng[eng_name]:
                _, fn, dma, deps = ins[i]
                wl = {}
                for j in deps:
                    ej, _, dj, _ = ins[j]
                    if dj is None and dma is None and ej == eng_name and eng_name == "pe":
                        continue
                    s, v = comp[j]
                    if wl.get(s, 0) < v:
                        wl[s] = v
                for s, v in wl.items():
                    if waited.get(s, 0) >= v:
                        continue
                    e.wait_ge(sems[s], v)
                    waited[s] = v
                r = fn(e)
                if comp[i] is not None:
                    r.then_inc(sems[comp[i][0]], 16 if dma is not None else 1)
            if eng_name == "sp":
                for s, v in final.items():
                    e.wait_ge(sems[s], v)

        @block.tensor
        def _(e):
            run("pe", e)

        @block.scalar
        def _(e):
            run("act", e)

        @block.vector
        def _(e):
            run("dve", e)

        @block.gpsimd
        def _(e):
            run("pool", e)

        @block.sync
        def _(e):
            run("sp", e)


def build(SEQ, dbg=False, skip=""):
    import os as _os
    assert SEQ % TP == 0
    NG = SEQ // TP
    TSM = NSMP * LS
    nc = bass.Bass("TRN2", target_bir_lowering=False)
    P = Prog()
    es = ExitStack()

    def din(name, shape):
        P.readonly.add(name)
        return nc.dram_tensor(name, list(shape), F32, kind="ExternalInput").ap()

    def dout(name, shape):
        return nc.dram_tensor(name, list(shape), F32, kind="ExternalOutput").ap()

    xp = din("xp", [SEQ, D]); pp = din("pp", [DEPTH, SEQ, PLE])
    xs = din("xs", [TSM, D]); psm = din("psm", [DEPTH, TSM, PLE])
    cak = din("cak", [DEPTH, NSMP, 512, 512]); cav = din("cav", [DEPTH, NSMP, 512, 512])
    sret = din("sret", [DEPTH, NSMP, 4, 128, 128]); srw = din("srw", [DEPTH, NSMP, 8, 64, 64])
    ssh = din("ssh", [DEPTH, NSMP, 1792])
    w_in = din("w_in", [DEPTH, D, INW]); w_br = din("w_br", [DEPTH, 3, 512, D]); w_out = din("w_out", [DEPTH, D, D])
    w_fg = din("w_fg", [DEPTH, D, DFF]); w_fu = din("w_fu", [DEPTH, D, DFF]); w_fd = din("w_fd", [DEPTH, DFF, D])
    w_pp = din("w_pp", [DEPTH, PLE, D]); w_pg = din("w_pg", [DEPTH, D, D])
    cg2 = din("cg2", [DEPTH, 128, 512]); cwa2 = din("cwa2", [DEPTH, 128, 512])
    prm_d = din("prm", [DEPTH, 128, NPRM]); cst_d = din("cst", [128, NCST])
    bias_d = din("biasT", [DEPTH, 128, 5 * 1024])
    rope_p = din("rope_p", [SEQ, 128]); rope_s = din("rope_s", [TSM, 128])

    yp = dout("yp", [SEQ, D]); ys = dout("ys", [TSM, D])
    akp = dout("akp", [DEPTH, 512, 512]); avp = dout("avp", [DEPTH, 512, 512])
    retp = dout("retp", [DEPTH, 4, 128, 128]); rwp = dout("rwp", [DEPTH, 8, 64, 64]); shp = dout("shp", [DEPTH, 1792])
    aks = dout("aks", [DEPTH, TSM, 512]); avs = dout("avs", [DEPTH, TSM, 512])
    rets = dout("rets", [DEPTH, NSMP, 4, 128, 128]); rws = dout("rws", [DEPTH, NSMP, 8, 64, 64])
    shs = dout("shs", [DEPTH, NSMP, 1792])
    wsrc = dict(w_in=w_in, w_br=w_br, w_out=w_out, w_fg=w_fg, w_fu=w_fu, w_fd=w_fd, w_pp=w_pp, w_pg=w_pg)
    wbf = {k: nc.dram_tensor(k + "_b", list(v.shape), BF16).ap() for k, v in wsrc.items()}
    x1p = nc.dram_tensor("x1p", [KC, 128, SEQ], F32).ap()
    x1s = nc.dram_tensor("x1s", [KC, 128, TSM], F32).ap()
    dbg_t = {}
    if dbg:
        for nm in ("d_oa", "d_ob", "d_oc"):
            dbg_t[nm] = dout(nm, [128, 4 * TP])
        dbg_t["d_x"] = dout("d_x", [128, KC * TP])

    with es:
        def sb(name, shape, dt=F32):
            return es.enter_context(nc.sbuf_tensor(name, list(shape), dt))

        def ps(name, shape, dt=F32, sub=None):
            if sub:
                P.sub[name] = sub
            P.excl.add(name)
            return es.enter_context(nc.psum_tensor(name, list(shape), dt))

        cst = sb("cst_sb", [128, NCST])
        prm = sb("prm_sb", [128, DEPTH * NPRM])
        identb = sb("identb", [128, 128], BF16)
        onesb = sb("onesb", [128, 128], BF16)
        blkb = sb("blkb", [128, 128], BF16)
        sqb = [sb(f"sqb{i}", [128, TP], BF16) for i in range(2)]
        aqs = sb("aqs", [128, DEPTH])
        cwa2b = sb("cwa2b", [128, 512], BF16)
        cg2b = sb("cg2b", [128, 512], BF16)
        biasT = sb("biasT_sb", [128, 5 * 1024], BF16)
        xT = sb("xT", [128, KC * TP]); P.sub["xT"] = KC
        hT = sb("hT", [128, KC * TP], BF16); P.sub["hT"] = KC
        NWB = 3
        WBN = 4096
        wb = [sb(f"wb{i}", [128, WBN], BF16) for i in range(NWB)]
        Kwin = sb("Kwin", [128, 4 * 1024], BF16)
        Vwin = sb("Vwin", [128, 8 * 520], BF16)
        Sret = sb("Sret", [128, 512]); Sretb = sb("Sretb", [128, 512], BF16)
        Srw = sb("Srw", [128, 512]); Srwb = sb("Srwb", [128, 512], BF16)
        shcol = sb("shcol", [128, 14])
        oT3 = [sb(f"oT{b}", [128, 4 * TP], BF16) for b in range(3)]
        FA = sb("FA", [128, 4 * 2048]); P.sub["FA"] = 4
        HA = sb("HA", [128, 6 * 2048], BF16); P.sub["HA"] = 6
        tmp = [sb(f"tmp{i}", [128, TP + 1]) for i in range(6)]
        t1, t2, t3, t4, t5, t6 = tmp
        ropet = [sb("ropet0", [128, 128])] * 2
        scsb = sb("scsb", [128, 1024])
        xin = scsb
        ptsb = [sb(f"ptsb{i}", [128, 1024], BF16) for i in range(2)]
        oatm = sb("oatm", [128, 512], BF16)
        scs = oatm
        rden = sb("rden", [128, 8])
        lob = sb("lob", [128, TP], BF16); sgl = sb("sgl", [128, TP], BF16)
        MM = sb("MM", [64, 4 * 512], BF16)
        CB = [sb(f"CB{i}", [64, 1024], BF16) for i in range(2)]
        AT_ = [sb(f"AaT{i}", [64, 512], BF16) for i in range(2)]
        hbk = sb("hbk", [128, 2 * 256], BF16)
        BhTM = sb("BhTM", [64, 512], BF16); KhTM = sb("KhTM", [64, 512], BF16)
        Vb = sb("Vb", [64, 512], BF16); Vpad = sb("Vpad", [64, 1024], BF16)
        Tf0 = sb("Tf0", [64, 512], BF16)
        Gpad = sb("Gpad", [64, 1024], BF16); Ub = sb("Ub", [64, 512], BF16); Upad = sb("Upad", [64, 1024], BF16)
        stmp = sb("stmp", [128, 512])
        otr = stmp
        rwcp = stmp[:, 0:256]
        wc = sb("wc", [128, 4 * 8])
        rwst = scsb[0:64, 0:512]
        shin = sb("shin", [128, 14 * NSMP]); shout = sb("shout", [128, 14 * NSMP])

        PA = ps("PA", [128, 1024], sub=2); PB = ps("PB", [128, 1024], sub=2); PC = ps("PC", [128, 1024], sub=2)
        PD = ps("PD", [128, 512]); PT = ps("PT", [128, 1024], BF16)

        def C(n, w=1):
            return cst[:, _c[n]:_c[n] + w]

        def PR(l, n, i=0, w=1):
            return prm[:, l * NPRM + _p[n] + i:l * NPRM + _p[n] + i + w]

        def A(x):
            return x[0] if isinstance(x, tuple) else x

        def mm(out, lhsT, rhs, start=True, stop=True, skip=False):
            kw = dict(skip_group_check=True) if skip else {}
            P.op("pe", lambda e: e.matmul(A(out), A(lhsT), A(rhs), start=start, stop=stop, **kw), [lhsT, rhs], [out])

        def tr(out, in_, ident):
            P.op("pe", lambda e: e.transpose(A(out), A(in_), A(ident)), [in_, ident], [out])

        def act(out, in_, func, bias=None, scale=1.0, eng="act"):
            kw = {}
            rd = [in_]
            if bias is not None:
                kw["bias"] = A(bias)
                if not isinstance(bias, float):
                    rd.append(bias)
            if not isinstance(scale, float):
                rd.append(scale)
            P.op(eng, lambda e: e.activation(out=A(out), in_=A(in_), func=func, scale=A(scale), **kw), rd, [out])

        def tt(out, in0, in1, op, eng="dve"):
            P.op(eng, lambda e: e.tensor_tensor(out=A(out), in0=A(in0), in1=A(in1), op=op), [in0, in1], [out])

        def ts(out, in0, s1, op0, s2=None, op1=None, eng="dve"):
            rd = [in0] + [s for s in (s1, s2) if s is not None and not isinstance(s, float)]
            kw = {}
            if op1 is not None:
                kw["op1"] = op1
            P.op(eng, lambda e: e.tensor_scalar(out=A(out), in0=A(in0), scalar1=A(s1),
                                                scalar2=(A(s2) if s2 is not None else None), op0=op0, **kw), rd, [out])

        def stt(out, in0, s, in1, op0, op1, eng="dve"):
            rd = [in0, in1] + ([s] if not isinstance(s, float) else [])
            P.op("dve", lambda e: e.scalar_tensor_tensor(out=A(out), in0=A(in0), scalar=A(s), in1=A(in1), op0=op0, op1=op1),
                 rd, [out])

        def cp(out, in_, eng="dve"):
            if eng == "act":
                P.op("act", lambda e: e.copy(out=A(out), in_=A(in_)), [in_], [out])
            else:
                P.op(eng, lambda e: e.tensor_copy(out=A(out), in_=A(in_)), [in_], [out])

        def recip(out, in_):
            P.op("dve", lambda e: e.reciprocal(out=A(out), in_=A(in_)), [in_], [out])

        def mset(out, val, eng="pool"):
            P.op(eng, lambda e: e.memset(A(out), val), [], [out])

        def scan(out, d0, d1, eng="dve"):
            P.op(eng, lambda e: e.tensor_tensor_scan(out=A(out), data0=A(d0), data1=A(d1), initial=0.0,
                                                     op0=ALU.mult, op1=ALU.add), [d0, d1], [out])

        def dma(out, in_, eng="sp", chan=None, wkey=None, rkey=None, slow=False):
            on = Prog.name(A(out))
            wr = [wkey] if wkey is not None else ([out] if on not in P.readonly else [])
            rd = (list(rkey) if isinstance(rkey, (list, tuple)) else [rkey]) if rkey is not None else [in_]
            ch = chan or (P.keys(wr[0])[0] if wr else Prog.name(A(in_)))
            ch = ch.replace(":", "_")
            kw = dict(allow_slow_non_contiguous=True) if slow else {}
            P.op(eng, lambda e: e.dma_start(out=A(out), in_=A(in_), **kw), rd, wr, dma=ch)

        def r3(ap, a):
            return ap.rearrange("p (a b) -> p a b", a=a)

        def bc_mid(base, n_mid):
            (ps_, rows), (st, w) = base.ap
            return bass.AP(base.tensor, base.offset, [[ps_, rows], [0, n_mid], [st, w]])

        def bc_in(base, w):
            (ps_, rows), (st, n) = base.ap
            return bass.AP(base.tensor, base.offset, [[ps_, rows], [st, n], [0, w]])

        def fa(k, j=0, c0=0, c1=TP):
            return (FA[:, k * 2048 + j * TP + c0:k * 2048 + j * TP + c1], k)

        def ha(k, j=0, c0=0, c1=TP, r0=0, r1=128):
            return (HA[r0:r1, k * 2048 + j * TP + c0:k * 2048 + j * TP + c1], k)

        class WS:
            def __init__(self):
                self.seq = []
                self.issued = 0
                self.used = 0

            def plan(self, l):
                w_in, w_br, w_out, w_fg, w_fu, w_fd, w_pp, w_pg = (wbf[k] for k in
                                                                      ("w_in", "w_br", "w_out", "w_fg", "w_fu", "w_fd", "w_pp", "w_pg"))
                wi = w_in[l]

                def colblk(src, c0, ncol, tag):
                    nk = src.shape[0] // 128
                    self.seq.append((tag, src[:, c0:c0 + ncol].rearrange("(k p) c -> p k c", p=128), nk, ncol))
                for i, tg in enumerate(["aq", "ak", "av", "bq", "bk", "bv", "bg"]):
                    colblk(wi, 512 * i, 512, tg)
                colblk(wi, 5120, 256, "cl")
                colblk(wi, 4096, 512, "ck")
                colblk(wi, 3584, 512, "cr")
                colblk(wi, 4608, 512, "cv")
                for b in range(3):
                    colblk(wi, 5376 + 1024 * b, 512, f"gl{b}a")
                    colblk(wi, 5376 + 1024 * b + 512, 512, f"gl{b}b")
                    colblk(w_br[l, b], 0, 1024, f"br{b}")
                colblk(w_out[l], 0, 512, "outa")
                colblk(w_out[l], 512, 512, "outb")
                for i in range(6):
                    ncol = 512 if i < 5 else 256
                    colblk(w_fg[l], 512 * i, ncol, f"fg{i}")
                    colblk(w_fu[l], 512 * i, ncol, f"fu{i}")
                for j in range(8):
                    colblk(w_fd[l], 128 * j, 128, f"fd{j}")
                colblk(w_pp[l], 0, 1024, "pp")
                colblk(w_pg[l], 0, 512, "pga")
                colblk(w_pg[l], 512, 512, "pgb")

            def _issue(self):
                i = self.issued
                tag, src, nk, ncol = self.seq[i]
                buf = wb[i % NWB]
                assert nk * ncol <= WBN
                dma(r3(buf[:, 0:nk * ncol], nk), src, eng="sp", rkey=["wcast:%d" % i_ for i_ in range(4)])
                self.issued += 1

            def get(self, tag):
                i = self.used
                assert self.seq[i][0] == tag, (self.seq[i][0], tag)
                while self.issued < min(len(self.seq), i + NWB - 1):
                    self._issue()
                self.used += 1
                _, _, nk, ncol = self.seq[i]
                buf = wb[i % NWB]
                return (lambda kc, c0, c1: buf[:, kc * ncol + c0:kc * ncol + c1]), nk, ncol

        W = WS()
        banks = [(PA, 0), (PA, 1), (PB, 0), (PB, 1)]
        bstate = [0]

        def nbank():
            t_, h = banks[bstate[0] % 4]
            bstate[0] += 1
            return (t_[:, h * 512:(h + 1) * 512], h)

        def bk(pb, r0, r1, c0, c1):
            return (pb[0][r0:r1, c0:c1], pb[1])

        def fm_proj(pb, T, wf, nk, c0, rhs_fn):
            for kc in range(nk):
                mm(bk(pb, 0, 128, 0, T), wf(kc, c0, c0 + 128), rhs_fn(kc), start=(kc == 0), stop=(kc == nk - 1))

        identf = C("ident", 128)
        onesf = C("ones", 128)
        blkf = C("blk", 128)

        dma(cst[:], cst_d)
        dma(r3(prm[:], DEPTH), prm_d.rearrange("l p n -> p l n"))
        cp(identb[:], identf, eng="pool")
        cp(onesb[:], onesf, eng="pool")
        cp(blkb[:], blkf, eng="pool")
        for l in range(DEPTH):
            ts(aqs[:, l:l + 1], PR(l, "aqn"), 0.125, ALU.mult)
        ci_ = 0
        for k_, src_ in wsrc.items():
            s2 = src_.flatten_outer_dims() if len(src_.shape) > 2 else src_
            d2 = wbf[k_].flatten_outer_dims() if len(src_.shape) > 2 else wbf[k_]
            rows = s2.shape[0]
            step = 256 if s2.shape[1] >= 4096 else 1024
            for r0 in range(0, rows, step):
                r1_ = min(rows, r0 + step)
                dma(d2[r0:r1_, :], s2[r0:r1_, :], eng="pool", wkey="wcast:%d" % (ci_ % 4), chan="wcast%d" % (ci_ % 4))
                ci_ += 1
        mset(Vwin[:], 1.0)
        for t_ in (Vpad, Gpad, Upad):
            mset(t_[:], 0.0)

        xin2 = [scsb[:, :], (FA[:, 0:1024], 0)]

        def load_x_tm(src, T):
            for ti in range(T // 128):
                xin = xin2[ti % 2]
                dma(xin, src[ti * 128:(ti + 1) * 128, :])
                xk = (lambda a_: (a_, 0)) if ti % 2 else (lambda a_: a_)
                xa = A(xin)
                for half in range(2):
                    pb = nbank()
                    for k4 in range(4):
                        kc = half * 4 + k4
                        tr(bk(pb, 0, 128, k4 * 128, (k4 + 1) * 128), xk(xa[:, kc * 128:(kc + 1) * 128]), identf)
                    dst = r3(xT[:, half * 4 * TP:(half * 4 + 4) * TP], 4)[:, :, ti * 128:(ti + 1) * 128]
                    cp(dst, (r3(pb[0], 4), pb[1]), eng="act" if half else "dve")

        def store_x_tm(dst, T):
            for ti in range(T // 128):
                xin = xin2[ti % 2]
                xk = (lambda a_: (a_, 0)) if ti % 2 else (lambda a_: a_)
                xa = A(xin)
                for half in range(2):
                    pb = nbank()
                    for k4 in range(4):
                        kc = half * 4 + k4
                        tr(bk(pb, 0, 128, k4 * 128, (k4 + 1) * 128), xT[:, kc * TP + ti * 128:kc * TP + (ti + 1) * 128], identf)
                    cp(xk(xa[:, half * 512:(half + 1) * 512]), pb, eng="act" if half else "dve")
                dma(dst[ti * 128:(ti + 1) * 128, :], xin)

        def rms_stats(T, dst, srcs_fn, nk, div, epsname, lhs=None):
            lhs = lhs if lhs is not None else onesb[:]
            for kc in range(nk):
                sq = sqb[kc % 2]
                if kc % 2 == 0:
                    act(sq[:, 0:T], srcs_fn(kc), AF.Square)
                else:
                    tt(sq[:, 0:T], srcs_fn(kc), srcs_fn(kc), ALU.mult)
                mm(PD[:, 0:T], lhs, sq[:, 0:T], start=(kc == 0), stop=(kc == nk - 1))
            act(dst, PD[:, 0:T], AF.Ln, bias=C(epsname), scale=1.0 / div)
            act(dst, dst, AF.Exp, scale=-0.5)

        def norm_to_hT(l, T, gain):
            rms_stats(T, t3[:, 0:T], lambda kc: (xT[:, kc * TP:kc * TP + T], kc), KC, float(D), "eps6")
            for kc in range(KC):
                eng = "dve" if kc % 2 == 0 else "pool"
                if gain is None:
                    tt((hT[:, kc * TP:kc * TP + T], kc), (xT[:, kc * TP:kc * TP + T], kc), t3[:, 0:T], ALU.mult, eng=eng)
                else:
                    stt((hT[:, kc * TP:kc * TP + T], kc), (xT[:, kc * TP:kc * TP + T], kc), PR(l, gain, kc), t3[:, 0:T],
                        ALU.mult, ALU.mult, eng=eng)

        def hrhs(T):
            return lambda kc: (hT[:, kc * TP:kc * TP + T], kc)

        def qk_norm(pb, T, gain_ap, dst_f32):
            cp(t4[:, 0:T], bk(pb, 0, 128, 0, T), eng="act")
            act(sqb[0][:, 0:T], bk(pb, 0, 128, 0, T), AF.Square)
            mm(PD[:, 0:T], blkb[:], sqb[0][:, 0:T])
            act(t6[:, 0:T], PD[:, 0:T], AF.Ln, bias=C("eps6"), scale=1.0 / 64)
            act(t6[:, 0:T], t6[:, 0:T], AF.Exp, scale=-0.5)
            stt(dst_f32, t4[:, 0:T], gain_ap, t6[:, 0:T], ALU.mult, ALU.mult)

        def mixer_A(l, T, G):
            kslot = G["kslot"]
            wf, nk, _ = W.get("aq")
            for j in range(4):
                pb = nbank()
                fm_proj(pb, T, wf, nk, j * 128, hrhs(T))
                qk_norm(pb, T, aqs[:, l:l + 1], ha(0, j, 0, T))
            wf, nk, _ = W.get("ak")
            for j in range(4):
                pb = nbank()
                fm_proj(pb, T, wf, nk, j * 128, hrhs(T))
                qk_norm(pb, T, PR(l, "akn"), fa(0, j, 0, T))
                cp(Kwin[:, j * 1024 + kslot * 512:j * 1024 + kslot * 512 + T], fa(0, j, 0, T), eng="pool")
            for (dst_k, dst_v, t0, L) in G["a_out"]:
                pb = nbank()
                for j in range(4):
                    tr(bk(pb, 0, L, j * 128, (j + 1) * 128), fa(0, j, t0, t0 + L), identf)
                cp(otr[0:L, :], bk(pb, 0, L, 0, 512), eng="act")
                dma(dst_k, otr[0:L, :])
            wf, nk, _ = W.get("av")
            for (t0, L, vtile) in G["vt"]:
                pb = nbank()
                for kc in range(nk):
                    mm(bk(pb, 0, L, 0, 512), hT[:, kc * TP + t0:kc * TP + t0 + L], wf(kc, 0, 512),
                       start=(kc == 0), stop=(kc == nk - 1))
                vdst = Vwin[0:L, vtile * 520:(vtile + 1) * 520].rearrange("p (h d) -> p h d", h=8)[:, :, 0:64]
                cp(vdst, (pb[0][0:L, :].rearrange("p (h d) -> p h d", h=8), pb[1]), eng="act")
                for (dst_k, dst_v, ot0, oL) in G["a_out"]:
                    if ot0 == t0 and oL == L:
                        cp(otr[0:L, :], bk(pb, 0, L, 0, 512), eng="dve")
                        dma(dst_v, otr[0:L, :])
            oaT = oT3[0]
            for sg in G["segs"]:
                if sg.get("cache") is not None:
                    b = sg["cache"]
                    dma((r3(HA[:, 2048:4096], 4), 1), cak[l, b].rearrange("(t p) c -> p t c", p=128), eng="pool")
                    for t_ in range(4):
                        dma(Vwin[:, t_ * 520:(t_ + 1) * 520].rearrange("p (h d) -> p h d", h=8)[:, :, 0:64],
                            cav[l, b, t_ * 128:(t_ + 1) * 128, :].rearrange("p (h d) -> p h d", h=8), eng="pool")
                    for t_ in range(4):
                        for j in range(4):
                            tr(PT[:, j * 128:(j + 1) * 128], (HA[:, 2048 + t_ * 512 + j * 128:2048 + t_ * 512 + (j + 1) * 128], 1), identb[:])
                        dst = r3(Kwin[:, :], 4)[:, :, t_ * 128:(t_ + 1) * 128]
                        cp(dst, r3(PT[:, 0:512], 4), eng="act" if t_ % 2 else "dve")
                for qb in sg["qblocks"]:
                    q0, nq, kts = qb["q0"], qb["nq"], qb["kts"]
                    def scores(ki):
                        kcol, nkk, vtile, bkt = kts[ki]
                        scp = PA if ki % 2 == 0 else PB
                        for bnk in range(2):
                            mm((scp[0:nkk, bnk * 512:(bnk + 1) * 512], bnk), identb[0:nkk, 0:nkk],
                               biasT[0:nkk, bkt * 1024 + bnk * 512:bkt * 1024 + (bnk + 1) * 512], start=True, stop=False, skip=True)
                        for h in range(8):
                            j, b0 = h // 2, (h % 2) * 64
                            sl_ = (h % 2) * 4 + h // 2
                            mm((scp[0:nkk, sl_ * 128:sl_ * 128 + nq], sl_ // 4),
                               Kwin[b0:b0 + 64, j * 1024 + kcol:j * 1024 + kcol + nkk],
                               ha(0, j, q0, q0 + nq, b0, b0 + 64), start=False, stop=True, skip=True)

                    def soft_pv(ki):
                        kcol, nkk, vtile, bkt = kts[ki]
                        scp = PA if ki % 2 == 0 else PB
                        pt_ = ptsb[ki % 2]
                        act(r3(pt_[0:nkk, :], 8)[:, :, 0:nq], r3(scp[0:nkk, :], 8)[:, :, 0:nq], AF.Exp)
                        for h in range(8):
                            half = h // 4
                            oc0 = half * 512 + (h % 4) * 65
                            sl_ = (h % 2) * 4 + h // 2
                            mm((PC[0:nq, oc0:oc0 + 65], half), pt_[0:nkk, sl_ * 128:sl_ * 128 + nq],
                               Vwin[0:nkk, vtile * 520 + h * 65:vtile * 520 + (h + 1) * 65],
                               start=(ki == 0 and h % 4 == 0), stop=(ki == len(kts) - 1), skip=True)
                    scores(0)
                    for ki in range(len(kts)):
                        if ki + 1 < len(kts):
                            scores(ki + 1)
                        soft_pv(ki)
                    for half in range(2):
                        ov = PC[0:nq, half * 512:half * 512 + 260].rearrange("p (h d) -> p h d", h=4)
                        recip(rden[0:nq, half * 4:half * 4 + 4], (ov[:, :, 64], half))
                        tt(oatm[0:nq, half * 256:(half + 1) * 256].rearrange("p (h d) -> p h d", h=4), (ov[:, :, 0:64], half),
                           bc_in(rden[0:nq, half * 4:half * 4 + 4], 64), ALU.mult)
                    for j in range(4):
                        tr(PT[:, j * 128:j * 128 + nq], oatm[0:nq, j * 128:(j + 1) * 128], identb[0:nq, 0:nq])
                    cp(r3(oaT[:, :], 4)[:, :, q0:q0 + nq], r3(PT[:, 0:512], 4)[:, :, 0:nq], eng="act")

        def mixer_B(l, T, G):
            obT = oT3[1]
            for bi, blk in enumerate(("bq", "bk", "bv")):
                wf, nk, _ = W.get(blk)
                tiles = G["tm"]

                def proj(ti_):
                    t0, L, idx, Cc, rp_src = tiles[ti_]
                    pb = nbank()
                    for kc in range(nk):
                        mm(bk(pb, 0, L, 0, 512), hT[:, kc * TP + t0:kc * TP + t0 + L], wf(kc, 0, 512),
                           start=(kc == 0), stop=(kc == nk - 1))
                    return pb

                def rot(ti_, pb):
                    t0, L, idx, Cc, rp_src = tiles[ti_]
                    src = bk(pb, 0, L, 0, 512)
                    if blk == "bv":
                        cp(ha(4, idx, 0, 512, 0, L), src, eng="act")
                        tt((r3(ha(5, idx, 0, 512, 0, L)[0], 4), 5), (r3(src[0], 4), src[1]),
                           bc_in(C("gc%d" % Cc, 4)[0:L, :], 128), ALU.mult)
                        return
                    rp = ropet[0]
                    dma(rp[0:L, :], rp_src)
                    x4 = pb[0][0:L, :].rearrange("p (h d) -> p h d", h=4)
                    x1 = (x4[:, :, 0:64], pb[1]); x2 = (x4[:, :, 64:128], pb[1])
                    cosb = bc_mid(rp[0:L, 0:64], 4)
                    sinb = bc_mid(rp[0:L, 64:128], 4)
                    ta, tb_ = (t4, t5) if ti_ % 2 == 0 else (t1, t2)
                    a1 = r3(ta[0:L, 0:256], 4); a2 = r3(ta[0:L, 256:512], 4)
                    a3 = r3(tb_[0:L, 0:256], 4); a4 = r3(tb_[0:L, 256:512], 4)
                    o4 = r3(t6[0:L, 0:512], 4)
                    tt(a1, x1, cosb, ALU.mult)
                    tt(a2, x2, sinb, ALU.mult)
                    tt(a3, x1, sinb, ALU.mult)
                    tt(a4, x2, cosb, ALU.mult)
                    tt(o4[:, :, 0:64], a1, a2, ALU.subtract, eng="pool")
                    tt(o4[:, :, 64:128], a3, a4, ALU.add, eng="pool")
                    slot = 2 if blk == "bq" else 3
                    tname = ("tq%d" if blk == "bq" else "tk%d") % Cc
                    dst = ha(slot, idx, 0, 512, 0, L)
                    tt((r3(dst[0], 4), slot), o4, bc_in(C(tname, 4)[0:L, :], 128), ALU.mult)

                def trans(ti_):
                    t0, L, idx, Cc, rp_src = tiles[ti_]
                    if blk == "bv":
                        return
                    slot = 2 if blk == "bq" else 3
                    for h in range(4):
                        tr(PT[:, h * 128:h * 128 + L], ha(slot, idx, h * 128, (h + 1) * 128, 0, L), identb[0:L, 0:L])
                    fslot = 0 if blk == "bq" else 1
                    cp((r3(HA[:, fslot * 2048:(fslot + 1) * 2048], 4)[:, :, t0:t0 + L], fslot), r3(PT[:, 0:512], 4)[:, :, 0:L],
                       eng="act")
                pbs = {0: proj(0)}
                for ti_ in range(len(tiles)):
                    if ti_ + 1 < len(tiles):
                        pbs[ti_ + 1] = proj(ti_ + 1)
                    rot(ti_, pbs.pop(ti_))
                    trans(ti_)
                chk("B_" + blk)
            wf, nk, _ = W.get("bg")
            for h in range(4):
                pb = nbank()
                fm_proj(pb, T, wf, nk, h * 128, hrhs(T))
                act(fa(0, h, 0, T), bk(pb, 0, 128, 0, T), AF.Silu)
            chk("B_bg")
            for sg in G["segs"]:
                if sg["state_in"] == "zero":
                    mset(Sret[:], 0.0)
                    mset(Sretb[:], 0.0)
                elif sg["state_in"] is not None:
                    dma(r3(Sret[:], 4), sret[l, sg["state_in"]].rearrange("h d e -> d h e"))
                    cp(Sretb[:], Sret[:], eng="act")
                for (t0, L, idx, Cc, _rs) in sg["tm"]:
                    for h in range(4):
                        mm((PC[0:L, h * 128:h * 128 + L], 0), ha(1, h, t0, t0 + L), ha(0, h, t0, t0 + L))
                    tt(r3(scs[0:L, :], 4)[:, :, 0:L], (r3(PC[0:L, 0:512], 4)[:, :, 0:L], 0),
                       bc_mid(C("m01", 128)[0:L, 0:L], 4), ALU.mult)
                    for h in range(4):
                        mm((PC[:, 512 + h * 128:512 + h * 128 + L], 1), ha(4, idx, h * 128, (h + 1) * 128, 0, L),
                           scs[0:L, h * 128:h * 128 + L], start=True, stop=False)
                        mm((PC[:, 512 + h * 128:512 + h * 128 + L], 1), Sretb[:, h * 128:(h + 1) * 128],
                           ha(0, h, t0, t0 + L), start=False, stop=True)
                    cp((r3(FA[:, 2048:4096], 4)[:, :, t0:t0 + L], 1), (r3(PC[:, 512:1024], 4)[:, :, 0:L], 1), eng="act")
                    for h in range(4):
                        mm(PD[:, h * 128:(h + 1) * 128], ha(3, idx, h * 128, (h + 1) * 128, 0, L),
                           ha(5, idx, h * 128, (h + 1) * 128, 0, L))
                    for h in range(4):
                        stt(Sret[:, h * 128:(h + 1) * 128], Sret[:, h * 128:(h + 1) * 128],
                            C("gc%d" % Cc, 4)[:, h:h + 1], PD[:, h * 128:(h + 1) * 128], ALU.mult, ALU.add)
                    cp(Sretb[:], Sret[:], eng="act")
                if sg["ret_out"] is not None:
                    dma(sg["ret_out"].rearrange("h d e -> d h e"), r3(Sret[:], 4))
            chk("B_chunks")
            for h in range(4):
                rms_stats(T, t3[:, 0:T], lambda kc, h=h: fa(1, h, 0, T), 1, 128.0, "eps6")
                tt(t4[:, 0:T], fa(1, h, 0, T), t3[:, 0:T], ALU.mult)
                tt(obT[:, h * TP:h * TP + T], t4[:, 0:T], fa(0, h, 0, T), ALU.mult, eng="pool")

        def mixer_C(l, T, G):
            ocT = oT3[2]
            nseg, Ls = G["nseg"], G["Ls"]
            Cc = G["Cc"]
            cmn = "cm%d" % Cc
            nch_seg = Ls // Cc
            nch = T // Cc

            def shifted(pb, c, dst):
                r_seg = t1[:, 0:nseg * (Ls + 1)].rearrange("p (s u) -> p s u", s=nseg)
                cp(r_seg[:, :, 1:Ls + 1], (pb[0][:, 0:T].rearrange("p (s u) -> p s u", s=nseg), pb[1]), eng="act")
                cp(r_seg[:, :, 0:1], G["prevcol"](c), eng="act")
                for fn in G["savecol"]:
                    fn(c, r_seg)
                d_seg = t2[:, 0:T].rearrange("p (s u) -> p s u", s=nseg)
                tt(d_seg, r_seg[:, :, 0:Ls], r_seg[:, :, 1:Ls + 1], ALU.subtract)
                dst_seg = dst.rearrange("p (s u) -> p s u", s=nseg)
                stt(dst_seg, d_seg, PR(l, "mu", c), r_seg[:, :, 1:Ls + 1], ALU.mult, ALU.add)

            wf, nk, _ = W.get("cl")
            pb = nbank()
            fm_proj(pb, T, wf, nk, 0, hrhs(T))
            shifted(pb, 12, t3[:, 0:T])
            act(lob[0:64, 0:T], t3[0:64, 0:T], AF.Tanh)
            cp(lob[64:128, 0:T], t3[64:128, 0:T], eng="act")
            pb = nbank()
            fm_proj(pb, T, wf, nk, 128, hrhs(T))
            shifted(pb, 13, t3[:, 0:T])
            act(sgl[:, 0:T], t3[:, 0:T], AF.Sigmoid)
            for j in range(4):
                pb = nbank()
                mm(bk(pb, 0, 128, 0, T), cwa2b[0:64, j * 128:(j + 1) * 128], lob[0:64, 0:T])
                act(fa(0, j, 0, T), bk(pb, 0, 128, 0, T), AF.Sigmoid, bias=PR(l, "w0", j))
                scan(fa(1, j, 0, T), C(cmn, T), fa(0, j, 0, T))
                pb = nbank()
                mm(bk(pb, 0, 128, 0, T), cwa2b[64:128, j * 128:(j + 1) * 128], lob[64:128, 0:T])
                act(fa(2, j, 0, T), bk(pb, 0, 128, 0, T), AF.Sigmoid, bias=PR(l, "a0", j))
                pb = nbank()
                mm(bk(pb, 0, 128, 0, T), cg2b[:, j * 128:(j + 1) * 128], sgl[:, 0:T])
                cp(ha(5, j, 0, T), bk(pb, 0, 128, 0, T), eng="act")
            for j in range(4):
                csg_j = FA[:, 2048 + j * TP:2048 + j * TP + T]
                act(wc[:, j * 8:j * 8 + nch], (csg_j[:, Cc - 1:T:Cc], 1), AF.Exp, scale=-CDEC)
            wf, nk, _ = W.get("ck")

            def _proj(j_, wf=wf, nk=nk):
                pb_ = nbank()
                fm_proj(pb_, T, wf, nk, j_ * 128, hrhs(T))
                return pb_
            _pbs = {0: _proj(0)}
            for j in range(4):
                if j + 1 < 4:
                    _pbs[j + 1] = _proj(j + 1)
                pb = _pbs.pop(j)
                shifted(pb, 4 + j, t3[:, 0:T])
                ts(t4[:, 0:T], t3[:, 0:T], PR(l, "kk", j), ALU.mult)
                act(sqb[0][:, 0:T], t4[:, 0:T], AF.Square)
                mm(PD[:, 0:T], blkb[:], sqb[0][:, 0:T])
                ts(t5[:, 0:T], PD[:, 0:T], 1e-24, ALU.max)
                act(t5[:, 0:T], t5[:, 0:T], AF.Ln)
                act(t5[:, 0:T], t5[:, 0:T], AF.Exp, scale=-0.5)
                tt(t4[:, 0:T], t4[:, 0:T], t5[:, 0:T], ALU.mult)
                tt(t5[:, 0:T], fa(1, j, 0, T), fa(0, j, 0, T), ALU.subtract)
                act(t5[:, 0:T], t5[:, 0:T], AF.Exp, scale=-CDEC)
                stt(ha(0, j, 0, T), t4[:, 0:T], C("negone"), t5[:, 0:T], ALU.mult, ALU.mult)
                act(t5[:, 0:T], fa(1, j, 0, T), AF.Exp, scale=CDEC)
                tt(t4[:, 0:T], t4[:, 0:T], fa(2, j, 0, T), ALU.mult)
                tt(ha(2, j, 0, T), t4[:, 0:T], t5[:, 0:T], ALU.mult)
                ts(t4[:, 0:T], fa(2, j, 0, T), C("negone"), ALU.add, PR(l, "ka", j), ALU.mult)
                stt(fa(3, j, 0, T), t4[:, 0:T], C("one"), t3[:, 0:T], ALU.add, ALU.mult)
                tt(ha(3, j, 0, T), fa(3, j, 0, T), t5[:, 0:T], ALU.mult)
            wf, nk, _ = W.get("cr")

            def _proj(j_, wf=wf, nk=nk):
                pb_ = nbank()
                fm_proj(pb_, T, wf, nk, j_ * 128, hrhs(T))
                return pb_
            _pbs = {0: _proj(0)}
            for j in range(4):
                if j + 1 < 4:
                    _pbs[j + 1] = _proj(j + 1)
                pb = _pbs.pop(j)
                shifted(pb, j, t3[:, 0:T])
                act(t5[:, 0:T], fa(1, j, 0, T), AF.Exp, scale=-CDEC)
                tt(ha(1, j, 0, T), t3[:, 0:T], t5[:, 0:T], ALU.mult)
                stt(sqb[1][:, 0:T], t3[:, 0:T], PR(l, "rk", j), fa(3, j, 0, T), ALU.mult, ALU.mult)
                mm(PD[:, 0:T], blkb[:], sqb[1][:, 0:T])
                cp(fa(3, j, 0, T), PD[:, 0:T], eng="act")
            wf, nk, _ = W.get("cv")

            def _proj(j_, wf=wf, nk=nk):
                pb_ = nbank()
                fm_proj(pb_, T, wf, nk, j_ * 128, hrhs(T))
                return pb_
            _pbs = {0: _proj(0)}
            for j in range(4):
                if j + 1 < 4:
                    _pbs[j + 1] = _proj(j + 1)
                pb = _pbs.pop(j)
                shifted(pb, 8 + j, t3[:, 0:T])
                cp(ha(4, j, 0, T), t3[:, 0:T], eng="act")
                tt(fa(3, j, 0, T), fa(3, j, 0, T), t3[:, 0:T], ALU.mult)
            chk("C_prep")
            nst = {64: 6, 32: 5}[Cc]
            mskv = C("msk", 256)[0:Cc, :]
            X1 = FA[:, 2048:4096].bitcast(BF16)
            X2 = FA[:, 4096:6144].bitcast(BF16)
            sets = [dict(MM=(MM[:, :], None), Bh=(BhTM[:, :], None), Kh=(KhTM[:, :], None), Vb=(Vb[:, :], None),
                         Vp=(Vpad[:, :], None), Tf=(Tf0[:, :], None)),
                    dict(MM=(X1[0:64, 0:2048], 1), Bh=(X1[0:64, 2048:2560], 1), Kh=(X1[0:64, 2560:3072], 1),
                         Vb=(X1[0:64, 3072:3584], 1), Vp=(X2[0:64, 0:1024], 2), Tf=(X2[0:64, 1024:1536], 2))]

            def V(buf, r0, r1, c0, c1):
                ap, key = buf
                v = ap[r0:r1, c0:c1]
                return v if key is None else (v, key)

            def K_(buf, ap):
                return ap if buf[1] is None else (ap, buf[1])
            mset(V(sets[1]["Vp"], 0, 64, 0, 1024), 0.0)

            def front(t0, ci, S_):
                MMb, Bhb, Khb, Vbb, Vpb, Tfb = S_["MM"], S_["Bh"], S_["Kh"], S_["Vb"], S_["Vp"], S_["Tf"]
                wc0 = wc[:, ci:ci + 1]
                wcb = bass.AP(wc0.tensor, wc0.offset, [[wc0.ap[0][0], 128], [8, 4], [0, Cc]])
                for hi_, (src_slot, dstb) in enumerate(((2, Bhb), (3, Khb))):
                    hv = hbk[:, hi_ * 256:(hi_ + 1) * 256].rearrange("p (q c) -> p q c", q=4)[:, :, 0:Cc]
                    tt(hv, (r3(HA[:, src_slot * 2048:(src_slot + 1) * 2048], 4)[:, :, t0:t0 + Cc], src_slot), wcb, ALU.mult)
                    for q in range(4):
                        tr(PT[0:Cc, q * 128:(q + 1) * 128], hbk[:, hi_ * 256 + q * 64:hi_ * 256 + q * 64 + Cc], identb[:])
                    cp(V(dstb, 0, Cc, 0, 512), PT[0:Cc, 0:512], eng="act")
                    yield
                for q in range(4):
                    tr(PT[0:Cc, q * 128:(q + 1) * 128], ha(4, q, t0, t0 + Cc), identb[:])
                cp(V(Vbb, 0, Cc, 0, 512), PT[0:Cc, 0:512], eng="act")
                for hh in range(2):
                    cp(K_(Vpb, Vpb[0][0:Cc, :].rearrange("p (q h c) -> p q h c", q=4, h=2)[:, :, hh, hh * 64:(hh + 1) * 64]),
                       r3(PT[0:Cc, 0:512], 4)[:, :, hh * 64:(hh + 1) * 64], eng="dve")
                yield
                for q in range(4):
                    for hh in range(2):
                        mt = PA if hh == 0 else PC
                        mh = q // 2
                        b0 = hh * 64
                        a_ap = HA[b0:b0 + 64, 0 * 2048 + q * TP + t0:0 * 2048 + q * TP + t0 + Cc]
                        (pst, _), (st, _) = a_ap.ap
                        rhs2 = bass.AP(a_ap.tensor, a_ap.offset, [[pst, 64], [2048, 2], [st, Cc]])
                        base = mh * 512 + (q % 2) * 256
                        o13 = mt[0:Cc, base:base + 128].rearrange("p (a b) -> p a b", a=2)[:, :, 0:Cc]
                        o24 = mt[0:Cc, base + 128:base + 256].rearrange("p (a b) -> p a b", a=2)[:, :, 0:Cc]
                        P.op("pe", lambda e, o=o13, lh=ha(2, q, t0, t0 + Cc, b0, b0 + 64)[0], rh=rhs2:
                             e.matmul(o, lh, rh, start=True, stop=True), [(HA, 2), (HA, 0), (HA, 1)], [(mt, mh)])
                        P.op("pe", lambda e, o=o24, lh=ha(3, q, t0, t0 + Cc, b0, b0 + 64)[0], rh=rhs2:
                             e.matmul(o, lh, rh, start=True, stop=True), [(HA, 3), (HA, 0), (HA, 1)], [(mt, mh)])
                mm_base = MMb[0][0:Cc, 0:64]
                mm_ps = mm_base.ap[0][0]
                for hh in range(2):
                    mt = PA if hh == 0 else PC
                    for mh in range(2):
                        mview = mt[0:Cc, mh * 512:(mh + 1) * 512].rearrange("p (q m t) -> p q m t", q=2, m=4)[:, :, :, 0:Cc]
                        mskb = bass.AP(mskv.tensor, mskv.offset, [[mskv.ap[0][0], Cc], [0, 2], [64, 4], [1, Cc]])
                        dstv = bass.AP(mm_base.tensor, mm_base.offset + (2 * mh) * 512 + hh * 256,
                                       [[mm_ps, Cc], [512, 2], [64, 4], [1, Cc]])
                        tt(K_(MMb, dstv), (mview, mh), mskb, ALU.mult)
                yield
                for h in range(8):
                    q, b0 = h // 2, (h % 2) * 64
                    bnk = PA if h % 2 == 0 else PC
                    mm((bnk[0:Cc, q * 64:q * 64 + Cc], 0),
                       ha(0, q, t0, t0 + Cc, b0, b0 + 64), ha(2, q, t0, t0 + Cc, b0, b0 + 64))
                at0 = AT_[0][0:Cc, 0:64]
                for hh in range(2):
                    bnk = PA if hh == 0 else PC
                    dstv = bass.AP(at0.tensor, at0.offset + hh * 64, [[at0.ap[0][0], Cc], [128, 4], [1, Cc]])
                    tt(dstv, (r3(bnk[0:Cc, 0:256], 4)[:, :, 0:Cc], 0),
                       bc_mid(C("lsm", 64)[0:Cc, 0:Cc], 4), ALU.mult)
                n1v = K_(MMb, MMb[0][0:Cc, :].rearrange("p (h m t) -> p h m t", h=8, m=4)[:, :, 0, 0:Cc])

                def cbv(p_, part, h0=0, h1=8):
                    return CB[p_][0:Cc, h0 * 128:h1 * 128].rearrange("p (h c) -> p h c", h=h1 - h0)[:, :, part * 64:part * 64 + Cc]

                def pcv(part):
                    return PC[0:Cc, :].rearrange("p (h c) -> p h c", h=8)[:, :, part * 64:part * 64 + Cc]
                cp(cbv(0, 0), n1v, eng="act")
                tt(cbv(1, 1), n1v, bc_mid(identf[0:Cc, 0:Cc], 8), ALU.add)
                yield
                cur = 0
                for r in range(1, nst + 1):
                    nxt = 1 - cur
                    p_, n_ = (r - 1) % 2, r % 2
                    for h in range(8):
                        lh = AT_[cur][0:Cc, h * 64:h * 64 + Cc]
                        if r == 1:
                            mm((PC[0:Cc, h * 128:h * 128 + Cc], h // 4), lh, CB[p_][0:Cc, h * 128:h * 128 + Cc])
                        elif r <= nst - 1:
                            mm((PC[0:Cc, h * 128:(h + 1) * 128].rearrange("p (a c) -> p a c", a=2)[:, :, 0:Cc], h // 4), lh,
                               CB[p_][0:Cc, h * 128:(h + 1) * 128].rearrange("p (a c) -> p a c", a=2)[:, :, 0:Cc])
                        else:
                            mm((PC[0:Cc, h * 128 + 64:h * 128 + 64 + Cc], h // 4), lh, CB[p_][0:Cc, h * 128 + 64:h * 128 + 64 + Cc])
                    if r <= nst - 1:
                        for h in range(8):
                            mm((PA[0:Cc, 512 + h * 64:512 + h * 64 + Cc], 1), CB[p_][0:Cc, h * 128:h * 128 + Cc],
                               AT_[cur][0:Cc, h * 64:h * 64 + Cc])
                    if r >= 2:
                        dst_t = cbv(n_, 1) if r < nst else K_(Tfb, r3(Tfb[0][0:Cc, :], 8)[:, :, 0:Cc])
                        tt(dst_t, cbv(p_, 1), pcv(1), ALU.add)
                    if r <= nst - 1:
                        cp(cbv(n_, 0), pcv(0), eng="act")
                        cp(r3(AT_[nxt][0:Cc, :], 8)[:, :, 0:Cc], (r3(PA[0:Cc, 512:1024], 8)[:, :, 0:Cc], 1), eng="act")
                        cur = nxt
                    yield

            def chain(t0, ci, S_):
                MMb, Bhb, Khb, Vbb, Vpb, Tfb = S_["MM"], S_["Bh"], S_["Kh"], S_["Vb"], S_["Vp"], S_["Tf"]
                for q in range(4):
                    mm((PB[0:Cc, q * 128:(q + 1) * 128], 0), ha(0, q, t0, t0 + Cc), Srwb[:, q * 128:(q + 1) * 128],
                       start=True, stop=False)
                    for hh in range(2):
                        mm((PB[0:Cc, q * 128:(q + 1) * 128], 0),
                           V(MMb, 0, Cc, q * 512 + hh * 256 + 128, q * 512 + hh * 256 + 128 + Cc),
                           V(Vpb, 0, Cc, (q * 2 + hh) * 128, (q * 2 + hh + 1) * 128), start=False, stop=(hh == 1))
                for hh in range(2):
                    cp(Gpad[0:Cc, :].rearrange("p (q h c) -> p q h c", q=4, h=2)[:, :, hh, hh * 64:(hh + 1) * 64],
                       (r3(PB[0:Cc, 0:512], 4)[:, :, hh * 64:(hh + 1) * 64], 0), eng="act" if hh else "dve")
                yield
                for q in range(4):
                    for hh in range(2):
                        h = q * 2 + hh
                        mm((PB[0:Cc, 512 + q * 128:512 + (q + 1) * 128], 1), V(Tfb, 0, Cc, h * 64, h * 64 + Cc),
                           Gpad[0:Cc, h * 128:(h + 1) * 128], start=(hh == 0), stop=(hh == 1))
                cp(Ub[0:Cc, :], (PB[0:Cc, 512:1024], 1), eng="act")
                for hh in range(2):
                    cp(Upad[0:Cc, :].rearrange("p (q h c) -> p q h c", q=4, h=2)[:, :, hh, hh * 64:(hh + 1) * 64],
                       (r3(PB[0:Cc, 512:1024], 4)[:, :, hh * 64:(hh + 1) * 64], 1), eng="dve")
                yield
                for q in range(4):
                    oy = PD[:, q * 64:q * 64 + Cc]
                    mm(oy, Srwb[:, q * 128:(q + 1) * 128], ha(1, q, t0, t0 + Cc), start=True, stop=False)
                    for hh in range(2):
                        h = q * 2 + hh
                        mm(oy, Upad[0:Cc, h * 128:(h + 1) * 128],
                           V(MMb, 0, Cc, q * 512 + hh * 256 + 64, q * 512 + hh * 256 + 64 + Cc), start=False, stop=False)
                        mm(oy, V(Vpb, 0, Cc, h * 128, (h + 1) * 128),
                           V(MMb, 0, Cc, q * 512 + hh * 256 + 192, q * 512 + hh * 256 + 192 + Cc), start=False, stop=(hh == 1))
                cp((r3(FA[:, 0:2048], 4)[:, :, t0:t0 + Cc], 0), r3(PD[:, 0:256], 4)[:, :, 0:Cc], eng="act")
                yield
                for q in range(4):
                    mm(PD[:, q * 128:(q + 1) * 128], V(Bhb, 0, Cc, q * 128, (q + 1) * 128), Ub[0:Cc, q * 128:(q + 1) * 128],
                       start=True, stop=False)
                    mm(PD[:, q * 128:(q + 1) * 128], V(Khb, 0, Cc, q * 128, (q + 1) * 128), V(Vbb, 0, Cc, q * 128, (q + 1) * 128),
                       start=False, stop=True)
                tt(r3(stmp[:], 4), r3(PD[:], 4), bc_mid(blkf, 4), ALU.mult)
                for q in range(4):
                    stt(Srw[:, q * 128:(q + 1) * 128], Srw[:, q * 128:(q + 1) * 128], wc[:, q * 8 + ci:q * 8 + ci + 1],
                        stmp[:, q * 128:(q + 1) * 128], ALU.mult, ALU.add)
                cp(Srwb[:], Srw[:], eng="act")
                yield

            def state_in(sg):
                if sg["rw_in"] == "zero":
                    mset(Srw[:], 0.0)
                    mset(Srwb[:], 0.0)
                elif sg["rw_in"] is not None:
                    mset(Srw[:], 0.0)
                    dma(r3(rwst, 8), srw[l, sg["rw_in"]].rearrange("h i j -> i h j"))
                    for q in range(4):
                        tr(PD[:, q * 64:(q + 1) * 64], rwst[:, q * 128:(q + 1) * 128], identf[0:64, 0:64])
                    for hh in range(2):
                        cp(r3(Srw[hh * 64:(hh + 1) * 64, :], 4)[:, :, hh * 64:(hh + 1) * 64],
                           r3(PD[hh * 64:(hh + 1) * 64, 0:256], 4), eng="act")
                    cp(Srwb[:], Srw[:], eng="act")

            def state_out(sg):
                if sg["rw_out"] is not None:
                    for hh in range(2):
                        cp(r3(rwcp[hh * 64:(hh + 1) * 64, :], 4), r3(Srw[hh * 64:(hh + 1) * 64, :], 4)[:, :, hh * 64:(hh + 1) * 64],
                           eng="pool")
                    for q in range(4):
                        tr(PD[0:64, q * 128:(q + 1) * 128], rwcp[:, q * 64:(q + 1) * 64], identf)
                    cp(rwst, PD[0:64, :], eng="act")
                    dma(sg["rw_out"].rearrange("h i j -> i h j"), r3(rwst, 8))

            items = [(sg, t0, ci, k == 0, k == len(sg["chunks"]) - 1) for sg in G["segs"]
                     for k, (t0, ci) in enumerate(sg["chunks"])]
            for _ in front(items[0][1], items[0][2], sets[0]):
                pass
            for n_, (sg, t0, ci, first, last_) in enumerate(items):
                gf = front(items[n_ + 1][1], items[n_ + 1][2], sets[(n_ + 1) % 2]) if n_ + 1 < len(items) else iter(())
                if first:
                    state_in(sg)
                gc = chain(t0, ci, sets[n_ % 2])
                alive_f = alive_c = True
                step = 0
                while alive_f or alive_c:
                    if alive_f:
                        alive_f = next(gf, "END") != "END"
                    if alive_c and (step % 2 == 1 or not alive_f):
                        alive_c = next(gc, "END") != "END"
                    step += 1
                if last_:
                    state_out(sg)
            chk("C_chunks")
            for q in range(4):
                cp(sqb[0][:, 0:T], fa(0, q, 0, T), eng="pool")
                mm(PD[:, 0:T], blkb[:], sqb[0][:, 0:T])
                stt(t3[:, 0:T], PD[:, 0:T], -1.0 / 64, fa(0, q, 0, T), ALU.mult, ALU.add)
                act(sqb[1][:, 0:T], t3[:, 0:T], AF.Square)
                mm(PD[:, 0:T], blkb[:], sqb[1][:, 0:T])
                act(t4[:, 0:T], PD[:, 0:T], AF.Ln, bias=C("gneps"), scale=1.0 / 64)
                act(t4[:, 0:T], t4[:, 0:T], AF.Exp, scale=-0.5)
                tt(t3[:, 0:T], t3[:, 0:T], t4[:, 0:T], ALU.mult)
                ts(t3[:, 0:T], t3[:, 0:T], PR(l, "lnw", q), ALU.mult, PR(l, "lnb", q), ALU.add)
                tt(t3[:, 0:T], t3[:, 0:T], fa(3, q, 0, T), ALU.add)
                tt(ocT[:, q * TP:q * TP + T], t3[:, 0:T], ha(5, q, 0, T), ALU.mult)

        def phase_G(l, T):
            for b in range(3):
                for half, tg in enumerate((f"gl{b}a", f"gl{b}b")):
                    wf, nk, _ = W.get(tg)
                    for j in range(4):
                        c = half * 4 + j
                        pb = nbank()
                        fm_proj(pb, T, wf, nk, j * 128, hrhs(T))
                        act(ha(c // 4, c % 4, 0, T), bk(pb, 0, 128, 0, T), AF.Sigmoid)
                wf, nk, _ = W.get(f"br{b}")
                for c in range(8):
                    pb = nbank()
                    fm_proj(pb, T, wf, nk, c * 128, lambda kc: oT3[b][:, kc * TP:kc * TP + T])
                    if b == 0:
                        tt(fa(c // 4, c % 4, 0, T), bk(pb, 0, 128, 0, T), ha(c // 4, c % 4, 0, T), ALU.mult)
                    else:
                        tt(t4[:, 0:T], bk(pb, 0, 128, 0, T), ha(c // 4, c % 4, 0, T), ALU.mult)
                        tt(fa(c // 4, c % 4, 0, T), fa(c // 4, c % 4, 0, T), t4[:, 0:T], ALU.add, eng="pool")
            for c in range(8):
                cp(ha(2 + c // 4, c % 4, 0, T), fa(c // 4, c % 4, 0, T), eng="act" if c % 2 else "pool")
            for half, tg in enumerate(("outa", "outb")):
                wf, nk, _ = W.get(tg)
                for j in range(4):
                    c = half * 4 + j
                    pb = nbank()
                    fm_proj(pb, T, wf, nk, j * 128, lambda kc: ha(2 + kc // 4, kc % 4, 0, T))
                    tt((xT[:, c * TP:c * TP + T], c), (xT[:, c * TP:c * TP + T], c), bk(pb, 0, 128, 0, T), ALU.add)

        def phase_F(l, T):
            norm_to_hT(l, T, "nffn")
            for i in range(6):
                wg, nk, ncol = W.get(f"fg{i}")
                wu, _, _ = W.get(f"fu{i}")
                for j in range(ncol // 128):
                    c = i * 4 + j
                    pg = nbank()
                    fm_proj(pg, T, wg, nk, j * 128, hrhs(T))
                    pu = nbank()
                    fm_proj(pu, T, wu, nk, j * 128, hrhs(T))
                    act(t4[:, 0:T], bk(pg, 0, 128, 0, T), AF.Silu)
                    tt(ha(c // 4, c % 4, 0, T), t4[:, 0:T], bk(pu, 0, 128, 0, T), ALU.mult)
            for j in range(8):
                wf, nk, _ = W.get(f"fd{j}")
                pb = nbank()
                fm_proj(pb, T, wf, nk, 0, lambda kc: ha(kc // 4, kc % 4, 0, T))
                tt((xT[:, j * TP:j * TP + T], j), (xT[:, j * TP:j * TP + T], j), bk(pb, 0, 128, 0, T), ALU.add)

        def phase_P(l, T, psrc):
            for ti in range((T + 127) // 128):
                dma(xin[:, 0:PLE], psrc[ti * 128:(ti + 1) * 128, :])
                pb = nbank()
                for k in range(2):
                    tr(bk(pb, 0, 128, k * 128, (k + 1) * 128), xin[:, k * 128:(k + 1) * 128], identf)
                cp((r3(HA[:, 4 * 2048:4 * 2048 + 2 * TP], 2)[:, :, ti * 128:(ti + 1) * 128], 4), (r3(pb[0][:, 0:256], 2), pb[1]),
                   eng="act")
            wf, nk, _ = W.get("pp")
            for c in range(8):
                pb = nbank()
                fm_proj(pb, T, wf, nk, c * 128, lambda kc: ha(4, kc, 0, T))
                cp(fa(c // 4, c % 4, 0, T), bk(pb, 0, 128, 0, T), eng="act")
            rms_stats(T, t3[:, 0:T], lambda kc: fa(kc // 4, kc % 4, 0, T), 8, float(D), "eps6")
            for c in range(8):
                stt(fa(c // 4, c % 4, 0, T), fa(c // 4, c % 4, 0, T), PR(l, "nple", c), t3[:, 0:T], ALU.mult, ALU.mult,
                    eng="dve" if c % 2 else "pool")
            norm_to_hT(l, T, None)
            for half, tg in enumerate(("pga", "pgb")):
                wf, nk, _ = W.get(tg)
                for j in range(4):
                    c = half * 4 + j
                    pb = nbank()
                    fm_proj(pb, T, wf, nk, j * 128, hrhs(T))
                    tg = t4 if c % 2 == 0 else t5
                    act(tg[:, 0:T], bk(pb, 0, 128, 0, T), AF.Sigmoid)
                    tt(tg[:, 0:T], tg[:, 0:T], fa(c // 4, c % 4, 0, T), ALU.mult)
                    tt((xT[:, c * TP:c * TP + T], c), (xT[:, c * TP:c * TP + T], c), tg[:, 0:T], ALU.add,
                       eng="dve" if c % 2 == 0 else "pool")

        KSTOP = _os.environ.get("KSTOP", "")

        class _Stop(Exception):
            pass

        def chk(tag):
            if KSTOP == tag:
                raise _Stop()

        def layer(l, T, G):
            chk("load")
            norm_to_hT(l, T, "nmix")
            chk("norm")
            mixer_A(l, T, G)
            chk("A")
            mixer_B(l, T, G)
            chk("B")
            mixer_C(l, T, G)
            chk("C")
            if dbg and G.get("dbg"):
                for b, nm in enumerate(("d_oa", "d_ob", "d_oc")):
                    dma(dbg_t[nm], oT3[b][:], eng="pool")
            phase_G(l, T)
            chk("G")
            phase_F(l, T)
            chk("F")
            phase_P(l, T, G["psrc"])
            chk("P")

        def x_in(l, src_tm, src_fm, T, key):
            if l == 0:
                load_x_tm(src_tm, T)
            else:
                dma(r3(xT[:], KC)[:, :, 0:T], src_fm.rearrange("k p t -> p k t"), rkey=key)

        def x_out(l, dst_tm, dst_fm, T, key):
            if l == DEPTH - 1:
                store_x_tm(dst_tm, T)
            else:
                dma(dst_fm.rearrange("k p t -> p k t"), r3(xT[:], KC)[:, :, 0:T], wkey=key)

        for l in range(DEPTH):
            for g in range(NG + 1):
                W.plan(l)

        def _main():
          for l in range(DEPTH):
              dma(cwa2b[:], cwa2[l], eng="pool")
              dma(cg2b[:], cg2[l], eng="pool")
              dma(biasT[:], bias_d[l], eng="pool")
              mset(shcol[:], 0.0)
              for g in range(NG):
                  kslot = g % 2
                  last = (g == NG - 1)
                  T = TP
                  x_in(l, xp[g * TP:(g + 1) * TP, :], x1p[:, :, g * TP:(g + 1) * TP], T, "x1p:%d" % (g % 4))
                  qbs = []
                  for m_ in range(4):
                      kts = []
                      for kt in range(5):
                          wcd = 128 * (m_ + kt)
                          if wcd < 512:
                              if g == 0:
                                  continue
                              slot, off = 1 - kslot, wcd
                          else:
                              slot, off = kslot, wcd - 512
                          kts.append((slot * 512 + off, 128, slot * 4 + off // 128, kt))
                      qbs.append(dict(q0=128 * m_, nq=128, kts=kts))
                  tmt = [(ti * 128, 128, ti, 128, rope_p[g * TP + ti * 128:g * TP + (ti + 1) * 128, :]) for ti in range(4)]

                  def prevcol(c):
                      return shcol[:, c:c + 1].rearrange("p (s u) -> p s u", s=1)

                  def savecol(c, r_seg):
                      cp(shcol[:, c:c + 1], r_seg[:, 0, TP:TP + 1], eng="act")
                  G = dict(kslot=kslot, vt=[(ti * 128, 128, kslot * 4 + ti) for ti in range(4)],
                           a_out=([(akp[l, ti * 128:(ti + 1) * 128, :], avp[l, ti * 128:(ti + 1) * 128, :], ti * 128, 128)
                                   for ti in range(4)] if last else []),
                           tm=tmt,
                           segs=[dict(qblocks=qbs, tm=tmt, state_in=("zero" if g == 0 else None),
                                      ret_out=(retp[l] if last else None),
                                      rw_in=("zero" if g == 0 else None), rw_out=(rwp[l] if last else None),
                                      chunks=[(64 * ci, ci) for ci in range(8)])],
                           nseg=1, Ls=TP, Cc=64, prevcol=prevcol, savecol=[savecol],
                           psrc=pp[l, g * TP:(g + 1) * TP, :], dbg=(l == 0 and g == 0))
                  layer(l, T, G)
                  if last:
                      dma(shp[l].rearrange("(c p) -> p c", p=128), shcol[:], slow=True)
                  if dbg and l == 0 and g == 0:
                      dma(dbg_t["d_x"], xT[:])
                  x_out(l, yp[g * TP:(g + 1) * TP, :], x1p[:, :, g * TP:(g + 1) * TP], T, "x1p:%d" % (g % 4))
              T = TSM
              x_in(l, xs, x1s, T, "x1s")
              segs = []
              tmt = []
              dma(shin[:, 0:14 * NSMP].rearrange("p (s c) -> p s c", s=NSMP), ssh[l].rearrange("s (c p) -> p s c", p=128), slow=True)
              for s in range(NSMP):
                  tms = [(LS * s, LS, s, 32, rope_s[LS * s:LS * (s + 1), :])]
                  tmt += tms
                  kts = [(kt * 128, 128, kt, kt) for kt in range(4)] + [(512 + LS * s, LS, 4 + s, 4)]
                  segs.append(dict(cache=s, qblocks=[dict(q0=LS * s, nq=LS, kts=kts)], tm=tms, state_in=s, ret_out=rets[l, s],
                                   rw_in=s, rw_out=rws[l, s], chunks=[(LS * s, s)]))

              def prevcol_s(c):
                  return shin[:, 0:14 * NSMP].rearrange("p (s c) -> p s c", s=NSMP)[:, :, c:c + 1]

              def savecol_s(c, r_seg):
                  cp(shout[:, 0:14 * NSMP].rearrange("p (s c) -> p s c", s=NSMP)[:, :, c:c + 1], r_seg[:, :, LS:LS + 1], eng="act")
              G = dict(kslot=1, vt=[(LS * s, LS, 4 + s) for s in range(NSMP)],
                       a_out=[(aks[l, LS * s:LS * (s + 1), :], avs[l, LS * s:LS * (s + 1), :], LS * s, LS) for s in range(NSMP)],
                       tm=tmt, segs=segs, nseg=NSMP, Ls=LS, Cc=32, prevcol=prevcol_s, savecol=[savecol_s],
                       psrc=psm[l], dbg=False)
              layer(l, T, G)
              dma(shs[l].rearrange("s (c p) -> p s c", p=128), shout[:, 0:14 * NSMP].rearrange("p (s c) -> p s c", s=NSMP), slow=True)
              x_out(l, ys, x1s, T, "x1s")

        try:
            _main()
        except _Stop:
            pass
        P.emit(nc, es)
    return nc, P


_NC_CACHE = {}


def _run(inp, ncores, dbg=False):
    xpr = np.asarray(inp["x_prompt"], np.float32)
    B, SEQ, _ = xpr.shape
    xsm = np.asarray(inp["x_sample"], np.float32)
    assert xsm.shape[0] == ncores * NSMP and xsm.shape[1] == LS and B <= ncores
    key = (SEQ, dbg)
    if key not in _NC_CACHE:
        _NC_CACHE[key] = build(SEQ, dbg=dbg)[0]
    nc = _NC_CACHE[key]
    cst, rope_p, rope_s = host_consts(SEQ)
    f = lambda k: np.ascontiguousarray(np.asarray(inp[k], np.float32))
    prm, cwa2 = host_params({k: f(k) for k in ("norm_mix", "norm_ffn", "ple_norm", "a_q_norm", "a_k_norm", "c_shift_mu",
                                                "c_w0", "c_a0", "c_k_k", "c_k_a", "c_ln_w", "c_ln_b", "c_r_k", "c_w2", "c_a2")})
    biasT = host_bias(f("a_rel_bias")).reshape(DEPTH, 128, 5 * 1024)
    shared = dict(w_in=f("w_in"), w_br=f("w_branch"), w_out=f("w_out"), w_fg=f("w_ffn_gate"), w_fu=f("w_ffn_up"),
                  w_fd=f("w_ffn_down"), w_pp=f("w_ple_proj"), w_pg=f("w_ple_gate"), cg2=f("c_g2"), cwa2=cwa2, prm=prm,
                  cst=cst, biasT=biasT, rope_p=rope_p, rope_s=rope_s)
    pp_ = f("p_prompt"); ps_ = f("p_sample"); cak = f("cache_a_k"); cav = f("cache_a_v")
    sr = f("state_ret"); sw = f("state_rwkv"); sh = f("state_rwkv_shift")
    zx = np.zeros((SEQ, D), np.float32); zp = np.zeros((DEPTH, SEQ, PLE), np.float32)
    in_maps = []
    for c in range(ncores):
        m = dict(shared)
        if c < B:
            m["xp"] = np.ascontiguousarray(xpr[c]); m["pp"] = np.ascontiguousarray(pp_[:, c])
        else:
            m["xp"] = zx; m["pp"] = zp
        sl = slice(c * NSMP, (c + 1) * NSMP)
        m["xs"] = np.ascontiguousarray(xsm[sl].reshape(NSMP * LS, D))
        m["psm"] = np.ascontiguousarray(ps_[:, sl].reshape(DEPTH, NSMP * LS, PLE))
        m["cak"] = np.ascontiguousarray(cak[:, sl].reshape(DEPTH, NSMP, 512, 512))
        m["cav"] = np.ascontiguousarray(cav[:, sl].reshape(DEPTH, NSMP, 512, 512))
        m["sret"] = np.ascontiguousarray(sr[:, sl]); m["srw"] = np.ascontiguousarray(sw[:, sl])
        m["ssh"] = np.ascontiguousarray(sh[:, sl].reshape(DEPTH, NSMP, 1792))
        in_maps.append(m)
    res = run_bass_kernel_spmd(nc, in_maps, core_ids=list(range(ncores))).results
    R = lambda k, cs: [np.asarray(res[c][k], np.float32) for c in cs]
    pc = list(range(B)); ac = list(range(ncores))
    NB = ncores * NSMP
    out = (
        np.stack(R("yp", pc)),
        np.concatenate(R("ys", ac)).reshape(NB, LS, D),
        np.stack(R("akp", pc), axis=1).reshape(DEPTH, B, 512, 8, 64),
        np.stack(R("avp", pc), axis=1).reshape(DEPTH, B, 512, 8, 64),
        np.stack(R("retp", pc), axis=1),
        np.stack(R("rwp", pc), axis=1),
        np.stack(R("shp", pc), axis=1).reshape(DEPTH, B, 1, 1792),
        np.concatenate([r.reshape(DEPTH, NSMP, LS, 8, 64) for r in R("aks", ac)], axis=1),
        np.concatenate([r.reshape(DEPTH, NSMP, LS, 8, 64) for r in R("avs", ac)], axis=1),
        np.concatenate(R("rets", ac), axis=1),
        np.concatenate(R("rws", ac), axis=1),
        np.concatenate(R("shs", ac), axis=1).reshape(DEPTH, NB, 1, 1792),
    )
    if dbg:
        return out, res
    return out


def kernel(**inputs):
    return _run(inputs, 8)
```
